# Optimizing a Trainium2 kernel written in Bass

```python
import math
import jax, jax.numpy as jnp
from jax import lax
import numpy as np

D_MODEL = 1024
BATCH = 8
SEQ = 2048
DEPTH = 1
DEC_BATCH = 128
DEC_SEQ = 4
PAST_LEN = 16384
PAGE_SIZE = 128

W_MIX = D_MODEL
W_SSM = W_MIX // 2
W_CONV = W_MIX - W_SSM
GROUP_P = 16
N_GROUPS = W_SSM // GROUP_P
N_STATE = 64
CONV_K = 31
IN_COLS = 2 * W_SSM + 3 * W_CONV
ALPHA = (2.0 * DEPTH) ** 0.25
BETA = (8.0 * DEPTH) ** -0.25
LN_EPS = 1e-5
DT_MIN = 0.001
DT_MAX = 0.1

kernel_name = "hymba_s5_conformer_conv_deepnorm_step"


def _layernorm(x, g, b):
    xf = x.astype(jnp.float32)
    mu = jnp.mean(xf, axis=-1, keepdims=True)
    var = jnp.mean(jnp.square(xf - mu), axis=-1, keepdims=True)
    y = (xf - mu) * lax.rsqrt(var + LN_EPS) * g.astype(jnp.float32) + b.astype(jnp.float32)
    return y.astype(x.dtype)


def _scan_combine(e1, e2):
    a1, b1 = e1
    a2, b2 = e2
    return a1 * a2, a2 * b1 + b2


def _s5(u, h0, lam_re, lam_im, log_dt, b_re, b_im, c_re, c_im, d_skip):
    bsz, length, _ = u.shape
    f32 = jnp.float32
    uf = u.astype(f32).reshape(bsz, length, N_GROUPS, GROUP_P)
    lam = lax.complex(lam_re.astype(f32), lam_im.astype(f32))
    dt = jnp.exp(log_dt.astype(f32))[:, None]
    lam_bar = jnp.exp(lam * dt)
    bmat = lax.complex(b_re.astype(f32), b_im.astype(f32))
    b_bar = ((lam_bar - 1.0) / lam)[..., None] * bmat
    bu = jnp.einsum('blgp,gnp->blgn', uf.astype(jnp.complex64), b_bar)
    bu = bu.at[:, 0].add(lam_bar[None] * h0)
    a = jnp.broadcast_to(lam_bar, bu.shape)
    _, h = lax.associative_scan(_scan_combine, (a, bu), axis=1)
    cmat = lax.complex(c_re.astype(f32), c_im.astype(f32))
    y = jnp.real(jnp.einsum('blgn,gpn->blgp', h, cmat)) + uf * d_skip.astype(f32).reshape(N_GROUPS, GROUP_P)
    return y.reshape(bsz, length, W_SSM), h[:, -1]


def _conformer_conv(a, b, buf, w_dw, b_dw, g_ln, b_ln, w_pw2, b_pw2):
    v = a * jax.nn.sigmoid(b)
    full = jnp.concatenate([buf.astype(v.dtype), v], axis=1)
    new_buf = full[:, -(CONV_K - 1):]
    out = lax.conv_general_dilated(
        full, w_dw.astype(full.dtype)[:, None, :], window_strides=(1,), padding='VALID',
        dimension_numbers=('NWC', 'WIO', 'NWC'), feature_group_count=W_CONV) + b_dw
    out = _layernorm(out, g_ln, b_ln)
    out = jax.nn.silu(out)
    out = out @ w_pw2 + b_pw2
    return out, new_buf


def _layer(x, h0, conv_buf, w_in, b_in, lam_re, lam_im, log_dt, b_re, b_im, c_re, c_im,
           d_skip, w_glu, b_glu, w_dw, b_dw, g_conv_ln, b_conv_ln, w_pw2, b_pw2,
           w_out, g_post, b_post):
    z = x @ w_in + b_in
    u_s, g_s, a_c, b_c, g_c = jnp.split(
        z, [W_SSM, 2 * W_SSM, 2 * W_SSM + W_CONV, 2 * W_SSM + 2 * W_CONV], axis=-1)
    y_s, h_last = _s5(u_s, h0, lam_re, lam_im, log_dt, b_re, b_im, c_re, c_im, d_skip)
    s = jax.nn.gelu(y_s).astype(x.dtype)
    s = s * jax.nn.sigmoid(s @ w_glu + b_glu)
    s = s * jax.nn.silu(g_s)
    c, new_buf = _conformer_conv(a_c, b_c, conv_buf, w_dw, b_dw, g_conv_ln, b_conv_ln, w_pw2, b_pw2)
    c = c * jax.nn.silu(g_c)
    mix = jnp.concatenate([s, c.astype(x.dtype)], axis=-1) @ w_out
    y = _layernorm(ALPHA * x + mix, g_post, b_post)
    return y, h_last, new_buf


def setup_inputs(seed: int = 0) -> dict:
    key = jax.random.key(seed)
    ks = jax.random.split(key, 32)
    f32 = jnp.float32
    nrm = lambda k, shape, s: (jax.random.normal(k, shape, f32) * s)
    n_idx = jnp.arange(N_STATE, dtype=f32)
    lam_re = -0.5 * jnp.ones((DEPTH, N_GROUPS, N_STATE), f32) + nrm(ks[5], (DEPTH, N_GROUPS, N_STATE), 0.01)
    lam_im = jnp.pi * n_idx[None, None, :] + nrm(ks[6], (DEPTH, N_GROUPS, N_STATE), 0.01)
    log_dt = jax.random.uniform(ks[7], (DEPTH, N_GROUPS), f32, math.log(DT_MIN), math.log(DT_MAX))
    bscale = (2.0 * GROUP_P) ** -0.5
    cscale = (2.0 * N_STATE) ** -0.5
    return {
        "x_prompt": nrm(ks[0], (BATCH, SEQ, D_MODEL), 1.0),
        "x_sample": nrm(ks[1], (DEC_BATCH, DEC_SEQ, D_MODEL), 1.0),
        "state_ssm_re": nrm(ks[2], (DEPTH, DEC_BATCH, N_GROUPS, N_STATE), 0.5),
        "state_ssm_im": nrm(ks[3], (DEPTH, DEC_BATCH, N_GROUPS, N_STATE), 0.5),
        "state_conv": nrm(ks[4], (DEPTH, DEC_BATCH, CONV_K - 1, W_CONV), 1.0),
        "w_in": nrm(ks[8], (DEPTH, D_MODEL, IN_COLS), D_MODEL ** -0.5),
        "b_in": nrm(ks[9], (DEPTH, IN_COLS), 0.01),
        "lam_re": lam_re,
        "lam_im": lam_im,
        "log_dt": log_dt,
        "b_re": nrm(ks[10], (DEPTH, N_GROUPS, N_STATE, GROUP_P), bscale),
        "b_im": nrm(ks[11], (DEPTH, N_GROUPS, N_STATE, GROUP_P), bscale),
        "c_re": nrm(ks[12], (DEPTH, N_GROUPS, GROUP_P, N_STATE), cscale),
        "c_im": nrm(ks[13], (DEPTH, N_GROUPS, GROUP_P, N_STATE), cscale),
        "d_skip": nrm(ks[14], (DEPTH, W_SSM), 1.0),
        "w_glu": nrm(ks[15], (DEPTH, W_SSM, W_SSM), W_SSM ** -0.5),
        "b_glu": nrm(ks[16], (DEPTH, W_SSM), 0.01),
        "w_dw": nrm(ks[17], (DEPTH, CONV_K, W_CONV), CONV_K ** -0.5),
        "b_dw": nrm(ks[18], (DEPTH, W_CONV), 0.01),
        "g_conv_ln": 1.0 + nrm(ks[19], (DEPTH, W_CONV), 0.02),
        "b_conv_ln": nrm(ks[20], (DEPTH, W_CONV), 0.01),
        "w_pw2": nrm(ks[21], (DEPTH, W_CONV, W_CONV), W_CONV ** -0.5),
        "b_pw2": nrm(ks[22], (DEPTH, W_CONV), 0.01),
        "w_out": nrm(ks[23], (DEPTH, W_MIX, D_MODEL), BETA * W_MIX ** -0.5),
        "g_post": 1.0 + nrm(ks[24], (DEPTH, D_MODEL), 0.02),
        "b_post": nrm(ks[25], (DEPTH, D_MODEL), 0.01),
    }


def reference(x_prompt, x_sample, state_ssm_re, state_ssm_im, state_conv, w_in, b_in,
              lam_re, lam_im, log_dt, b_re, b_im, c_re, c_im, d_skip, w_glu, b_glu,
              w_dw, b_dw, g_conv_ln, b_conv_ln, w_pw2, b_pw2, w_out, g_post, b_post):
    f32 = jnp.float32
    hp = x_prompt
    hs = x_sample
    re_p, im_p, cv_p, re_s, im_s, cv_s = [], [], [], [], [], []
    for layer in range(DEPTH):
        params = (w_in[layer], b_in[layer], lam_re[layer], lam_im[layer], log_dt[layer],
                  b_re[layer], b_im[layer], c_re[layer], c_im[layer], d_skip[layer],
                  w_glu[layer], b_glu[layer], w_dw[layer], b_dw[layer], g_conv_ln[layer],
                  b_conv_ln[layer], w_pw2[layer], b_pw2[layer], w_out[layer],
                  g_post[layer], b_post[layer])
        h0_p = jnp.zeros((hp.shape[0], N_GROUPS, N_STATE), jnp.complex64)
        buf_p = jnp.zeros((hp.shape[0], CONV_K - 1, W_CONV), hp.dtype)
        hp, hl_p, nb_p = _layer(hp, h0_p, buf_p, *params)
        h0_s = lax.complex(state_ssm_re[layer].astype(f32), state_ssm_im[layer].astype(f32))
        hs, hl_s, nb_s = _layer(hs, h0_s, state_conv[layer], *params)
        re_p.append(jnp.real(hl_p)); im_p.append(jnp.imag(hl_p)); cv_p.append(nb_p)
        re_s.append(jnp.real(hl_s)); im_s.append(jnp.imag(hl_s)); cv_s.append(nb_s)
    return (hp, hs,
            jnp.stack(re_p), jnp.stack(im_p), jnp.stack(cv_p),
            jnp.stack(re_s), jnp.stack(im_s), jnp.stack(cv_s))
```

```python
import numpy as np
import ml_dtypes
from contextlib import ExitStack
import concourse.bass as bass
import concourse.mybir as mybir
from concourse.bass_utils import run_bass_kernel_spmd

F32 = mybir.dt.float32
BF16 = mybir.dt.bfloat16
I32 = mybir.dt.int32
AF = mybir.ActivationFunctionType
ALU = mybir.AluOpType

NCORES = 8
DM = 1024
NP = 2048
NSEQ = 16
NS = 64
NT = NP + NS
KP = 256
KC = KP + NSEQ
KV = KP + 5 * NSEQ
ALPHA = 2.0 ** 0.25
EPS = 1e-5
TWO_PI = 6.283185307179586
BLOCKS = [(0, 512), (512, 512), (1024, 512), (1536, 512), (2048, 64)]


class KB:
    def __init__(self, nc, es):
        self.nc = nc
        self.E = {"pe": nc.tensor, "act": nc.scalar, "dve": nc.vector, "pool": nc.gpsimd, "sp": nc.sync}
        self.sem = {}
        self.cnt = {}
        for k in self.E:
            self.sem[k] = es.enter_context(nc.semaphore("sem_" + k))
            self.cnt[k] = 0
        self.NDS = 20
        self.dq = ("sp", "pool", "act")
        self.dsem = {q: [es.enter_context(nc.semaphore(f"d_{q}{i}")) for i in range(self.NDS)] for q in self.dq}
        self.dval = {q: [0] * self.NDS for q in self.dq}
        self.drr = {q: 0 for q in self.dq}
        self.seen = {k: {} for k in self.E}
        self.lw = {}
        self.rd = {}
        self.pend = {k: [] for k in self.E}

    def _semobj(self, sk):
        return self.sem[sk] if isinstance(sk, str) else self.dsem[sk[0]][sk[1]]

    def _wait(self, eng, sk, val):
        if val <= 0:
            return
        if eng == "pe" and sk == "pe":
            return
        if self.seen[eng].get(sk, 0) >= val:
            return
        self.E[eng].wait_ge(self._semobj(sk), val)
        self.seen[eng][sk] = val

    def _deps(self, eng, r, w):
        for k in r:
            t = self.lw.get(k)
            if t:
                self._wait(eng, *t)
        for k in w:
            t = self.lw.get(k)
            if t:
                self._wait(eng, *t)
            for sk, v in self.rd.get(k, {}).items():
                self._wait(eng, sk, v)

    def _commit(self, t, r, w):
        for k in w:
            self.lw[k] = t
            self.rd[k] = {}
        for k in r:
            d = self.rd.setdefault(k, {})
            d[t[0]] = max(d.get(t[0], 0), t[1])

    def op(self, eng, fn, r=(), w=(), signal=True):
        r = list(r)
        w = list(w)
        self._deps(eng, r, w)
        ins = fn(self.E[eng])
        if not signal:
            self.pend[eng].append((r, w))
            return None
        self.cnt[eng] += 1
        ins.then_inc(self.sem[eng], 1)
        t = (eng, self.cnt[eng])
        for (pr, pw) in self.pend[eng]:
            self._commit(t, pr, pw)
        self.pend[eng] = []
        self._commit(t, r, w)
        return t

    def dma(self, q, out, in_, r=(), w=()):
        r = list(r)
        w = list(w)
        self._deps(q, r, w)
        i = self.drr[q] % self.NDS
        self.drr[q] += 1
        self._wait(q, (q, i), self.dval[q][i])
        ins = self.E[q].dma_start(out=out, in_=in_)
        self.dval[q][i] += 16
        ins.then_inc(self.dsem[q][i], 16)
        t = ((q, i), self.dval[q][i])
        self._commit(t, r, w)
        return t

    def barrier(self, engines=None, dma_queues=None):
        engines = engines or list(self.E)
        dma_queues = self.dq if dma_queues is None else dma_queues
        for e in engines:
            for k in self.E:
                if k == "sp" and "sp" not in dma_queues:
                    continue
                self._wait(e, k, self.cnt[k])
            for q in dma_queues:
                for i in range(self.NDS):
                    self._wait(e, (q, i), self.dval[q][i])


def build_program(upto=9, dbg=False):
    nc = bass.Bass("TRN2", target_bir_lowering=False)
    es = ExitStack()
    K = KB(nc, es)

    def din(name, shape, dt=F32):
        return nc.dram_tensor(name, list(shape), dt, kind="ExternalInput").ap()

    def dout(name, shape, dt=F32):
        return nc.dram_tensor(name, list(shape), dt, kind="ExternalOutput").ap()

    def dscr(name, shape, dt):
        return nc.dram_tensor(name, list(shape), dt).ap()

    xp = din("xp", [NP, DM])
    xs = din("xs", [NS, DM])
    w_in = din("w_in", [DM, 2560])
    w_glu = din("w_glu", [512, 512])
    w_pw2 = din("w_pw2", [512, 512])
    w_out = din("w_out", [DM, DM])
    vecs_d = din("vecs", [128, 40])
    wdwT_d = din("wdwT", [128, 4, 31])
    gpost_d = din("gpost", [128, DM])
    bpost_d = din("bpost", [128, DM])
    ident_d = din("ident", [128, 128])
    lamT_d = din("lamT", [128, 2, 32])
    logdt_d = din("logdt", [128, 32])
    BA_d = din("BA", [128, 32, 16])
    BB_d = din("BBraw", [128, 32, 16])
    CA_d = din("CAraw", [128, 32, 16])
    CB_d = din("CBraw", [128, 32, 16])
    dskT_d = din("dskT", [16, 32])
    h0A_d = din("h0A", [128, 32, NSEQ])
    h0B_d = din("h0Braw", [128, 32, NSEQ])
    stT_d = din("stT", [512, NSEQ, 30])
    stP_d = din("stP", [512, 8, 5 * NSEQ])
    wdg_d = din("wdg", [128, 32 * 5 * 8])
    sccol_d = din("sccol", [128, 3, 32])

    y_p = dout("y_p", [NP, DM])
    y_s = dout("y_s", [NS, DM])
    hfin_p = dout("hfin_p", [128, 32])
    hnew_s = dout("hnew_s", [128, 32, NSEQ])
    ncvT_p = dout("ncvT_p", [512, 30])
    ncvT_s = dout("ncvT_s", [512, NSEQ * 30])

    d_u = dscr("d_u", [128, 32 * KC], BF16)
    d_s = dscr("d_s", [512, 8 * KC], BF16)
    d_v = dscr("d_v", [128, 32 * KV], BF16)
    d_c = dscr("d_c", [512, 8 * KC], BF16)

    dbg_out = {}
    if dbg:
        dbg_out["u_perm"] = dout("dbg_u_perm", [512, 8 * KC], BF16)
        dbg_out["gs"] = dout("dbg_gs", [512, 8 * KC], BF16)
        dbg_out["gc"] = dout("dbg_gc", [512, 8 * KC], BF16)
        dbg_out["vperm"] = dout("dbg_vperm", [512, 8 * KV], BF16)
        dbg_out["U"] = dout("dbg_U", [128, 32 * KC], BF16)

    def sb(name, shape, dt, stack=es, side=None):
        return stack.enter_context(nc.sbuf_tensor("sb_" + name, list(shape), dt, side=side))

    ps = [es.enter_context(nc.psum_tensor(f"ps{i}", [128, 512], F32)) for i in range(8)]

    ident = sb("ident", [128, 128], F32, side="right")
    vecs = sb("vecs", [128, 40], F32, side="right")
    hvec = sb("hvec", [128, 12], F32, side="right")
    gs_perm = [sb(f"gs_perm{c}", [128, 8, KC], BF16, side="right") for c in range(4)]
    sA = ExitStack()
    gc_perm = [sb(f"gc_perm{c}", [128, 8, KC], BF16, sA) for c in range(4)]

    wdg = sb("wdg", [128, 32 * 5 * 8], F32, side="right")
    sccol = sb("sccol", [128, 3, 32], F32, side="right")
    K.dma("sp", ident[:], ident_d[:, :], w=["ident"])
    K.dma("sp", vecs[:], vecs_d[:, :], w=["vecs"])
    K.dma("sp", wdg[:], wdg_d[:, :], w=["wdg"])
    K.dma("sp", sccol[:], sccol_d[:, :, :], w=["sccol"])
    K.op("dve", lambda e: e.tensor_scalar(out=hvec[:, 0:8], in0=vecs[:, 8:16], scalar1=0.5, scalar2=None, op0=ALU.mult),
         r=["vecs"], w=["hvec"])
    K.op("dve", lambda e: e.tensor_scalar(out=hvec[:, 8:12], in0=vecs[:, 20:24], scalar1=0.5, scalar2=None, op0=ALU.mult),
         r=["vecs"], w=["hvec"])
    for c in range(4):
        K.op("dve", lambda e, c=c: e.memset(gc_perm[c][:, 0:4, KP:KC], 0.0), w=[("gc", c, -1)])
        K.op("dve", lambda e, c=c: e.memset(gs_perm[c][:, 0:4, KP:KC], 0.0), w=[("gs", c, -1)])

    fr = lambda name, shape, dt=F32: sb(name, shape, dt, side="right")
    lam = fr("lam", [128, 2, 32]); dtb = fr("dtb", [128, 32])
    T = [fr(f"T{i}", [128, 4, 32]) for i in range(4)]
    PR = fr("PR", [128, 9, 32]); PI = fr("PI", [128, 9, 32])
    QR = fr("QR", [128, 8, 32]); QI = fr("QI", [128, 8, 32])
    mag = fr("mag", [128, 32]); phi = fr("phi", [128, 32]); kf = fr("kf", [128, 32]); ki = fr("ki", [128, 32], I32)
    rr = fr("rr", [128, 32]); rc = fr("rc", [128, 32]); msk = fr("msk", [128, 32]); sinv = fr("sinv", [128, 32]); cosv = fr("cosv", [128, 32])
    KR = fr("KR", [128, 32]); KI = fr("KI", [128, 32]); den = fr("den", [128, 32]); am1 = fr("am1", [128, 32])
    S4 = fr("S4", [128, 32, 8, 2]); S_bf = fr("S_bf", [128, 32, 8, 2], BF16); I2 = fr("I2", [128, 64], BF16)
    PRr = fr("PRr", [128, 8, 32]); PIr = fr("PIr", [128, 8, 32])

    def V(fn, r, w, eng="dve"):
        return K.op(eng, fn, r=r, w=w)

    def tt(o, a, b, op, r, w, eng="dve"):
        return K.op(eng, lambda e: e.tensor_tensor(out=o, in0=a, in1=b, op=op), r=r, w=w)

    def ts(o, a, s1_, op0, r, w, s2_=None, op1=None, eng="dve"):
        if op1 is None:
            return K.op(eng, lambda e: e.tensor_scalar(out=o, in0=a, scalar1=s1_, scalar2=None, op0=op0), r=r, w=w)
        return K.op(eng, lambda e: e.tensor_scalar(out=o, in0=a, scalar1=s1_, scalar2=s2_, op0=op0, op1=op1), r=r, w=w)

    G = "pool"

    def cmul(oR, oI, xR, xI, yR, yI, nslots, r, w):
        a, b, c_, d_ = (T[i][:, 0:nslots, :] for i in range(4))
        tk = ["T0", "T1", "T2", "T3"]
        tt(a, xR, yR, ALU.mult, r, [tk[0]], G); tt(b, xI, yI, ALU.mult, r, [tk[1]], G)
        tt(c_, xR, yI, ALU.mult, r, [tk[2]], G); tt(d_, xI, yR, ALU.mult, r, [tk[3]], G)
        tt(oR, a, b, ALU.subtract, [tk[0], tk[1]], w, G); tt(oI, c_, d_, ALU.add, [tk[2], tk[3]], w, G)

    def g1_gen():
        K.dma("sp", lam[:], lamT_d[:, :, :], w=["lam"]); K.dma("sp", dtb[:], logdt_d[:, :], w=["dtb"])
        lr = lam[:, 0, :]; li = lam[:, 1, :]
        K.op("act", lambda e: e.activation(out=dtb[:], in_=dtb[:], func=AF.Exp), r=["dtb"], w=["dtb"])
        yield
        tt(mag[:], lr, dtb[:], ALU.mult, ["lam", "dtb"], ["mag"], G)
        yield
        K.op("act", lambda e: e.activation(out=mag[:], in_=mag[:], func=AF.Exp), r=["mag"], w=["mag"])
        yield
        tt(phi[:], li, dtb[:], ALU.mult, ["lam", "dtb"], ["phi"], G)
        ts(kf[:], phi[:], 1.0 / TWO_PI, ALU.mult, ["phi"], ["kf"], eng=G)
        yield
        V(lambda e: e.tensor_copy(out=ki[:], in_=kf[:]), ["kf"], ["ki"])
        V(lambda e: e.tensor_copy(out=kf[:], in_=ki[:]), ["ki"], ["kf"])
        yield
        ts(kf[:], kf[:], -TWO_PI, ALU.mult, ["kf"], ["kf"], eng=G)
        tt(rr[:], kf[:], phi[:], ALU.add, ["kf", "phi"], ["rr"], G)
        PIS = 3.141592
        ts(rr[:], rr[:], -PIS, ALU.max, ["rr"], ["rr"], PIS, ALU.min, eng=G)
        ts(rc[:], rr[:], TWO_PI / 4.0, ALU.add, ["rr"], ["rc"], eng=G)
        ts(msk[:], rc[:], PIS, ALU.is_gt, ["rc"], ["msk"], eng=G)
        ts(msk[:], msk[:], -TWO_PI, ALU.mult, ["msk"], ["msk"], eng=G)
        tt(rc[:], rc[:], msk[:], ALU.add, ["msk", "rc"], ["rc"], G)
        ts(rc[:], rc[:], -PIS, ALU.max, ["rc"], ["rc"], PIS, ALU.min, eng=G)
        yield
        K.op("act", lambda e: e.activation(out=sinv[:], in_=rr[:], func=AF.Sin), r=["rr"], w=["sinv"])
        K.op("act", lambda e: e.activation(out=cosv[:], in_=rc[:], func=AF.Sin), r=["rc"], w=["cosv"])
        yield
        V(lambda e: e.memset(PR[:, 0, :], 1.0), [], ["P0"], G); V(lambda e: e.memset(PI[:, 0, :], 0.0), [], ["P0"], G)
        tt(PR[:, 1, :], mag[:], cosv[:], ALU.mult, ["mag", "cosv"], ["P1"], G); tt(PI[:, 1, :], mag[:], sinv[:], ALU.mult, ["mag", "sinv"], ["P1"], G)
        cmul(PR[:, 2:3, :], PI[:, 2:3, :], PR[:, 1:2, :], PI[:, 1:2, :], PR[:, 1:2, :], PI[:, 1:2, :], 1, ["P1"], ["P2"])
        cmul(PR[:, 3:5, :], PI[:, 3:5, :], PR[:, 1:3, :], PI[:, 1:3, :], PR[:, 2:3, :].to_broadcast([128, 2, 32]), PI[:, 2:3, :].to_broadcast([128, 2, 32]), 2, ["P1", "P2"], ["P34"])
        cmul(PR[:, 5:9, :], PI[:, 5:9, :], PR[:, 1:5, :], PI[:, 1:5, :], PR[:, 4:5, :].to_broadcast([128, 4, 32]), PI[:, 4:5, :].to_broadcast([128, 4, 32]), 4, ["P1", "P2", "P34"], ["P58"])
        PK = ["P0", "P1", "P2", "P34", "P58"]
        V(lambda e: e.tensor_copy(out=QR[:, 0, :], in_=PR[:, 8, :]), PK, [("Q", 0)], G); V(lambda e: e.tensor_copy(out=QI[:, 0, :], in_=PI[:, 8, :]), PK, [("Q", 0)], G)
        for j in range(7):
            cmul(QR[:, j + 1:j + 2, :], QI[:, j + 1:j + 2, :], QR[:, j:j + 1, :], QI[:, j:j + 1, :], QR[:, j:j + 1, :], QI[:, j:j + 1, :], 1, [("Q", j)], [("Q", j + 1)])
        QK = [("Q", j) for j in range(8)]
        tt(den[:], lr, lr, ALU.mult, ["lam"], ["den"], G); tt(kf[:], li, li, ALU.mult, ["lam"], ["kf"], G)
        tt(den[:], den[:], kf[:], ALU.add, ["den", "kf"], ["den"], G)
        yield
        V(lambda e: e.reciprocal(out=den[:], in_=den[:]), ["den"], ["den"])
        yield
        ts(am1[:], PR[:, 1, :], -1.0, ALU.add, ["P1"], ["am1"], eng=G)
        tt(KR[:], am1[:], lr, ALU.mult, ["am1", "lam"], ["KR"], G); tt(kf[:], PI[:, 1, :], li, ALU.mult, ["P1", "lam"], ["kf"], G)
        tt(KR[:], KR[:], kf[:], ALU.add, ["KR", "kf"], ["KR"], G); tt(KR[:], KR[:], den[:], ALU.mult, ["KR", "den"], ["KR"], G)
        tt(KI[:], PI[:, 1, :], lr, ALU.mult, ["P1", "lam"], ["KI"], G); tt(kf[:], am1[:], li, ALU.mult, ["am1", "lam"], ["kf"], G)
        tt(KI[:], KI[:], kf[:], ALU.subtract, ["KI", "kf"], ["KI"], G); tt(KI[:], KI[:], den[:], ALU.mult, ["KI", "den"], ["KI"], G)
        V(lambda e: e.tensor_copy(out=S4[0:64, :, :, 0], in_=QR[0:64, :, :].rearrange("p j g -> p g j")), QK, ["S4a"], G)
        ts(S4[64:128, :, :, 0], QI[64:128, :, :].rearrange("p j g -> p g j"), -1.0, ALU.mult, QK, ["S4b"], eng=G)
        V(lambda e: e.tensor_copy(out=S4[0:64, :, :, 1], in_=QI[0:64, :, :].rearrange("p j g -> p g j")), QK, ["S4c"], G)
        V(lambda e: e.tensor_copy(out=S4[64:128, :, :, 1], in_=QR[64:128, :, :].rearrange("p j g -> p g j")), QK, ["S4d"], G)
        V(lambda e: e.tensor_copy(out=S_bf[:], in_=S4[:]), ["S4a", "S4b", "S4c", "S4d"], ["S_bf"], G)
        tt(I2[:], ident[:, 0:64], ident[:, 64:128], ALU.add, ["ident"], ["I2"], G)
        for sx_ in range(8):
            V(lambda e, sx_=sx_: e.tensor_copy(out=PRr[:, sx_, :], in_=PR[:, 7 - sx_, :]), PK, [("Pr", sx_)], G)
            V(lambda e, sx_=sx_: e.tensor_copy(out=PIr[:, sx_, :], in_=PI[:, 7 - sx_, :]), PK, [("Pr", sx_)], G)


        yield

    g1 = g1_gen()
    next(g1)

    PK = ["P0", "P1", "P2", "P34", "P58"]
    QK = [("Q", j) for j in range(8)]

    with ExitStack() as s1:
        w_bf = sb("w_bf", [128, 8, 2560], BF16, s1)
        wst = [sb(f"wst{i}", [128, 8, 256], F32, s1) for i in range(2)]
        x_sb = [sb(f"x_sb{i}", [128, DM], F32, s1) for i in range(4)]
        xT = [sb(f"xT{i}", [128, 8, 512], BF16, s1) for i in range(2)]
        th = [sb(f"th{i}", [128, 512], F32, s1) for i in range(2)]
        ah = [sb(f"ah{i}", [128, 512], F32, s1) for i in range(2)]
        st32 = sb("st32", [128, 4, NSEQ, 30], F32, s1)
        stp32 = sb("stp32", [128, 8, 5 * NSEQ], F32, s1)
        v32p = sb("v32p", [128, 4, 30], F32, s1)
        ncs = sb("ncs", [128, 4, NSEQ, 30], F32, s1)
        u_perm = [sb(f"u_perm{c}", [128, 8, KC], BF16, s1) for c in range(4)]
        v_perm = [sb(f"v_perm{c}", [128, 8, KV], BF16, s1) for c in range(4)]

        def load_x(bi):
            t0, n = BLOCKS[bi]
            if n == 512:
                for i in range(4):
                    K.dma("sp", x_sb[i][:, :], xp[t0 + 128 * i:t0 + 128 * (i + 1), :], w=[("x", i)])
            else:
                K.dma("sp", x_sb[0][0:NS, :], xs[:, :], w=[("x", 0)])

        for c in range(4):
            K.op("dve", lambda e, c=c: e.memset(u_perm[c][:, 0:4, KP:KC], 0.0), w=[("uperm", c, -1)])
        BORD = [0, 4, 1, 2, 3]
        load_x(BORD[0])
        M_ORDER = [12, 8, 13, 9, 14, 10, 15, 11, 0, 1, 2, 3, 4, 5, 6, 7, 16, 17, 18, 19]
        CH_ORDER = []
        for m_ in M_ORDER:
            if m_ // 2 not in CH_ORDER:
                CH_ORDER.append(m_ // 2)
        w_state = {"dma": 0, "cast": set()}

        def w_issue_dma(upto_n):
            while w_state["dma"] < min(upto_n, len(CH_ORDER)):
                i = w_state["dma"]
                cc = CH_ORDER[i]
                for kk in range(8):
                    K.dma("sp", wst[i % 2][:, kk, :], w_in[128 * kk:128 * (kk + 1), 256 * cc:256 * (cc + 1)], w=[("wst", i % 2, kk)])
                w_state["dma"] += 1

        def w_ensure(cc):
            if cc in w_state["cast"]:
                return
            i = CH_ORDER.index(cc)
            w_issue_dma(i + 1)
            K.op("dve", lambda e: e.tensor_copy(out=w_bf[:, 0:4, 256 * cc:256 * (cc + 1)], in_=wst[i % 2][:, 0:4, :]),
                 r=[("wst", i % 2, kk) for kk in range(0, 4)], w=[("wbf", cc, 0)])
            K.op("act", lambda e: e.activation(out=w_bf[:, 4:8, 256 * cc:256 * (cc + 1)], in_=wst[i % 2][:, 4:8, :], func=AF.Identity),
                 r=[("wst", i % 2, kk) for kk in range(4, 8)], w=[("wbf", cc, 1)])
            w_state["cast"].add(cc)
            w_issue_dma(i + 3)

        w_issue_dma(2)
        def st_dma(c):
            K.dma("sp", st32[:, c, :, :], stT_d[128 * c:128 * (c + 1), :, :], w=[("st32", c)])
            K.dma("sp", stp32[:, :, :], stP_d[128 * c:128 * (c + 1), :, :], w=["stp32"])

        def st_copy(c):
            K.op("dve", lambda e, c=c: e.tensor_copy(out=ncs[:, c, :, 0:26], in_=st32[:, c, :, 4:30]), r=[("st32", c)], w=[("ncs", c, 0)])
            K.op("dve", lambda e, c=c: e.tensor_copy(out=v_perm[c][:, :, KP:KV], in_=stp32[:, :, :]), r=["stp32"], w=[("vperm", c, -1)])
        ST_AT = {1: ("d", 0), 3: ("c", 0), 4: ("d", 1), 6: ("c", 1), 7: ("d", 2), 9: ("c", 2), 11: ("d", 3), 13: ("c", 3)}
        zi = 0
        ev = 0
        ct_count = [0]
        G1_AT = {2: 1, 3: 1, 4: 1, 10: 1, 11: 1, 22: 1, 23: 1, 60: 1, 61: 1}

        def g1_tick():
            ct_count[0] += 1
            if ct_count[0] in G1_AT:
                try:
                    next(g1)
                except StopIteration:
                    pass
        du_v = d_u.rearrange("(r c) (g k) -> c r g k", c=16, g=32)
        dv_v = d_v.rearrange("(r c) (g k) -> c r g k", c=16, g=32)

        def scratch_piece(pc):
            for c in range(4):
                for gl in range(8):
                    g = 8 * c + gl
                    ps_ = slice(16 * gl, 16 * (gl + 1))
                    if pc < 2:
                        ks = slice(128 * pc, 128 * (pc + 1))
                        rk_u = [("uperm", c, 2 * pc), ("uperm", c, 2 * pc + 1)]
                        rk_v = [("vperm", c, 2 * pc), ("vperm", c, 2 * pc + 1)]
                        K.dma("sp", du_v[:, :, g, ks], u_perm[c][ps_, :, ks], r=rk_u, w=[("d_u", c, pc, gl)])
                        K.dma("act" if pc == 1 else "sp", dv_v[:, :, g, ks], v_perm[c][ps_, :, ks], r=rk_v, w=[("d_v", c, pc, gl)])
                    else:
                        K.dma("sp", du_v[:, :, g, KP:KC], u_perm[c][ps_, :, KP:KC], r=[("uperm", c, -1), ("uperm", c, 4)], w=[("d_u", c, pc, gl)])
                        K.dma("sp", dv_v[:, :, g, KP:KV], v_perm[c][ps_, :, KP:KV], r=[("vperm", c, -1), ("vperm", c, 4)], w=[("d_v", c, pc, gl)])

        def do_transposes(pos):
            bj = BORD[pos]
            t0_, n_ = BLOCKS[bj]
            bb_ = pos % 2
            nt_ = 4 if n_ == 512 else 1
            rows_ = 128 if n_ == 512 else NS
            for kk in range(8):
                bank = 4 + (kk % 2)
                for i in range(nt_):
                    K.op("pe", lambda e, i=i, kk=kk, bank=bank: e.transpose(out=ps[bank][:, rows_ * i:rows_ * (i + 1)], in_=x_sb[i][0:rows_, 128 * kk:128 * (kk + 1)], identity=ident[0:rows_, 0:rows_]),
                         r=[("x", i), "ident"], w=[("ps", bank)], signal=(i == nt_ - 1))
                if kk % 2 == 0:
                    K.op("dve", lambda e, kk=kk, bank=bank: e.tensor_copy(out=xT[bb_][:, kk, 0:n_], in_=ps[bank][:, 0:n_]), r=[("ps", bank)], w=[("xT", bb_, kk)])
                else:
                    K.op("act", lambda e, kk=kk, bank=bank: e.activation(out=xT[bb_][:, kk, 0:n_], in_=ps[bank][:, 0:n_], func=AF.Identity), r=[("ps", bank)], w=[("xT", bb_, kk)])
            if pos + 1 < len(BORD):
                load_x(BORD[pos + 1])

        do_transposes(0)
        for pos_, bi in enumerate(BORD):
            t0, n = BLOCKS[bi]
            bb = pos_ % 2
            if pos_ == 2:
                scratch_piece(2)
            nt = 4 if n == 512 else 1
            rows = 128 if n == 512 else NS
            if bi == 2:
                scratch_piece(0)
            for mi_, m in enumerate(M_ORDER):
                if mi_ == 10 and pos_ + 1 < len(BORD):
                    do_transposes(pos_ + 1)
                if bi == 3 and mi_ == 12:
                    scratch_piece(1)
                if pos_ == 0 and mi_ in ST_AT:
                    kind_, c_ = ST_AT[mi_]
                    (st_dma if kind_ == "d" else st_copy)(c_)
                bank = zi % 4
                zi += 1
                cc = m // 2
                w_ensure(cc)
                for kk in range(8):
                    K.op("pe", lambda e, m=m, kk=kk, bank=bank: e.matmul(out=ps[bank][:, 0:n], lhsT=w_bf[:, kk, 128 * m:128 * (m + 1)], rhs=xT[bb][:, kk, 0:n], start=(kk == 0), stop=(kk == 7)),
                         r=[("wbf", cc, kk // 4), ("xT", bb, kk)], w=[("ps", bank)], signal=(kk == 7))
                g1_tick()
                pz = ps[bank][:, 0:n]
                bcol = vecs[:, m:m + 1]
                c = m % 4
                if m < 8:
                    dst_t = u_perm[c] if m < 4 else gs_perm[c]
                    key = ("uperm" if m < 4 else "gs", c, bi)
                    func = AF.Identity if m < 4 else AF.Silu
                    if n == 512:
                        k0 = t0 // 8
                        o_ap = dst_t[:, :, k0:k0 + 64]
                        i_ap = pz.rearrange("p (k r) -> p r k", r=8)
                    else:
                        o_ap = dst_t[:, 4:8, KP:KC]
                        i_ap = pz.rearrange("p (s t) -> p t s", t=4)
                    K.op("act", lambda e, o_ap=o_ap, i_ap=i_ap, func=func, bcol=bcol: e.activation(out=o_ap, in_=i_ap, func=func, bias=bcol),
                         r=[("ps", bank), "vecs"], w=[key])
                elif m >= 16:
                    if n == 512:
                        k0 = t0 // 8
                        o_ap = gc_perm[c][:, :, k0:k0 + 64]
                        i_ap = pz.rearrange("p (k r) -> p r k", r=8)
                    else:
                        o_ap = gc_perm[c][:, 4:8, KP:KC]
                        i_ap = pz.rearrange("p (s t) -> p t s", t=4)
                    K.op("act", lambda e, o_ap=o_ap, i_ap=i_ap, bcol=bcol: e.activation(out=o_ap, in_=i_ap, func=AF.Silu, bias=bcol),
                         r=[("ps", bank), "vecs"], w=[("gc", c, bi)])
                elif m >= 12:
                    K.op("act", lambda e, c=c, pz=pz: e.activation(out=th[c % 2][:, 0:n], in_=pz, func=AF.Tanh, bias=hvec[:, 4 + c:5 + c], scale=0.5),
                         r=[("ps", bank), "hvec"], w=[("th", c % 2)])
                else:
                    K.op("act", lambda e, c=c, pz=pz: e.activation(out=ah[c % 2][:, 0:n], in_=pz, func=AF.Identity, bias=hvec[:, c:c + 1], scale=0.5),
                         r=[("ps", bank), "hvec"], w=[("ah", c % 2)])
                    if n == 512:
                        k0 = t0 // 8
                        o_ap = v_perm[c][:, :, k0:k0 + 64]
                        i0 = th[c % 2][:, 0:n].rearrange("p (k r) -> p r k", r=8)
                        i1 = ah[c % 2][:, 0:n].rearrange("p (k r) -> p r k", r=8)
                        key = ("vperm", c, bi)
                    else:
                        o_ap = v_perm[c][:, 4:8, KP:KV].rearrange("p t (s j) -> p t s j", j=5)[:, :, :, 4]
                        i0 = th[c % 2][:, 0:n].rearrange("p (s t) -> p t s", t=4)
                        i1 = ah[c % 2][:, 0:n].rearrange("p (s t) -> p t s", t=4)
                        key = ("vperm", c, bi)
                    K.op("dve", lambda e, o_ap=o_ap, i0=i0, i1=i1: e.scalar_tensor_tensor(out=o_ap, in0=i0, scalar=1.0, in1=i1, op0=ALU.add, op1=ALU.mult),
                         r=[("th", c % 2), ("ah", c % 2)], w=[key])
                    if bi == 3:
                        K.op("dve", lambda e, c=c: e.scalar_tensor_tensor(out=v32p[:, c, :], in0=th[c % 2][:, 482:512], scalar=1.0, in1=ah[c % 2][:, 482:512], op0=ALU.add, op1=ALU.mult),
                             r=[("th", c % 2), ("ah", c % 2)], w=[("v32p", c)])
                    if bi == 4:
                        j0 = th[c % 2][:, 0:n].rearrange("p (s t) -> p s t", t=4)
                        j1 = ah[c % 2][:, 0:n].rearrange("p (s t) -> p s t", t=4)
                        K.op("dve", lambda e, c=c, i0=j0, i1=j1: e.scalar_tensor_tensor(out=ncs[:, c, :, 26:30], in0=i0, scalar=1.0, in1=i1, op0=ALU.add, op1=ALU.mult),
                             r=[("th", c % 2), ("ah", c % 2)], w=[("ncs", c, 1)])
        for _ in g1:
            pass
        for c in range(4):
            K.dma("pool", ncvT_p[128 * c:128 * (c + 1), :], v32p[:, c, :], r=[("v32p", c)])
            K.dma("pool", ncvT_s[128 * c:128 * (c + 1), :], ncs[:, c, :, :].rearrange("p s j -> p (s j)"), r=[("ncs", c, 0), ("ncs", c, 1)])
        if dbg:
            for c in range(4):
                K.dma("pool", dbg_out["u_perm"][128 * c:128 * (c + 1), :], u_perm[c][:, :, :].rearrange("p r k -> p (r k)"), r=[("uperm", c, b) for b in range(-1, 5)])
                K.dma("pool", dbg_out["gs"][128 * c:128 * (c + 1), :], gs_perm[c][:, :, :].rearrange("p r k -> p (r k)"), r=[("gs", c, b) for b in range(-1, 5)])
                K.dma("pool", dbg_out["gc"][128 * c:128 * (c + 1), :], gc_perm[c][:, :, :].rearrange("p r k -> p (r k)"), r=[("gc", c, b) for b in range(-1, 5)])
                K.dma("pool", dbg_out["vperm"][128 * c:128 * (c + 1), :], v_perm[c][:, :, :].rearrange("p r k -> p (r k)"), r=[("vperm", c, b) for b in range(-1, 5)])
        K.barrier(dma_queues=("pool",))

    DU_KEYS = [("d_u", c, pc, gl) for c in range(4) for pc in range(3) for gl in range(8)]
    DV_KEYS = [("d_v", c, pc, gl) for c in range(4) for pc in range(3) for gl in range(8)]
    BA = fr("BA", [128, 32, 16]); BB = fr("BB", [128, 32, 16]); CA = fr("CA", [128, 32, 16]); CB = fr("CB", [128, 32, 16])
    BbA = fr("BbA", [128, 32, 16]); BbB = fr("BbB", [128, 32, 16]); BbA_bf = fr("BbA_bf", [128, 32, 16], BF16)
    G1t = fr("G1t", [128, 32, 16]); G2t = fr("G2t", [128, 32, 16])
    dskT = fr("dskT", [16, 32]); Dd = fr("Dd", [16, 32, 16])
    h0bf = fr("h0bf", [128, 32, NSEQ], BF16); H4 = fr("H4", [128, 32, NSEQ])
    Mj = [fr("Mj0", [128, 8, 8, 128], BF16), None]

    def gen_mj(gb, extra_r=(), eng="pool"):
        K.op(eng, lambda e: e.tensor_tensor(out=Mj[gb % 2][:, :, :, :].rearrange("p g j (h m) -> p (g j h) m", h=2),
                                               in0=I2[:].unsqueeze(1).to_broadcast([128, 128, 64]),
                                               in1=S_bf[:, 8 * gb:8 * gb + 8, :, :].rearrange("p g j h -> p (g j h)").unsqueeze(2).to_broadcast([128, 128, 64]),
                                               op=ALU.mult),
             r=["I2", "S_bf"] + list(extra_r), w=[("Mj", gb % 2)])

    cfin_perm = [sb(f"cfin{c}", [128, 8, KC], BF16, side="right") for c in range(4)]
    if dbg:
        dbg_out["cfin"] = dout("dbg_cfin", [512, 8 * KC], BF16)
    if dbg:
        dbg_out["c1sc"] = dout("dbg_c1sc", [128, 32 * KC], BF16)
    if upto >= 2:
      with ExitStack() as s2:
        f2 = lambda name, shape, dt=F32: sb(name, shape, dt, s2)
        Wc = f2("Wc", [128, 32, 5, 128], BF16)
        Vv = f2("Vv", [128, 32, KV], BF16)
        maskc = f2("maskc", [128, 16])
        RB = f2("RB", [128, 8])
        bones = f2("bones", [128, 128], BF16)
        wp_st = f2("wp_st", [128, 4, 512])
        wp_bf = f2("wp_bf", [128, 4, 512], BF16)
        Ybf = [f2(f"Ybf{i}", [128, KC], BF16) for i in range(2)]
        Ysq = [f2(f"Ysq{i}", [128, KC], BF16) for i in range(2)]
        mean = f2("mean", [128, KC]); var = f2("var", [128, KC]); rstd = f2("rstd", [128, KC])
        t1 = [f2(f"t1_{i}", [128, KC]) for i in range(2)]
        c1sc = f2("c1sc", [128, 32, KC], BF16)
        Vflat = Vv[:, :, :].rearrange("p g k -> p (g k)")
        c1f = [Vflat[:, 8 * KC * c:8 * KC * (c + 1)] for c in range(4)]
        K.dma("sp", Vv[:, :, :].rearrange("p g k -> p (g k)"), d_v[:, :], r=DV_KEYS + DU_KEYS, w=[("Vv", 0)] + [("Vvg", g_) for g_ in range(32)])
        VK = [("Vv", 0)]
        for ci in range(4):
            K.dma("sp", wp_st[:, ci, :], w_pw2[128 * ci:128 * (ci + 1), :], w=[("wp_st", ci)])
        K.op("dve", lambda e: e.tensor_reduce(out=maskc[:], in_=ident[:, :].rearrange("p (s c) -> p c s", s=8), axis=mybir.AxisListType.X, op=ALU.add), r=["ident"], w=["maskc"])
        K.op("dve", lambda e: e.tensor_reduce(out=RB[:], in_=ident[:, :].rearrange("p (s c) -> p s c", s=8), axis=mybir.AxisListType.X, op=ALU.add), r=["ident"], w=["RB"])
        K.op("dve", lambda e: e.tensor_copy(out=bones[:].rearrange("p (s c) -> p s c", s=8), in_=RB[:].unsqueeze(2).to_broadcast([128, 8, 16])), r=["RB"], w=["bones"])
        for gq in range(4):
            K.op("dve", lambda e, gq=gq: e.tensor_tensor(out=Wc[:, 8 * gq:8 * gq + 8, :, :].rearrange("p g d (r c) -> p (g d r) c", c=16),
                                                        in0=wdg[:, 320 * gq:320 * (gq + 1)].unsqueeze(2).to_broadcast([128, 320, 16]),
                                                        in1=maskc[:].unsqueeze(1).to_broadcast([128, 320, 16]), op=ALU.mult),
                 r=["wdg", "maskc"], w=[("Wc", gq)])
        for ci in range(4):
            K.op("act", lambda e, ci=ci: e.activation(out=wp_bf[:, ci, :], in_=wp_st[:, ci, :], func=AF.Identity), r=[("wp_st", ci)], w=[("wp_bf", ci)])
        if upto >= 3:
            X1 = G1t
            X2 = G2t
            K.dma("sp", BA[:], BA_d[:, :, :], w=["BA"]); K.dma("sp", BB[:], BB_d[:, :, :], w=["BB"])
            K.dma("sp", CA[:], CA_d[:, :, :], w=["CA"]); K.dma("sp", CB[:], CB_d[:, :, :], w=["CB"])
            K.dma("sp", dskT[:], dskT_d[:, :], w=["dskT"])
            K.dma("sp", X1[:], h0A_d[:, :, :], w=["G1t"]); K.dma("sp", X2[:], h0B_d[:, :, :], w=["G2t"])
            ts(X2[0:64, :, :], X2[0:64, :, :], -1.0, ALU.mult, ["G2t", ("Wc", 3)], ["G2t"], eng=G)
            b16s = lambda ap: ap.unsqueeze(2).to_broadcast([128, 32, NSEQ])
            V(lambda e: e.tensor_copy(out=h0bf[:], in_=X1[:]), ["G1t"], ["h0bf"], G)
            tt(H4[:], b16s(PR[:, 4, :]), X1[:], ALU.mult, PK + ["G1t"], ["H4"], G)
            tt(X1[:], b16s(PI[:, 4, :]), X2[:], ALU.mult, PK + ["G2t", "H4"], ["G1t"], G)
            tt(H4[:], H4[:], X1[:], ALU.add, ["H4", "G1t"], ["H4"], G)
            ts(BB[0:64, :, :], BB[0:64, :, :], -1.0, ALU.mult, ["BB"], ["BB"], eng=G)
            ts(CA[64:128, :, :], CA[64:128, :, :], -1.0, ALU.mult, ["CA"], ["CA"], eng=G)
            ts(CB[:], CB[:], -1.0, ALU.mult, ["CB"], ["CB"], eng=G)
            bc16 = lambda ap: ap.unsqueeze(2).to_broadcast([128, 32, 16])
            tt(G1t[:], bc16(KR[:]), BA[:], ALU.mult, ["KR", "BA", "H4"], ["G1t"], G); tt(G2t[:], bc16(KI[:]), BB[:], ALU.mult, ["KI", "BB", "H4", "G1t"], ["G2t"], G)
            tt(BbA[:], G1t[:], G2t[:], ALU.add, ["G1t", "G2t"], ["BbA"], G)
            tt(G1t[:], bc16(KR[:]), BB[:], ALU.mult, ["KR", "BB"], ["G1t"], G); tt(G2t[:], bc16(KI[:]), BA[:], ALU.mult, ["KI", "BA"], ["G2t"], G)
            tt(BbB[:], G1t[:], G2t[:], ALU.subtract, ["G1t", "G2t"], ["BbB"], G)
            V(lambda e: e.tensor_copy(out=BbA_bf[:], in_=BbA[:]), ["BbA"], ["BbA_bf"], G)
            tt(Dd[:], ident[0:16, 0:16].unsqueeze(1).to_broadcast([16, 32, 16]), dskT[:].unsqueeze(2).to_broadcast([16, 32, 16]), ALU.mult, ["ident", "dskT"], ["Dd"], G)
        bdw = lambda g: sccol[:, 0, g:g + 1]
        gln = lambda g: sccol[:, 1, g:g + 1]
        bln = lambda g: sccol[:, 2, g:g + 1]

        def conv_g(g, bank):
            for d in range(5):
                K.op("pe", lambda e, g=g, d=d, bank=bank: e.matmul(out=ps[bank][:, d:KP], lhsT=Wc[:, g, d, :], rhs=Vv[:, g, 0:KP - d], start=(d == 0), stop=(d == 4), skip_group_check=True),
                     r=[("Vvg", g), ("Wc", g // 8)], w=[("ps", bank)], signal=False)
            for d in range(5):
                K.op("pe", lambda e, g=g, d=d, bank=bank: e.matmul(out=ps[bank][:, KP:KC], lhsT=Wc[:, g, d, :], rhs=Vv[:, g, KP:KV].rearrange("p (s j) -> p s j", j=5)[:, :, 4 - d], start=False, stop=(d == 4), skip_group_check=True),
                     r=[("Vvg", g), ("Wc", g // 8)], w=[("ps", bank)], signal=(d == 4))

        def stats_g(g, bank):
            i = g % 2
            K.op("act", lambda e: e.activation(out=Ybf[i][:, :], in_=ps[bank][:, 0:KC], func=AF.Identity, bias=bdw(g)), r=[("ps", bank), "sccol"], w=[("Ybf", i)])
            K.op("dve", lambda e: e.scalar_tensor_tensor(out=Ysq[i][:, :], in0=ps[bank][:, 0:KC], scalar=bdw(g), in1=Ybf[i][:, :], op0=ALU.add, op1=ALU.mult), r=[("ps", bank), "sccol", ("Ybf", i)], w=[("Ysq", i)])
            K.op("pe", lambda e: e.matmul(out=ps[6][:, 0:KC], lhsT=bones[:], rhs=Ybf[i][:, :], start=(g == 0), stop=(g == 31), skip_group_check=True), r=["bones", ("Ybf", i)], w=[("ps", 6)], signal=(g == 31))
            K.op("pe", lambda e: e.matmul(out=ps[7][:, 0:KC], lhsT=bones[:], rhs=Ysq[i][:, :], start=(g == 0), stop=(g == 31), skip_group_check=True), r=["bones", ("Ysq", i)], w=[("ps", 7)], signal=(g == 31))

        conv_g(0, 0)
        for g in range(32):
            if g + 1 < 32:
                conv_g(g + 1, (g + 1) % 4)
            stats_g(g, g % 4)
        K.op("act", lambda e: e.activation(out=mean[:], in_=ps[6][:, 0:KC], func=AF.Identity, scale=1.0 / 512.0), r=[("ps", 6)], w=["mean"])
        K.op("dve", lambda e: e.tensor_tensor(out=var[:], in0=mean[:], in1=mean[:], op=ALU.mult), r=["mean"], w=["var"])
        K.op("dve", lambda e: e.scalar_tensor_tensor(out=var[:], in0=ps[7][:, 0:KC], scalar=1.0 / 512.0, in1=var[:], op0=ALU.mult, op1=ALU.subtract), r=[("ps", 7), "var"], w=["var"])
        K.op("dve", lambda e: e.tensor_scalar(out=var[:], in0=var[:], scalar1=EPS, scalar2=None, op0=ALU.add), r=["var"], w=["var"])
        K.op("act", lambda e: e.activation(out=var[:], in_=var[:], func=AF.Sqrt), r=["var"], w=["var"])
        K.op("dve", lambda e: e.reciprocal(out=rstd[:], in_=var[:]), r=["var"], w=["rstd"])

        def norm_g(g, bank):
            i = g % 2
            K.op("dve", lambda e: e.scalar_tensor_tensor(out=t1[i][:, :], in0=ps[bank][:, 0:KC], scalar=bdw(g), in1=mean[:], op0=ALU.add, op1=ALU.subtract),
                 r=[("ps", bank), "sccol", "mean"], w=[("t1", i)])
            K.op("dve", lambda e: e.tensor_tensor(out=t1[i][:, :], in0=t1[i][:, :], in1=rstd[:], op=ALU.mult), r=[("t1", i), "rstd"], w=[("t1", i)])
            K.op("act", lambda e: e.activation(out=c1sc[:, g, :], in_=t1[i][:, :], func=AF.Silu, bias=bln(g), scale=gln(g)), r=[("t1", i), "sccol"], w=[("c1sc", g)])

        conv_g(0, 0)
        for g in range(32):
            if g + 1 < 32:
                conv_g(g + 1, (g + 1) % 4)
            norm_g(g, g % 4)
            if g % 8 == 7:
                c = g // 8
                for rx in range(8):
                    K.dma("sp", d_c[128 * c:128 * (c + 1), KC * rx:KC * (rx + 1)].rearrange("(g q) k -> q g k", q=16), c1sc[16 * rx:16 * (rx + 1), 8 * c:8 * c + 8, :],
                          r=[("c1sc", g_) for g_ in range(8 * c, 8 * c + 8)], w=[("d_c", c, rx)])
                glo = (8 * KC * c) // KV
                ghi = min(31, (8 * KC * (c + 1) - 1) // KV)
                K.dma("sp", c1f[c], d_c[128 * c:128 * (c + 1), :], r=[("d_c", c, rx) for rx in range(8)], w=[("c1f", c)] + [("Vvg", g_) for g_ in range(glo, ghi + 1)])
        CK = [("c1sc", g) for g in range(32)]
        if upto >= 3:
            gen_mj(0, eng="dve")
        if dbg:
            K.dma("pool", dbg_out["c1sc"][:, :], c1sc[:, :, :].rearrange("p g k -> p (g k)"), r=CK)
        NPC = 8 * KC
        pcount = 0
        for col0 in range(0, NPC, 512):
            n = min(512, NPC - col0)
            for mo in range(4):
                bank = pcount % 4
                pcount += 1
                for ci in range(4):
                    K.op("pe", lambda e, mo=mo, ci=ci, bank=bank: e.matmul(out=ps[bank][:, 0:n], lhsT=wp_bf[:, ci, 128 * mo:128 * (mo + 1)], rhs=c1f[ci][:, col0:col0 + n], start=(ci == 0), stop=(ci == 3)),
                         r=[("wp_bf", ci), ("c1f", ci)], w=[("ps", bank)], signal=(ci == 3))
                K.op("dve", lambda e, mo=mo, bank=bank: e.scalar_tensor_tensor(out=cfin_perm[mo][:, :, :].rearrange("p r k -> p (r k)")[:, col0:col0 + n], in0=ps[bank][:, 0:n], scalar=vecs[:, 36 + mo:37 + mo],
                                                                             in1=gc_perm[mo][:, :, :].rearrange("p r k -> p (r k)")[:, col0:col0 + n], op0=ALU.add, op1=ALU.mult),
                     r=[("ps", bank), "vecs"] + [("gc", mo, b) for b in range(-1, 5)], w=[("cfin", mo, col0)])
        if dbg:
            for c in range(4):
                K.dma("pool", dbg_out["cfin"][128 * c:128 * (c + 1), :], cfin_perm[c][:, :, :].rearrange("p r k -> p (r k)"), r=[("cfin", c, col0) for col0 in range(0, NPC, 512)] + [("cfin", c, -1)])
        K.barrier()
    sA.close()

    if dbg:
        dbg_out["P"] = dout("dbg_P", [128, 2 * 9 * 32], F32)
        dbg_out["Tc"] = dout("dbg_Tc", [128, 32 * 128], BF16)
        dbg_out["BcT"] = dout("dbg_BcT", [128, 32 * 128], BF16)
        dbg_out["Fc"] = dout("dbg_Fc", [128, 32 * 192], BF16)
        dbg_out["s1rq"] = dout("dbg_s1rq", [128, 32 * KC], BF16)
        dbg_out["Hbf"] = dout("dbg_Hbf", [128, 32 * KP], BF16)
    if upto >= 3:
      with ExitStack() as s3:
        f3 = lambda name, shape, dt=F32: sb(name, shape, dt, s3)
        Fc = f3("Fc", [128, 32, 192], BF16)
        Tc = f3("Tc", [128, 32, 128], BF16); BcT = f3("BcT", [128, 32, 128], BF16)
        Hfin = f3("Hfin", [128, 32]); Hns = f3("Hns", [128, 32, NSEQ])
        E32 = f3("E32", [128, 8, 128]); Gb = f3("Gb", [128, 8, 128])
        FT1 = f3("FT1", [128, 8, 144]); FT2 = f3("FT2", [128, 8, 144])
        K_bf = f3("K_bf", [16, 8, 128], BF16)
        Mj[1] = f3("Mj1", [128, 8, 8, 128], BF16)
        U = f3("U", [128, 32, KC], BF16)
        Hbf2 = [f3(f"Hbf{i}", [128, 8, KP], BF16) for i in range(2)]
        s1rq2 = [f3(f"s1rq{i}", [128, 8, KC], BF16) for i in range(2)]
        s1f = [sb(f"s1f{c}", [128, 8, KC], BF16, side="right") for c in range(4)]
        K.dma("sp", U[:, :, :].rearrange("p g k -> p (g k)"), d_u[:, :], r=DU_KEYS, w=[("U", 0)])
        UK = [("U", 0)]
        K.op("act", lambda e: e.memzero(Tc[:]), w=["Tc0"])
        K.op("act", lambda e: e.memzero(Fc[:, :, 0:48]), w=[("Fc", -1)])
        if dbg:
            K.dma("pool", dbg_out["P"][:, 0:288], PR[:, :, :].rearrange("p t g -> p (t g)"), r=PK)
            K.dma("pool", dbg_out["P"][:, 288:576], PI[:, :, :].rearrange("p t g -> p (t g)"), r=PK)

        def gen_thunks(gb):
            gs_ = slice(8 * gb, 8 * gb + 8)
            pw = lambda P_, nt: P_[:, 0:nt, gs_].rearrange("p t g -> p g t").unsqueeze(3).to_broadcast([128, 8, nt, 16])
            bt = lambda X_, nt: X_[:, gs_, :].unsqueeze(2).to_broadcast([128, 8, nt, 16])
            v8 = lambda X_: X_[:].rearrange("p g (t q) -> p g t q", t=8)
            v9 = lambda X_: X_[:].rearrange("p g (t q) -> p g t q", t=9)
            PRK = [("Pr", sx_) for sx_ in range(8)]

            def t_e1():
                tt(v8(E32), pw(PRr, 8), bt(BbA, 8), ALU.mult, PRK + ["BbA"], ["E32a"])

            def t_e2():
                tt(v8(Gb), pw(PIr, 8), bt(BbB, 8), ALU.mult, PRK + ["BbB"], ["Gb"])

            def t_e3():
                tt(E32[:], E32[:], Gb[:], ALU.add, ["E32a", "Gb"], ["E32"])

            def t_f1():
                tt(v9(FT1), pw(PR, 9), bt(CA, 9), ALU.mult, PK + ["CA"], ["FT1"])

            def t_f2():
                tt(v9(FT2), pw(PI, 9), bt(CB, 9), ALU.mult, PK + ["CB"], ["FT2"])

            def t_f3():
                tt(Fc[:, gs_, 48:192], FT1[:], FT2[:], ALU.add, ["FT1", "FT2"], [("Fc", gb)])

            def t_tr():
                for h2 in range(2):
                    bank = 4 + h2
                    for gi in range(4):
                        gl = 4 * h2 + gi
                        K.op("pe", lambda e, gl=gl, gi=gi, bank=bank: e.transpose(out=ps[bank][:, 128 * gi:128 * (gi + 1)], in_=E32[:, gl, :], identity=ident[:]),
                             r=["E32", "ident"], w=[("ps", bank)], signal=(gi == 3))
                    K.op("act", lambda e, h2=h2, bank=bank: e.activation(out=BcT[:, 8 * gb + 4 * h2:8 * gb + 4 * h2 + 4, :], in_=ps[bank][:, :].rearrange("p (g m) -> p g m", g=4), func=AF.Identity),
                         r=[("ps", bank)], w=[("BcT", gb, h2)])

            def t_k():
                for h2 in range(2):
                    bank = 6 + h2
                    for gi in range(4):
                        gl = 4 * h2 + gi
                        g = 8 * gb + gl
                        K.op("pe", lambda e, g=g, gi=gi, bank=bank: e.matmul(out=ps[bank][0:16, 128 * gi:128 * (gi + 1)], lhsT=BbA_bf[:, g, :], rhs=Fc[:, g, 48:176], start=True, stop=False, skip_group_check=True),
                             r=["BbA_bf", ("Fc", gb)], w=[("ps", bank)], signal=False)
                        K.op("pe", lambda e, g=g, gi=gi, bank=bank: e.matmul(out=ps[bank][0:16, 128 * gi:128 * gi + 16], lhsT=Dd[:, g, :], rhs=ident[0:16, 0:16], start=False, stop=True, skip_group_check=True),
                             r=["Dd", "ident"], w=[("ps", bank)], signal=(gi == 3))
                    K.op("act", lambda e, h2=h2, bank=bank: e.activation(out=K_bf[:, 4 * h2:4 * h2 + 4, :], in_=ps[bank][0:16, :].rearrange("p (g m) -> p g m", g=4), func=AF.Identity),
                         r=[("ps", bank)], w=[("K_bf", h2)])

            def t_tc():
                for sx in range(8):
                    K.dma("sp", Tc[16 * sx:16 * (sx + 1), gs_, 16 * sx:128], K_bf[0:16, :, 0:128 - 16 * sx], r=[("K_bf", 0), ("K_bf", 1), "Tc0"], w=[("Tc", gb, sx)])

            return [t_e1, t_e2, t_e3, t_f1, t_f2, t_f3, t_tr, t_k, t_tc]

        def gen_batch(gb):
            for th in gen_thunks(gb):
                th()

        TKb = lambda gb: [("Tc", gb, sx) for sx in range(8)]
        BKb = lambda gb: [("BcT", gb, 0), ("BcT", gb, 1)]
        FKb = lambda gb: [("Fc", -1), ("Fc", gb)]

        def sample_batch(gb):
            for gl in range(8):
                g = 8 * gb + gl
                K.op("pe", lambda e, g=g, gl=gl: e.matmul(out=ps[0][:, NSEQ * gl:NSEQ * (gl + 1)], lhsT=BcT[:, g, :], rhs=U[:, g, KP:KC], start=(gl == 0), stop=False, skip_group_check=True),
                     r=BKb(gb) + UK, w=[("ps", 0)], signal=(gl == 7))
            for gl in range(8):
                g = 8 * gb + gl
                K.op("pe", lambda e, g=g, gl=gl: e.matmul(out=ps[1][:, NSEQ * gl:NSEQ * (gl + 1)], lhsT=Tc[:, g, :], rhs=U[:, g, KP:KC], start=True, stop=False, skip_group_check=True),
                     r=TKb(gb) + UK, w=[("ps", 1)], signal=False)
                K.op("pe", lambda e, g=g, gl=gl: e.matmul(out=ps[1][:, NSEQ * gl:NSEQ * (gl + 1)], lhsT=Fc[:, g, 0:128], rhs=h0bf[:, g, :], start=False, stop=True, skip_group_check=True),
                     r=FKb(gb) + ["h0bf"], w=[("ps", 1)], signal=(gl == 7))
            gs_ = slice(8 * gb, 8 * gb + 8)
            K.op("pe", lambda e: e.matmul(out=ps[0][:, 0:8 * NSEQ], lhsT=ident[:], rhs=H4[:, gs_, :].rearrange("p g s -> p (g s)"), start=False, stop=True, skip_group_check=True),
                 r=["ident", "H4"], w=[("ps", 0)], signal=True)
            K.op("act", lambda e: e.activation(out=Hns[:, gs_, :], in_=ps[0][:, 0:8 * NSEQ].rearrange("p (g s) -> p g s", g=8), func=AF.Identity), r=[("ps", 0)], w=[("Hns", gb)])
            K.op("act", lambda e: e.activation(out=s1rq2[gb % 2][:, :, KP:KC], in_=ps[1][:, 0:8 * NSEQ].rearrange("p (g s) -> p g s", g=8), func=AF.Gelu_apprx_tanh),
                 r=[("ps", 1)], w=[("s1rq", gb % 2, "s")])

        def shuffle_out(gb):
            gs_ = slice(8 * gb, 8 * gb + 8)
            rk = [("s1rq", gb % 2, b_) for b_ in range(4)] + [("s1rq", gb % 2, "s")]
            for rx in range(8):
                K.dma("sp", d_s[128 * gb:128 * (gb + 1), KC * rx:KC * (rx + 1)].rearrange("(g q) k -> q g k", q=16), s1rq2[gb % 2][16 * rx:16 * (rx + 1), :, :], r=rk, w=[("d_s", gb, rx)])
            K.dma("sp", s1f[gb][:, :, :].rearrange("p r k -> p (r k)"), d_s[128 * gb:128 * (gb + 1), :], r=[("d_s", gb, rx) for rx in range(8)], w=[("s1f", gb)])

        def cast_bank(gb, bank, eng, lo=0, hi=KP):
            hb = Hbf2[gb % 2]
            key = ("Hbf", gb % 2, bank)
            src = ps[bank][:, :].rearrange("p (g k) -> p g k", g=2)[:, :, lo:hi]
            dst = hb[:, 2 * bank:2 * bank + 2, lo:hi]
            if eng == "act":
                K.op("act", lambda e: e.activation(out=dst, in_=src, func=AF.Identity), r=[("ps", bank)], w=[key])
            else:
                V(lambda e: e.tensor_copy(out=dst, in_=src), [("ps", bank)], [key])

        CAST_ENG = {0: "act", 1: "dve", 2: "act", 3: "dve"}

        NWARM = 0
        CAST3 = {0: "act", 1: "act", 2: "act", 3: "dve"}

        def scan_batch(gb, extra=()):
            extra = list(extra)
            mj = Mj[gb % 2]
            for half in range(2):
                for gl in range(4 * half, 4 * half + 4):
                    g = 8 * gb + gl
                    bank = gl // 2
                    K.op("pe", lambda e, g=g, gl=gl, bank=bank: e.matmul(out=ps[bank][:, 256 * (gl % 2):256 * (gl % 2 + 1)], lhsT=BcT[:, g, :], rhs=U[:, g, 0:KP], start=(gl % 2 == 0), stop=True, skip_group_check=True),
                         r=BKb(gb) + UK, w=[("ps", bank)], signal=(gl % 2 == 1))
            for bank in range(4):
                cast_bank(gb, bank, CAST_ENG[bank])
            for j in range(8):
                d = 1 << j
                for half in range(2):
                    for gl in range(4 * half, 4 * half + 4):
                        g = 8 * gb + gl
                        bank = gl // 2
                        c0 = 256 * (gl % 2)
                        K.op("pe", lambda e, g=g, gl=gl, bank=bank, c0=c0, d=d, j=j: e.matmul(out=ps[bank][:, c0 + d:c0 + 256], lhsT=mj[:, gl, j, :], rhs=Hbf2[gb % 2][:, gl, 0:256 - d], start=False, stop=True, skip_group_check=True),
                             r=[("Mj", gb % 2), ("Hbf", gb % 2, bank)], w=[("ps", bank)], signal=(gl % 2 == 1))
                if 0 <= j <= 5:
                    for _w in range(NWARM):
                        K.op("pe", lambda e: e.matmul(out=ps[7][:, 0:KP], lhsT=BcT[:, 8 * gb, :], rhs=U[:, 8 * gb, 0:KP], start=True, stop=True, skip_group_check=True),
                             r=BKb(gb) + UK, w=[("ps", 7)], signal=(_w == NWARM - 1))
                heavy = bool(extra) and 0 <= j <= 5
                for bank in range(4):
                    if heavy:
                        eng_ = CAST3[bank] if j % 2 == 0 else CAST3[3 - bank]
                    else:
                        eng_ = CAST_ENG[bank] if j % 2 == 0 else CAST_ENG[3 - bank]
                    lo_, hi_ = (d, KP - 2 * d) if j < 7 else (0, KP)
                    cast_bank(gb, bank, eng_, lo_, hi_)
                if extra:
                    extra.pop(0)()
            for th in extra:
                th()
            for bank in range(4):
                g0 = 8 * gb + 2 * bank
                K.op("act", lambda e, g0=g0, bank=bank: e.activation(out=Hfin[:, g0:g0 + 2], in_=ps[bank][:, :].rearrange("p (g k) -> p g k", g=2)[:, :, 255], func=AF.Identity), r=[("ps", bank)], w=[("Hfin", g0)])

        def y_batch(gb):
            for gl in range(8):
                g = 8 * gb + gl
                bank = 4 + gl // 2
                c0 = 256 * (gl % 2)
                K.op("pe", lambda e, g=g, bank=bank, c0=c0: e.matmul(out=ps[bank][:, c0:c0 + 256], lhsT=Tc[:, g, :], rhs=U[:, g, 0:KP], start=True, stop=False, skip_group_check=True),
                     r=TKb(gb) + UK, w=[("ps", bank)], signal=False)
                K.op("pe", lambda e, g=g, bank=bank, c0=c0: e.matmul(out=ps[bank][:, c0 + 1:c0 + 256], lhsT=Fc[:, g, 64:192], rhs=Hbf2[gb % 2][:, gl, 0:255], start=False, stop=True, skip_group_check=True),
                     r=FKb(gb) + [("Hbf", gb % 2, gl // 2)], w=[("ps", bank)], signal=(gl % 2 == 1))
            for bank in range(4, 8):
                gl0 = 2 * (bank - 4)
                K.op("act", lambda e, gl0=gl0, bank=bank: e.activation(out=s1rq2[gb % 2][:, gl0:gl0 + 2, 0:KP], in_=ps[bank][:, :].rearrange("p (g k) -> p g k", g=2), func=AF.Gelu_apprx_tanh),
                     r=[("ps", bank)], w=[("s1rq", gb % 2, bank - 4)])

        import os
        ILV = os.environ.get("ILV", "1") == "1"
        def thunks_with_mj(gb_next):
            ths = gen_thunks(gb_next)
            nop = lambda: None
            return [nop] + ths[0:6] + [lambda: gen_mj(gb_next, extra_r=[("Fc", gb_next)])] + ths[6:]

        gen_batch(0)
        scan_batch(0, thunks_with_mj(1))
        y_batch(0)
        sample_batch(0)
        shuffle_out(0)
        scan_batch(1, thunks_with_mj(2))
        y_batch(1)
        sample_batch(1)
        shuffle_out(1)
        scan_batch(2, thunks_with_mj(3))
        y_batch(2)
        sample_batch(2)
        shuffle_out(2)
        scan_batch(3)
        y_batch(3)
        sample_batch(3)
        shuffle_out(3)
        K.dma("pool", hnew_s[:, :, :], Hns[:], r=[("Hns", gb) for gb in range(4)])
        K.dma("pool", hfin_p[:, :], Hfin[:], r=[("Hfin", g0) for g0 in range(0, 32, 2)])
        if dbg:
            K.dma("pool", dbg_out["U"][:, :], U[:, :, :].rearrange("p g k -> p (g k)"), r=UK)
            K.dma("pool", dbg_out["Tc"][:, :], Tc[:, :, :].rearrange("p g m -> p (g m)"), r=[k_ for gb in range(4) for k_ in TKb(gb)])
            K.dma("pool", dbg_out["BcT"][:, :], BcT[:, :, :].rearrange("p g m -> p (g m)"), r=[k_ for gb in range(4) for k_ in BKb(gb)])
            K.dma("pool", dbg_out["Fc"][:, :], Fc[:, :, :].rearrange("p g m -> p (g m)"), r=[("Fc", -1)] + [("Fc", gb) for gb in range(4)])
        K.barrier()

    if dbg:
        dbg_out["s2p"] = dout("dbg_s2p", [512, 8 * KC], BF16)
    if upto >= 4:
      with ExitStack() as s4:
        f4 = lambda name, shape, dt=F32: sb(name, shape, dt, s4)
        s2p = [f4(f"s2p{c}", [128, 8, KC], BF16) for c in range(4)]
        wg_st = f4("wg_st", [128, 4, 512]); wg_bf = f4("wg_bf", [128, 4, 512], BF16)
        wo_st = [f4(f"wo_st{i}", [128, DM]) for i in range(2)]; wo_bf = f4("wo_bf", [128, 8, DM], BF16)
        gpb = f4("gpb", [128, DM]); bpb = f4("bpb", [128, DM])
        thg = [f4(f"thg{i}", [128, 512]) for i in range(2)]; tg = [f4(f"tg{i}", [128, 512]) for i in range(2)]
        NXB = 3
        xt = [f4(f"xt{i}", [128, DM]) for i in range(NXB)]
        yn = [f4(f"yn{i}", [128, DM]) for i in range(2)]
        yo = [f4(f"yo{i}", [128, DM]) for i in range(2)]
        st6 = [f4(f"st6_{i}", [128, 2, 6]) for i in range(2)]
        mv = [f4(f"mv{i}", [128, 2]) for i in range(2)]
        ve = [f4(f"ve{i}", [128, 1]) for i in range(2)]
        rsd = [f4(f"rsd{i}", [128, 1]) for i in range(2)]
        nbv = [f4(f"nbv{i}", [128, 1]) for i in range(2)]
        smix = f4("smix", [128, 8, NS], BF16)
        aI = f4("aI", [128, 128])
        K.op("dve", lambda e: e.tensor_scalar(out=aI[:], in0=ident[:], scalar1=ALPHA, scalar2=None, op0=ALU.mult), r=["ident"], w=["aI"])
        epsc = f4("epsc", [128, 1])
        K.op("dve", lambda e: e.memset(epsc[:], EPS), w=["epsc"])
        for ci in range(4):
            K.dma("sp", wg_st[:, ci, :], w_glu[128 * ci:128 * (ci + 1), :], w=[("wg_st", ci)])
            K.op("act", lambda e, ci=ci: e.activation(out=wg_bf[:, ci, :], in_=wg_st[:, ci, :], func=AF.Identity), r=[("wg_st", ci)], w=[("wg_bf", ci)])
        wo_state = {"dma": 0, "cast": 0}

        def wo_dma(n_):
            while wo_state["dma"] < min(n_, 8):
                kk = wo_state["dma"]
                K.dma("sp", wo_st[kk % 2][:, :], w_out[128 * kk:128 * (kk + 1), :], w=[("wo_st", kk % 2)])
                wo_state["dma"] += 1

        def wo_cast_next():
            kk = wo_state["cast"]
            if kk >= 8:
                return
            wo_dma(kk + 1)
            if kk % 2 == 0:
                K.op("act", lambda e: e.activation(out=wo_bf[:, kk, :], in_=wo_st[kk % 2][:, :], func=AF.Identity), r=[("wo_st", kk % 2)], w=[("wo_bf", kk)])
            else:
                K.op("dve", lambda e: e.tensor_copy(out=wo_bf[:, kk, :], in_=wo_st[kk % 2][:, :]), r=[("wo_st", kk % 2)], w=[("wo_bf", kk)])
            wo_state["cast"] += 1
            wo_dma(kk + 3)

        wo_dma(2)
        K.dma("sp", gpb[:], gpost_d[:, :], w=["gpb"])
        K.dma("sp", bpb[:], bpost_d[:, :], w=["bpb"])

        xp_v = xp.rearrange("(k r) d -> r k d", r=8)
        yp_v = y_p.rearrange("(k r) d -> r k d", r=8)
        xs_v = xs.rearrange("(s t) d -> t s d", t=4)
        ys_v = y_s.rearrange("(s t) d -> t s d", t=4)
        blocks4 = [("p", r_) for r_ in range(8)] + [("s", 4)]
        gcount = 0
        units = [("p", r_) for r_ in range(0, 8, 2)] + [("s", 4)]
        for (kind, r_) in units:
            n = 512 if kind == "p" else NS
            view = (lambda t_, r_=r_: t_[:, r_:r_ + 2, 0:KP]) if kind == "p" else (lambda t_: t_[:, 4:8, KP:KC])
            pview = (lambda ap: ap.rearrange("p (a k) -> p a k", a=2)) if kind == "p" else (lambda ap: ap.rearrange("p (t s) -> p t s", t=4))
            wkeys = (lambda mo: [("s2p", mo, ("p", r_)), ("s2p", mo, ("p", r_ + 1))]) if kind == "p" else (lambda mo: [("s2p", mo, ("s", 4))])
            for mo in range(4):
                bank = gcount % 4
                ti = gcount % 2
                gcount += 1
                for ci in range(4):
                    K.op("pe", lambda e, mo=mo, ci=ci, bank=bank: e.matmul(out=pview(ps[bank][:, 0:n]), lhsT=wg_bf[:, ci, 128 * mo:128 * (mo + 1)], rhs=view(s1f[ci]), start=(ci == 0), stop=(ci == 3)),
                         r=[("wg_bf", ci), ("s1f", ci)], w=[("ps", bank)], signal=(ci == 3))
                K.op("act", lambda e, mo=mo, bank=bank, ti=ti: e.activation(out=thg[ti][:, 0:n], in_=ps[bank][:, 0:n], func=AF.Tanh, bias=hvec[:, 8 + mo:9 + mo], scale=0.5),
                     r=[("ps", bank), "hvec"], w=[("thg", ti)])
                K.op("dve", lambda e, mo=mo, ti=ti: e.scalar_tensor_tensor(out=pview(tg[ti][:, 0:n]), in0=pview(thg[ti][:, 0:n]), scalar=1.0, in1=view(s1f[mo]), op0=ALU.add, op1=ALU.mult),
                     r=[("thg", ti), ("s1f", mo)], w=[("tg", ti)])
                K.op("dve", lambda e, mo=mo, ti=ti: e.scalar_tensor_tensor(out=view(s2p[mo]), in0=pview(tg[ti][:, 0:n]), scalar=0.5, in1=view(gs_perm[mo]), op0=ALU.mult, op1=ALU.mult),
                     r=[("tg", ti)] + [("gs", mo, b_) for b_ in range(-1, 5)], w=wkeys(mo))
            wo_cast_next()
            wo_cast_next()
        while wo_state["cast"] < 8:
            wo_cast_next()
        for kk in range(8):
            src = s2p[kk] if kk < 4 else cfin_perm[kk - 4]
            rk = [("s2p", kk, ("s", 4)), ("s2p", kk, "z")] if kk < 4 else [("cfin", kk - 4, col0_) for col0_ in range(0, 8 * KC, 512)] + [("cfin", kk - 4, -1)]
            K.op("dve", lambda e, kk=kk, src=src: e.tensor_copy(out=smix[:, kk, :].rearrange("p (t s) -> p t s", t=4), in_=src[:, 4:8, KP:KC]), r=rk, w=[("smix", kk)])
        tcount = 0
        ocount = 0
        ln_back = []
        for (kind, r_) in blocks4:
            bkey = (kind, r_)
            tiles = [0, 128] if kind == "p" else [None]
            for k0 in tiles:
                rows = 128 if kind == "p" else NS
                xi = tcount % NXB
                oi = tcount % 2
                tcount += 1
                if kind == "p":
                    K.dma("sp", xt[xi][:, :], xp_v[r_, k0:k0 + 128, :], w=[("xt", xi)])
                else:
                    for t_ in range(4):
                        K.dma("sp", xt[xi][16 * t_:16 * (t_ + 1), :], xs_v[t_, :, :], w=[("xt", xi)] if t_ == 0 else [("xt", xi, t_)])
                xkeys = [("xt", xi)] + ([("xt", xi, t_) for t_ in range(1, 4)] if kind == "s" else [])
                banks = []
                for half in range(2):
                    bank = 2 + (ocount % 6)
                    ocount += 1
                    banks.append(bank)
                    for kk in range(8):
                        src = s2p[kk] if kk < 4 else cfin_perm[kk - 4]
                        if kind == "p":
                            lt = src[:, r_, k0:k0 + 128]
                            rk = [("s2p", kk, bkey)] if kk < 4 else [("cfin", kk - 4, col0_) for col0_ in range(0, 8 * KC, 512)] + [("cfin", kk - 4, -1)]
                        else:
                            lt = smix[:, kk, :]
                            rk = [("smix", kk)]
                        K.op("pe", lambda e, lt=lt, kk=kk, half=half, bank=bank: e.matmul(out=ps[bank][0:rows, 0:512], lhsT=lt, rhs=wo_bf[:, kk, 512 * half:512 * (half + 1)], start=(kk == 0), stop=False),
                             r=rk + [("wo_bf", kk)], w=[("ps", bank)], signal=False)
                    K.op("pe", lambda e, half=half, bank=bank: e.matmul(out=ps[bank][0:rows, 0:512], lhsT=aI[0:rows, 0:rows], rhs=xt[xi][0:rows, 512 * half:512 * (half + 1)], start=False, stop=True),
                         r=xkeys + ["aI"], w=[("ps", bank)], signal=True)
                for half in range(2):
                    K.op("dve", lambda e, half=half, bank=banks[half]: e.bn_stats(out=st6[oi][0:rows, half, :], in_=ps[bank][0:rows, 0:512]), r=[("ps", banks[half])], w=[("st6", oi, half)])
                K.op("dve", lambda e: e.bn_aggr(out=mv[oi][0:rows, :], in_=st6[oi][0:rows, :, :].rearrange("p a b -> p (a b)")), r=[("st6", oi, 0), ("st6", oi, 1)], w=[("mv", oi)])
                K.op("act", lambda e: e.activation(out=ve[oi][0:rows, :], in_=mv[oi][0:rows, 1:2], func=AF.Sqrt, bias=epsc[0:rows, :]), r=[("mv", oi), "epsc"], w=[("ve", oi)])
                for th in ln_back:
                    th()
                ln_back.clear()
                K.op("dve", lambda e: e.reciprocal(out=rsd[oi][0:rows, :], in_=ve[oi][0:rows, :]), r=[("ve", oi)], w=[("rsd", oi)])
                K.op("dve", lambda e: e.scalar_tensor_tensor(out=nbv[oi][0:rows, :], in0=mv[oi][0:rows, 0:1], scalar=-1.0, in1=rsd[oi][0:rows, :], op0=ALU.mult, op1=ALU.mult),
                     r=[("mv", oi), ("rsd", oi)], w=[("nbv", oi)])
                for half in range(2):
                    K.op("act", lambda e, half=half, bank=banks[half]: e.activation(out=yn[oi][0:rows, 512 * half:512 * (half + 1)], in_=ps[bank][0:rows, 0:512], func=AF.Identity, bias=nbv[oi][0:rows, :], scale=rsd[oi][0:rows, :]),
                         r=[("ps", banks[half]), ("nbv", oi), ("rsd", oi)], w=[("yn", oi, half)])
                def back(oi=oi, rows=rows, kind=kind, r_=r_, k0=k0):
                    K.op("dve", lambda e: e.tensor_tensor(out=yn[oi][0:rows, :], in0=yn[oi][0:rows, :], in1=gpb[0:rows, :], op=ALU.mult), r=[("yn", oi, 0), ("yn", oi, 1), "gpb"], w=[("yn", oi, 2)])
                    K.op("dve", lambda e: e.tensor_tensor(out=yo[oi][0:rows, :], in0=yn[oi][0:rows, :], in1=bpb[0:rows, :], op=ALU.add), r=[("yn", oi, 2), "bpb"], w=[("yo", oi)])
                    if kind == "p":
                        K.dma("pool", yp_v[r_, k0:k0 + 128, :], yo[oi][:, :], r=[("yo", oi)])
                    else:
                        for t_ in range(4):
                            K.dma("pool", ys_v[t_, :, :], yo[oi][16 * t_:16 * (t_ + 1), :], r=[("yo", oi)])
                ln_back.append(back)
        for th in ln_back:
            th()
        ln_back.clear()
        if dbg:
            for c in range(4):
                K.dma("pool", dbg_out["s2p"][128 * c:128 * (c + 1), :], s2p[c][:, :, :].rearrange("p r k -> p (r k)"), r=[("s2p", c, bk) for bk in [("p", r_) for r_ in range(8)] + [("s", 4), "z"]])
        K.barrier()

    K.barrier(["sp"])
    es.close()
    return nc


def _prep_inputs(inp):
    f = lambda k: np.ascontiguousarray(np.asarray(inp[k], np.float32))
    x_prompt = f("x_prompt")
    x_sample = f("x_sample")
    sre = f("state_ssm_re")[0]
    sim = f("state_ssm_im")[0]
    scv = f("state_conv")[0]
    b_in = f("b_in")[0]
    vecs = np.concatenate([
        b_in.reshape(20, 128), f("b_glu")[0].reshape(4, 128), f("b_dw")[0].reshape(4, 128),
        f("g_conv_ln")[0].reshape(4, 128), f("b_conv_ln")[0].reshape(4, 128), f("b_pw2")[0].reshape(4, 128)], 0).T
    wdwT = f("w_dw")[0].reshape(31, 4, 128).transpose(2, 1, 0)
    lam_re = f("lam_re")[0]
    lam_im = f("lam_im")[0]
    lamT = np.stack([lam_re.T, lam_im.T], 1)
    lamT = np.concatenate([lamT, lamT], 0)
    br = f("b_re")[0].transpose(1, 0, 2)
    bi = f("b_im")[0].transpose(1, 0, 2)
    cr = f("c_re")[0].transpose(2, 0, 1)
    ci = f("c_im")[0].transpose(2, 0, 1)
    wdw = f("w_dw")[0]
    wdg = np.zeros((8, 16, 32, 5, 8), np.float32)
    for s_ in range(8):
        for d_ in range(5):
            for r_ in range(8):
                tau = 8 * d_ + r_ - s_
                if 0 <= tau <= 30:
                    wdg[s_, :, :, d_, r_] = wdw[30 - tau].reshape(32, 16).T
    sccol = np.stack([np.tile(f(k)[0].reshape(32, 16).T, (8, 1)) for k in ("b_dw", "g_conv_ln", "b_conv_ln")], 1)
    shared = {
        "w_in": f("w_in")[0], "w_glu": f("w_glu")[0], "w_pw2": f("w_pw2")[0], "w_out": f("w_out")[0],
        "vecs": np.ascontiguousarray(vecs), "wdwT": np.ascontiguousarray(wdwT),
        "gpost": np.ascontiguousarray(np.broadcast_to(f("g_post")[0].reshape(1, DM), (128, DM))), "bpost": np.ascontiguousarray(np.broadcast_to(f("b_post")[0].reshape(1, DM), (128, DM))),
        "ident": np.eye(128, dtype=np.float32),
        "lamT": np.ascontiguousarray(lamT), "logdt": np.ascontiguousarray(np.broadcast_to(f("log_dt")[0].reshape(1, 32), (128, 32))),
        "BA": np.ascontiguousarray(np.concatenate([br, bi], 0)), "BBraw": np.ascontiguousarray(np.concatenate([bi, br], 0)),
        "CAraw": np.ascontiguousarray(np.concatenate([cr, ci], 0)), "CBraw": np.ascontiguousarray(np.concatenate([ci, cr], 0)),
        "dskT": np.ascontiguousarray(f("d_skip")[0].reshape(32, 16).T),
        "wdg": np.ascontiguousarray(wdg.reshape(128, 32 * 5 * 8)), "sccol": np.ascontiguousarray(sccol.astype(np.float32)),
    }
    in_maps = []
    for i in range(NCORES):
        sl = slice(NSEQ * i, NSEQ * (i + 1))
        hre = sre[sl].transpose(2, 1, 0)
        him = sim[sl].transpose(2, 1, 0)
        m = dict(shared)
        m["xp"] = x_prompt[i]
        m["xs"] = np.ascontiguousarray(x_sample[sl].reshape(NS, DM))
        m["h0A"] = np.ascontiguousarray(np.concatenate([hre, him], 0))
        m["h0Braw"] = np.ascontiguousarray(np.concatenate([him, hre], 0))
        m["stT"] = np.ascontiguousarray(scv[sl].transpose(2, 0, 1))
        pad = np.zeros((NSEQ, 40, 512), np.float32)
        pad[:, 6:36, :] = scv[sl]
        m["stP"] = np.ascontiguousarray(pad.reshape(NSEQ, 5, 8, 512).transpose(3, 2, 0, 1).reshape(512, 8, 5 * NSEQ))
        in_maps.append(m)
    return in_maps


def _assemble(results):
    y_p = np.stack([r["y_p"] for r in results], 0)
    y_s = np.concatenate([r["y_s"].reshape(NSEQ, 4, DM) for r in results], 0)
    hp = np.stack([r["hfin_p"] for r in results], 0)
    re_p = hp[:, 0:64, :].transpose(0, 2, 1)[None]
    im_p = hp[:, 64:128, :].transpose(0, 2, 1)[None]
    cv_p = np.stack([r["ncvT_p"].T for r in results], 0)[None]
    hs = np.concatenate([r["hnew_s"].transpose(2, 1, 0) for r in results], 0)
    re_s = hs[:, :, 0:64][None]
    im_s = hs[:, :, 64:128][None]
    cv_s = np.concatenate([r["ncvT_s"].reshape(512, NSEQ, 30).transpose(1, 2, 0) for r in results], 0)[None]
    c = lambda a: np.ascontiguousarray(a.astype(np.float32))
    return (c(y_p), c(y_s), c(re_p), c(im_p), c(cv_p), c(re_s), c(im_s), c(cv_s))


_NC_CACHE = {}


def kernel(**inputs):
    in_maps = _prep_inputs(inputs)
    if "nc" not in _NC_CACHE:
        _NC_CACHE["nc"] = build_program()
    res = run_bass_kernel_spmd(_NC_CACHE["nc"], in_maps, core_ids=list(range(NCORES)))
    return _assemble(res.results)
```

```python
import numpy as np
import ml_dtypes
from contextlib import ExitStack
import concourse.bass as bass
import concourse.mybir as mybir
from concourse.bass_utils import run_bass_kernel_spmd

F32 = mybir.dt.float32
BF16 = mybir.dt.bfloat16
I32 = mybir.dt.int32
AF = mybir.ActivationFunctionType
ALU = mybir.AluOpType

NCORES = 8
DM = 1024
NP = 2048
NSEQ = 16
NS = 64
NT = NP + NS
KP = 256
KC = KP + NSEQ
KV = KP + 5 * NSEQ
ALPHA = 2.0 ** 0.25
EPS = 1e-5
TWO_PI = 6.283185307179586
BLOCKS = [(0, 512), (512, 512), (1024, 512), (1536, 512), (2048, 64)]


class KB:
    def __init__(self, nc, es):
        self.nc = nc
        self.E = {"pe": nc.tensor, "act": nc.scalar, "dve": nc.vector, "pool": nc.gpsimd, "sp": nc.sync}
        self.sem = {}
        self.cnt = {}
        for k in self.E:
            self.sem[k] = es.enter_context(nc.semaphore("sem_" + k))
            self.cnt[k] = 0
        self.NDS = 20
        self.dq = ("sp", "pool", "act")
        self.dsem = {q: [es.enter_context(nc.semaphore(f"d_{q}{i}")) for i in range(self.NDS)] for q in self.dq}
        self.dval = {q: [0] * self.NDS for q in self.dq}
        self.drr = {q: 0 for q in self.dq}
        self.seen = {k: {} for k in self.E}
        self.lw = {}
        self.rd = {}
        self.pend = {k: [] for k in self.E}

    def _semobj(self, sk):
        return self.sem[sk] if isinstance(sk, str) else self.dsem[sk[0]][sk[1]]

    def _wait(self, eng, sk, val):
        if val <= 0:
            return
        if eng == "pe" and sk == "pe":
            return
        if self.seen[eng].get(sk, 0) >= val:
            return
        self.E[eng].wait_ge(self._semobj(sk), val)
        self.seen[eng][sk] = val

    def _deps(self, eng, r, w):
        for k in r:
            t = self.lw.get(k)
            if t:
                self._wait(eng, *t)
        for k in w:
            t = self.lw.get(k)
            if t:
                self._wait(eng, *t)
            for sk, v in self.rd.get(k, {}).items():
                self._wait(eng, sk, v)

    def _commit(self, t, r, w):
        for k in w:
            self.lw[k] = t
            self.rd[k] = {}
        for k in r:
            d = self.rd.setdefault(k, {})
            d[t[0]] = max(d.get(t[0], 0), t[1])

    def op(self, eng, fn, r=(), w=(), signal=True):
        r = list(r)
        w = list(w)
        self._deps(eng, r, w)
        ins = fn(self.E[eng])
        if not signal:
            self.pend[eng].append((r, w))
            return None
        self.cnt[eng] += 1
        ins.then_inc(self.sem[eng], 1)
        t = (eng, self.cnt[eng])
        for (pr, pw) in self.pend[eng]:
            self._commit(t, pr, pw)
        self.pend[eng] = []
        self._commit(t, r, w)
        return t

    def dma(self, q, out, in_, r=(), w=()):
        r = list(r)
        w = list(w)
        self._deps(q, r, w)
        i = self.drr[q] % self.NDS
        self.drr[q] += 1
        self._wait(q, (q, i), self.dval[q][i])
        ins = self.E[q].dma_start(out=out, in_=in_)
        self.dval[q][i] += 16
        ins.then_inc(self.dsem[q][i], 16)
        t = ((q, i), self.dval[q][i])
        self._commit(t, r, w)
        return t

    def barrier(self, engines=None, dma_queues=None):
        engines = engines or list(self.E)
        dma_queues = self.dq if dma_queues is None else dma_queues
        for e in engines:
            for k in self.E:
                if k == "sp" and "sp" not in dma_queues:
                    continue
                self._wait(e, k, self.cnt[k])
            for q in dma_queues:
                for i in range(self.NDS):
                    self._wait(e, (q, i), self.dval[q][i])


def build_program(upto=9, dbg=False):
    nc = bass.Bass("TRN2", target_bir_lowering=False)
    es = ExitStack()
    K = KB(nc, es)

    def din(name, shape, dt=F32):
        return nc.dram_tensor(name, list(shape), dt, kind="ExternalInput").ap()

    def dout(name, shape, dt=F32):
        return nc.dram_tensor(name, list(shape), dt, kind="ExternalOutput").ap()

    def dscr(name, shape, dt):
        return nc.dram_tensor(name, list(shape), dt).ap()

    xp = din("xp", [NP, DM])
    xs = din("xs", [NS, DM])
    w_in = din("w_in", [DM, 2560])
    w_glu = din("w_glu", [512, 512])
    w_pw2 = din("w_pw2", [512, 512])
    w_out = din("w_out", [DM, DM])
    vecs_d = din("vecs", [128, 40])
    wdwT_d = din("wdwT", [128, 4, 31])
    gpost_d = din("gpost", [128, DM])
    bpost_d = din("bpost", [128, DM])
    ident_d = din("ident", [128, 128])
    lamT_d = din("lamT", [128, 2, 32])
    logdt_d = din("logdt", [128, 32])
    BA_d = din("BA", [128, 32, 16])
    BB_d = din("BBraw", [128, 32, 16])
    CA_d = din("CAraw", [128, 32, 16])
    CB_d = din("CBraw", [128, 32, 16])
    dskT_d = din("dskT", [16, 32])
    h0A_d = din("h0A", [128, 32, NSEQ])
    h0B_d = din("h0Braw", [128, 32, NSEQ])
    stT_d = din("stT", [512, NSEQ, 30])
    stP_d = din("stP", [512, 8, 5 * NSEQ])
    wdg_d = din("wdg", [128, 32 * 5 * 8])
    sccol_d = din("sccol", [128, 3, 32])

    y_p = dout("y_p", [NP, DM])
    y_s = dout("y_s", [NS, DM])
    hfin_p = dout("hfin_p", [128, 32])
    hnew_s = dout("hnew_s", [128, 32, NSEQ])
    ncvT_p = dout("ncvT_p", [512, 30])
    ncvT_s = dout("ncvT_s", [512, NSEQ * 30])

    d_u = dscr("d_u", [128, 32 * KC], BF16)
    d_s = dscr("d_s", [512, 8 * KC], BF16)
    d_v = dscr("d_v", [128, 32 * KV], BF16)
    d_c = dscr("d_c", [512, 8 * KC], BF16)

    dbg_out = {}
    if dbg:
        dbg_out["u_perm"] = dout("dbg_u_perm", [512, 8 * KC], BF16)
        dbg_out["gs"] = dout("dbg_gs", [512, 8 * KC], BF16)
        dbg_out["gc"] = dout("dbg_gc", [512, 8 * KC], BF16)
        dbg_out["vperm"] = dout("dbg_vperm", [512, 8 * KV], BF16)
        dbg_out["U"] = dout("dbg_U", [128, 32 * KC], BF16)

    def sb(name, shape, dt, stack=es, side=None):
        return stack.enter_context(nc.sbuf_tensor("sb_" + name, list(shape), dt, side=side))

    ps = [es.enter_context(nc.psum_tensor(f"ps{i}", [128, 512], F32)) for i in range(8)]

    ident = sb("ident", [128, 128], F32, side="right")
    vecs = sb("vecs", [128, 40], F32, side="right")
    hvec = sb("hvec", [128, 12], F32, side="right")
    gs_perm = [sb(f"gs_perm{c}", [128, 8, KC], BF16, side="right") for c in range(4)]
    sA = ExitStack()
    gc_perm = [sb(f"gc_perm{c}", [128, 8, KC], BF16, sA) for c in range(4)]

    wdg = sb("wdg", [128, 32 * 5 * 8], F32, side="right")
    sccol = sb("sccol", [128, 3, 32], F32, side="right")
    K.dma("sp", ident[:], ident_d[:, :], w=["ident"])
    K.dma("sp", vecs[:], vecs_d[:, :], w=["vecs"])
    K.dma("sp", wdg[:], wdg_d[:, :], w=["wdg"])
    K.dma("sp", sccol[:], sccol_d[:, :, :], w=["sccol"])
    K.op("dve", lambda e: e.tensor_scalar(out=hvec[:, 0:8], in0=vecs[:, 8:16], scalar1=0.5, scalar2=None, op0=ALU.mult),
         r=["vecs"], w=["hvec"])
    K.op("dve", lambda e: e.tensor_scalar(out=hvec[:, 8:12], in0=vecs[:, 20:24], scalar1=0.5, scalar2=None, op0=ALU.mult),
         r=["vecs"], w=["hvec"])
    for c in range(4):
        K.op("dve", lambda e, c=c: e.memset(gc_perm[c][:, 0:4, KP:KC], 0.0), w=[("gc", c, -1)])
        K.op("dve", lambda e, c=c: e.memset(gs_perm[c][:, 0:4, KP:KC], 0.0), w=[("gs", c, -1)])

    fr = lambda name, shape, dt=F32: sb(name, shape, dt, side="right")
    lam = fr("lam", [128, 2, 32]); dtb = fr("dtb", [128, 32])
    T = [fr(f"T{i}", [128, 4, 32]) for i in range(4)]
    PR = fr("PR", [128, 9, 32]); PI = fr("PI", [128, 9, 32])
    QR = fr("QR", [128, 8, 32]); QI = fr("QI", [128, 8, 32])
    mag = fr("mag", [128, 32]); phi = fr("phi", [128, 32]); kf = fr("kf", [128, 32]); ki = fr("ki", [128, 32], I32)
    rr = fr("rr", [128, 32]); rc = fr("rc", [128, 32]); msk = fr("msk", [128, 32]); sinv = fr("sinv", [128, 32]); cosv = fr("cosv", [128, 32])
    KR = fr("KR", [128, 32]); KI = fr("KI", [128, 32]); den = fr("den", [128, 32]); am1 = fr("am1", [128, 32])
    S4 = fr("S4", [128, 32, 8, 2]); S_bf = fr("S_bf", [128, 32, 8, 2], BF16); I2 = fr("I2", [128, 64], BF16)
    PRr = fr("PRr", [128, 8, 32]); PIr = fr("PIr", [128, 8, 32])

    def V(fn, r, w, eng="dve"):
        return K.op(eng, fn, r=r, w=w)

    def tt(o, a, b, op, r, w, eng="dve"):
        return K.op(eng, lambda e: e.tensor_tensor(out=o, in0=a, in1=b, op=op), r=r, w=w)

    def ts(o, a, s1_, op0, r, w, s2_=None, op1=None, eng="dve"):
        if op1 is None:
            return K.op(eng, lambda e: e.tensor_scalar(out=o, in0=a, scalar1=s1_, scalar2=None, op0=op0), r=r, w=w)
        return K.op(eng, lambda e: e.tensor_scalar(out=o, in0=a, scalar1=s1_, scalar2=s2_, op0=op0, op1=op1), r=r, w=w)

    G = "pool"

    def cmul(oR, oI, xR, xI, yR, yI, nslots, r, w):
        a, b, c_, d_ = (T[i][:, 0:nslots, :] for i in range(4))
        tk = ["T0", "T1", "T2", "T3"]
        tt(a, xR, yR, ALU.mult, r, [tk[0]], G); tt(b, xI, yI, ALU.mult, r, [tk[1]], G)
        tt(c_, xR, yI, ALU.mult, r, [tk[2]], G); tt(d_, xI, yR, ALU.mult, r, [tk[3]], G)
        tt(oR, a, b, ALU.subtract, [tk[0], tk[1]], w, G); tt(oI, c_, d_, ALU.add, [tk[2], tk[3]], w, G)

    def g1_gen():
        K.dma("sp", lam[:], lamT_d[:, :, :], w=["lam"]); K.dma("sp", dtb[:], logdt_d[:, :], w=["dtb"])
        lr = lam[:, 0, :]; li = lam[:, 1, :]
        K.op("act", lambda e: e.activation(out=dtb[:], in_=dtb[:], func=AF.Exp), r=["dtb"], w=["dtb"])
        yield
        tt(mag[:], lr, dtb[:], ALU.mult, ["lam", "dtb"], ["mag"], G)
        yield
        K.op("act", lambda e: e.activation(out=mag[:], in_=mag[:], func=AF.Exp), r=["mag"], w=["mag"])
        yield
        tt(phi[:], li, dtb[:], ALU.mult, ["lam", "dtb"], ["phi"], G)
        ts(kf[:], phi[:], 1.0 / TWO_PI, ALU.mult, ["phi"], ["kf"], eng=G)
        yield
        V(lambda e: e.tensor_copy(out=ki[:], in_=kf[:]), ["kf"], ["ki"])
        V(lambda e: e.tensor_copy(out=kf[:], in_=ki[:]), ["ki"], ["kf"])
        yield
        ts(kf[:], kf[:], -TWO_PI, ALU.mult, ["kf"], ["kf"], eng=G)
        tt(rr[:], kf[:], phi[:], ALU.add, ["kf", "phi"], ["rr"], G)
        PIS = 3.141592
        ts(rr[:], rr[:], -PIS, ALU.max, ["rr"], ["rr"], PIS, ALU.min, eng=G)
        ts(rc[:], rr[:], TWO_PI / 4.0, ALU.add, ["rr"], ["rc"], eng=G)
        ts(msk[:], rc[:], PIS, ALU.is_gt, ["rc"], ["msk"], eng=G)
        ts(msk[:], msk[:], -TWO_PI, ALU.mult, ["msk"], ["msk"], eng=G)
        tt(rc[:], rc[:], msk[:], ALU.add, ["msk", "rc"], ["rc"], G)
        ts(rc[:], rc[:], -PIS, ALU.max, ["rc"], ["rc"], PIS, ALU.min, eng=G)
        yield
        K.op("act", lambda e: e.activation(out=sinv[:], in_=rr[:], func=AF.Sin), r=["rr"], w=["sinv"])
        K.op("act", lambda e: e.activation(out=cosv[:], in_=rc[:], func=AF.Sin), r=["rc"], w=["cosv"])
        yield
        V(lambda e: e.memset(PR[:, 0, :], 1.0), [], ["P0"], G); V(lambda e: e.memset(PI[:, 0, :], 0.0), [], ["P0"], G)
        tt(PR[:, 1, :], mag[:], cosv[:], ALU.mult, ["mag", "cosv"], ["P1"], G); tt(PI[:, 1, :], mag[:], sinv[:], ALU.mult, ["mag", "sinv"], ["P1"], G)
        cmul(PR[:, 2:3, :], PI[:, 2:3, :], PR[:, 1:2, :], PI[:, 1:2, :], PR[:, 1:2, :], PI[:, 1:2, :], 1, ["P1"], ["P2"])
        cmul(PR[:, 3:5, :], PI[:, 3:5, :], PR[:, 1:3, :], PI[:, 1:3, :], PR[:, 2:3, :].to_broadcast([128, 2, 32]), PI[:, 2:3, :].to_broadcast([128, 2, 32]), 2, ["P1", "P2"], ["P34"])
        cmul(PR[:, 5:9, :], PI[:, 5:9, :], PR[:, 1:5, :], PI[:, 1:5, :], PR[:, 4:5, :].to_broadcast([128, 4, 32]), PI[:, 4:5, :].to_broadcast([128, 4, 32]), 4, ["P1", "P2", "P34"], ["P58"])
        PK = ["P0", "P1", "P2", "P34", "P58"]
        V(lambda e: e.tensor_copy(out=QR[:, 0, :], in_=PR[:, 8, :]), PK, [("Q", 0)], G); V(lambda e: e.tensor_copy(out=QI[:, 0, :], in_=PI[:, 8, :]), PK, [("Q", 0)], G)
        for j in range(7):
            cmul(QR[:, j + 1:j + 2, :], QI[:, j + 1:j + 2, :], QR[:, j:j + 1, :], QI[:, j:j + 1, :], QR[:, j:j + 1, :], QI[:, j:j + 1, :], 1, [("Q", j)], [("Q", j + 1)])
        QK = [("Q", j) for j in range(8)]
        tt(den[:], lr, lr, ALU.mult, ["lam"], ["den"], G); tt(kf[:], li, li, ALU.mult, ["lam"], ["kf"], G)
        tt(den[:], den[:], kf[:], ALU.add, ["den", "kf"], ["den"], G)
        yield
        V(lambda e: e.reciprocal(out=den[:], in_=den[:]), ["den"], ["den"])
        yield
        ts(am1[:], PR[:, 1, :], -1.0, ALU.add, ["P1"], ["am1"], eng=G)
        tt(KR[:], am1[:], lr, ALU.mult, ["am1", "lam"], ["KR"], G); tt(kf[:], PI[:, 1, :], li, ALU.mult, ["P1", "lam"], ["kf"], G)
        tt(KR[:], KR[:], kf[:], ALU.add, ["KR", "kf"], ["KR"], G); tt(KR[:], KR[:], den[:], ALU.mult, ["KR", "den"], ["KR"], G)
        tt(KI[:], PI[:, 1, :], lr, ALU.mult, ["P1", "lam"], ["KI"], G); tt(kf[:], am1[:], li, ALU.mult, ["am1", "lam"], ["kf"], G)
        tt(KI[:], KI[:], kf[:], ALU.subtract, ["KI", "kf"], ["KI"], G); tt(KI[:], KI[:], den[:], ALU.mult, ["KI", "den"], ["KI"], G)
        V(lambda e: e.tensor_copy(out=S4[0:64, :, :, 0], in_=QR[0:64, :, :].rearrange("p j g -> p g j")), QK, ["S4a"], G)
        ts(S4[64:128, :, :, 0], QI[64:128, :, :].rearrange("p j g -> p g j"), -1.0, ALU.mult, QK, ["S4b"], eng=G)
        V(lambda e: e.tensor_copy(out=S4[0:64, :, :, 1], in_=QI[0:64, :, :].rearrange("p j g -> p g j")), QK, ["S4c"], G)
        V(lambda e: e.tensor_copy(out=S4[64:128, :, :, 1], in_=QR[64:128, :, :].rearrange("p j g -> p g j")), QK, ["S4d"], G)
        V(lambda e: e.tensor_copy(out=S_bf[:], in_=S4[:]), ["S4a", "S4b", "S4c", "S4d"], ["S_bf"], G)
        tt(I2[:], ident[:, 0:64], ident[:, 64:128], ALU.add, ["ident"], ["I2"], G)
        for sx_ in range(8):
            V(lambda e, sx_=sx_: e.tensor_copy(out=PRr[:, sx_, :], in_=PR[:, 7 - sx_, :]), PK, [("Pr", sx_)], G)
            V(lambda e, sx_=sx_: e.tensor_copy(out=PIr[:, sx_, :], in_=PI[:, 7 - sx_, :]), PK, [("Pr", sx_)], G)


        yield

    g1 = g1_gen()
    next(g1)

    PK = ["P0", "P1", "P2", "P34", "P58"]
    QK = [("Q", j) for j in range(8)]

    with ExitStack() as s1:
        w_bf = sb("w_bf", [128, 8, 2560], BF16, s1)
        wst = [sb(f"wst{i}", [128, 8, 256], F32, s1) for i in range(2)]
        x_sb = [sb(f"x_sb{i}", [128, DM], F32, s1) for i in range(4)]
        xT = [sb(f"xT{i}", [128, 8, 512], BF16, s1) for i in range(2)]
        th = [sb(f"th{i}", [128, 512], F32, s1) for i in range(2)]
        ah = [sb(f"ah{i}", [128, 512], F32, s1) for i in range(2)]
        st32 = sb("st32", [128, 4, NSEQ, 30], F32, s1)
        stp32 = sb("stp32", [128, 8, 5 * NSEQ], F32, s1)
        v32p = sb("v32p", [128, 4, 30], F32, s1)
        ncs = sb("ncs", [128, 4, NSEQ, 30], F32, s1)
        u_perm = [sb(f"u_perm{c}", [128, 8, KC], BF16, s1) for c in range(4)]
        v_perm = [sb(f"v_perm{c}", [128, 8, KV], BF16, s1) for c in range(4)]

        def load_x(bi):
            t0, n = BLOCKS[bi]
            if n == 512:
                for i in range(4):
                    K.dma("sp", x_sb[i][:, :], xp[t0 + 128 * i:t0 + 128 * (i + 1), :], w=[("x", i)])
            else:
                K.dma("sp", x_sb[0][0:NS, :], xs[:, :], w=[("x", 0)])

        for c in range(4):
            K.op("dve", lambda e, c=c: e.memset(u_perm[c][:, 0:4, KP:KC], 0.0), w=[("uperm", c, -1)])
        BORD = [0, 4, 1, 2, 3]
        load_x(BORD[0])
        M_ORDER = [12, 8, 13, 9, 14, 10, 15, 11, 0, 1, 2, 3, 4, 5, 6, 7, 16, 17, 18, 19]
        CH_ORDER = []
        for m_ in M_ORDER:
            if m_ // 2 not in CH_ORDER:
                CH_ORDER.append(m_ // 2)
        w_state = {"dma": 0, "cast": set()}

        def w_issue_dma(upto_n):
            while w_state["dma"] < min(upto_n, len(CH_ORDER)):
                i = w_state["dma"]
                cc = CH_ORDER[i]
                for kk in range(8):
                    K.dma("sp", wst[i % 2][:, kk, :], w_in[128 * kk:128 * (kk + 1), 256 * cc:256 * (cc + 1)], w=[("wst", i % 2, kk)])
                w_state["dma"] += 1

        def w_ensure(cc):
            if cc in w_state["cast"]:
                return
            i = CH_ORDER.index(cc)
            w_issue_dma(i + 1)
            K.op("dve", lambda e: e.tensor_copy(out=w_bf[:, 0:4, 256 * cc:256 * (cc + 1)], in_=wst[i % 2][:, 0:4, :]),
                 r=[("wst", i % 2, kk) for kk in range(0, 4)], w=[("wbf", cc, 0)])
            K.op("act", lambda e: e.activation(out=w_bf[:, 4:8, 256 * cc:256 * (cc + 1)], in_=wst[i % 2][:, 4:8, :], func=AF.Identity),
                 r=[("wst", i % 2, kk) for kk in range(4, 8)], w=[("wbf", cc, 1)])
            w_state["cast"].add(cc)
            w_issue_dma(i + 3)

        w_issue_dma(2)
        def st_dma(c):
            K.dma("sp", st32[:, c, :, :], stT_d[128 * c:128 * (c + 1), :, :], w=[("st32", c)])
            K.dma("sp", stp32[:, :, :], stP_d[128 * c:128 * (c + 1), :, :], w=["stp32"])

        def st_copy(c):
            K.op("dve", lambda e, c=c: e.tensor_copy(out=ncs[:, c, :, 0:26], in_=st32[:, c, :, 4:30]), r=[("st32", c)], w=[("ncs", c, 0)])
            K.op("dve", lambda e, c=c: e.tensor_copy(out=v_perm[c][:, :, KP:KV], in_=stp32[:, :, :]), r=["stp32"], w=[("vperm", c, -1)])
        ST_AT = {1: ("d", 0), 3: ("c", 0), 4: ("d", 1), 6: ("c", 1), 7: ("d", 2), 9: ("c", 2), 11: ("d", 3), 13: ("c", 3)}
        zi = 0
        ev = 0
        ct_count = [0]
        G1_AT = {2: 1, 3: 1, 4: 1, 10: 1, 11: 1, 22: 1, 23: 1, 60: 1, 61: 1}

        def g1_tick():
            ct_count[0] += 1
            if ct_count[0] in G1_AT:
                try:
                    next(g1)
                except StopIteration:
                    pass
        du_v = d_u.rearrange("(r c) (g k) -> c r g k", c=16, g=32)
        dv_v = d_v.rearrange("(r c) (g k) -> c r g k", c=16, g=32)

        def scratch_piece(pc):
            for c in range(4):
                for gl in range(8):
                    g = 8 * c + gl
                    ps_ = slice(16 * gl, 16 * (gl + 1))
                    if pc < 2:
                        ks = slice(128 * pc, 128 * (pc + 1))
                        rk_u = [("uperm", c, 2 * pc), ("uperm", c, 2 * pc + 1)]
                        rk_v = [("vperm", c, 2 * pc), ("vperm", c, 2 * pc + 1)]
                        K.dma("sp", du_v[:, :, g, ks], u_perm[c][ps_, :, ks], r=rk_u, w=[("d_u", c, pc, gl)])
                        K.dma("sp", dv_v[:, :, g, ks], v_perm[c][ps_, :, ks], r=rk_v, w=[("d_v", c, pc, gl)])
                    else:
                        K.dma("sp", du_v[:, :, g, KP:KC], u_perm[c][ps_, :, KP:KC], r=[("uperm", c, -1), ("uperm", c, 4)], w=[("d_u", c, pc, gl)])
                        K.dma("sp", dv_v[:, :, g, KP:KV], v_perm[c][ps_, :, KP:KV], r=[("vperm", c, -1), ("vperm", c, 4)], w=[("d_v", c, pc, gl)])

        def do_transposes(pos):
            bj = BORD[pos]
            t0_, n_ = BLOCKS[bj]
            bb_ = pos % 2
            nt_ = 4 if n_ == 512 else 1
            rows_ = 128 if n_ == 512 else NS
            for kk in range(8):
                bank = 4 + (kk % 2)
                for i in range(nt_):
                    K.op("pe", lambda e, i=i, kk=kk, bank=bank: e.transpose(out=ps[bank][:, rows_ * i:rows_ * (i + 1)], in_=x_sb[i][0:rows_, 128 * kk:128 * (kk + 1)], identity=ident[0:rows_, 0:rows_]),
                         r=[("x", i), "ident"], w=[("ps", bank)], signal=(i == nt_ - 1))
                if kk % 2 == 0:
                    K.op("dve", lambda e, kk=kk, bank=bank: e.tensor_copy(out=xT[bb_][:, kk, 0:n_], in_=ps[bank][:, 0:n_]), r=[("ps", bank)], w=[("xT", bb_, kk)])
                else:
                    K.op("act", lambda e, kk=kk, bank=bank: e.activation(out=xT[bb_][:, kk, 0:n_], in_=ps[bank][:, 0:n_], func=AF.Identity), r=[("ps", bank)], w=[("xT", bb_, kk)])
            if pos + 1 < len(BORD):
                load_x(BORD[pos + 1])

        do_transposes(0)
        for pos_, bi in enumerate(BORD):
            t0, n = BLOCKS[bi]
            bb = pos_ % 2
            if pos_ == 2:
                scratch_piece(2)
            nt = 4 if n == 512 else 1
            rows = 128 if n == 512 else NS
            if bi == 2:
                scratch_piece(0)
            for mi_, m in enumerate(M_ORDER):
                if mi_ == 10 and pos_ + 1 < len(BORD):
                    do_transposes(pos_ + 1)
                if bi == 3 and mi_ == 12:
                    scratch_piece(1)
                if pos_ == 0 and mi_ in ST_AT:
                    kind_, c_ = ST_AT[mi_]
                    (st_dma if kind_ == "d" else st_copy)(c_)
                bank = zi % 4
                zi += 1
                cc = m // 2
                w_ensure(cc)
                for kk in range(8):
                    K.op("pe", lambda e, m=m, kk=kk, bank=bank: e.matmul(out=ps[bank][:, 0:n], lhsT=w_bf[:, kk, 128 * m:128 * (m + 1)], rhs=xT[bb][:, kk, 0:n], start=(kk == 0), stop=(kk == 7)),
                         r=[("wbf", cc, kk // 4), ("xT", bb, kk)], w=[("ps", bank)], signal=(kk == 7))
                g1_tick()
                pz = ps[bank][:, 0:n]
                bcol = vecs[:, m:m + 1]
                c = m % 4
                if m < 8:
                    dst_t = u_perm[c] if m < 4 else gs_perm[c]
                    key = ("uperm" if m < 4 else "gs", c, bi)
                    func = AF.Identity if m < 4 else AF.Silu
                    if n == 512:
                        k0 = t0 // 8
                        o_ap = dst_t[:, :, k0:k0 + 64]
                        i_ap = pz.rearrange("p (k r) -> p r k", r=8)
                    else:
                        o_ap = dst_t[:, 4:8, KP:KC]
                        i_ap = pz.rearrange("p (s t) -> p t s", t=4)
                    K.op("act", lambda e, o_ap=o_ap, i_ap=i_ap, func=func, bcol=bcol: e.activation(out=o_ap, in_=i_ap, func=func, bias=bcol),
                         r=[("ps", bank), "vecs"], w=[key])
                elif m >= 16:
                    if n == 512:
                        k0 = t0 // 8
                        o_ap = gc_perm[c][:, :, k0:k0 + 64]
                        i_ap = pz.rearrange("p (k r) -> p r k", r=8)
                    else:
                        o_ap = gc_perm[c][:, 4:8, KP:KC]
                        i_ap = pz.rearrange("p (s t) -> p t s", t=4)
                    K.op("act", lambda e, o_ap=o_ap, i_ap=i_ap, bcol=bcol: e.activation(out=o_ap, in_=i_ap, func=AF.Silu, bias=bcol),
                         r=[("ps", bank), "vecs"], w=[("gc", c, bi)])
                elif m >= 12:
                    K.op("act", lambda e, c=c, pz=pz: e.activation(out=th[c % 2][:, 0:n], in_=pz, func=AF.Tanh, bias=hvec[:, 4 + c:5 + c], scale=0.5),
                         r=[("ps", bank), "hvec"], w=[("th", c % 2)])
                else:
                    K.op("act", lambda e, c=c, pz=pz: e.activation(out=ah[c % 2][:, 0:n], in_=pz, func=AF.Identity, bias=hvec[:, c:c + 1], scale=0.5),
                         r=[("ps", bank), "hvec"], w=[("ah", c % 2)])
                    if n == 512:
                        k0 = t0 // 8
                        o_ap = v_perm[c][:, :, k0:k0 + 64]
                        i0 = th[c % 2][:, 0:n].rearrange("p (k r) -> p r k", r=8)
                        i1 = ah[c % 2][:, 0:n].rearrange("p (k r) -> p r k", r=8)
                        key = ("vperm", c, bi)
                    else:
                        o_ap = v_perm[c][:, 4:8, KP:KV].rearrange("p t (s j) -> p t s j", j=5)[:, :, :, 4]
                        i0 = th[c % 2][:, 0:n].rearrange("p (s t) -> p t s", t=4)
                        i1 = ah[c % 2][:, 0:n].rearrange("p (s t) -> p t s", t=4)
                        key = ("vperm", c, bi)
                    K.op("dve", lambda e, o_ap=o_ap, i0=i0, i1=i1: e.scalar_tensor_tensor(out=o_ap, in0=i0, scalar=1.0, in1=i1, op0=ALU.add, op1=ALU.mult),
                         r=[("th", c % 2), ("ah", c % 2)], w=[key])
                    if bi == 3:
                        K.op("dve", lambda e, c=c: e.scalar_tensor_tensor(out=v32p[:, c, :], in0=th[c % 2][:, 482:512], scalar=1.0, in1=ah[c % 2][:, 482:512], op0=ALU.add, op1=ALU.mult),
                             r=[("th", c % 2), ("ah", c % 2)], w=[("v32p", c)])
                    if bi == 4:
                        j0 = th[c % 2][:, 0:n].rearrange("p (s t) -> p s t", t=4)
                        j1 = ah[c % 2][:, 0:n].rearrange("p (s t) -> p s t", t=4)
                        K.op("dve", lambda e, c=c, i0=j0, i1=j1: e.scalar_tensor_tensor(out=ncs[:, c, :, 26:30], in0=i0, scalar=1.0, in1=i1, op0=ALU.add, op1=ALU.mult),
                             r=[("th", c % 2), ("ah", c % 2)], w=[("ncs", c, 1)])
        for _ in g1:
            pass
        for c in range(4):
            K.dma("pool", ncvT_p[128 * c:128 * (c + 1), :], v32p[:, c, :], r=[("v32p", c)])
            K.dma("pool", ncvT_s[128 * c:128 * (c + 1), :], ncs[:, c, :, :].rearrange("p s j -> p (s j)"), r=[("ncs", c, 0), ("ncs", c, 1)])
        if dbg:
            for c in range(4):
                K.dma("pool", dbg_out["u_perm"][128 * c:128 * (c + 1), :], u_perm[c][:, :, :].rearrange("p r k -> p (r k)"), r=[("uperm", c, b) for b in range(-1, 5)])
                K.dma("pool", dbg_out["gs"][128 * c:128 * (c + 1), :], gs_perm[c][:, :, :].rearrange("p r k -> p (r k)"), r=[("gs", c, b) for b in range(-1, 5)])
                K.dma("pool", dbg_out["gc"][128 * c:128 * (c + 1), :], gc_perm[c][:, :, :].rearrange("p r k -> p (r k)"), r=[("gc", c, b) for b in range(-1, 5)])
                K.dma("pool", dbg_out["vperm"][128 * c:128 * (c + 1), :], v_perm[c][:, :, :].rearrange("p r k -> p (r k)"), r=[("vperm", c, b) for b in range(-1, 5)])
        K.barrier(dma_queues=("pool",))

    DU_KEYS = [("d_u", c, pc, gl) for c in range(4) for pc in range(3) for gl in range(8)]
    DV_KEYS = [("d_v", c, pc, gl) for c in range(4) for pc in range(3) for gl in range(8)]
    BA = fr("BA", [128, 32, 16]); BB = fr("BB", [128, 32, 16]); CA = fr("CA", [128, 32, 16]); CB = fr("CB", [128, 32, 16])
    BbA = fr("BbA", [128, 32, 16]); BbB = fr("BbB", [128, 32, 16]); BbA_bf = fr("BbA_bf", [128, 32, 16], BF16)
    G1t = fr("G1t", [128, 32, 16]); G2t = fr("G2t", [128, 32, 16])
    dskT = fr("dskT", [16, 32]); Dd = fr("Dd", [16, 32, 16])
    h0bf = fr("h0bf", [128, 32, NSEQ], BF16); H4 = fr("H4", [128, 32, NSEQ])
    Mj = [fr("Mj0", [128, 8, 8, 128], BF16), None]

    def gen_mj(gb, extra_r=(), eng="pool"):
        K.op(eng, lambda e: e.tensor_tensor(out=Mj[gb % 2][:, :, :, :].rearrange("p g j (h m) -> p (g j h) m", h=2),
                                               in0=I2[:].unsqueeze(1).to_broadcast([128, 128, 64]),
                                               in1=S_bf[:, 8 * gb:8 * gb + 8, :, :].rearrange("p g j h -> p (g j h)").unsqueeze(2).to_broadcast([128, 128, 64]),
                                               op=ALU.mult),
             r=["I2", "S_bf"] + list(extra_r), w=[("Mj", gb % 2)])

    cfin_perm = [sb(f"cfin{c}", [128, 8, KC], BF16, side="right") for c in range(4)]
    if dbg:
        dbg_out["cfin"] = dout("dbg_cfin", [512, 8 * KC], BF16)
    if dbg:
        dbg_out["c1sc"] = dout("dbg_c1sc", [128, 32 * KC], BF16)
    if upto >= 2:
      with ExitStack() as s2:
        f2 = lambda name, shape, dt=F32: sb(name, shape, dt, s2)
        Wc = f2("Wc", [128, 32, 5, 128], BF16)
        Vv = f2("Vv", [128, 32, KV], BF16)
        maskc = f2("maskc", [128, 16])
        RB = f2("RB", [128, 8])
        bones = f2("bones", [128, 128], BF16)
        wp_st = f2("wp_st", [128, 4, 512])
        wp_bf = f2("wp_bf", [128, 4, 512], BF16)
        Ybf = [f2(f"Ybf{i}", [128, KC], BF16) for i in range(2)]
        Ysq = [f2(f"Ysq{i}", [128, KC], BF16) for i in range(2)]
        mean = f2("mean", [128, KC]); var = f2("var", [128, KC]); rstd = f2("rstd", [128, KC])
        t1 = [f2(f"t1_{i}", [128, KC]) for i in range(2)]
        c1sc = f2("c1sc", [128, 32, KC], BF16)
        Vflat = Vv[:, :, :].rearrange("p g k -> p (g k)")
        c1f = [Vflat[:, 8 * KC * c:8 * KC * (c + 1)] for c in range(4)]
        K.dma("sp", Vv[:, :, :].rearrange("p g k -> p (g k)"), d_v[:, :], r=DV_KEYS + DU_KEYS, w=[("Vv", 0)] + [("Vvg", g_) for g_ in range(32)])
        VK = [("Vv", 0)]
        for ci in range(4):
            K.dma("sp", wp_st[:, ci, :], w_pw2[128 * ci:128 * (ci + 1), :], w=[("wp_st", ci)])
        K.op("dve", lambda e: e.tensor_reduce(out=maskc[:], in_=ident[:, :].rearrange("p (s c) -> p c s", s=8), axis=mybir.AxisListType.X, op=ALU.add), r=["ident"], w=["maskc"])
        K.op("dve", lambda e: e.tensor_reduce(out=RB[:], in_=ident[:, :].rearrange("p (s c) -> p s c", s=8), axis=mybir.AxisListType.X, op=ALU.add), r=["ident"], w=["RB"])
        K.op("dve", lambda e: e.tensor_copy(out=bones[:].rearrange("p (s c) -> p s c", s=8), in_=RB[:].unsqueeze(2).to_broadcast([128, 8, 16])), r=["RB"], w=["bones"])
        for gq in range(4):
            K.op("dve", lambda e, gq=gq: e.tensor_tensor(out=Wc[:, 8 * gq:8 * gq + 8, :, :].rearrange("p g d (r c) -> p (g d r) c", c=16),
                                                        in0=wdg[:, 320 * gq:320 * (gq + 1)].unsqueeze(2).to_broadcast([128, 320, 16]),
                                                        in1=maskc[:].unsqueeze(1).to_broadcast([128, 320, 16]), op=ALU.mult),
                 r=["wdg", "maskc"], w=[("Wc", gq)])
        for ci in range(4):
            K.op("act", lambda e, ci=ci: e.activation(out=wp_bf[:, ci, :], in_=wp_st[:, ci, :], func=AF.Identity), r=[("wp_st", ci)], w=[("wp_bf", ci)])
        if upto >= 3:
            X1 = G1t
            X2 = G2t
            K.dma("sp", BA[:], BA_d[:, :, :], w=["BA"]); K.dma("sp", BB[:], BB_d[:, :, :], w=["BB"])
            K.dma("sp", CA[:], CA_d[:, :, :], w=["CA"]); K.dma("sp", CB[:], CB_d[:, :, :], w=["CB"])
            K.dma("sp", dskT[:], dskT_d[:, :], w=["dskT"])
            K.dma("sp", X1[:], h0A_d[:, :, :], w=["G1t"]); K.dma("sp", X2[:], h0B_d[:, :, :], w=["G2t"])
            ts(X2[0:64, :, :], X2[0:64, :, :], -1.0, ALU.mult, ["G2t", ("Wc", 3)], ["G2t"], eng=G)
            b16s = lambda ap: ap.unsqueeze(2).to_broadcast([128, 32, NSEQ])
            V(lambda e: e.tensor_copy(out=h0bf[:], in_=X1[:]), ["G1t"], ["h0bf"], G)
            tt(H4[:], b16s(PR[:, 4, :]), X1[:], ALU.mult, PK + ["G1t"], ["H4"], G)
            tt(X1[:], b16s(PI[:, 4, :]), X2[:], ALU.mult, PK + ["G2t", "H4"], ["G1t"], G)
            tt(H4[:], H4[:], X1[:], ALU.add, ["H4", "G1t"], ["H4"], G)
            ts(BB[0:64, :, :], BB[0:64, :, :], -1.0, ALU.mult, ["BB"], ["BB"], eng=G)
            ts(CA[64:128, :, :], CA[64:128, :, :], -1.0, ALU.mult, ["CA"], ["CA"], eng=G)
            ts(CB[:], CB[:], -1.0, ALU.mult, ["CB"], ["CB"], eng=G)
            bc16 = lambda ap: ap.unsqueeze(2).to_broadcast([128, 32, 16])
            tt(G1t[:], bc16(KR[:]), BA[:], ALU.mult, ["KR", "BA", "H4"], ["G1t"], G); tt(G2t[:], bc16(KI[:]), BB[:], ALU.mult, ["KI", "BB", "H4", "G1t"], ["G2t"], G)
            tt(BbA[:], G1t[:], G2t[:], ALU.add, ["G1t", "G2t"], ["BbA"], G)
            tt(G1t[:], bc16(KR[:]), BB[:], ALU.mult, ["KR", "BB"], ["G1t"], G); tt(G2t[:], bc16(KI[:]), BA[:], ALU.mult, ["KI", "BA"], ["G2t"], G)
            tt(BbB[:], G1t[:], G2t[:], ALU.subtract, ["G1t", "G2t"], ["BbB"], G)
            V(lambda e: e.tensor_copy(out=BbA_bf[:], in_=BbA[:]), ["BbA"], ["BbA_bf"], G)
            tt(Dd[:], ident[0:16, 0:16].unsqueeze(1).to_broadcast([16, 32, 16]), dskT[:].unsqueeze(2).to_broadcast([16, 32, 16]), ALU.mult, ["ident", "dskT"], ["Dd"], G)
        bdw = lambda g: sccol[:, 0, g:g + 1]
        gln = lambda g: sccol[:, 1, g:g + 1]
        bln = lambda g: sccol[:, 2, g:g + 1]

        def conv_g(g, bank):
            for d in range(5):
                K.op("pe", lambda e, g=g, d=d, bank=bank: e.matmul(out=ps[bank][:, d:KP], lhsT=Wc[:, g, d, :], rhs=Vv[:, g, 0:KP - d], start=(d == 0), stop=(d == 4), skip_group_check=True),
                     r=[("Vvg", g), ("Wc", g // 8)], w=[("ps", bank)], signal=False)
            for d in range(5):
                K.op("pe", lambda e, g=g, d=d, bank=bank: e.matmul(out=ps[bank][:, KP:KC], lhsT=Wc[:, g, d, :], rhs=Vv[:, g, KP:KV].rearrange("p (s j) -> p s j", j=5)[:, :, 4 - d], start=False, stop=(d == 4), skip_group_check=True),
                     r=[("Vvg", g), ("Wc", g // 8)], w=[("ps", bank)], signal=(d == 4))

        def stats_g(g, bank):
            i = g % 2
            K.op("act", lambda e: e.activation(out=Ybf[i][:, :], in_=ps[bank][:, 0:KC], func=AF.Identity, bias=bdw(g)), r=[("ps", bank), "sccol"], w=[("Ybf", i)])
            K.op("dve", lambda e: e.scalar_tensor_tensor(out=Ysq[i][:, :], in0=ps[bank][:, 0:KC], scalar=bdw(g), in1=Ybf[i][:, :], op0=ALU.add, op1=ALU.mult), r=[("ps", bank), "sccol", ("Ybf", i)], w=[("Ysq", i)])
            K.op("pe", lambda e: e.matmul(out=ps[6][:, 0:KC], lhsT=bones[:], rhs=Ybf[i][:, :], start=(g == 0), stop=(g == 31), skip_group_check=True), r=["bones", ("Ybf", i)], w=[("ps", 6)], signal=(g == 31))
            K.op("pe", lambda e: e.matmul(out=ps[7][:, 0:KC], lhsT=bones[:], rhs=Ysq[i][:, :], start=(g == 0), stop=(g == 31), skip_group_check=True), r=["bones", ("Ysq", i)], w=[("ps", 7)], signal=(g == 31))

        conv_g(0, 0)
        for g in range(32):
            if g + 1 < 32:
                conv_g(g + 1, (g + 1) % 4)
            stats_g(g, g % 4)
        K.op("act", lambda e: e.activation(out=mean[:], in_=ps[6][:, 0:KC], func=AF.Identity, scale=1.0 / 512.0), r=[("ps", 6)], w=["mean"])
        K.op("dve", lambda e: e.tensor_tensor(out=var[:], in0=mean[:], in1=mean[:], op=ALU.mult), r=["mean"], w=["var"])
        K.op("dve", lambda e: e.scalar_tensor_tensor(out=var[:], in0=ps[7][:, 0:KC], scalar=1.0 / 512.0, in1=var[:], op0=ALU.mult, op1=ALU.subtract), r=[("ps", 7), "var"], w=["var"])
        K.op("dve", lambda e: e.tensor_scalar(out=var[:], in0=var[:], scalar1=EPS, scalar2=None, op0=ALU.add), r=["var"], w=["var"])
        K.op("act", lambda e: e.activation(out=var[:], in_=var[:], func=AF.Sqrt), r=["var"], w=["var"])
        K.op("dve", lambda e: e.reciprocal(out=rstd[:], in_=var[:]), r=["var"], w=["rstd"])

        def norm_g(g, bank):
            i = g % 2
            K.op("dve", lambda e: e.scalar_tensor_tensor(out=t1[i][:, :], in0=ps[bank][:, 0:KC], scalar=bdw(g), in1=mean[:], op0=ALU.add, op1=ALU.subtract),
                 r=[("ps", bank), "sccol", "mean"], w=[("t1", i)])
            K.op("dve", lambda e: e.tensor_tensor(out=t1[i][:, :], in0=t1[i][:, :], in1=rstd[:], op=ALU.mult), r=[("t1", i), "rstd"], w=[("t1", i)])
            K.op("act", lambda e: e.activation(out=c1sc[:, g, :], in_=t1[i][:, :], func=AF.Silu, bias=bln(g), scale=gln(g)), r=[("t1", i), "sccol"], w=[("c1sc", g)])

        conv_g(0, 0)
        for g in range(32):
            if g + 1 < 32:
                conv_g(g + 1, (g + 1) % 4)
            norm_g(g, g % 4)
            if g % 8 == 7:
                c = g // 8
                for rx in range(8):
                    K.dma("sp", d_c[128 * c:128 * (c + 1), KC * rx:KC * (rx + 1)].rearrange("(g q) k -> q g k", q=16), c1sc[16 * rx:16 * (rx + 1), 8 * c:8 * c + 8, :],
                          r=[("c1sc", g_) for g_ in range(8 * c, 8 * c + 8)], w=[("d_c", c, rx)])
                glo = (8 * KC * c) // KV
                ghi = min(31, (8 * KC * (c + 1) - 1) // KV)
                K.dma("sp", c1f[c], d_c[128 * c:128 * (c + 1), :], r=[("d_c", c, rx) for rx in range(8)], w=[("c1f", c)] + [("Vvg", g_) for g_ in range(glo, ghi + 1)])
        CK = [("c1sc", g) for g in range(32)]
        if upto >= 3:
            gen_mj(0, eng="dve")
        if dbg:
            K.dma("pool", dbg_out["c1sc"][:, :], c1sc[:, :, :].rearrange("p g k -> p (g k)"), r=CK)
        NPC = 8 * KC
        pcount = 0
        for col0 in range(0, NPC, 512):
            n = min(512, NPC - col0)
            for mo in range(4):
                bank = pcount % 4
                pcount += 1
                for ci in range(4):
                    K.op("pe", lambda e, mo=mo, ci=ci, bank=bank: e.matmul(out=ps[bank][:, 0:n], lhsT=wp_bf[:, ci, 128 * mo:128 * (mo + 1)], rhs=c1f[ci][:, col0:col0 + n], start=(ci == 0), stop=(ci == 3)),
                         r=[("wp_bf", ci), ("c1f", ci)], w=[("ps", bank)], signal=(ci == 3))
                K.op("dve", lambda e, mo=mo, bank=bank: e.scalar_tensor_tensor(out=cfin_perm[mo][:, :, :].rearrange("p r k -> p (r k)")[:, col0:col0 + n], in0=ps[bank][:, 0:n], scalar=vecs[:, 36 + mo:37 + mo],
                                                                             in1=gc_perm[mo][:, :, :].rearrange("p r k -> p (r k)")[:, col0:col0 + n], op0=ALU.add, op1=ALU.mult),
                     r=[("ps", bank), "vecs"] + [("gc", mo, b) for b in range(-1, 5)], w=[("cfin", mo, col0)])
        if dbg:
            for c in range(4):
                K.dma("pool", dbg_out["cfin"][128 * c:128 * (c + 1), :], cfin_perm[c][:, :, :].rearrange("p r k -> p (r k)"), r=[("cfin", c, col0) for col0 in range(0, NPC, 512)] + [("cfin", c, -1)])
        K.barrier()
    sA.close()

    if dbg:
        dbg_out["P"] = dout("dbg_P", [128, 2 * 9 * 32], F32)
        dbg_out["Tc"] = dout("dbg_Tc", [128, 32 * 128], BF16)
        dbg_out["BcT"] = dout("dbg_BcT", [128, 32 * 128], BF16)
        dbg_out["Fc"] = dout("dbg_Fc", [128, 32 * 192], BF16)
        dbg_out["s1rq"] = dout("dbg_s1rq", [128, 32 * KC], BF16)
        dbg_out["Hbf"] = dout("dbg_Hbf", [128, 32 * KP], BF16)
    if upto >= 3:
      with ExitStack() as s3:
        f3 = lambda name, shape, dt=F32: sb(name, shape, dt, s3)
        Fc = f3("Fc", [128, 32, 192], BF16)
        Tc = f3("Tc", [128, 32, 128], BF16); BcT = f3("BcT", [128, 32, 128], BF16)
        Hfin = f3("Hfin", [128, 32]); Hns = f3("Hns", [128, 32, NSEQ])
        E32 = f3("E32", [128, 8, 128]); Gb = f3("Gb", [128, 8, 128])
        FT1 = f3("FT1", [128, 8, 144]); FT2 = f3("FT2", [128, 8, 144])
        K_bf = f3("K_bf", [16, 8, 128], BF16)
        Mj[1] = f3("Mj1", [128, 8, 8, 128], BF16)
        U = f3("U", [128, 32, KC], BF16)
        Hbf2 = [f3(f"Hbf{i}", [128, 8, KP], BF16) for i in range(2)]
        s1rq2 = [f3(f"s1rq{i}", [128, 8, KC], BF16) for i in range(2)]
        s1f = [sb(f"s1f{c}", [128, 8, KC], BF16, side="right") for c in range(4)]
        K.dma("sp", U[:, :, :].rearrange("p g k -> p (g k)"), d_u[:, :], r=DU_KEYS, w=[("U", 0)])
        UK = [("U", 0)]
        K.op("act", lambda e: e.memzero(Tc[:]), w=["Tc0"])
        K.op("act", lambda e: e.memzero(Fc[:, :, 0:48]), w=[("Fc", -1)])
        if dbg:
            K.dma("pool", dbg_out["P"][:, 0:288], PR[:, :, :].rearrange("p t g -> p (t g)"), r=PK)
            K.dma("pool", dbg_out["P"][:, 288:576], PI[:, :, :].rearrange("p t g -> p (t g)"), r=PK)

        def gen_thunks(gb):
            gs_ = slice(8 * gb, 8 * gb + 8)
            pw = lambda P_, nt: P_[:, 0:nt, gs_].rearrange("p t g -> p g t").unsqueeze(3).to_broadcast([128, 8, nt, 16])
            bt = lambda X_, nt: X_[:, gs_, :].unsqueeze(2).to_broadcast([128, 8, nt, 16])
            v8 = lambda X_: X_[:].rearrange("p g (t q) -> p g t q", t=8)
            v9 = lambda X_: X_[:].rearrange("p g (t q) -> p g t q", t=9)
            PRK = [("Pr", sx_) for sx_ in range(8)]

            def t_e1():
                tt(v8(E32), pw(PRr, 8), bt(BbA, 8), ALU.mult, PRK + ["BbA"], ["E32a"])

            def t_e2():
                tt(v8(Gb), pw(PIr, 8), bt(BbB, 8), ALU.mult, PRK + ["BbB"], ["Gb"])

            def t_e3():
                tt(E32[:], E32[:], Gb[:], ALU.add, ["E32a", "Gb"], ["E32"])

            def t_f1():
                tt(v9(FT1), pw(PR, 9), bt(CA, 9), ALU.mult, PK + ["CA"], ["FT1"])

            def t_f2():
                tt(v9(FT2), pw(PI, 9), bt(CB, 9), ALU.mult, PK + ["CB"], ["FT2"])

            def t_f3():
                tt(Fc[:, gs_, 48:192], FT1[:], FT2[:], ALU.add, ["FT1", "FT2"], [("Fc", gb)])

            def t_tr():
                for h2 in range(2):
                    bank = 4 + h2
                    for gi in range(4):
                        gl = 4 * h2 + gi
                        K.op("pe", lambda e, gl=gl, gi=gi, bank=bank: e.transpose(out=ps[bank][:, 128 * gi:128 * (gi + 1)], in_=E32[:, gl, :], identity=ident[:]),
                             r=["E32", "ident"], w=[("ps", bank)], signal=(gi == 3))
                    K.op("act", lambda e, h2=h2, bank=bank: e.activation(out=BcT[:, 8 * gb + 4 * h2:8 * gb + 4 * h2 + 4, :], in_=ps[bank][:, :].rearrange("p (g m) -> p g m", g=4), func=AF.Identity),
                         r=[("ps", bank)], w=[("BcT", gb, h2)])

            def t_k():
                for h2 in range(2):
                    bank = 6 + h2
                    for gi in range(4):
                        gl = 4 * h2 + gi
                        g = 8 * gb + gl
                        K.op("pe", lambda e, g=g, gi=gi, bank=bank: e.matmul(out=ps[bank][0:16, 128 * gi:128 * (gi + 1)], lhsT=BbA_bf[:, g, :], rhs=Fc[:, g, 48:176], start=True, stop=False, skip_group_check=True),
                             r=["BbA_bf", ("Fc", gb)], w=[("ps", bank)], signal=False)
                        K.op("pe", lambda e, g=g, gi=gi, bank=bank: e.matmul(out=ps[bank][0:16, 128 * gi:128 * gi + 16], lhsT=Dd[:, g, :], rhs=ident[0:16, 0:16], start=False, stop=True, skip_group_check=True),
                             r=["Dd", "ident"], w=[("ps", bank)], signal=(gi == 3))
                    K.op("act", lambda e, h2=h2, bank=bank: e.activation(out=K_bf[:, 4 * h2:4 * h2 + 4, :], in_=ps[bank][0:16, :].rearrange("p (g m) -> p g m", g=4), func=AF.Identity),
                         r=[("ps", bank)], w=[("K_bf", h2)])

            def t_tc():
                for sx in range(8):
                    K.dma("sp", Tc[16 * sx:16 * (sx + 1), gs_, 16 * sx:128], K_bf[0:16, :, 0:128 - 16 * sx], r=[("K_bf", 0), ("K_bf", 1), "Tc0"], w=[("Tc", gb, sx)])

            return [t_e1, t_e2, t_e3, t_f1, t_f2, t_f3, t_tr, t_k, t_tc]

        def gen_batch(gb):
            for th in gen_thunks(gb):
                th()

        TKb = lambda gb: [("Tc", gb, sx) for sx in range(8)]
        BKb = lambda gb: [("BcT", gb, 0), ("BcT", gb, 1)]
        FKb = lambda gb: [("Fc", -1), ("Fc", gb)]

        def sample_batch(gb):
            for gl in range(8):
                g = 8 * gb + gl
                K.op("pe", lambda e, g=g, gl=gl: e.matmul(out=ps[0][:, NSEQ * gl:NSEQ * (gl + 1)], lhsT=BcT[:, g, :], rhs=U[:, g, KP:KC], start=(gl == 0), stop=False, skip_group_check=True),
                     r=BKb(gb) + UK, w=[("ps", 0)], signal=(gl == 7))
            for gl in range(8):
                g = 8 * gb + gl
                K.op("pe", lambda e, g=g, gl=gl: e.matmul(out=ps[1][:, NSEQ * gl:NSEQ * (gl + 1)], lhsT=Tc[:, g, :], rhs=U[:, g, KP:KC], start=True, stop=False, skip_group_check=True),
                     r=TKb(gb) + UK, w=[("ps", 1)], signal=False)
                K.op("pe", lambda e, g=g, gl=gl: e.matmul(out=ps[1][:, NSEQ * gl:NSEQ * (gl + 1)], lhsT=Fc[:, g, 0:128], rhs=h0bf[:, g, :], start=False, stop=True, skip_group_check=True),
                     r=FKb(gb) + ["h0bf"], w=[("ps", 1)], signal=(gl == 7))
            gs_ = slice(8 * gb, 8 * gb + 8)
            K.op("pe", lambda e: e.matmul(out=ps[0][:, 0:8 * NSEQ], lhsT=ident[:], rhs=H4[:, gs_, :].rearrange("p g s -> p (g s)"), start=False, stop=True, skip_group_check=True),
                 r=["ident", "H4"], w=[("ps", 0)], signal=True)
            K.op("act", lambda e: e.activation(out=Hns[:, gs_, :], in_=ps[0][:, 0:8 * NSEQ].rearrange("p (g s) -> p g s", g=8), func=AF.Identity), r=[("ps", 0)], w=[("Hns", gb)])
            K.op("act", lambda e: e.activation(out=s1rq2[gb % 2][:, :, KP:KC], in_=ps[1][:, 0:8 * NSEQ].rearrange("p (g s) -> p g s", g=8), func=AF.Gelu_apprx_tanh),
                 r=[("ps", 1)], w=[("s1rq", gb % 2, "s")])

        def shuffle_out(gb):
            gs_ = slice(8 * gb, 8 * gb + 8)
            rk = [("s1rq", gb % 2, b_) for b_ in range(4)] + [("s1rq", gb % 2, "s")]
            for rx in range(8):
                K.dma("sp", d_s[128 * gb:128 * (gb + 1), KC * rx:KC * (rx + 1)].rearrange("(g q) k -> q g k", q=16), s1rq2[gb % 2][16 * rx:16 * (rx + 1), :, :], r=rk, w=[("d_s", gb, rx)])
            K.dma("sp", s1f[gb][:, :, :].rearrange("p r k -> p (r k)"), d_s[128 * gb:128 * (gb + 1), :], r=[("d_s", gb, rx) for rx in range(8)], w=[("s1f", gb)])

        def cast_bank(gb, bank, eng, lo=0, hi=KP):
            hb = Hbf2[gb % 2]
            key = ("Hbf", gb % 2, bank)
            src = ps[bank][:, :].rearrange("p (g k) -> p g k", g=2)[:, :, lo:hi]
            dst = hb[:, 2 * bank:2 * bank + 2, lo:hi]
            if eng == "act":
                K.op("act", lambda e: e.activation(out=dst, in_=src, func=AF.Identity), r=[("ps", bank)], w=[key])
            else:
                V(lambda e: e.tensor_copy(out=dst, in_=src), [("ps", bank)], [key])

        CAST_ENG = {0: "act", 1: "dve", 2: "act", 3: "dve"}

        NWARM = 0
        CAST3 = {0: "act", 1: "act", 2: "act", 3: "dve"}

        def scan_batch(gb, extra=()):
            extra = list(extra)
            mj = Mj[gb % 2]
            for half in range(2):
                for gl in range(4 * half, 4 * half + 4):
                    g = 8 * gb + gl
                    bank = gl // 2
                    K.op("pe", lambda e, g=g, gl=gl, bank=bank: e.matmul(out=ps[bank][:, 256 * (gl % 2):256 * (gl % 2 + 1)], lhsT=BcT[:, g, :], rhs=U[:, g, 0:KP], start=(gl % 2 == 0), stop=True, skip_group_check=True),
                         r=BKb(gb) + UK, w=[("ps", bank)], signal=(gl % 2 == 1))
            for bank in range(4):
                cast_bank(gb, bank, CAST_ENG[bank])
            for j in range(8):
                d = 1 << j
                for half in range(2):
                    for gl in range(4 * half, 4 * half + 4):
                        g = 8 * gb + gl
                        bank = gl // 2
                        c0 = 256 * (gl % 2)
                        K.op("pe", lambda e, g=g, gl=gl, bank=bank, c0=c0, d=d, j=j: e.matmul(out=ps[bank][:, c0 + d:c0 + 256], lhsT=mj[:, gl, j, :], rhs=Hbf2[gb % 2][:, gl, 0:256 - d], start=False, stop=True, skip_group_check=True),
                             r=[("Mj", gb % 2), ("Hbf", gb % 2, bank)], w=[("ps", bank)], signal=(gl % 2 == 1))
                if 0 <= j <= 5:
                    for _w in range(NWARM):
                        K.op("pe", lambda e: e.matmul(out=ps[7][:, 0:KP], lhsT=BcT[:, 8 * gb, :], rhs=U[:, 8 * gb, 0:KP], start=True, stop=True, skip_group_check=True),
                             r=BKb(gb) + UK, w=[("ps", 7)], signal=(_w == NWARM - 1))
                heavy = bool(extra) and 0 <= j <= 5
                for bank in range(4):
                    if heavy:
                        eng_ = CAST3[bank] if j % 2 == 0 else CAST3[3 - bank]
                    else:
                        eng_ = CAST_ENG[bank] if j % 2 == 0 else CAST_ENG[3 - bank]
                    lo_, hi_ = (d, KP - 2 * d) if j < 7 else (0, KP)
                    cast_bank(gb, bank, eng_, lo_, hi_)
                if extra:
                    extra.pop(0)()
            for th in extra:
                th()
            for bank in range(4):
                g0 = 8 * gb + 2 * bank
                K.op("act", lambda e, g0=g0, bank=bank: e.activation(out=Hfin[:, g0:g0 + 2], in_=ps[bank][:, :].rearrange("p (g k) -> p g k", g=2)[:, :, 255], func=AF.Identity), r=[("ps", bank)], w=[("Hfin", g0)])

        def y_batch(gb):
            for gl in range(8):
                g = 8 * gb + gl
                bank = 4 + gl // 2
                c0 = 256 * (gl % 2)
                K.op("pe", lambda e, g=g, bank=bank, c0=c0: e.matmul(out=ps[bank][:, c0:c0 + 256], lhsT=Tc[:, g, :], rhs=U[:, g, 0:KP], start=True, stop=False, skip_group_check=True),
                     r=TKb(gb) + UK, w=[("ps", bank)], signal=False)
                K.op("pe", lambda e, g=g, bank=bank, c0=c0: e.matmul(out=ps[bank][:, c0 + 1:c0 + 256], lhsT=Fc[:, g, 64:192], rhs=Hbf2[gb % 2][:, gl, 0:255], start=False, stop=True, skip_group_check=True),
                     r=FKb(gb) + [("Hbf", gb % 2, gl // 2)], w=[("ps", bank)], signal=(gl % 2 == 1))
            for bank in range(4, 8):
                gl0 = 2 * (bank - 4)
                K.op("act", lambda e, gl0=gl0, bank=bank: e.activation(out=s1rq2[gb % 2][:, gl0:gl0 + 2, 0:KP], in_=ps[bank][:, :].rearrange("p (g k) -> p g k", g=2), func=AF.Gelu_apprx_tanh),
                     r=[("ps", bank)], w=[("s1rq", gb % 2, bank - 4)])

        import os
        ILV = os.environ.get("ILV", "1") == "1"
        def thunks_with_mj(gb_next):
            ths = gen_thunks(gb_next)
            nop = lambda: None
            return [nop] + ths[0:6] + [lambda: gen_mj(gb_next, extra_r=[("Fc", gb_next)])] + ths[6:]

        gen_batch(0)
        scan_batch(0, thunks_with_mj(1))
        y_batch(0)
        sample_batch(0)
        shuffle_out(0)
        scan_batch(1, thunks_with_mj(2))
        y_batch(1)
        sample_batch(1)
        shuffle_out(1)
        scan_batch(2, thunks_with_mj(3))
        y_batch(2)
        sample_batch(2)
        shuffle_out(2)
        scan_batch(3)
        y_batch(3)
        sample_batch(3)
        shuffle_out(3)
        K.dma("pool", hnew_s[:, :, :], Hns[:], r=[("Hns", gb) for gb in range(4)])
        K.dma("pool", hfin_p[:, :], Hfin[:], r=[("Hfin", g0) for g0 in range(0, 32, 2)])
        if dbg:
            K.dma("pool", dbg_out["U"][:, :], U[:, :, :].rearrange("p g k -> p (g k)"), r=UK)
            K.dma("pool", dbg_out["Tc"][:, :], Tc[:, :, :].rearrange("p g m -> p (g m)"), r=[k_ for gb in range(4) for k_ in TKb(gb)])
            K.dma("pool", dbg_out["BcT"][:, :], BcT[:, :, :].rearrange("p g m -> p (g m)"), r=[k_ for gb in range(4) for k_ in BKb(gb)])
            K.dma("pool", dbg_out["Fc"][:, :], Fc[:, :, :].rearrange("p g m -> p (g m)"), r=[("Fc", -1)] + [("Fc", gb) for gb in range(4)])
        K.barrier()

    if dbg:
        dbg_out["s2p"] = dout("dbg_s2p", [512, 8 * KC], BF16)
    if upto >= 4:
      with ExitStack() as s4:
        f4 = lambda name, shape, dt=F32: sb(name, shape, dt, s4)
        s2p = [f4(f"s2p{c}", [128, 8, KC], BF16) for c in range(4)]
        wg_st = f4("wg_st", [128, 4, 512]); wg_bf = f4("wg_bf", [128, 4, 512], BF16)
        wo_st = [f4(f"wo_st{i}", [128, DM]) for i in range(2)]; wo_bf = f4("wo_bf", [128, 8, DM], BF16)
        gpb = f4("gpb", [128, DM]); bpb = f4("bpb", [128, DM])
        thg = [f4(f"thg{i}", [128, 512]) for i in range(2)]; tg = [f4(f"tg{i}", [128, 512]) for i in range(2)]
        NXB = 3
        xt = [f4(f"xt{i}", [128, DM]) for i in range(NXB)]
        yn = [f4(f"yn{i}", [128, DM]) for i in range(2)]
        yo = [f4(f"yo{i}", [128, DM]) for i in range(2)]
        st6 = [f4(f"st6_{i}", [128, 2, 6]) for i in range(2)]
        mv = [f4(f"mv{i}", [128, 2]) for i in range(2)]
        ve = [f4(f"ve{i}", [128, 1]) for i in range(2)]
        rsd = [f4(f"rsd{i}", [128, 1]) for i in range(2)]
        nbv = [f4(f"nbv{i}", [128, 1]) for i in range(2)]
        smix = f4("smix", [128, 8, NS], BF16)
        aI = f4("aI", [128, 128])
        K.op("dve", lambda e: e.tensor_scalar(out=aI[:], in0=ident[:], scalar1=ALPHA, scalar2=None, op0=ALU.mult), r=["ident"], w=["aI"])
        epsc = f4("epsc", [128, 1])
        K.op("dve", lambda e: e.memset(epsc[:], EPS), w=["epsc"])
        for ci in range(4):
            K.dma("sp", wg_st[:, ci, :], w_glu[128 * ci:128 * (ci + 1), :], w=[("wg_st", ci)])
            if ci % 2 == 0:
                K.op("act", lambda e, ci=ci: e.activation(out=wg_bf[:, ci, :], in_=wg_st[:, ci, :], func=AF.Identity), r=[("wg_st", ci)], w=[("wg_bf", ci)])
            else:
                K.op("dve", lambda e, ci=ci: e.tensor_copy(out=wg_bf[:, ci, :], in_=wg_st[:, ci, :]), r=[("wg_st", ci)], w=[("wg_bf", ci)])
        wo_state = {"dma": 0, "cast": 0}

        def wo_dma(n_):
            while wo_state["dma"] < min(n_, 8):
                kk = wo_state["dma"]
                K.dma("sp", wo_st[kk % 2][:, :], w_out[128 * kk:128 * (kk + 1), :], w=[("wo_st", kk % 2)])
                wo_state["dma"] += 1

        def wo_cast_next():
            kk = wo_state["cast"]
            if kk >= 8:
                return
            wo_dma(kk + 1)
            if kk % 2 == 0:
                K.op("act", lambda e: e.activation(out=wo_bf[:, kk, :], in_=wo_st[kk % 2][:, :], func=AF.Identity), r=[("wo_st", kk % 2)], w=[("wo_bf", kk)])
            else:
                K.op("dve", lambda e: e.tensor_copy(out=wo_bf[:, kk, :], in_=wo_st[kk % 2][:, :]), r=[("wo_st", kk % 2)], w=[("wo_bf", kk)])
            wo_state["cast"] += 1
            wo_dma(kk + 3)

        wo_dma(2)
        K.dma("sp", gpb[:], gpost_d[:, :], w=["gpb"])
        K.dma("sp", bpb[:], bpost_d[:, :], w=["bpb"])

        xp_v = xp.rearrange("(k r) d -> r k d", r=8)
        yp_v = y_p.rearrange("(k r) d -> r k d", r=8)
        xs_v = xs.rearrange("(s t) d -> t s d", t=4)
        ys_v = y_s.rearrange("(s t) d -> t s d", t=4)
        blocks4 = [("p", r_) for r_ in range(8)] + [("s", 4)]
        gcount = 0
        units = [("p", r_) for r_ in range(0, 8, 2)] + [("s", 4)]
        for (kind, r_) in units:
            n = 512 if kind == "p" else NS
            view = (lambda t_, r_=r_: t_[:, r_:r_ + 2, 0:KP]) if kind == "p" else (lambda t_: t_[:, 4:8, KP:KC])
            pview = (lambda ap: ap.rearrange("p (a k) -> p a k", a=2)) if kind == "p" else (lambda ap: ap.rearrange("p (t s) -> p t s", t=4))
            wkeys = (lambda mo: [("s2p", mo, ("p", r_)), ("s2p", mo, ("p", r_ + 1))]) if kind == "p" else (lambda mo: [("s2p", mo, ("s", 4))])
            for mo in range(4):
                bank = gcount % 4
                ti = gcount % 2
                gcount += 1
                for ci in range(4):
                    K.op("pe", lambda e, mo=mo, ci=ci, bank=bank: e.matmul(out=pview(ps[bank][:, 0:n]), lhsT=wg_bf[:, ci, 128 * mo:128 * (mo + 1)], rhs=view(s1f[ci]), start=(ci == 0), stop=(ci == 3)),
                         r=[("wg_bf", ci), ("s1f", ci)], w=[("ps", bank)], signal=(ci == 3))
                K.op("act", lambda e, mo=mo, bank=bank, ti=ti: e.activation(out=thg[ti][:, 0:n], in_=ps[bank][:, 0:n], func=AF.Tanh, bias=hvec[:, 8 + mo:9 + mo], scale=0.5),
                     r=[("ps", bank), "hvec"], w=[("thg", ti)])
                K.op("dve", lambda e, mo=mo, ti=ti: e.scalar_tensor_tensor(out=pview(tg[ti][:, 0:n]), in0=pview(thg[ti][:, 0:n]), scalar=1.0, in1=view(s1f[mo]), op0=ALU.add, op1=ALU.mult),
                     r=[("thg", ti), ("s1f", mo)], w=[("tg", ti)])
                K.op("dve", lambda e, mo=mo, ti=ti: e.scalar_tensor_tensor(out=view(s2p[mo]), in0=pview(tg[ti][:, 0:n]), scalar=0.5, in1=view(gs_perm[mo]), op0=ALU.mult, op1=ALU.mult),
                     r=[("tg", ti)] + [("gs", mo, b_) for b_ in range(-1, 5)], w=wkeys(mo))
            wo_cast_next()
            wo_cast_next()
        while wo_state["cast"] < 8:
            wo_cast_next()
        for kk in range(8):
            src = s2p[kk] if kk < 4 else cfin_perm[kk - 4]
            rk = [("s2p", kk, ("s", 4)), ("s2p", kk, "z")] if kk < 4 else [("cfin", kk - 4, col0_) for col0_ in range(0, 8 * KC, 512)] + [("cfin", kk - 4, -1)]
            K.op("dve", lambda e, kk=kk, src=src: e.tensor_copy(out=smix[:, kk, :].rearrange("p (t s) -> p t s", t=4), in_=src[:, 4:8, KP:KC]), r=rk, w=[("smix", kk)])
        tcount = 0
        ocount = 0
        ln_back = []
        for (kind, r_) in blocks4:
            bkey = (kind, r_)
            tiles = [0, 128] if kind == "p" else [None]
            for k0 in tiles:
                rows = 128 if kind == "p" else NS
                xi = tcount % NXB
                oi = tcount % 2
                tcount += 1
                if kind == "p":
                    K.dma("sp", xt[xi][:, :], xp_v[r_, k0:k0 + 128, :], w=[("xt", xi)])
                else:
                    for t_ in range(4):
                        K.dma("sp", xt[xi][16 * t_:16 * (t_ + 1), :], xs_v[t_, :, :], w=[("xt", xi)] if t_ == 0 else [("xt", xi, t_)])
                xkeys = [("xt", xi)] + ([("xt", xi, t_) for t_ in range(1, 4)] if kind == "s" else [])
                banks = []
                for half in range(2):
                    bank = 2 + (ocount % 6)
                    ocount += 1
                    banks.append(bank)
                    for kk in range(8):
                        src = s2p[kk] if kk < 4 else cfin_perm[kk - 4]
                        if kind == "p":
                            lt = src[:, r_, k0:k0 + 128]
                            rk = [("s2p", kk, bkey)] if kk < 4 else [("cfin", kk - 4, col0_) for col0_ in range(0, 8 * KC, 512)] + [("cfin", kk - 4, -1)]
                        else:
                            lt = smix[:, kk, :]
                            rk = [("smix", kk)]
                        K.op("pe", lambda e, lt=lt, kk=kk, half=half, bank=bank: e.matmul(out=ps[bank][0:rows, 0:512], lhsT=lt, rhs=wo_bf[:, kk, 512 * half:512 * (half + 1)], start=(kk == 0), stop=False),
                             r=rk + [("wo_bf", kk)], w=[("ps", bank)], signal=False)
                    K.op("pe", lambda e, half=half, bank=bank: e.matmul(out=ps[bank][0:rows, 0:512], lhsT=aI[0:rows, 0:rows], rhs=xt[xi][0:rows, 512 * half:512 * (half + 1)], start=False, stop=True),
                         r=xkeys + ["aI"], w=[("ps", bank)], signal=True)
                for half in range(2):
                    K.op("dve", lambda e, half=half, bank=banks[half]: e.bn_stats(out=st6[oi][0:rows, half, :], in_=ps[bank][0:rows, 0:512]), r=[("ps", banks[half])], w=[("st6", oi, half)])
                K.op("dve", lambda e: e.bn_aggr(out=mv[oi][0:rows, :], in_=st6[oi][0:rows, :, :].rearrange("p a b -> p (a b)")), r=[("st6", oi, 0), ("st6", oi, 1)], w=[("mv", oi)])
                K.op("act", lambda e: e.activation(out=ve[oi][0:rows, :], in_=mv[oi][0:rows, 1:2], func=AF.Sqrt, bias=epsc[0:rows, :]), r=[("mv", oi), "epsc"], w=[("ve", oi)])
                for th in ln_back:
                    th()
                ln_back.clear()
                K.op("dve", lambda e: e.reciprocal(out=rsd[oi][0:rows, :], in_=ve[oi][0:rows, :]), r=[("ve", oi)], w=[("rsd", oi)])
                K.op("dve", lambda e: e.scalar_tensor_tensor(out=nbv[oi][0:rows, :], in0=mv[oi][0:rows, 0:1], scalar=-1.0, in1=rsd[oi][0:rows, :], op0=ALU.mult, op1=ALU.mult),
                     r=[("mv", oi), ("rsd", oi)], w=[("nbv", oi)])
                for half in range(2):
                    K.op("act", lambda e, half=half, bank=banks[half]: e.activation(out=yn[oi][0:rows, 512 * half:512 * (half + 1)], in_=ps[bank][0:rows, 0:512], func=AF.Identity, bias=nbv[oi][0:rows, :], scale=rsd[oi][0:rows, :]),
                         r=[("ps", banks[half]), ("nbv", oi), ("rsd", oi)], w=[("yn", oi, half)])
                def back(oi=oi, rows=rows, kind=kind, r_=r_, k0=k0):
                    K.op("dve", lambda e: e.tensor_tensor(out=yn[oi][0:rows, :], in0=yn[oi][0:rows, :], in1=gpb[0:rows, :], op=ALU.mult), r=[("yn", oi, 0), ("yn", oi, 1), "gpb"], w=[("yn", oi, 2)])
                    K.op("dve", lambda e: e.tensor_tensor(out=yo[oi][0:rows, :], in0=yn[oi][0:rows, :], in1=bpb[0:rows, :], op=ALU.add), r=[("yn", oi, 2), "bpb"], w=[("yo", oi)])
                    if kind == "p":
                        K.dma("pool", yp_v[r_, k0:k0 + 128, :], yo[oi][:, :], r=[("yo", oi)])
                    else:
                        for t_ in range(4):
                            K.dma("pool", ys_v[t_, :, :], yo[oi][16 * t_:16 * (t_ + 1), :], r=[("yo", oi)])
                ln_back.append(back)
        for th in ln_back:
            th()
        ln_back.clear()
        if dbg:
            for c in range(4):
                K.dma("pool", dbg_out["s2p"][128 * c:128 * (c + 1), :], s2p[c][:, :, :].rearrange("p r k -> p (r k)"), r=[("s2p", c, bk) for bk in [("p", r_) for r_ in range(8)] + [("s", 4), "z"]])
        K.barrier()

    K.barrier(["sp"])
    es.close()
    return nc


def _prep_inputs(inp):
    f = lambda k: np.ascontiguousarray(np.asarray(inp[k], np.float32))
    x_prompt = f("x_prompt")
    x_sample = f("x_sample")
    sre = f("state_ssm_re")[0]
    sim = f("state_ssm_im")[0]
    scv = f("state_conv")[0]
    b_in = f("b_in")[0]
    vecs = np.concatenate([
        b_in.reshape(20, 128), f("b_glu")[0].reshape(4, 128), f("b_dw")[0].reshape(4, 128),
        f("g_conv_ln")[0].reshape(4, 128), f("b_conv_ln")[0].reshape(4, 128), f("b_pw2")[0].reshape(4, 128)], 0).T
    wdwT = f("w_dw")[0].reshape(31, 4, 128).transpose(2, 1, 0)
    lam_re = f("lam_re")[0]
    lam_im = f("lam_im")[0]
    lamT = np.stack([lam_re.T, lam_im.T], 1)
    lamT = np.concatenate([lamT, lamT], 0)
    br = f("b_re")[0].transpose(1, 0, 2)
    bi = f("b_im")[0].transpose(1, 0, 2)
    cr = f("c_re")[0].transpose(2, 0, 1)
    ci = f("c_im")[0].transpose(2, 0, 1)
    wdw = f("w_dw")[0]
    wdg = np.zeros((8, 16, 32, 5, 8), np.float32)
    for s_ in range(8):
        for d_ in range(5):
            for r_ in range(8):
                tau = 8 * d_ + r_ - s_
                if 0 <= tau <= 30:
                    wdg[s_, :, :, d_, r_] = wdw[30 - tau].reshape(32, 16).T
    sccol = np.stack([np.tile(f(k)[0].reshape(32, 16).T, (8, 1)) for k in ("b_dw", "g_conv_ln", "b_conv_ln")], 1)
    shared = {
        "w_in": f("w_in")[0], "w_glu": f("w_glu")[0], "w_pw2": f("w_pw2")[0], "w_out": f("w_out")[0],
        "vecs": np.ascontiguousarray(vecs), "wdwT": np.ascontiguousarray(wdwT),
        "gpost": np.ascontiguousarray(np.broadcast_to(f("g_post")[0].reshape(1, DM), (128, DM))), "bpost": np.ascontiguousarray(np.broadcast_to(f("b_post")[0].reshape(1, DM), (128, DM))),
        "ident": np.eye(128, dtype=np.float32),
        "lamT": np.ascontiguousarray(lamT), "logdt": np.ascontiguousarray(np.broadcast_to(f("log_dt")[0].reshape(1, 32), (128, 32))),
        "BA": np.ascontiguousarray(np.concatenate([br, bi], 0)), "BBraw": np.ascontiguousarray(np.concatenate([bi, br], 0)),
        "CAraw": np.ascontiguousarray(np.concatenate([cr, ci], 0)), "CBraw": np.ascontiguousarray(np.concatenate([ci, cr], 0)),
        "dskT": np.ascontiguousarray(f("d_skip")[0].reshape(32, 16).T),
        "wdg": np.ascontiguousarray(wdg.reshape(128, 32 * 5 * 8)), "sccol": np.ascontiguousarray(sccol.astype(np.float32)),
    }
    in_maps = []
    for i in range(NCORES):
        sl = slice(NSEQ * i, NSEQ * (i + 1))
        hre = sre[sl].transpose(2, 1, 0)
        him = sim[sl].transpose(2, 1, 0)
        m = dict(shared)
        m["xp"] = x_prompt[i]
        m["xs"] = np.ascontiguousarray(x_sample[sl].reshape(NS, DM))
        m["h0A"] = np.ascontiguousarray(np.concatenate([hre, him], 0))
        m["h0Braw"] = np.ascontiguousarray(np.concatenate([him, hre], 0))
        m["stT"] = np.ascontiguousarray(scv[sl].transpose(2, 0, 1))
        pad = np.zeros((NSEQ, 40, 512), np.float32)
        pad[:, 6:36, :] = scv[sl]
        m["stP"] = np.ascontiguousarray(pad.reshape(NSEQ, 5, 8, 512).transpose(3, 2, 0, 1).reshape(512, 8, 5 * NSEQ))
        in_maps.append(m)
    return in_maps


def _assemble(results):
    y_p = np.stack([r["y_p"] for r in results], 0)
    y_s = np.concatenate([r["y_s"].reshape(NSEQ, 4, DM) for r in results], 0)
    hp = np.stack([r["hfin_p"] for r in results], 0)
    re_p = hp[:, 0:64, :].transpose(0, 2, 1)[None]
    im_p = hp[:, 64:128, :].transpose(0, 2, 1)[None]
    cv_p = np.stack([r["ncvT_p"].T for r in results], 0)[None]
    hs = np.concatenate([r["hnew_s"].transpose(2, 1, 0) for r in results], 0)
    re_s = hs[:, :, 0:64][None]
    im_s = hs[:, :, 64:128][None]
    cv_s = np.concatenate([r["ncvT_s"].reshape(512, NSEQ, 30).transpose(1, 2, 0) for r in results], 0)[None]
    c = lambda a: np.ascontiguousarray(a.astype(np.float32))
    return (c(y_p), c(y_s), c(re_p), c(im_p), c(cv_p), c(re_s), c(im_s), c(cv_s))


_NC_CACHE = {}


def kernel(**inputs):
    in_maps = _prep_inputs(inputs)
    if "nc" not in _NC_CACHE:
        _NC_CACHE["nc"] = build_program()
    res = run_bass_kernel_spmd(_NC_CACHE["nc"], in_maps, core_ids=list(range(NCORES)))
    return _assemble(res.results)
```

```python
import numpy as np
import ml_dtypes
from contextlib import ExitStack
import concourse.bass as bass
import concourse.mybir as mybir
from concourse.bass_utils import run_bass_kernel_spmd

F32 = mybir.dt.float32
BF16 = mybir.dt.bfloat16
I32 = mybir.dt.int32
AF = mybir.ActivationFunctionType
ALU = mybir.AluOpType

NCORES = 8
DM = 1024
NP = 2048
NSEQ = 16
NS = 64
NT = NP + NS
KP = 256
KC = KP + NSEQ
KV = KP + 5 * NSEQ
ALPHA = 2.0 ** 0.25
EPS = 1e-5
TWO_PI = 6.283185307179586
BLOCKS = [(0, 512), (512, 512), (1024, 512), (1536, 512), (2048, 64)]


class KB:
    def __init__(self, nc, es):
        self.nc = nc
        self.E = {"pe": nc.tensor, "act": nc.scalar, "dve": nc.vector, "pool": nc.gpsimd, "sp": nc.sync}
        self.sem = {}
        self.cnt = {}
        for k in self.E:
            self.sem[k] = es.enter_context(nc.semaphore("sem_" + k))
            self.cnt[k] = 0
        self.NDS = 20
        self.dq = ("sp", "pool", "act")
        self.dsem = {q: [es.enter_context(nc.semaphore(f"d_{q}{i}")) for i in range(self.NDS)] for q in self.dq}
        self.dval = {q: [0] * self.NDS for q in self.dq}
        self.drr = {q: 0 for q in self.dq}
        self.seen = {k: {} for k in self.E}
        self.lw = {}
        self.rd = {}
        self.pend = {k: [] for k in self.E}

    def _semobj(self, sk):
        return self.sem[sk] if isinstance(sk, str) else self.dsem[sk[0]][sk[1]]

    def _wait(self, eng, sk, val):
        if val <= 0:
            return
        if eng == "pe" and sk == "pe":
            return
        if self.seen[eng].get(sk, 0) >= val:
            return
        self.E[eng].wait_ge(self._semobj(sk), val)
        self.seen[eng][sk] = val

    def _deps(self, eng, r, w):
        for k in r:
            t = self.lw.get(k)
            if t:
                self._wait(eng, *t)
        for k in w:
            t = self.lw.get(k)
            if t:
                self._wait(eng, *t)
            for sk, v in self.rd.get(k, {}).items():
                self._wait(eng, sk, v)

    def _commit(self, t, r, w):
        for k in w:
            self.lw[k] = t
            self.rd[k] = {}
        for k in r:
            d = self.rd.setdefault(k, {})
            d[t[0]] = max(d.get(t[0], 0), t[1])

    def op(self, eng, fn, r=(), w=(), signal=True):
        r = list(r)
        w = list(w)
        self._deps(eng, r, w)
        ins = fn(self.E[eng])
        if not signal:
            self.pend[eng].append((r, w))
            return None
        self.cnt[eng] += 1
        ins.then_inc(self.sem[eng], 1)
        t = (eng, self.cnt[eng])
        for (pr, pw) in self.pend[eng]:
            self._commit(t, pr, pw)
        self.pend[eng] = []
        self._commit(t, r, w)
        return t

    def dma(self, q, out, in_, r=(), w=()):
        r = list(r)
        w = list(w)
        self._deps(q, r, w)
        i = self.drr[q] % self.NDS
        self.drr[q] += 1
        self._wait(q, (q, i), self.dval[q][i])
        ins = self.E[q].dma_start(out=out, in_=in_)
        self.dval[q][i] += 16
        ins.then_inc(self.dsem[q][i], 16)
        t = ((q, i), self.dval[q][i])
        self._commit(t, r, w)
        return t

    def barrier(self, engines=None, dma_queues=None):
        engines = engines or list(self.E)
        dma_queues = self.dq if dma_queues is None else dma_queues
        for e in engines:
            for k in self.E:
                if k == "sp" and "sp" not in dma_queues:
                    continue
                self._wait(e, k, self.cnt[k])
            for q in dma_queues:
                for i in range(self.NDS):
                    self._wait(e, (q, i), self.dval[q][i])


def build_program(upto=9, dbg=False):
    nc = bass.Bass("TRN2", target_bir_lowering=False)
    es = ExitStack()
    K = KB(nc, es)

    def din(name, shape, dt=F32):
        return nc.dram_tensor(name, list(shape), dt, kind="ExternalInput").ap()

    def dout(name, shape, dt=F32):
        return nc.dram_tensor(name, list(shape), dt, kind="ExternalOutput").ap()

    def dscr(name, shape, dt):
        return nc.dram_tensor(name, list(shape), dt).ap()

    xp = din("xp", [NP, DM])
    xs = din("xs", [NS, DM])
    w_in = din("w_in", [DM, 2560])
    w_glu = din("w_glu", [512, 512])
    w_pw2 = din("w_pw2", [512, 512])
    w_out = din("w_out", [DM, DM])
    vecs_d = din("vecs", [128, 40])
    wdwT_d = din("wdwT", [128, 4, 31])
    gpost_d = din("gpost", [128, DM])
    bpost_d = din("bpost", [128, DM])
    ident_d = din("ident", [128, 128])
    lamT_d = din("lamT", [128, 2, 32])
    logdt_d = din("logdt", [128, 32])
    BA_d = din("BA", [128, 32, 16])
    BB_d = din("BBraw", [128, 32, 16])
    CA_d = din("CAraw", [128, 32, 16])
    CB_d = din("CBraw", [128, 32, 16])
    dskT_d = din("dskT", [16, 32])
    h0A_d = din("h0A", [128, 32, NSEQ])
    h0B_d = din("h0Braw", [128, 32, NSEQ])
    stT_d = din("stT", [512, NSEQ, 30])
    stP_d = din("stP", [512, 8, 5 * NSEQ])
    wdg_d = din("wdg", [128, 32 * 5 * 8])
    sccol_d = din("sccol", [128, 3, 32])

    y_p = dout("y_p", [NP, DM])
    y_s = dout("y_s", [NS, DM])
    hfin_p = dout("hfin_p", [128, 32])
    hnew_s = dout("hnew_s", [128, 32, NSEQ])
    ncvT_p = dout("ncvT_p", [512, 30])
    ncvT_s = dout("ncvT_s", [512, NSEQ * 30])

    d_u = dscr("d_u", [128, 32 * KC], BF16)
    d_s = dscr("d_s", [512, 8 * KC], BF16)
    d_v = dscr("d_v", [128, 32 * KV], BF16)
    d_c = dscr("d_c", [512, 8 * KC], BF16)

    dbg_out = {}
    if dbg:
        dbg_out["u_perm"] = dout("dbg_u_perm", [512, 8 * KC], BF16)
        dbg_out["gs"] = dout("dbg_gs", [512, 8 * KC], BF16)
        dbg_out["gc"] = dout("dbg_gc", [512, 8 * KC], BF16)
        dbg_out["vperm"] = dout("dbg_vperm", [512, 8 * KV], BF16)
        dbg_out["U"] = dout("dbg_U", [128, 32 * KC], BF16)

    def sb(name, shape, dt, stack=es, side=None):
        return stack.enter_context(nc.sbuf_tensor("sb_" + name, list(shape), dt, side=side))

    ps = [es.enter_context(nc.psum_tensor(f"ps{i}", [128, 512], F32)) for i in range(8)]

    ident = sb("ident", [128, 128], F32, side="right")
    vecs = sb("vecs", [128, 40], F32, side="right")
    hvec = sb("hvec", [128, 12], F32, side="right")
    gs_perm = [sb(f"gs_perm{c}", [128, 8, KC], BF16, side="right") for c in range(4)]
    sA = ExitStack()
    gc_perm = [sb(f"gc_perm{c}", [128, 8, KC], BF16, sA) for c in range(4)]

    wdg = sb("wdg", [128, 32 * 5 * 8], F32, side="right")
    sccol = sb("sccol", [128, 3, 32], F32, side="right")
    K.dma("sp", ident[:], ident_d[:, :], w=["ident"])
    K.dma("sp", vecs[:], vecs_d[:, :], w=["vecs"])
    K.dma("sp", wdg[:], wdg_d[:, :], w=["wdg"])
    K.dma("sp", sccol[:], sccol_d[:, :, :], w=["sccol"])
    K.op("dve", lambda e: e.tensor_scalar(out=hvec[:, 0:8], in0=vecs[:, 8:16], scalar1=0.5, scalar2=None, op0=ALU.mult),
         r=["vecs"], w=["hvec"])
    K.op("dve", lambda e: e.tensor_scalar(out=hvec[:, 8:12], in0=vecs[:, 20:24], scalar1=0.5, scalar2=None, op0=ALU.mult),
         r=["vecs"], w=["hvec"])
    for c in range(4):
        K.op("dve", lambda e, c=c: e.memset(gc_perm[c][:, 0:4, KP:KC], 0.0), w=[("gc", c, -1)])
        K.op("dve", lambda e, c=c: e.memset(gs_perm[c][:, 0:4, KP:KC], 0.0), w=[("gs", c, -1)])

    fr = lambda name, shape, dt=F32: sb(name, shape, dt, side="right")
    lam = fr("lam", [128, 2, 32]); dtb = fr("dtb", [128, 32])
    T = [fr(f"T{i}", [128, 4, 32]) for i in range(4)]
    PR = fr("PR", [128, 9, 32]); PI = fr("PI", [128, 9, 32])
    QR = fr("QR", [128, 8, 32]); QI = fr("QI", [128, 8, 32])
    mag = fr("mag", [128, 32]); phi = fr("phi", [128, 32]); kf = fr("kf", [128, 32]); ki = fr("ki", [128, 32], I32)
    rr = fr("rr", [128, 32]); rc = fr("rc", [128, 32]); msk = fr("msk", [128, 32]); sinv = fr("sinv", [128, 32]); cosv = fr("cosv", [128, 32])
    KR = fr("KR", [128, 32]); KI = fr("KI", [128, 32]); den = fr("den", [128, 32]); am1 = fr("am1", [128, 32])
    S4 = fr("S4", [128, 32, 8, 2]); S_bf = fr("S_bf", [128, 32, 8, 2], BF16); I2 = fr("I2", [128, 64], BF16)
    PRr = fr("PRr", [128, 8, 32]); PIr = fr("PIr", [128, 8, 32])

    def V(fn, r, w, eng="dve"):
        return K.op(eng, fn, r=r, w=w)

    def tt(o, a, b, op, r, w, eng="dve"):
        return K.op(eng, lambda e: e.tensor_tensor(out=o, in0=a, in1=b, op=op), r=r, w=w)

    def ts(o, a, s1_, op0, r, w, s2_=None, op1=None, eng="dve"):
        if op1 is None:
            return K.op(eng, lambda e: e.tensor_scalar(out=o, in0=a, scalar1=s1_, scalar2=None, op0=op0), r=r, w=w)
        return K.op(eng, lambda e: e.tensor_scalar(out=o, in0=a, scalar1=s1_, scalar2=s2_, op0=op0, op1=op1), r=r, w=w)

    G = "pool"

    def cmul(oR, oI, xR, xI, yR, yI, nslots, r, w):
        a, b, c_, d_ = (T[i][:, 0:nslots, :] for i in range(4))
        tk = ["T0", "T1", "T2", "T3"]
        tt(a, xR, yR, ALU.mult, r, [tk[0]], G); tt(b, xI, yI, ALU.mult, r, [tk[1]], G)
        tt(c_, xR, yI, ALU.mult, r, [tk[2]], G); tt(d_, xI, yR, ALU.mult, r, [tk[3]], G)
        tt(oR, a, b, ALU.subtract, [tk[0], tk[1]], w, G); tt(oI, c_, d_, ALU.add, [tk[2], tk[3]], w, G)

    def g1_gen():
        K.dma("sp", lam[:], lamT_d[:, :, :], w=["lam"]); K.dma("sp", dtb[:], logdt_d[:, :], w=["dtb"])
        lr = lam[:, 0, :]; li = lam[:, 1, :]
        K.op("act", lambda e: e.activation(out=dtb[:], in_=dtb[:], func=AF.Exp), r=["dtb"], w=["dtb"])
        yield
        tt(mag[:], lr, dtb[:], ALU.mult, ["lam", "dtb"], ["mag"], G)
        yield
        K.op("act", lambda e: e.activation(out=mag[:], in_=mag[:], func=AF.Exp), r=["mag"], w=["mag"])
        yield
        tt(phi[:], li, dtb[:], ALU.mult, ["lam", "dtb"], ["phi"], G)
        ts(kf[:], phi[:], 1.0 / TWO_PI, ALU.mult, ["phi"], ["kf"], eng=G)
        yield
        V(lambda e: e.tensor_copy(out=ki[:], in_=kf[:]), ["kf"], ["ki"])
        V(lambda e: e.tensor_copy(out=kf[:], in_=ki[:]), ["ki"], ["kf"])
        yield
        ts(kf[:], kf[:], -TWO_PI, ALU.mult, ["kf"], ["kf"], eng=G)
        tt(rr[:], kf[:], phi[:], ALU.add, ["kf", "phi"], ["rr"], G)
        PIS = 3.141592
        ts(rr[:], rr[:], -PIS, ALU.max, ["rr"], ["rr"], PIS, ALU.min, eng=G)
        ts(rc[:], rr[:], TWO_PI / 4.0, ALU.add, ["rr"], ["rc"], eng=G)
        ts(msk[:], rc[:], PIS, ALU.is_gt, ["rc"], ["msk"], eng=G)
        ts(msk[:], msk[:], -TWO_PI, ALU.mult, ["msk"], ["msk"], eng=G)
        tt(rc[:], rc[:], msk[:], ALU.add, ["msk", "rc"], ["rc"], G)
        ts(rc[:], rc[:], -PIS, ALU.max, ["rc"], ["rc"], PIS, ALU.min, eng=G)
        yield
        K.op("act", lambda e: e.activation(out=sinv[:], in_=rr[:], func=AF.Sin), r=["rr"], w=["sinv"])
        K.op("act", lambda e: e.activation(out=cosv[:], in_=rc[:], func=AF.Sin), r=["rc"], w=["cosv"])
        yield
        V(lambda e: e.memset(PR[:, 0, :], 1.0), [], ["P0"], G); V(lambda e: e.memset(PI[:, 0, :], 0.0), [], ["P0"], G)
        tt(PR[:, 1, :], mag[:], cosv[:], ALU.mult, ["mag", "cosv"], ["P1"], G); tt(PI[:, 1, :], mag[:], sinv[:], ALU.mult, ["mag", "sinv"], ["P1"], G)
        cmul(PR[:, 2:3, :], PI[:, 2:3, :], PR[:, 1:2, :], PI[:, 1:2, :], PR[:, 1:2, :], PI[:, 1:2, :], 1, ["P1"], ["P2"])
        cmul(PR[:, 3:5, :], PI[:, 3:5, :], PR[:, 1:3, :], PI[:, 1:3, :], PR[:, 2:3, :].to_broadcast([128, 2, 32]), PI[:, 2:3, :].to_broadcast([128, 2, 32]), 2, ["P1", "P2"], ["P34"])
        cmul(PR[:, 5:9, :], PI[:, 5:9, :], PR[:, 1:5, :], PI[:, 1:5, :], PR[:, 4:5, :].to_broadcast([128, 4, 32]), PI[:, 4:5, :].to_broadcast([128, 4, 32]), 4, ["P1", "P2", "P34"], ["P58"])
        PK = ["P0", "P1", "P2", "P34", "P58"]
        V(lambda e: e.tensor_copy(out=QR[:, 0, :], in_=PR[:, 8, :]), PK, [("Q", 0)], G); V(lambda e: e.tensor_copy(out=QI[:, 0, :], in_=PI[:, 8, :]), PK, [("Q", 0)], G)
        for j in range(7):
            cmul(QR[:, j + 1:j + 2, :], QI[:, j + 1:j + 2, :], QR[:, j:j + 1, :], QI[:, j:j + 1, :], QR[:, j:j + 1, :], QI[:, j:j + 1, :], 1, [("Q", j)], [("Q", j + 1)])
        QK = [("Q", j) for j in range(8)]
        tt(den[:], lr, lr, ALU.mult, ["lam"], ["den"], G); tt(kf[:], li, li, ALU.mult, ["lam"], ["kf"], G)
        tt(den[:], den[:], kf[:], ALU.add, ["den", "kf"], ["den"], G)
        yield
        V(lambda e: e.reciprocal(out=den[:], in_=den[:]), ["den"], ["den"])
        yield
        ts(am1[:], PR[:, 1, :], -1.0, ALU.add, ["P1"], ["am1"], eng=G)
        tt(KR[:], am1[:], lr, ALU.mult, ["am1", "lam"], ["KR"], G); tt(kf[:], PI[:, 1, :], li, ALU.mult, ["P1", "lam"], ["kf"], G)
        tt(KR[:], KR[:], kf[:], ALU.add, ["KR", "kf"], ["KR"], G); tt(KR[:], KR[:], den[:], ALU.mult, ["KR", "den"], ["KR"], G)
        tt(KI[:], PI[:, 1, :], lr, ALU.mult, ["P1", "lam"], ["KI"], G); tt(kf[:], am1[:], li, ALU.mult, ["am1", "lam"], ["kf"], G)
        tt(KI[:], KI[:], kf[:], ALU.subtract, ["KI", "kf"], ["KI"], G); tt(KI[:], KI[:], den[:], ALU.mult, ["KI", "den"], ["KI"], G)
        V(lambda e: e.tensor_copy(out=S4[0:64, :, :, 0], in_=QR[0:64, :, :].rearrange("p j g -> p g j")), QK, ["S4a"], G)
        ts(S4[64:128, :, :, 0], QI[64:128, :, :].rearrange("p j g -> p g j"), -1.0, ALU.mult, QK, ["S4b"], eng=G)
        V(lambda e: e.tensor_copy(out=S4[0:64, :, :, 1], in_=QI[0:64, :, :].rearrange("p j g -> p g j")), QK, ["S4c"], G)
        V(lambda e: e.tensor_copy(out=S4[64:128, :, :, 1], in_=QR[64:128, :, :].rearrange("p j g -> p g j")), QK, ["S4d"], G)
        V(lambda e: e.tensor_copy(out=S_bf[:], in_=S4[:]), ["S4a", "S4b", "S4c", "S4d"], ["S_bf"], G)
        tt(I2[:], ident[:, 0:64], ident[:, 64:128], ALU.add, ["ident"], ["I2"], G)
        for sx_ in range(8):
            V(lambda e, sx_=sx_: e.tensor_copy(out=PRr[:, sx_, :], in_=PR[:, 7 - sx_, :]), PK, [("Pr", sx_)], G)
            V(lambda e, sx_=sx_: e.tensor_copy(out=PIr[:, sx_, :], in_=PI[:, 7 - sx_, :]), PK, [("Pr", sx_)], G)


        yield

    g1 = g1_gen()
    next(g1)

    PK = ["P0", "P1", "P2", "P34", "P58"]
    QK = [("Q", j) for j in range(8)]

    with ExitStack() as s1:
        w_bf = sb("w_bf", [128, 8, 2560], BF16, s1)
        wst = [sb(f"wst{i}", [128, 8, 256], F32, s1) for i in range(2)]
        x_sb = [sb(f"x_sb{i}", [128, DM], F32, s1) for i in range(4)]
        xT = [sb(f"xT{i}", [128, 8, 512], BF16, s1) for i in range(2)]
        th = [sb(f"th{i}", [128, 512], F32, s1) for i in range(2)]
        ah = [sb(f"ah{i}", [128, 512], F32, s1) for i in range(2)]
        st32 = sb("st32", [128, 4, NSEQ, 30], F32, s1)
        stp32 = sb("stp32", [128, 8, 5 * NSEQ], F32, s1)
        v32p = sb("v32p", [128, 4, 30], F32, s1)
        ncs = sb("ncs", [128, 4, NSEQ, 30], F32, s1)
        u_perm = [sb(f"u_perm{c}", [128, 8, KC], BF16, s1) for c in range(4)]
        v_perm = [sb(f"v_perm{c}", [128, 8, KV], BF16, s1) for c in range(4)]

        def load_x(bi):
            t0, n = BLOCKS[bi]
            if n == 512:
                for i in range(4):
                    K.dma("sp", x_sb[i][:, :], xp[t0 + 128 * i:t0 + 128 * (i + 1), :], w=[("x", i)])
            else:
                K.dma("sp", x_sb[0][0:NS, :], xs[:, :], w=[("x", 0)])

        for c in range(4):
            K.op("dve", lambda e, c=c: e.memset(u_perm[c][:, 0:4, KP:KC], 0.0), w=[("uperm", c, -1)])
        BORD = [0, 4, 1, 2, 3]
        load_x(BORD[0])
        M_ORDER = [12, 8, 13, 9, 14, 10, 15, 11, 0, 1, 2, 3, 4, 5, 6, 7, 16, 17, 18, 19]
        CH_ORDER = []
        for m_ in M_ORDER:
            if m_ // 2 not in CH_ORDER:
                CH_ORDER.append(m_ // 2)
        w_state = {"dma": 0, "cast": set()}

        def w_issue_dma(upto_n):
            while w_state["dma"] < min(upto_n, len(CH_ORDER)):
                i = w_state["dma"]
                cc = CH_ORDER[i]
                for kk in range(8):
                    K.dma("sp", wst[i % 2][:, kk, :], w_in[128 * kk:128 * (kk + 1), 256 * cc:256 * (cc + 1)], w=[("wst", i % 2, kk)])
                w_state["dma"] += 1

        def w_ensure(cc):
            if cc in w_state["cast"]:
                return
            i = CH_ORDER.index(cc)
            w_issue_dma(i + 1)
            K.op("dve", lambda e: e.tensor_copy(out=w_bf[:, 0:4, 256 * cc:256 * (cc + 1)], in_=wst[i % 2][:, 0:4, :]),
                 r=[("wst", i % 2, kk) for kk in range(0, 4)], w=[("wbf", cc, 0)])
            K.op("act", lambda e: e.activation(out=w_bf[:, 4:8, 256 * cc:256 * (cc + 1)], in_=wst[i % 2][:, 4:8, :], func=AF.Identity),
                 r=[("wst", i % 2, kk) for kk in range(4, 8)], w=[("wbf", cc, 1)])
            w_state["cast"].add(cc)
            w_issue_dma(i + 3)

        w_issue_dma(2)
        def st_dma(c):
            K.dma("sp", st32[:, c, :, :], stT_d[128 * c:128 * (c + 1), :, :], w=[("st32", c)])
            K.dma("sp", stp32[:, :, :], stP_d[128 * c:128 * (c + 1), :, :], w=["stp32"])

        def st_copy(c):
            K.op("dve", lambda e, c=c: e.tensor_copy(out=ncs[:, c, :, 0:26], in_=st32[:, c, :, 4:30]), r=[("st32", c)], w=[("ncs", c, 0)])
            K.op("dve", lambda e, c=c: e.tensor_copy(out=v_perm[c][:, :, KP:KV], in_=stp32[:, :, :]), r=["stp32"], w=[("vperm", c, -1)])
        ST_AT = {1: ("d", 0), 3: ("c", 0), 4: ("d", 1), 6: ("c", 1), 7: ("d", 2), 9: ("c", 2), 11: ("d", 3), 13: ("c", 3)}
        zi = 0
        ev = 0
        ct_count = [0]
        G1_AT = {2: 1, 3: 1, 4: 1, 10: 1, 11: 1, 22: 1, 23: 1, 60: 1, 61: 1}

        def g1_tick():
            ct_count[0] += 1
            if ct_count[0] in G1_AT:
                try:
                    next(g1)
                except StopIteration:
                    pass
        du_v = d_u.rearrange("(r c) (g k) -> c r g k", c=16, g=32)
        dv_v = d_v.rearrange("(r c) (g k) -> c r g k", c=16, g=32)

        act_pending = []

        def scratch_piece(pc):
            for c in range(4):
                for gl in range(8):
                    g = 8 * c + gl
                    ps_ = slice(16 * gl, 16 * (gl + 1))
                    if pc < 2:
                        ks = slice(128 * pc, 128 * (pc + 1))
                        rk_u = [("uperm", c, 2 * pc), ("uperm", c, 2 * pc + 1)]
                        rk_v = [("vperm", c, 2 * pc), ("vperm", c, 2 * pc + 1)]
                        K.dma("sp", du_v[:, :, g, ks], u_perm[c][ps_, :, ks], r=rk_u, w=[("d_u", c, pc, gl)])
                        if pc == 1 and gl % 2 == 0:
                            act_pending.append(lambda g=g, ks=ks, c=c, ps_=ps_, rk_v=rk_v, pc=pc, gl=gl: K.dma("act", dv_v[:, :, g, ks], v_perm[c][ps_, :, ks], r=rk_v, w=[("d_v", c, pc, gl)]))
                        else:
                            K.dma("sp", dv_v[:, :, g, ks], v_perm[c][ps_, :, ks], r=rk_v, w=[("d_v", c, pc, gl)])
                    else:
                        K.dma("sp", du_v[:, :, g, KP:KC], u_perm[c][ps_, :, KP:KC], r=[("uperm", c, -1), ("uperm", c, 4)], w=[("d_u", c, pc, gl)])
                        K.dma("sp", dv_v[:, :, g, KP:KV], v_perm[c][ps_, :, KP:KV], r=[("vperm", c, -1), ("vperm", c, 4)], w=[("d_v", c, pc, gl)])

        def do_transposes(pos):
            bj = BORD[pos]
            t0_, n_ = BLOCKS[bj]
            bb_ = pos % 2
            nt_ = 4 if n_ == 512 else 1
            rows_ = 128 if n_ == 512 else NS
            for kk in range(8):
                bank = 4 + (kk % 2)
                for i in range(nt_):
                    K.op("pe", lambda e, i=i, kk=kk, bank=bank: e.transpose(out=ps[bank][:, rows_ * i:rows_ * (i + 1)], in_=x_sb[i][0:rows_, 128 * kk:128 * (kk + 1)], identity=ident[0:rows_, 0:rows_]),
                         r=[("x", i), "ident"], w=[("ps", bank)], signal=(i == nt_ - 1))
                if kk % 2 == 0:
                    K.op("dve", lambda e, kk=kk, bank=bank: e.tensor_copy(out=xT[bb_][:, kk, 0:n_], in_=ps[bank][:, 0:n_]), r=[("ps", bank)], w=[("xT", bb_, kk)])
                else:
                    K.op("act", lambda e, kk=kk, bank=bank: e.activation(out=xT[bb_][:, kk, 0:n_], in_=ps[bank][:, 0:n_], func=AF.Identity), r=[("ps", bank)], w=[("xT", bb_, kk)])
            if pos + 1 < len(BORD):
                load_x(BORD[pos + 1])

        do_transposes(0)
        for pos_, bi in enumerate(BORD):
            t0, n = BLOCKS[bi]
            bb = pos_ % 2
            if pos_ == 2:
                scratch_piece(2)
            nt = 4 if n == 512 else 1
            rows = 128 if n == 512 else NS
            if bi == 2:
                scratch_piece(0)
            for mi_, m in enumerate(M_ORDER):
                for _ in range(min(2, len(act_pending))):
                    act_pending.pop(0)()
                if mi_ == 10 and pos_ + 1 < len(BORD):
                    do_transposes(pos_ + 1)
                if bi == 3 and mi_ == 12:
                    scratch_piece(1)
                if pos_ == 0 and mi_ in ST_AT:
                    kind_, c_ = ST_AT[mi_]
                    (st_dma if kind_ == "d" else st_copy)(c_)
                bank = zi % 4
                zi += 1
                cc = m // 2
                w_ensure(cc)
                for kk in range(8):
                    K.op("pe", lambda e, m=m, kk=kk, bank=bank: e.matmul(out=ps[bank][:, 0:n], lhsT=w_bf[:, kk, 128 * m:128 * (m + 1)], rhs=xT[bb][:, kk, 0:n], start=(kk == 0), stop=(kk == 7)),
                         r=[("wbf", cc, kk // 4), ("xT", bb, kk)], w=[("ps", bank)], signal=(kk == 7))
                g1_tick()
                pz = ps[bank][:, 0:n]
                bcol = vecs[:, m:m + 1]
                c = m % 4
                if m < 8:
                    dst_t = u_perm[c] if m < 4 else gs_perm[c]
                    key = ("uperm" if m < 4 else "gs", c, bi)
                    func = AF.Identity if m < 4 else AF.Silu
                    if n == 512:
                        k0 = t0 // 8
                        o_ap = dst_t[:, :, k0:k0 + 64]
                        i_ap = pz.rearrange("p (k r) -> p r k", r=8)
                    else:
                        o_ap = dst_t[:, 4:8, KP:KC]
                        i_ap = pz.rearrange("p (s t) -> p t s", t=4)
                    K.op("act", lambda e, o_ap=o_ap, i_ap=i_ap, func=func, bcol=bcol: e.activation(out=o_ap, in_=i_ap, func=func, bias=bcol),
                         r=[("ps", bank), "vecs"], w=[key])
                elif m >= 16:
                    if n == 512:
                        k0 = t0 // 8
                        o_ap = gc_perm[c][:, :, k0:k0 + 64]
                        i_ap = pz.rearrange("p (k r) -> p r k", r=8)
                    else:
                        o_ap = gc_perm[c][:, 4:8, KP:KC]
                        i_ap = pz.rearrange("p (s t) -> p t s", t=4)
                    K.op("act", lambda e, o_ap=o_ap, i_ap=i_ap, bcol=bcol: e.activation(out=o_ap, in_=i_ap, func=AF.Silu, bias=bcol),
                         r=[("ps", bank), "vecs"], w=[("gc", c, bi)])
                elif m >= 12:
                    K.op("act", lambda e, c=c, pz=pz: e.activation(out=th[c % 2][:, 0:n], in_=pz, func=AF.Tanh, bias=hvec[:, 4 + c:5 + c], scale=0.5),
                         r=[("ps", bank), "hvec"], w=[("th", c % 2)])
                else:
                    K.op("act", lambda e, c=c, pz=pz: e.activation(out=ah[c % 2][:, 0:n], in_=pz, func=AF.Identity, bias=hvec[:, c:c + 1], scale=0.5),
                         r=[("ps", bank), "hvec"], w=[("ah", c % 2)])
                    if n == 512:
                        k0 = t0 // 8
                        o_ap = v_perm[c][:, :, k0:k0 + 64]
                        i0 = th[c % 2][:, 0:n].rearrange("p (k r) -> p r k", r=8)
                        i1 = ah[c % 2][:, 0:n].rearrange("p (k r) -> p r k", r=8)
                        key = ("vperm", c, bi)
                    else:
                        o_ap = v_perm[c][:, 4:8, KP:KV].rearrange("p t (s j) -> p t s j", j=5)[:, :, :, 4]
                        i0 = th[c % 2][:, 0:n].rearrange("p (s t) -> p t s", t=4)
                        i1 = ah[c % 2][:, 0:n].rearrange("p (s t) -> p t s", t=4)
                        key = ("vperm", c, bi)
                    K.op("dve", lambda e, o_ap=o_ap, i0=i0, i1=i1: e.scalar_tensor_tensor(out=o_ap, in0=i0, scalar=1.0, in1=i1, op0=ALU.add, op1=ALU.mult),
                         r=[("th", c % 2), ("ah", c % 2)], w=[key])
                    if bi == 3:
                        K.op("dve", lambda e, c=c: e.scalar_tensor_tensor(out=v32p[:, c, :], in0=th[c % 2][:, 482:512], scalar=1.0, in1=ah[c % 2][:, 482:512], op0=ALU.add, op1=ALU.mult),
                             r=[("th", c % 2), ("ah", c % 2)], w=[("v32p", c)])
                    if bi == 4:
                        j0 = th[c % 2][:, 0:n].rearrange("p (s t) -> p s t", t=4)
                        j1 = ah[c % 2][:, 0:n].rearrange("p (s t) -> p s t", t=4)
                        K.op("dve", lambda e, c=c, i0=j0, i1=j1: e.scalar_tensor_tensor(out=ncs[:, c, :, 26:30], in0=i0, scalar=1.0, in1=i1, op0=ALU.add, op1=ALU.mult),
                             r=[("th", c % 2), ("ah", c % 2)], w=[("ncs", c, 1)])
        while act_pending:
            act_pending.pop(0)()
        for _ in g1:
            pass
        for c in range(4):
            K.dma("pool", ncvT_p[128 * c:128 * (c + 1), :], v32p[:, c, :], r=[("v32p", c)])
            K.dma("pool", ncvT_s[128 * c:128 * (c + 1), :], ncs[:, c, :, :].rearrange("p s j -> p (s j)"), r=[("ncs", c, 0), ("ncs", c, 1)])
        if dbg:
            for c in range(4):
                K.dma("pool", dbg_out["u_perm"][128 * c:128 * (c + 1), :], u_perm[c][:, :, :].rearrange("p r k -> p (r k)"), r=[("uperm", c, b) for b in range(-1, 5)])
                K.dma("pool", dbg_out["gs"][128 * c:128 * (c + 1), :], gs_perm[c][:, :, :].rearrange("p r k -> p (r k)"), r=[("gs", c, b) for b in range(-1, 5)])
                K.dma("pool", dbg_out["gc"][128 * c:128 * (c + 1), :], gc_perm[c][:, :, :].rearrange("p r k -> p (r k)"), r=[("gc", c, b) for b in range(-1, 5)])
                K.dma("pool", dbg_out["vperm"][128 * c:128 * (c + 1), :], v_perm[c][:, :, :].rearrange("p r k -> p (r k)"), r=[("vperm", c, b) for b in range(-1, 5)])
        K.barrier(dma_queues=("pool",))

    DU_KEYS = [("d_u", c, pc, gl) for c in range(4) for pc in range(3) for gl in range(8)]
    DV_KEYS = [("d_v", c, pc, gl) for c in range(4) for pc in range(3) for gl in range(8)]
    BA = fr("BA", [128, 32, 16]); BB = fr("BB", [128, 32, 16]); CA = fr("CA", [128, 32, 16]); CB = fr("CB", [128, 32, 16])
    BbA = fr("BbA", [128, 32, 16]); BbB = fr("BbB", [128, 32, 16]); BbA_bf = fr("BbA_bf", [128, 32, 16], BF16)
    G1t = fr("G1t", [128, 32, 16]); G2t = fr("G2t", [128, 32, 16])
    dskT = fr("dskT", [16, 32]); Dd = fr("Dd", [16, 32, 16])
    h0bf = fr("h0bf", [128, 32, NSEQ], BF16); H4 = fr("H4", [128, 32, NSEQ])
    Mj = [fr("Mj0", [128, 8, 8, 128], BF16), None]

    def gen_mj(gb, extra_r=(), eng="pool"):
        K.op(eng, lambda e: e.tensor_tensor(out=Mj[gb % 2][:, :, :, :].rearrange("p g j (h m) -> p (g j h) m", h=2),
                                               in0=I2[:].unsqueeze(1).to_broadcast([128, 128, 64]),
                                               in1=S_bf[:, 8 * gb:8 * gb + 8, :, :].rearrange("p g j h -> p (g j h)").unsqueeze(2).to_broadcast([128, 128, 64]),
                                               op=ALU.mult),
             r=["I2", "S_bf"] + list(extra_r), w=[("Mj", gb % 2)])

    cfin_perm = [sb(f"cfin{c}", [128, 8, KC], BF16, side="right") for c in range(4)]
    if dbg:
        dbg_out["cfin"] = dout("dbg_cfin", [512, 8 * KC], BF16)
    if dbg:
        dbg_out["c1sc"] = dout("dbg_c1sc", [128, 32 * KC], BF16)
    if upto >= 2:
      with ExitStack() as s2:
        f2 = lambda name, shape, dt=F32: sb(name, shape, dt, s2)
        Wc = f2("Wc", [128, 32, 5, 128], BF16)
        Vv = f2("Vv", [128, 32, KV], BF16)
        maskc = f2("maskc", [128, 16])
        RB = f2("RB", [128, 8])
        bones = f2("bones", [128, 128], BF16)
        wp_st = f2("wp_st", [128, 4, 512])
        wp_bf = f2("wp_bf", [128, 4, 512], BF16)
        Ybf = [f2(f"Ybf{i}", [128, KC], BF16) for i in range(2)]
        Ysq = [f2(f"Ysq{i}", [128, KC], BF16) for i in range(2)]
        mean = f2("mean", [128, KC]); var = f2("var", [128, KC]); rstd = f2("rstd", [128, KC])
        t1 = [f2(f"t1_{i}", [128, KC]) for i in range(2)]
        c1sc = f2("c1sc", [128, 32, KC], BF16)
        Vflat = Vv[:, :, :].rearrange("p g k -> p (g k)")
        c1f = [Vflat[:, 8 * KC * c:8 * KC * (c + 1)] for c in range(4)]
        K.dma("sp", Vv[:, :, :].rearrange("p g k -> p (g k)"), d_v[:, :], r=DV_KEYS + DU_KEYS, w=[("Vv", 0)] + [("Vvg", g_) for g_ in range(32)])
        VK = [("Vv", 0)]
        for ci in range(4):
            K.dma("sp", wp_st[:, ci, :], w_pw2[128 * ci:128 * (ci + 1), :], w=[("wp_st", ci)])
        K.op("dve", lambda e: e.tensor_reduce(out=maskc[:], in_=ident[:, :].rearrange("p (s c) -> p c s", s=8), axis=mybir.AxisListType.X, op=ALU.add), r=["ident"], w=["maskc"])
        K.op("dve", lambda e: e.tensor_reduce(out=RB[:], in_=ident[:, :].rearrange("p (s c) -> p s c", s=8), axis=mybir.AxisListType.X, op=ALU.add), r=["ident"], w=["RB"])
        K.op("dve", lambda e: e.tensor_copy(out=bones[:].rearrange("p (s c) -> p s c", s=8), in_=RB[:].unsqueeze(2).to_broadcast([128, 8, 16])), r=["RB"], w=["bones"])
        for gq in range(4):
            K.op("dve", lambda e, gq=gq: e.tensor_tensor(out=Wc[:, 8 * gq:8 * gq + 8, :, :].rearrange("p g d (r c) -> p (g d r) c", c=16),
                                                        in0=wdg[:, 320 * gq:320 * (gq + 1)].unsqueeze(2).to_broadcast([128, 320, 16]),
                                                        in1=maskc[:].unsqueeze(1).to_broadcast([128, 320, 16]), op=ALU.mult),
                 r=["wdg", "maskc"], w=[("Wc", gq)])
        for ci in range(4):
            K.op("act", lambda e, ci=ci: e.activation(out=wp_bf[:, ci, :], in_=wp_st[:, ci, :], func=AF.Identity), r=[("wp_st", ci)], w=[("wp_bf", ci)])
        if upto >= 3:
            X1 = G1t
            X2 = G2t
            K.dma("sp", BA[:], BA_d[:, :, :], w=["BA"]); K.dma("sp", BB[:], BB_d[:, :, :], w=["BB"])
            K.dma("sp", CA[:], CA_d[:, :, :], w=["CA"]); K.dma("sp", CB[:], CB_d[:, :, :], w=["CB"])
            K.dma("sp", dskT[:], dskT_d[:, :], w=["dskT"])
            K.dma("sp", X1[:], h0A_d[:, :, :], w=["G1t"]); K.dma("sp", X2[:], h0B_d[:, :, :], w=["G2t"])
            ts(X2[0:64, :, :], X2[0:64, :, :], -1.0, ALU.mult, ["G2t", ("Wc", 3)], ["G2t"], eng=G)
            b16s = lambda ap: ap.unsqueeze(2).to_broadcast([128, 32, NSEQ])
            V(lambda e: e.tensor_copy(out=h0bf[:], in_=X1[:]), ["G1t"], ["h0bf"], G)
            tt(H4[:], b16s(PR[:, 4, :]), X1[:], ALU.mult, PK + ["G1t"], ["H4"], G)
            tt(X1[:], b16s(PI[:, 4, :]), X2[:], ALU.mult, PK + ["G2t", "H4"], ["G1t"], G)
            tt(H4[:], H4[:], X1[:], ALU.add, ["H4", "G1t"], ["H4"], G)
            ts(BB[0:64, :, :], BB[0:64, :, :], -1.0, ALU.mult, ["BB"], ["BB"], eng=G)
            ts(CA[64:128, :, :], CA[64:128, :, :], -1.0, ALU.mult, ["CA"], ["CA"], eng=G)
            ts(CB[:], CB[:], -1.0, ALU.mult, ["CB"], ["CB"], eng=G)
            bc16 = lambda ap: ap.unsqueeze(2).to_broadcast([128, 32, 16])
            tt(G1t[:], bc16(KR[:]), BA[:], ALU.mult, ["KR", "BA", "H4"], ["G1t"], G); tt(G2t[:], bc16(KI[:]), BB[:], ALU.mult, ["KI", "BB", "H4", "G1t"], ["G2t"], G)
            tt(BbA[:], G1t[:], G2t[:], ALU.add, ["G1t", "G2t"], ["BbA"], G)
            tt(G1t[:], bc16(KR[:]), BB[:], ALU.mult, ["KR", "BB"], ["G1t"], G); tt(G2t[:], bc16(KI[:]), BA[:], ALU.mult, ["KI", "BA"], ["G2t"], G)
            tt(BbB[:], G1t[:], G2t[:], ALU.subtract, ["G1t", "G2t"], ["BbB"], G)
            V(lambda e: e.tensor_copy(out=BbA_bf[:], in_=BbA[:]), ["BbA"], ["BbA_bf"], G)
            tt(Dd[:], ident[0:16, 0:16].unsqueeze(1).to_broadcast([16, 32, 16]), dskT[:].unsqueeze(2).to_broadcast([16, 32, 16]), ALU.mult, ["ident", "dskT"], ["Dd"], G)
        bdw = lambda g: sccol[:, 0, g:g + 1]
        gln = lambda g: sccol[:, 1, g:g + 1]
        bln = lambda g: sccol[:, 2, g:g + 1]

        def conv_g(g, bank):
            for d in range(5):
                K.op("pe", lambda e, g=g, d=d, bank=bank: e.matmul(out=ps[bank][:, d:KP], lhsT=Wc[:, g, d, :], rhs=Vv[:, g, 0:KP - d], start=(d == 0), stop=(d == 4), skip_group_check=True),
                     r=[("Vvg", g), ("Wc", g // 8)], w=[("ps", bank)], signal=False)
            for d in range(5):
                K.op("pe", lambda e, g=g, d=d, bank=bank: e.matmul(out=ps[bank][:, KP:KC], lhsT=Wc[:, g, d, :], rhs=Vv[:, g, KP:KV].rearrange("p (s j) -> p s j", j=5)[:, :, 4 - d], start=False, stop=(d == 4), skip_group_check=True),
                     r=[("Vvg", g), ("Wc", g // 8)], w=[("ps", bank)], signal=(d == 4))

        def stats_g(g, bank):
            i = g % 2
            K.op("act", lambda e: e.activation(out=Ybf[i][:, :], in_=ps[bank][:, 0:KC], func=AF.Identity, bias=bdw(g)), r=[("ps", bank), "sccol"], w=[("Ybf", i)])
            K.op("dve", lambda e: e.scalar_tensor_tensor(out=Ysq[i][:, :], in0=ps[bank][:, 0:KC], scalar=bdw(g), in1=Ybf[i][:, :], op0=ALU.add, op1=ALU.mult), r=[("ps", bank), "sccol", ("Ybf", i)], w=[("Ysq", i)])
            K.op("pe", lambda e: e.matmul(out=ps[6][:, 0:KC], lhsT=bones[:], rhs=Ybf[i][:, :], start=(g == 0), stop=(g == 31), skip_group_check=True), r=["bones", ("Ybf", i)], w=[("ps", 6)], signal=(g == 31))
            K.op("pe", lambda e: e.matmul(out=ps[7][:, 0:KC], lhsT=bones[:], rhs=Ysq[i][:, :], start=(g == 0), stop=(g == 31), skip_group_check=True), r=["bones", ("Ysq", i)], w=[("ps", 7)], signal=(g == 31))

        conv_g(0, 0)
        for g in range(32):
            if g + 1 < 32:
                conv_g(g + 1, (g + 1) % 4)
            stats_g(g, g % 4)
        K.op("act", lambda e: e.activation(out=mean[:], in_=ps[6][:, 0:KC], func=AF.Identity, scale=1.0 / 512.0), r=[("ps", 6)], w=["mean"])
        K.op("dve", lambda e: e.tensor_tensor(out=var[:], in0=mean[:], in1=mean[:], op=ALU.mult), r=["mean"], w=["var"])
        K.op("dve", lambda e: e.scalar_tensor_tensor(out=var[:], in0=ps[7][:, 0:KC], scalar=1.0 / 512.0, in1=var[:], op0=ALU.mult, op1=ALU.subtract), r=[("ps", 7), "var"], w=["var"])
        K.op("dve", lambda e: e.tensor_scalar(out=var[:], in0=var[:], scalar1=EPS, scalar2=None, op0=ALU.add), r=["var"], w=["var"])
        K.op("act", lambda e: e.activation(out=var[:], in_=var[:], func=AF.Sqrt), r=["var"], w=["var"])
        K.op("dve", lambda e: e.reciprocal(out=rstd[:], in_=var[:]), r=["var"], w=["rstd"])

        def norm_g(g, bank):
            i = g % 2
            K.op("dve", lambda e: e.scalar_tensor_tensor(out=t1[i][:, :], in0=ps[bank][:, 0:KC], scalar=bdw(g), in1=mean[:], op0=ALU.add, op1=ALU.subtract),
                 r=[("ps", bank), "sccol", "mean"], w=[("t1", i)])
            K.op("dve", lambda e: e.tensor_tensor(out=t1[i][:, :], in0=t1[i][:, :], in1=rstd[:], op=ALU.mult), r=[("t1", i), "rstd"], w=[("t1", i)])
            K.op("act", lambda e: e.activation(out=c1sc[:, g, :], in_=t1[i][:, :], func=AF.Silu, bias=bln(g), scale=gln(g)), r=[("t1", i), "sccol"], w=[("c1sc", g)])

        conv_g(0, 0)
        for g in range(32):
            if g + 1 < 32:
                conv_g(g + 1, (g + 1) % 4)
            norm_g(g, g % 4)
            if g % 8 == 7:
                c = g // 8
                for rx in range(8):
                    K.dma("sp", d_c[128 * c:128 * (c + 1), KC * rx:KC * (rx + 1)].rearrange("(g q) k -> q g k", q=16), c1sc[16 * rx:16 * (rx + 1), 8 * c:8 * c + 8, :],
                          r=[("c1sc", g_) for g_ in range(8 * c, 8 * c + 8)], w=[("d_c", c, rx)])
                glo = (8 * KC * c) // KV
                ghi = min(31, (8 * KC * (c + 1) - 1) // KV)
                K.dma("sp", c1f[c], d_c[128 * c:128 * (c + 1), :], r=[("d_c", c, rx) for rx in range(8)], w=[("c1f", c)] + [("Vvg", g_) for g_ in range(glo, ghi + 1)])
        CK = [("c1sc", g) for g in range(32)]
        if upto >= 3:
            gen_mj(0, eng="dve")
        if dbg:
            K.dma("pool", dbg_out["c1sc"][:, :], c1sc[:, :, :].rearrange("p g k -> p (g k)"), r=CK)
        NPC = 8 * KC
        pcount = 0
        for col0 in range(0, NPC, 512):
            n = min(512, NPC - col0)
            for mo in range(4):
                bank = pcount % 4
                pcount += 1
                for ci in range(4):
                    K.op("pe", lambda e, mo=mo, ci=ci, bank=bank: e.matmul(out=ps[bank][:, 0:n], lhsT=wp_bf[:, ci, 128 * mo:128 * (mo + 1)], rhs=c1f[ci][:, col0:col0 + n], start=(ci == 0), stop=(ci == 3)),
                         r=[("wp_bf", ci), ("c1f", ci)], w=[("ps", bank)], signal=(ci == 3))
                K.op("dve", lambda e, mo=mo, bank=bank: e.scalar_tensor_tensor(out=cfin_perm[mo][:, :, :].rearrange("p r k -> p (r k)")[:, col0:col0 + n], in0=ps[bank][:, 0:n], scalar=vecs[:, 36 + mo:37 + mo],
                                                                             in1=gc_perm[mo][:, :, :].rearrange("p r k -> p (r k)")[:, col0:col0 + n], op0=ALU.add, op1=ALU.mult),
                     r=[("ps", bank), "vecs"] + [("gc", mo, b) for b in range(-1, 5)], w=[("cfin", mo, col0)])
        if dbg:
            for c in range(4):
                K.dma("pool", dbg_out["cfin"][128 * c:128 * (c + 1), :], cfin_perm[c][:, :, :].rearrange("p r k -> p (r k)"), r=[("cfin", c, col0) for col0 in range(0, NPC, 512)] + [("cfin", c, -1)])
        K.barrier()
    sA.close()

    if dbg:
        dbg_out["P"] = dout("dbg_P", [128, 2 * 9 * 32], F32)
        dbg_out["Tc"] = dout("dbg_Tc", [128, 32 * 128], BF16)
        dbg_out["BcT"] = dout("dbg_BcT", [128, 32 * 128], BF16)
        dbg_out["Fc"] = dout("dbg_Fc", [128, 32 * 192], BF16)
        dbg_out["s1rq"] = dout("dbg_s1rq", [128, 32 * KC], BF16)
        dbg_out["Hbf"] = dout("dbg_Hbf", [128, 32 * KP], BF16)
    if upto >= 3:
      with ExitStack() as s3:
        f3 = lambda name, shape, dt=F32: sb(name, shape, dt, s3)
        Fc = f3("Fc", [128, 32, 192], BF16)
        Tc = f3("Tc", [128, 32, 128], BF16); BcT = f3("BcT", [128, 32, 128], BF16)
        Hfin = f3("Hfin", [128, 32]); Hns = f3("Hns", [128, 32, NSEQ])
        E32 = f3("E32", [128, 8, 128]); Gb = f3("Gb", [128, 8, 128])
        FT1 = f3("FT1", [128, 8, 144]); FT2 = f3("FT2", [128, 8, 144])
        K_bf = f3("K_bf", [16, 8, 128], BF16)
        Mj[1] = f3("Mj1", [128, 8, 8, 128], BF16)
        U = f3("U", [128, 32, KC], BF16)
        Hbf2 = [f3(f"Hbf{i}", [128, 8, KP], BF16) for i in range(2)]
        s1rq2 = [f3(f"s1rq{i}", [128, 8, KC], BF16) for i in range(2)]
        s1f = [sb(f"s1f{c}", [128, 8, KC], BF16, side="right") for c in range(4)]
        K.dma("sp", U[:, :, :].rearrange("p g k -> p (g k)"), d_u[:, :], r=DU_KEYS, w=[("U", 0)])
        UK = [("U", 0)]
        K.op("act", lambda e: e.memzero(Tc[:]), w=["Tc0"])
        K.op("act", lambda e: e.memzero(Fc[:, :, 0:48]), w=[("Fc", -1)])
        if dbg:
            K.dma("pool", dbg_out["P"][:, 0:288], PR[:, :, :].rearrange("p t g -> p (t g)"), r=PK)
            K.dma("pool", dbg_out["P"][:, 288:576], PI[:, :, :].rearrange("p t g -> p (t g)"), r=PK)

        def gen_thunks(gb):
            gs_ = slice(8 * gb, 8 * gb + 8)
            pw = lambda P_, nt: P_[:, 0:nt, gs_].rearrange("p t g -> p g t").unsqueeze(3).to_broadcast([128, 8, nt, 16])
            bt = lambda X_, nt: X_[:, gs_, :].unsqueeze(2).to_broadcast([128, 8, nt, 16])
            v8 = lambda X_: X_[:].rearrange("p g (t q) -> p g t q", t=8)
            v9 = lambda X_: X_[:].rearrange("p g (t q) -> p g t q", t=9)
            PRK = [("Pr", sx_) for sx_ in range(8)]

            def t_e1():
                tt(v8(E32), pw(PRr, 8), bt(BbA, 8), ALU.mult, PRK + ["BbA"], ["E32a"])

            def t_e2():
                tt(v8(Gb), pw(PIr, 8), bt(BbB, 8), ALU.mult, PRK + ["BbB"], ["Gb"])

            def t_e3():
                tt(E32[:], E32[:], Gb[:], ALU.add, ["E32a", "Gb"], ["E32"])

            def t_f1():
                tt(v9(FT1), pw(PR, 9), bt(CA, 9), ALU.mult, PK + ["CA"], ["FT1"])

            def t_f2():
                tt(v9(FT2), pw(PI, 9), bt(CB, 9), ALU.mult, PK + ["CB"], ["FT2"])

            def t_f3():
                tt(Fc[:, gs_, 48:192], FT1[:], FT2[:], ALU.add, ["FT1", "FT2"], [("Fc", gb)])

            def t_tr():
                for h2 in range(2):
                    bank = 4 + h2
                    for gi in range(4):
                        gl = 4 * h2 + gi
                        K.op("pe", lambda e, gl=gl, gi=gi, bank=bank: e.transpose(out=ps[bank][:, 128 * gi:128 * (gi + 1)], in_=E32[:, gl, :], identity=ident[:]),
                             r=["E32", "ident"], w=[("ps", bank)], signal=(gi == 3))
                    K.op("act", lambda e, h2=h2, bank=bank: e.activation(out=BcT[:, 8 * gb + 4 * h2:8 * gb + 4 * h2 + 4, :], in_=ps[bank][:, :].rearrange("p (g m) -> p g m", g=4), func=AF.Identity),
                         r=[("ps", bank)], w=[("BcT", gb, h2)])

            def t_k():
                for h2 in range(2):
                    bank = 6 + h2
                    for gi in range(4):
                        gl = 4 * h2 + gi
                        g = 8 * gb + gl
                        K.op("pe", lambda e, g=g, gi=gi, bank=bank: e.matmul(out=ps[bank][0:16, 128 * gi:128 * (gi + 1)], lhsT=BbA_bf[:, g, :], rhs=Fc[:, g, 48:176], start=True, stop=False, skip_group_check=True),
                             r=["BbA_bf", ("Fc", gb)], w=[("ps", bank)], signal=False)
                        K.op("pe", lambda e, g=g, gi=gi, bank=bank: e.matmul(out=ps[bank][0:16, 128 * gi:128 * gi + 16], lhsT=Dd[:, g, :], rhs=ident[0:16, 0:16], start=False, stop=True, skip_group_check=True),
                             r=["Dd", "ident"], w=[("ps", bank)], signal=(gi == 3))
                    K.op("act", lambda e, h2=h2, bank=bank: e.activation(out=K_bf[:, 4 * h2:4 * h2 + 4, :], in_=ps[bank][0:16, :].rearrange("p (g m) -> p g m", g=4), func=AF.Identity),
                         r=[("ps", bank)], w=[("K_bf", h2)])

            def t_tc():
                for sx in range(8):
                    K.dma("sp", Tc[16 * sx:16 * (sx + 1), gs_, 16 * sx:128], K_bf[0:16, :, 0:128 - 16 * sx], r=[("K_bf", 0), ("K_bf", 1), "Tc0"], w=[("Tc", gb, sx)])

            return [t_e1, t_e2, t_e3, t_f1, t_f2, t_f3, t_tr, t_k, t_tc]

        def gen_batch(gb):
            for th in gen_thunks(gb):
                th()

        TKb = lambda gb: [("Tc", gb, sx) for sx in range(8)]
        BKb = lambda gb: [("BcT", gb, 0), ("BcT", gb, 1)]
        FKb = lambda gb: [("Fc", -1), ("Fc", gb)]

        def sample_batch(gb):
            for gl in range(8):
                g = 8 * gb + gl
                K.op("pe", lambda e, g=g, gl=gl: e.matmul(out=ps[0][:, NSEQ * gl:NSEQ * (gl + 1)], lhsT=BcT[:, g, :], rhs=U[:, g, KP:KC], start=(gl == 0), stop=False, skip_group_check=True),
                     r=BKb(gb) + UK, w=[("ps", 0)], signal=(gl == 7))
            for gl in range(8):
                g = 8 * gb + gl
                K.op("pe", lambda e, g=g, gl=gl: e.matmul(out=ps[1][:, NSEQ * gl:NSEQ * (gl + 1)], lhsT=Tc[:, g, :], rhs=U[:, g, KP:KC], start=True, stop=False, skip_group_check=True),
                     r=TKb(gb) + UK, w=[("ps", 1)], signal=False)
                K.op("pe", lambda e, g=g, gl=gl: e.matmul(out=ps[1][:, NSEQ * gl:NSEQ * (gl + 1)], lhsT=Fc[:, g, 0:128], rhs=h0bf[:, g, :], start=False, stop=True, skip_group_check=True),
                     r=FKb(gb) + ["h0bf"], w=[("ps", 1)], signal=(gl == 7))
            gs_ = slice(8 * gb, 8 * gb + 8)
            K.op("pe", lambda e: e.matmul(out=ps[0][:, 0:8 * NSEQ], lhsT=ident[:], rhs=H4[:, gs_, :].rearrange("p g s -> p (g s)"), start=False, stop=True, skip_group_check=True),
                 r=["ident", "H4"], w=[("ps", 0)], signal=True)
            K.op("act", lambda e: e.activation(out=Hns[:, gs_, :], in_=ps[0][:, 0:8 * NSEQ].rearrange("p (g s) -> p g s", g=8), func=AF.Identity), r=[("ps", 0)], w=[("Hns", gb)])
            K.op("act", lambda e: e.activation(out=s1rq2[gb % 2][:, :, KP:KC], in_=ps[1][:, 0:8 * NSEQ].rearrange("p (g s) -> p g s", g=8), func=AF.Gelu_apprx_tanh),
                 r=[("ps", 1)], w=[("s1rq", gb % 2, "s")])

        def shuffle_out(gb):
            gs_ = slice(8 * gb, 8 * gb + 8)
            rk = [("s1rq", gb % 2, b_) for b_ in range(4)] + [("s1rq", gb % 2, "s")]
            for rx in range(8):
                K.dma("sp", d_s[128 * gb:128 * (gb + 1), KC * rx:KC * (rx + 1)].rearrange("(g q) k -> q g k", q=16), s1rq2[gb % 2][16 * rx:16 * (rx + 1), :, :], r=rk, w=[("d_s", gb, rx)])
            K.dma("sp", s1f[gb][:, :, :].rearrange("p r k -> p (r k)"), d_s[128 * gb:128 * (gb + 1), :], r=[("d_s", gb, rx) for rx in range(8)], w=[("s1f", gb)])

        def cast_bank(gb, bank, eng, lo=0, hi=KP):
            hb = Hbf2[gb % 2]
            key = ("Hbf", gb % 2, bank)
            src = ps[bank][:, :].rearrange("p (g k) -> p g k", g=2)[:, :, lo:hi]
            dst = hb[:, 2 * bank:2 * bank + 2, lo:hi]
            if eng == "act":
                K.op("act", lambda e: e.activation(out=dst, in_=src, func=AF.Identity), r=[("ps", bank)], w=[key])
            else:
                V(lambda e: e.tensor_copy(out=dst, in_=src), [("ps", bank)], [key])

        CAST_ENG = {0: "act", 1: "dve", 2: "act", 3: "dve"}

        NWARM = 0
        CAST3 = {0: "act", 1: "act", 2: "act", 3: "dve"}

        def scan_batch(gb, extra=()):
            extra = list(extra)
            mj = Mj[gb % 2]
            for half in range(2):
                for gl in range(4 * half, 4 * half + 4):
                    g = 8 * gb + gl
                    bank = gl // 2
                    K.op("pe", lambda e, g=g, gl=gl, bank=bank: e.matmul(out=ps[bank][:, 256 * (gl % 2):256 * (gl % 2 + 1)], lhsT=BcT[:, g, :], rhs=U[:, g, 0:KP], start=(gl % 2 == 0), stop=True, skip_group_check=True),
                         r=BKb(gb) + UK, w=[("ps", bank)], signal=(gl % 2 == 1))
            for bank in range(4):
                cast_bank(gb, bank, CAST_ENG[bank])
            for j in range(8):
                d = 1 << j
                for half in range(2):
                    for gl in range(4 * half, 4 * half + 4):
                        g = 8 * gb + gl
                        bank = gl // 2
                        c0 = 256 * (gl % 2)
                        K.op("pe", lambda e, g=g, gl=gl, bank=bank, c0=c0, d=d, j=j: e.matmul(out=ps[bank][:, c0 + d:c0 + 256], lhsT=mj[:, gl, j, :], rhs=Hbf2[gb % 2][:, gl, 0:256 - d], start=False, stop=True, skip_group_check=True),
                             r=[("Mj", gb % 2), ("Hbf", gb % 2, bank)], w=[("ps", bank)], signal=(gl % 2 == 1))
                if 0 <= j <= 5:
                    for _w in range(NWARM):
                        K.op("pe", lambda e: e.matmul(out=ps[7][:, 0:KP], lhsT=BcT[:, 8 * gb, :], rhs=U[:, 8 * gb, 0:KP], start=True, stop=True, skip_group_check=True),
                             r=BKb(gb) + UK, w=[("ps", 7)], signal=(_w == NWARM - 1))
                heavy = bool(extra) and 0 <= j <= 5
                for bank in range(4):
                    if heavy:
                        eng_ = CAST3[bank] if j % 2 == 0 else CAST3[3 - bank]
                    else:
                        eng_ = CAST_ENG[bank] if j % 2 == 0 else CAST_ENG[3 - bank]
                    lo_, hi_ = (d, KP - 2 * d) if j < 7 else (0, KP)
                    cast_bank(gb, bank, eng_, lo_, hi_)
                if extra:
                    extra.pop(0)()
            for th in extra:
                th()
            for bank in range(4):
                g0 = 8 * gb + 2 * bank
                K.op("act", lambda e, g0=g0, bank=bank: e.activation(out=Hfin[:, g0:g0 + 2], in_=ps[bank][:, :].rearrange("p (g k) -> p g k", g=2)[:, :, 255], func=AF.Identity), r=[("ps", bank)], w=[("Hfin", g0)])

        def y_batch(gb):
            for gl in range(8):
                g = 8 * gb + gl
                bank = 4 + gl // 2
                c0 = 256 * (gl % 2)
                K.op("pe", lambda e, g=g, bank=bank, c0=c0: e.matmul(out=ps[bank][:, c0:c0 + 256], lhsT=Tc[:, g, :], rhs=U[:, g, 0:KP], start=True, stop=False, skip_group_check=True),
                     r=TKb(gb) + UK, w=[("ps", bank)], signal=False)
                K.op("pe", lambda e, g=g, bank=bank, c0=c0: e.matmul(out=ps[bank][:, c0 + 1:c0 + 256], lhsT=Fc[:, g, 64:192], rhs=Hbf2[gb % 2][:, gl, 0:255], start=False, stop=True, skip_group_check=True),
                     r=FKb(gb) + [("Hbf", gb % 2, gl // 2)], w=[("ps", bank)], signal=(gl % 2 == 1))
            for bank in range(4, 8):
                gl0 = 2 * (bank - 4)
                K.op("act", lambda e, gl0=gl0, bank=bank: e.activation(out=s1rq2[gb % 2][:, gl0:gl0 + 2, 0:KP], in_=ps[bank][:, :].rearrange("p (g k) -> p g k", g=2), func=AF.Gelu_apprx_tanh),
                     r=[("ps", bank)], w=[("s1rq", gb % 2, bank - 4)])

        import os
        ILV = os.environ.get("ILV", "1") == "1"
        def thunks_with_mj(gb_next):
            ths = gen_thunks(gb_next)
            nop = lambda: None
            return [nop] + ths[0:6] + [lambda: gen_mj(gb_next, extra_r=[("Fc", gb_next)])] + ths[6:]

        gen_batch(0)
        scan_batch(0, thunks_with_mj(1))
        y_batch(0)
        sample_batch(0)
        shuffle_out(0)
        scan_batch(1, thunks_with_mj(2))
        y_batch(1)
        sample_batch(1)
        shuffle_out(1)
        scan_batch(2, thunks_with_mj(3))
        y_batch(2)
        sample_batch(2)
        shuffle_out(2)
        scan_batch(3)
        y_batch(3)
        sample_batch(3)
        shuffle_out(3)
        K.dma("pool", hnew_s[:, :, :], Hns[:], r=[("Hns", gb) for gb in range(4)])
        K.dma("pool", hfin_p[:, :], Hfin[:], r=[("Hfin", g0) for g0 in range(0, 32, 2)])
        if dbg:
            K.dma("pool", dbg_out["U"][:, :], U[:, :, :].rearrange("p g k -> p (g k)"), r=UK)
            K.dma("pool", dbg_out["Tc"][:, :], Tc[:, :, :].rearrange("p g m -> p (g m)"), r=[k_ for gb in range(4) for k_ in TKb(gb)])
            K.dma("pool", dbg_out["BcT"][:, :], BcT[:, :, :].rearrange("p g m -> p (g m)"), r=[k_ for gb in range(4) for k_ in BKb(gb)])
            K.dma("pool", dbg_out["Fc"][:, :], Fc[:, :, :].rearrange("p g m -> p (g m)"), r=[("Fc", -1)] + [("Fc", gb) for gb in range(4)])
        K.barrier()

    if dbg:
        dbg_out["s2p"] = dout("dbg_s2p", [512, 8 * KC], BF16)
    if upto >= 4:
      with ExitStack() as s4:
        f4 = lambda name, shape, dt=F32: sb(name, shape, dt, s4)
        s2p = [f4(f"s2p{c}", [128, 8, KC], BF16) for c in range(4)]
        wg_st = f4("wg_st", [128, 4, 512]); wg_bf = f4("wg_bf", [128, 4, 512], BF16)
        wo_st = [f4(f"wo_st{i}", [128, DM]) for i in range(2)]; wo_bf = f4("wo_bf", [128, 8, DM], BF16)
        gpb = f4("gpb", [128, DM]); bpb = f4("bpb", [128, DM])
        thg = [f4(f"thg{i}", [128, 512]) for i in range(2)]; tg = [f4(f"tg{i}", [128, 512]) for i in range(2)]
        NXB = 3
        xt = [f4(f"xt{i}", [128, DM]) for i in range(NXB)]
        yn = [f4(f"yn{i}", [128, DM]) for i in range(2)]
        yo = [f4(f"yo{i}", [128, DM]) for i in range(2)]
        st6 = [f4(f"st6_{i}", [128, 2, 6]) for i in range(2)]
        mv = [f4(f"mv{i}", [128, 2]) for i in range(2)]
        ve = [f4(f"ve{i}", [128, 1]) for i in range(2)]
        rsd = [f4(f"rsd{i}", [128, 1]) for i in range(2)]
        nbv = [f4(f"nbv{i}", [128, 1]) for i in range(2)]
        smix = f4("smix", [128, 8, NS], BF16)
        aI = f4("aI", [128, 128])
        K.op("dve", lambda e: e.tensor_scalar(out=aI[:], in0=ident[:], scalar1=ALPHA, scalar2=None, op0=ALU.mult), r=["ident"], w=["aI"])
        epsc = f4("epsc", [128, 1])
        K.op("dve", lambda e: e.memset(epsc[:], EPS), w=["epsc"])
        for ci in range(4):
            K.dma("sp", wg_st[:, ci, :], w_glu[128 * ci:128 * (ci + 1), :], w=[("wg_st", ci)])
            K.op("act", lambda e, ci=ci: e.activation(out=wg_bf[:, ci, :], in_=wg_st[:, ci, :], func=AF.Identity), r=[("wg_st", ci)], w=[("wg_bf", ci)])
        wo_state = {"dma": 0, "cast": 0}

        def wo_dma(n_):
            while wo_state["dma"] < min(n_, 8):
                kk = wo_state["dma"]
                K.dma("sp", wo_st[kk % 2][:, :], w_out[128 * kk:128 * (kk + 1), :], w=[("wo_st", kk % 2)])
                wo_state["dma"] += 1

        def wo_cast_next():
            kk = wo_state["cast"]
            if kk >= 8:
                return
            wo_dma(kk + 1)
            if kk % 2 == 0:
                K.op("act", lambda e: e.activation(out=wo_bf[:, kk, :], in_=wo_st[kk % 2][:, :], func=AF.Identity), r=[("wo_st", kk % 2)], w=[("wo_bf", kk)])
            else:
                K.op("dve", lambda e: e.tensor_copy(out=wo_bf[:, kk, :], in_=wo_st[kk % 2][:, :]), r=[("wo_st", kk % 2)], w=[("wo_bf", kk)])
            wo_state["cast"] += 1
            wo_dma(kk + 3)

        wo_dma(2)
        K.dma("sp", gpb[:], gpost_d[:, :], w=["gpb"])
        K.dma("sp", bpb[:], bpost_d[:, :], w=["bpb"])

        xp_v = xp.rearrange("(k r) d -> r k d", r=8)
        yp_v = y_p.rearrange("(k r) d -> r k d", r=8)
        xs_v = xs.rearrange("(s t) d -> t s d", t=4)
        ys_v = y_s.rearrange("(s t) d -> t s d", t=4)
        blocks4 = [("p", r_) for r_ in range(8)] + [("s", 4)]
        gcount = 0
        units = [("p", r_) for r_ in range(0, 8, 2)] + [("s", 4)]
        for (kind, r_) in units:
            n = 512 if kind == "p" else NS
            view = (lambda t_, r_=r_: t_[:, r_:r_ + 2, 0:KP]) if kind == "p" else (lambda t_: t_[:, 4:8, KP:KC])
            pview = (lambda ap: ap.rearrange("p (a k) -> p a k", a=2)) if kind == "p" else (lambda ap: ap.rearrange("p (t s) -> p t s", t=4))
            wkeys = (lambda mo: [("s2p", mo, ("p", r_)), ("s2p", mo, ("p", r_ + 1))]) if kind == "p" else (lambda mo: [("s2p", mo, ("s", 4))])
            for mo in range(4):
                bank = gcount % 4
                ti = gcount % 2
                gcount += 1
                for ci in range(4):
                    K.op("pe", lambda e, mo=mo, ci=ci, bank=bank: e.matmul(out=pview(ps[bank][:, 0:n]), lhsT=wg_bf[:, ci, 128 * mo:128 * (mo + 1)], rhs=view(s1f[ci]), start=(ci == 0), stop=(ci == 3)),
                         r=[("wg_bf", ci), ("s1f", ci)], w=[("ps", bank)], signal=(ci == 3))
                K.op("act", lambda e, mo=mo, bank=bank, ti=ti: e.activation(out=thg[ti][:, 0:n], in_=ps[bank][:, 0:n], func=AF.Tanh, bias=hvec[:, 8 + mo:9 + mo], scale=0.5),
                     r=[("ps", bank), "hvec"], w=[("thg", ti)])
                K.op("dve", lambda e, mo=mo, ti=ti: e.scalar_tensor_tensor(out=pview(tg[ti][:, 0:n]), in0=pview(thg[ti][:, 0:n]), scalar=1.0, in1=view(s1f[mo]), op0=ALU.add, op1=ALU.mult),
                     r=[("thg", ti), ("s1f", mo)], w=[("tg", ti)])
                K.op("dve", lambda e, mo=mo, ti=ti: e.scalar_tensor_tensor(out=view(s2p[mo]), in0=pview(tg[ti][:, 0:n]), scalar=0.5, in1=view(gs_perm[mo]), op0=ALU.mult, op1=ALU.mult),
                     r=[("tg", ti)] + [("gs", mo, b_) for b_ in range(-1, 5)], w=wkeys(mo))
            wo_cast_next()
            wo_cast_next()
        while wo_state["cast"] < 8:
            wo_cast_next()
        for kk in range(8):
            src = s2p[kk] if kk < 4 else cfin_perm[kk - 4]
            rk = [("s2p", kk, ("s", 4)), ("s2p", kk, "z")] if kk < 4 else [("cfin", kk - 4, col0_) for col0_ in range(0, 8 * KC, 512)] + [("cfin", kk - 4, -1)]
            K.op("dve", lambda e, kk=kk, src=src: e.tensor_copy(out=smix[:, kk, :].rearrange("p (t s) -> p t s", t=4), in_=src[:, 4:8, KP:KC]), r=rk, w=[("smix", kk)])
        tcount = 0
        ocount = 0
        ln_back = []
        for (kind, r_) in blocks4:
            bkey = (kind, r_)
            tiles = [0, 128] if kind == "p" else [None]
            for k0 in tiles:
                rows = 128 if kind == "p" else NS
                xi = tcount % NXB
                oi = tcount % 2
                tcount += 1
                if kind == "p":
                    K.dma("sp", xt[xi][:, :], xp_v[r_, k0:k0 + 128, :], w=[("xt", xi)])
                else:
                    for t_ in range(4):
                        K.dma("sp", xt[xi][16 * t_:16 * (t_ + 1), :], xs_v[t_, :, :], w=[("xt", xi)] if t_ == 0 else [("xt", xi, t_)])
                xkeys = [("xt", xi)] + ([("xt", xi, t_) for t_ in range(1, 4)] if kind == "s" else [])
                banks = []
                for half in range(2):
                    bank = 2 + (ocount % 6)
                    ocount += 1
                    banks.append(bank)
                    for kk in range(8):
                        src = s2p[kk] if kk < 4 else cfin_perm[kk - 4]
                        if kind == "p":
                            lt = src[:, r_, k0:k0 + 128]
                            rk = [("s2p", kk, bkey)] if kk < 4 else [("cfin", kk - 4, col0_) for col0_ in range(0, 8 * KC, 512)] + [("cfin", kk - 4, -1)]
                        else:
                            lt = smix[:, kk, :]
                            rk = [("smix", kk)]
                        K.op("pe", lambda e, lt=lt, kk=kk, half=half, bank=bank: e.matmul(out=ps[bank][0:rows, 0:512], lhsT=lt, rhs=wo_bf[:, kk, 512 * half:512 * (half + 1)], start=(kk == 0), stop=False),
                             r=rk + [("wo_bf", kk)], w=[("ps", bank)], signal=False)
                    K.op("pe", lambda e, half=half, bank=bank: e.matmul(out=ps[bank][0:rows, 0:512], lhsT=aI[0:rows, 0:rows], rhs=xt[xi][0:rows, 512 * half:512 * (half + 1)], start=False, stop=True),
                         r=xkeys + ["aI"], w=[("ps", bank)], signal=True)
                for half in range(2):
                    K.op("dve", lambda e, half=half, bank=banks[half]: e.bn_stats(out=st6[oi][0:rows, half, :], in_=ps[bank][0:rows, 0:512]), r=[("ps", banks[half])], w=[("st6", oi, half)])
                K.op("dve", lambda e: e.bn_aggr(out=mv[oi][0:rows, :], in_=st6[oi][0:rows, :, :].rearrange("p a b -> p (a b)")), r=[("st6", oi, 0), ("st6", oi, 1)], w=[("mv", oi)])
                K.op("act", lambda e: e.activation(out=ve[oi][0:rows, :], in_=mv[oi][0:rows, 1:2], func=AF.Sqrt, bias=epsc[0:rows, :]), r=[("mv", oi), "epsc"], w=[("ve", oi)])
                for th in ln_back:
                    th()
                ln_back.clear()
                K.op("dve", lambda e: e.reciprocal(out=rsd[oi][0:rows, :], in_=ve[oi][0:rows, :]), r=[("ve", oi)], w=[("rsd", oi)])
                K.op("dve", lambda e: e.scalar_tensor_tensor(out=nbv[oi][0:rows, :], in0=mv[oi][0:rows, 0:1], scalar=-1.0, in1=rsd[oi][0:rows, :], op0=ALU.mult, op1=ALU.mult),
                     r=[("mv", oi), ("rsd", oi)], w=[("nbv", oi)])
                for half in range(2):
                    K.op("act", lambda e, half=half, bank=banks[half]: e.activation(out=yn[oi][0:rows, 512 * half:512 * (half + 1)], in_=ps[bank][0:rows, 0:512], func=AF.Identity, bias=nbv[oi][0:rows, :], scale=rsd[oi][0:rows, :]),
                         r=[("ps", banks[half]), ("nbv", oi), ("rsd", oi)], w=[("yn", oi, half)])
                def back(oi=oi, rows=rows, kind=kind, r_=r_, k0=k0):
                    K.op("dve", lambda e: e.tensor_tensor(out=yn[oi][0:rows, :], in0=yn[oi][0:rows, :], in1=gpb[0:rows, :], op=ALU.mult), r=[("yn", oi, 0), ("yn", oi, 1), "gpb"], w=[("yn", oi, 2)])
                    K.op("dve", lambda e: e.tensor_tensor(out=yo[oi][0:rows, :], in0=yn[oi][0:rows, :], in1=bpb[0:rows, :], op=ALU.add), r=[("yn", oi, 2), "bpb"], w=[("yo", oi)])
                    if kind == "p":
                        K.dma("pool", yp_v[r_, k0:k0 + 128, :], yo[oi][:, :], r=[("yo", oi)])
                    else:
                        for t_ in range(4):
                            K.dma("pool", ys_v[t_, :, :], yo[oi][16 * t_:16 * (t_ + 1), :], r=[("yo", oi)])
                ln_back.append(back)
        for th in ln_back:
            th()
        ln_back.clear()
        if dbg:
            for c in range(4):
                K.dma("pool", dbg_out["s2p"][128 * c:128 * (c + 1), :], s2p[c][:, :, :].rearrange("p r k -> p (r k)"), r=[("s2p", c, bk) for bk in [("p", r_) for r_ in range(8)] + [("s", 4), "z"]])
        K.barrier()

    K.barrier(["sp"])
    es.close()
    return nc


def _prep_inputs(inp):
    f = lambda k: np.ascontiguousarray(np.asarray(inp[k], np.float32))
    x_prompt = f("x_prompt")
    x_sample = f("x_sample")
    sre = f("state_ssm_re")[0]
    sim = f("state_ssm_im")[0]
    scv = f("state_conv")[0]
    b_in = f("b_in")[0]
    vecs = np.concatenate([
        b_in.reshape(20, 128), f("b_glu")[0].reshape(4, 128), f("b_dw")[0].reshape(4, 128),
        f("g_conv_ln")[0].reshape(4, 128), f("b_conv_ln")[0].reshape(4, 128), f("b_pw2")[0].reshape(4, 128)], 0).T
    wdwT = f("w_dw")[0].reshape(31, 4, 128).transpose(2, 1, 0)
    lam_re = f("lam_re")[0]
    lam_im = f("lam_im")[0]
    lamT = np.stack([lam_re.T, lam_im.T], 1)
    lamT = np.concatenate([lamT, lamT], 0)
    br = f("b_re")[0].transpose(1, 0, 2)
    bi = f("b_im")[0].transpose(1, 0, 2)
    cr = f("c_re")[0].transpose(2, 0, 1)
    ci = f("c_im")[0].transpose(2, 0, 1)
    wdw = f("w_dw")[0]
    wdg = np.zeros((8, 16, 32, 5, 8), np.float32)
    for s_ in range(8):
        for d_ in range(5):
            for r_ in range(8):
                tau = 8 * d_ + r_ - s_
                if 0 <= tau <= 30:
                    wdg[s_, :, :, d_, r_] = wdw[30 - tau].reshape(32, 16).T
    sccol = np.stack([np.tile(f(k)[0].reshape(32, 16).T, (8, 1)) for k in ("b_dw", "g_conv_ln", "b_conv_ln")], 1)
    shared = {
        "w_in": f("w_in")[0], "w_glu": f("w_glu")[0], "w_pw2": f("w_pw2")[0], "w_out": f("w_out")[0],
        "vecs": np.ascontiguousarray(vecs), "wdwT": np.ascontiguousarray(wdwT),
        "gpost": np.ascontiguousarray(np.broadcast_to(f("g_post")[0].reshape(1, DM), (128, DM))), "bpost": np.ascontiguousarray(np.broadcast_to(f("b_post")[0].reshape(1, DM), (128, DM))),
        "ident": np.eye(128, dtype=np.float32),
        "lamT": np.ascontiguousarray(lamT), "logdt": np.ascontiguousarray(np.broadcast_to(f("log_dt")[0].reshape(1, 32), (128, 32))),
        "BA": np.ascontiguousarray(np.concatenate([br, bi], 0)), "BBraw": np.ascontiguousarray(np.concatenate([bi, br], 0)),
        "CAraw": np.ascontiguousarray(np.concatenate([cr, ci], 0)), "CBraw": np.ascontiguousarray(np.concatenate([ci, cr], 0)),
        "dskT": np.ascontiguousarray(f("d_skip")[0].reshape(32, 16).T),
        "wdg": np.ascontiguousarray(wdg.reshape(128, 32 * 5 * 8)), "sccol": np.ascontiguousarray(sccol.astype(np.float32)),
    }
    in_maps = []
    for i in range(NCORES):
        sl = slice(NSEQ * i, NSEQ * (i + 1))
        hre = sre[sl].transpose(2, 1, 0)
        him = sim[sl].transpose(2, 1, 0)
        m = dict(shared)
        m["xp"] = x_prompt[i]
        m["xs"] = np.ascontiguousarray(x_sample[sl].reshape(NS, DM))
        m["h0A"] = np.ascontiguousarray(np.concatenate([hre, him], 0))
        m["h0Braw"] = np.ascontiguousarray(np.concatenate([him, hre], 0))
        m["stT"] = np.ascontiguousarray(scv[sl].transpose(2, 0, 1))
        pad = np.zeros((NSEQ, 40, 512), np.float32)
        pad[:, 6:36, :] = scv[sl]
        m["stP"] = np.ascontiguousarray(pad.reshape(NSEQ, 5, 8, 512).transpose(3, 2, 0, 1).reshape(512, 8, 5 * NSEQ))
        in_maps.append(m)
    return in_maps


def _assemble(results):
    y_p = np.stack([r["y_p"] for r in results], 0)
    y_s = np.concatenate([r["y_s"].reshape(NSEQ, 4, DM) for r in results], 0)
    hp = np.stack([r["hfin_p"] for r in results], 0)
    re_p = hp[:, 0:64, :].transpose(0, 2, 1)[None]
    im_p = hp[:, 64:128, :].transpose(0, 2, 1)[None]
    cv_p = np.stack([r["ncvT_p"].T for r in results], 0)[None]
    hs = np.concatenate([r["hnew_s"].transpose(2, 1, 0) for r in results], 0)
    re_s = hs[:, :, 0:64][None]
    im_s = hs[:, :, 64:128][None]
    cv_s = np.concatenate([r["ncvT_s"].reshape(512, NSEQ, 30).transpose(1, 2, 0) for r in results], 0)[None]
    c = lambda a: np.ascontiguousarray(a.astype(np.float32))
    return (c(y_p), c(y_s), c(re_p), c(im_p), c(cv_p), c(re_s), c(im_s), c(cv_s))


_NC_CACHE = {}


def kernel(**inputs):
    in_maps = _prep_inputs(inputs)
    if "nc" not in _NC_CACHE:
        _NC_CACHE["nc"] = build_program()
    res = run_bass_kernel_spmd(_NC_CACHE["nc"], in_maps, core_ids=list(range(NCORES)))
    return _assemble(res.results)
```

```python
import numpy as np
import ml_dtypes
from contextlib import ExitStack
import concourse.bass as bass
import concourse.mybir as mybir
from concourse.bass_utils import run_bass_kernel_spmd

F32 = mybir.dt.float32
BF16 = mybir.dt.bfloat16
I32 = mybir.dt.int32
AF = mybir.ActivationFunctionType
ALU = mybir.AluOpType

NCORES = 8
DM = 1024
NP = 2048
NSEQ = 16
NS = 64
NT = NP + NS
KP = 256
KC = KP + NSEQ
KV = KP + 5 * NSEQ
ALPHA = 2.0 ** 0.25
EPS = 1e-5
TWO_PI = 6.283185307179586
BLOCKS = [(0, 512), (512, 512), (1024, 512), (1536, 512), (2048, 64)]


class KB:
    def __init__(self, nc, es):
        self.nc = nc
        self.E = {"pe": nc.tensor, "act": nc.scalar, "dve": nc.vector, "pool": nc.gpsimd, "sp": nc.sync}
        self.sem = {}
        self.cnt = {}
        for k in self.E:
            self.sem[k] = es.enter_context(nc.semaphore("sem_" + k))
            self.cnt[k] = 0
        self.NDS = 20
        self.dq = ("sp", "pool", "act")
        self.dsem = {q: [es.enter_context(nc.semaphore(f"d_{q}{i}")) for i in range(self.NDS)] for q in self.dq}
        self.dval = {q: [0] * self.NDS for q in self.dq}
        self.drr = {q: 0 for q in self.dq}
        self.seen = {k: {} for k in self.E}
        self.lw = {}
        self.rd = {}
        self.pend = {k: [] for k in self.E}

    def _semobj(self, sk):
        return self.sem[sk] if isinstance(sk, str) else self.dsem[sk[0]][sk[1]]

    def _wait(self, eng, sk, val):
        if val <= 0:
            return
        if eng == "pe" and sk == "pe":
            return
        if self.seen[eng].get(sk, 0) >= val:
            return
        self.E[eng].wait_ge(self._semobj(sk), val)
        self.seen[eng][sk] = val

    def _deps(self, eng, r, w):
        for k in r:
            t = self.lw.get(k)
            if t:
                self._wait(eng, *t)
        for k in w:
            t = self.lw.get(k)
            if t:
                self._wait(eng, *t)
            for sk, v in self.rd.get(k, {}).items():
                self._wait(eng, sk, v)

    def _commit(self, t, r, w):
        for k in w:
            self.lw[k] = t
            self.rd[k] = {}
        for k in r:
            d = self.rd.setdefault(k, {})
            d[t[0]] = max(d.get(t[0], 0), t[1])

    def op(self, eng, fn, r=(), w=(), signal=True):
        r = list(r)
        w = list(w)
        self._deps(eng, r, w)
        ins = fn(self.E[eng])
        if not signal:
            self.pend[eng].append((r, w))
            return None
        self.cnt[eng] += 1
        ins.then_inc(self.sem[eng], 1)
        t = (eng, self.cnt[eng])
        for (pr, pw) in self.pend[eng]:
            self._commit(t, pr, pw)
        self.pend[eng] = []
        self._commit(t, r, w)
        return t

    def dma(self, q, out, in_, r=(), w=()):
        r = list(r)
        w = list(w)
        self._deps(q, r, w)
        i = self.drr[q] % self.NDS
        self.drr[q] += 1
        self._wait(q, (q, i), self.dval[q][i])
        ins = self.E[q].dma_start(out=out, in_=in_)
        self.dval[q][i] += 16
        ins.then_inc(self.dsem[q][i], 16)
        t = ((q, i), self.dval[q][i])
        self._commit(t, r, w)
        return t

    def barrier(self, engines=None, dma_queues=None):
        engines = engines or list(self.E)
        dma_queues = self.dq if dma_queues is None else dma_queues
        for e in engines:
            for k in self.E:
                if k == "sp" and "sp" not in dma_queues:
                    continue
                self._wait(e, k, self.cnt[k])
            for q in dma_queues:
                for i in range(self.NDS):
                    self._wait(e, (q, i), self.dval[q][i])


def build_program(upto=9, dbg=False):
    nc = bass.Bass("TRN2", target_bir_lowering=False)
    es = ExitStack()
    K = KB(nc, es)

    def din(name, shape, dt=F32):
        return nc.dram_tensor(name, list(shape), dt, kind="ExternalInput").ap()

    def dout(name, shape, dt=F32):
        return nc.dram_tensor(name, list(shape), dt, kind="ExternalOutput").ap()

    def dscr(name, shape, dt):
        return nc.dram_tensor(name, list(shape), dt).ap()

    xp = din("xp", [NP, DM])
    xs = din("xs", [NS, DM])
    w_in = din("w_in", [DM, 2560])
    w_glu = din("w_glu", [512, 512])
    w_pw2 = din("w_pw2", [512, 512])
    w_out = din("w_out", [DM, DM])
    vecs_d = din("vecs", [128, 40])
    wdwT_d = din("wdwT", [128, 4, 31])
    gpost_d = din("gpost", [128, DM])
    bpost_d = din("bpost", [128, DM])
    ident_d = din("ident", [128, 128])
    lamT_d = din("lamT", [128, 2, 32])
    logdt_d = din("logdt", [128, 32])
    BA_d = din("BA", [128, 32, 16])
    BB_d = din("BBraw", [128, 32, 16])
    CA_d = din("CAraw", [128, 32, 16])
    CB_d = din("CBraw", [128, 32, 16])
    dskT_d = din("dskT", [16, 32])
    h0A_d = din("h0A", [128, 32, NSEQ])
    h0B_d = din("h0Braw", [128, 32, NSEQ])
    stT_d = din("stT", [512, NSEQ, 30])
    stP_d = din("stP", [512, 8, 5 * NSEQ])
    wdg_d = din("wdg", [128, 32 * 5 * 8])
    sccol_d = din("sccol", [128, 3, 32])

    y_p = dout("y_p", [NP, DM])
    y_s = dout("y_s", [NS, DM])
    hfin_p = dout("hfin_p", [128, 32])
    hnew_s = dout("hnew_s", [128, 32, NSEQ])
    ncvT_p = dout("ncvT_p", [512, 30])
    ncvT_s = dout("ncvT_s", [512, NSEQ * 30])

    d_u = dscr("d_u", [128, 32 * KC], BF16)
    d_s = dscr("d_s", [512, 8 * KC], BF16)
    d_v = dscr("d_v", [128, 32 * KV], BF16)
    d_c = dscr("d_c", [512, 8 * KC], BF16)

    dbg_out = {}
    if dbg:
        dbg_out["u_perm"] = dout("dbg_u_perm", [512, 8 * KC], BF16)
        dbg_out["gs"] = dout("dbg_gs", [512, 8 * KC], BF16)
        dbg_out["gc"] = dout("dbg_gc", [512, 8 * KC], BF16)
        dbg_out["vperm"] = dout("dbg_vperm", [512, 8 * KV], BF16)
        dbg_out["U"] = dout("dbg_U", [128, 32 * KC], BF16)

    def sb(name, shape, dt, stack=es, side=None):
        return stack.enter_context(nc.sbuf_tensor("sb_" + name, list(shape), dt, side=side))

    ps = [es.enter_context(nc.psum_tensor(f"ps{i}", [128, 512], F32)) for i in range(8)]

    ident = sb("ident", [128, 128], F32, side="right")
    vecs = sb("vecs", [128, 40], F32, side="right")
    hvec = sb("hvec", [128, 12], F32, side="right")
    gs_perm = [sb(f"gs_perm{c}", [128, 8, KC], BF16, side="right") for c in range(4)]
    sA = ExitStack()
    gc_perm = [sb(f"gc_perm{c}", [128, 8, KC], BF16, sA) for c in range(4)]

    wdg = sb("wdg", [128, 32 * 5 * 8], F32, side="right")
    sccol = sb("sccol", [128, 3, 32], F32, side="right")
    K.dma("sp", ident[:], ident_d[:, :], w=["ident"])
    K.dma("sp", vecs[:], vecs_d[:, :], w=["vecs"])
    K.dma("sp", wdg[:], wdg_d[:, :], w=["wdg"])
    K.dma("sp", sccol[:], sccol_d[:, :, :], w=["sccol"])
    K.op("dve", lambda e: e.tensor_scalar(out=hvec[:, 0:8], in0=vecs[:, 8:16], scalar1=0.5, scalar2=None, op0=ALU.mult),
         r=["vecs"], w=["hvec"])
    K.op("dve", lambda e: e.tensor_scalar(out=hvec[:, 8:12], in0=vecs[:, 20:24], scalar1=0.5, scalar2=None, op0=ALU.mult),
         r=["vecs"], w=["hvec"])
    for c in range(4):
        K.op("dve", lambda e, c=c: e.memset(gc_perm[c][:, 0:4, KP:KC], 0.0), w=[("gc", c, -1)])
        K.op("dve", lambda e, c=c: e.memset(gs_perm[c][:, 0:4, KP:KC], 0.0), w=[("gs", c, -1)])

    fr = lambda name, shape, dt=F32: sb(name, shape, dt, side="right")
    lam = fr("lam", [128, 2, 32]); dtb = fr("dtb", [128, 32])
    T = [fr(f"T{i}", [128, 4, 32]) for i in range(4)]
    PR = fr("PR", [128, 9, 32]); PI = fr("PI", [128, 9, 32])
    QR = fr("QR", [128, 8, 32]); QI = fr("QI", [128, 8, 32])
    mag = fr("mag", [128, 32]); phi = fr("phi", [128, 32]); kf = fr("kf", [128, 32]); ki = fr("ki", [128, 32], I32)
    rr = fr("rr", [128, 32]); rc = fr("rc", [128, 32]); msk = fr("msk", [128, 32]); sinv = fr("sinv", [128, 32]); cosv = fr("cosv", [128, 32])
    KR = fr("KR", [128, 32]); KI = fr("KI", [128, 32]); den = fr("den", [128, 32]); am1 = fr("am1", [128, 32])
    S4 = fr("S4", [128, 32, 8, 2]); S_bf = fr("S_bf", [128, 32, 8, 2], BF16); I2 = fr("I2", [128, 64], BF16)
    PRr = fr("PRr", [128, 8, 32]); PIr = fr("PIr", [128, 8, 32])

    def V(fn, r, w, eng="dve"):
        return K.op(eng, fn, r=r, w=w)

    def tt(o, a, b, op, r, w, eng="dve"):
        return K.op(eng, lambda e: e.tensor_tensor(out=o, in0=a, in1=b, op=op), r=r, w=w)

    def ts(o, a, s1_, op0, r, w, s2_=None, op1=None, eng="dve"):
        if op1 is None:
            return K.op(eng, lambda e: e.tensor_scalar(out=o, in0=a, scalar1=s1_, scalar2=None, op0=op0), r=r, w=w)
        return K.op(eng, lambda e: e.tensor_scalar(out=o, in0=a, scalar1=s1_, scalar2=s2_, op0=op0, op1=op1), r=r, w=w)

    G = "pool"

    def cmul(oR, oI, xR, xI, yR, yI, nslots, r, w):
        a, b, c_, d_ = (T[i][:, 0:nslots, :] for i in range(4))
        tk = ["T0", "T1", "T2", "T3"]
        tt(a, xR, yR, ALU.mult, r, [tk[0]], G); tt(b, xI, yI, ALU.mult, r, [tk[1]], G)
        tt(c_, xR, yI, ALU.mult, r, [tk[2]], G); tt(d_, xI, yR, ALU.mult, r, [tk[3]], G)
        tt(oR, a, b, ALU.subtract, [tk[0], tk[1]], w, G); tt(oI, c_, d_, ALU.add, [tk[2], tk[3]], w, G)

    def g1_gen():
        K.dma("sp", lam[:], lamT_d[:, :, :], w=["lam"]); K.dma("sp", dtb[:], logdt_d[:, :], w=["dtb"])
        lr = lam[:, 0, :]; li = lam[:, 1, :]
        K.op("act", lambda e: e.activation(out=dtb[:], in_=dtb[:], func=AF.Exp), r=["dtb"], w=["dtb"])
        yield
        tt(mag[:], lr, dtb[:], ALU.mult, ["lam", "dtb"], ["mag"], G)
        yield
        K.op("act", lambda e: e.activation(out=mag[:], in_=mag[:], func=AF.Exp), r=["mag"], w=["mag"])
        yield
        tt(phi[:], li, dtb[:], ALU.mult, ["lam", "dtb"], ["phi"], G)
        ts(kf[:], phi[:], 1.0 / TWO_PI, ALU.mult, ["phi"], ["kf"], eng=G)
        yield
        V(lambda e: e.tensor_copy(out=ki[:], in_=kf[:]), ["kf"], ["ki"])
        V(lambda e: e.tensor_copy(out=kf[:], in_=ki[:]), ["ki"], ["kf"])
        yield
        ts(kf[:], kf[:], -TWO_PI, ALU.mult, ["kf"], ["kf"], eng=G)
        tt(rr[:], kf[:], phi[:], ALU.add, ["kf", "phi"], ["rr"], G)
        PIS = 3.141592
        ts(rr[:], rr[:], -PIS, ALU.max, ["rr"], ["rr"], PIS, ALU.min, eng=G)
        ts(rc[:], rr[:], TWO_PI / 4.0, ALU.add, ["rr"], ["rc"], eng=G)
        ts(msk[:], rc[:], PIS, ALU.is_gt, ["rc"], ["msk"], eng=G)
        ts(msk[:], msk[:], -TWO_PI, ALU.mult, ["msk"], ["msk"], eng=G)
        tt(rc[:], rc[:], msk[:], ALU.add, ["msk", "rc"], ["rc"], G)
        ts(rc[:], rc[:], -PIS, ALU.max, ["rc"], ["rc"], PIS, ALU.min, eng=G)
        yield
        K.op("act", lambda e: e.activation(out=sinv[:], in_=rr[:], func=AF.Sin), r=["rr"], w=["sinv"])
        K.op("act", lambda e: e.activation(out=cosv[:], in_=rc[:], func=AF.Sin), r=["rc"], w=["cosv"])
        yield
        V(lambda e: e.memset(PR[:, 0, :], 1.0), [], ["P0"], G); V(lambda e: e.memset(PI[:, 0, :], 0.0), [], ["P0"], G)
        tt(PR[:, 1, :], mag[:], cosv[:], ALU.mult, ["mag", "cosv"], ["P1"], G); tt(PI[:, 1, :], mag[:], sinv[:], ALU.mult, ["mag", "sinv"], ["P1"], G)
        cmul(PR[:, 2:3, :], PI[:, 2:3, :], PR[:, 1:2, :], PI[:, 1:2, :], PR[:, 1:2, :], PI[:, 1:2, :], 1, ["P1"], ["P2"])
        cmul(PR[:, 3:5, :], PI[:, 3:5, :], PR[:, 1:3, :], PI[:, 1:3, :], PR[:, 2:3, :].to_broadcast([128, 2, 32]), PI[:, 2:3, :].to_broadcast([128, 2, 32]), 2, ["P1", "P2"], ["P34"])
        cmul(PR[:, 5:9, :], PI[:, 5:9, :], PR[:, 1:5, :], PI[:, 1:5, :], PR[:, 4:5, :].to_broadcast([128, 4, 32]), PI[:, 4:5, :].to_broadcast([128, 4, 32]), 4, ["P1", "P2", "P34"], ["P58"])
        PK = ["P0", "P1", "P2", "P34", "P58"]
        V(lambda e: e.tensor_copy(out=QR[:, 0, :], in_=PR[:, 8, :]), PK, [("Q", 0)], G); V(lambda e: e.tensor_copy(out=QI[:, 0, :], in_=PI[:, 8, :]), PK, [("Q", 0)], G)
        for j in range(7):
            cmul(QR[:, j + 1:j + 2, :], QI[:, j + 1:j + 2, :], QR[:, j:j + 1, :], QI[:, j:j + 1, :], QR[:, j:j + 1, :], QI[:, j:j + 1, :], 1, [("Q", j)], [("Q", j + 1)])
        QK = [("Q", j) for j in range(8)]
        tt(den[:], lr, lr, ALU.mult, ["lam"], ["den"], G); tt(kf[:], li, li, ALU.mult, ["lam"], ["kf"], G)
        tt(den[:], den[:], kf[:], ALU.add, ["den", "kf"], ["den"], G)
        yield
        V(lambda e: e.reciprocal(out=den[:], in_=den[:]), ["den"], ["den"])
        yield
        ts(am1[:], PR[:, 1, :], -1.0, ALU.add, ["P1"], ["am1"], eng=G)
        tt(KR[:], am1[:], lr, ALU.mult, ["am1", "lam"], ["KR"], G); tt(kf[:], PI[:, 1, :], li, ALU.mult, ["P1", "lam"], ["kf"], G)
        tt(KR[:], KR[:], kf[:], ALU.add, ["KR", "kf"], ["KR"], G); tt(KR[:], KR[:], den[:], ALU.mult, ["KR", "den"], ["KR"], G)
        tt(KI[:], PI[:, 1, :], lr, ALU.mult, ["P1", "lam"], ["KI"], G); tt(kf[:], am1[:], li, ALU.mult, ["am1", "lam"], ["kf"], G)
        tt(KI[:], KI[:], kf[:], ALU.subtract, ["KI", "kf"], ["KI"], G); tt(KI[:], KI[:], den[:], ALU.mult, ["KI", "den"], ["KI"], G)
        V(lambda e: e.tensor_copy(out=S4[0:64, :, :, 0], in_=QR[0:64, :, :].rearrange("p j g -> p g j")), QK, ["S4a"], G)
        ts(S4[64:128, :, :, 0], QI[64:128, :, :].rearrange("p j g -> p g j"), -1.0, ALU.mult, QK, ["S4b"], eng=G)
        V(lambda e: e.tensor_copy(out=S4[0:64, :, :, 1], in_=QI[0:64, :, :].rearrange("p j g -> p g j")), QK, ["S4c"], G)
        V(lambda e: e.tensor_copy(out=S4[64:128, :, :, 1], in_=QR[64:128, :, :].rearrange("p j g -> p g j")), QK, ["S4d"], G)
        V(lambda e: e.tensor_copy(out=S_bf[:], in_=S4[:]), ["S4a", "S4b", "S4c", "S4d"], ["S_bf"], G)
        tt(I2[:], ident[:, 0:64], ident[:, 64:128], ALU.add, ["ident"], ["I2"], G)
        for sx_ in range(8):
            V(lambda e, sx_=sx_: e.tensor_copy(out=PRr[:, sx_, :], in_=PR[:, 7 - sx_, :]), PK, [("Pr", sx_)], G)
            V(lambda e, sx_=sx_: e.tensor_copy(out=PIr[:, sx_, :], in_=PI[:, 7 - sx_, :]), PK, [("Pr", sx_)], G)


        yield

    g1 = g1_gen()
    next(g1)

    PK = ["P0", "P1", "P2", "P34", "P58"]
    QK = [("Q", j) for j in range(8)]

    with ExitStack() as s1:
        w_bf = sb("w_bf", [128, 8, 2560], BF16, s1)
        wst = [sb(f"wst{i}", [128, 8, 256], F32, s1) for i in range(2)]
        x_sb = [sb(f"x_sb{i}", [128, DM], F32, s1) for i in range(4)]
        xT = [sb(f"xT{i}", [128, 8, 512], BF16, s1) for i in range(2)]
        th = [sb(f"th{i}", [128, 512], F32, s1) for i in range(2)]
        ah = [sb(f"ah{i}", [128, 512], F32, s1) for i in range(2)]
        st32 = sb("st32", [128, 4, NSEQ, 30], F32, s1)
        stp32 = sb("stp32", [128, 8, 5 * NSEQ], F32, s1)
        v32p = sb("v32p", [128, 4, 30], F32, s1)
        ncs = sb("ncs", [128, 4, NSEQ, 30], F32, s1)
        u_perm = [sb(f"u_perm{c}", [128, 8, KC], BF16, s1) for c in range(4)]
        v_perm = [sb(f"v_perm{c}", [128, 8, KV], BF16, s1) for c in range(4)]

        def load_x(bi):
            t0, n = BLOCKS[bi]
            if n == 512:
                for i in range(4):
                    K.dma("sp", x_sb[i][:, :], xp[t0 + 128 * i:t0 + 128 * (i + 1), :], w=[("x", i)])
            else:
                K.dma("sp", x_sb[0][0:NS, :], xs[:, :], w=[("x", 0)])

        for c in range(4):
            K.op("dve", lambda e, c=c: e.memset(u_perm[c][:, 0:4, KP:KC], 0.0), w=[("uperm", c, -1)])
        BORD = [0, 4, 1, 2, 3]
        load_x(BORD[0])
        M_ORDER = [12, 8, 13, 9, 14, 10, 15, 11, 0, 1, 2, 3, 4, 5, 6, 7, 16, 17, 18, 19]
        CH_ORDER = []
        for m_ in M_ORDER:
            if m_ // 2 not in CH_ORDER:
                CH_ORDER.append(m_ // 2)
        w_state = {"dma": 0, "cast": set()}

        def w_issue_dma(upto_n):
            while w_state["dma"] < min(upto_n, len(CH_ORDER)):
                i = w_state["dma"]
                cc = CH_ORDER[i]
                for kk in range(8):
                    K.dma("sp", wst[i % 2][:, kk, :], w_in[128 * kk:128 * (kk + 1), 256 * cc:256 * (cc + 1)], w=[("wst", i % 2, kk)])
                w_state["dma"] += 1

        def w_ensure(cc):
            if cc in w_state["cast"]:
                return
            i = CH_ORDER.index(cc)
            w_issue_dma(i + 1)
            K.op("dve", lambda e: e.tensor_copy(out=w_bf[:, 0:4, 256 * cc:256 * (cc + 1)], in_=wst[i % 2][:, 0:4, :]),
                 r=[("wst", i % 2, kk) for kk in range(0, 4)], w=[("wbf", cc, 0)])
            K.op("act", lambda e: e.activation(out=w_bf[:, 4:8, 256 * cc:256 * (cc + 1)], in_=wst[i % 2][:, 4:8, :], func=AF.Identity),
                 r=[("wst", i % 2, kk) for kk in range(4, 8)], w=[("wbf", cc, 1)])
            w_state["cast"].add(cc)
            w_issue_dma(i + 3)

        w_issue_dma(2)
        def st_dma(c):
            K.dma("sp", st32[:, c, :, :], stT_d[128 * c:128 * (c + 1), :, :], w=[("st32", c)])
            K.dma("sp", stp32[:, :, :], stP_d[128 * c:128 * (c + 1), :, :], w=["stp32"])

        def st_copy(c):
            K.op("dve", lambda e, c=c: e.tensor_copy(out=ncs[:, c, :, 0:26], in_=st32[:, c, :, 4:30]), r=[("st32", c)], w=[("ncs", c, 0)])
            K.op("dve", lambda e, c=c: e.tensor_copy(out=v_perm[c][:, :, KP:KV], in_=stp32[:, :, :]), r=["stp32"], w=[("vperm", c, -1)])
        ST_AT = {1: ("d", 0), 3: ("c", 0), 4: ("d", 1), 6: ("c", 1), 7: ("d", 2), 9: ("c", 2), 11: ("d", 3), 13: ("c", 3)}
        zi = 0
        ev = 0
        ct_count = [0]
        G1_AT = {2: 1, 3: 1, 4: 1, 10: 1, 11: 1, 22: 1, 23: 1, 60: 1, 61: 1}

        def g1_tick():
            ct_count[0] += 1
            if ct_count[0] in G1_AT:
                try:
                    next(g1)
                except StopIteration:
                    pass
        du_v = d_u.rearrange("(r c) (g k) -> c r g k", c=16, g=32)
        dv_v = d_v.rearrange("(r c) (g k) -> c r g k", c=16, g=32)

        act_pending = []

        def scratch_piece(pc):
            for c in range(4):
                for gl in range(8):
                    g = 8 * c + gl
                    ps_ = slice(16 * gl, 16 * (gl + 1))
                    if pc < 2:
                        ks = slice(128 * pc, 128 * (pc + 1))
                        rk_u = [("uperm", c, 2 * pc), ("uperm", c, 2 * pc + 1)]
                        rk_v = [("vperm", c, 2 * pc), ("vperm", c, 2 * pc + 1)]
                        K.dma("sp", du_v[:, :, g, ks], u_perm[c][ps_, :, ks], r=rk_u, w=[("d_u", c, pc, gl)])
                        if pc == 1 and gl % 4 != 3:
                            act_pending.append(lambda g=g, ks=ks, c=c, ps_=ps_, rk_v=rk_v, pc=pc, gl=gl: K.dma("act", dv_v[:, :, g, ks], v_perm[c][ps_, :, ks], r=rk_v, w=[("d_v", c, pc, gl)]))
                        else:
                            K.dma("sp", dv_v[:, :, g, ks], v_perm[c][ps_, :, ks], r=rk_v, w=[("d_v", c, pc, gl)])
                    else:
                        K.dma("sp", du_v[:, :, g, KP:KC], u_perm[c][ps_, :, KP:KC], r=[("uperm", c, -1), ("uperm", c, 4)], w=[("d_u", c, pc, gl)])
                        K.dma("sp", dv_v[:, :, g, KP:KV], v_perm[c][ps_, :, KP:KV], r=[("vperm", c, -1), ("vperm", c, 4)], w=[("d_v", c, pc, gl)])

        def do_transposes(pos):
            bj = BORD[pos]
            t0_, n_ = BLOCKS[bj]
            bb_ = pos % 2
            nt_ = 4 if n_ == 512 else 1
            rows_ = 128 if n_ == 512 else NS
            for kk in range(8):
                bank = 4 + (kk % 2)
                for i in range(nt_):
                    K.op("pe", lambda e, i=i, kk=kk, bank=bank: e.transpose(out=ps[bank][:, rows_ * i:rows_ * (i + 1)], in_=x_sb[i][0:rows_, 128 * kk:128 * (kk + 1)], identity=ident[0:rows_, 0:rows_]),
                         r=[("x", i), "ident"], w=[("ps", bank)], signal=(i == nt_ - 1))
                if kk % 2 == 0:
                    K.op("dve", lambda e, kk=kk, bank=bank: e.tensor_copy(out=xT[bb_][:, kk, 0:n_], in_=ps[bank][:, 0:n_]), r=[("ps", bank)], w=[("xT", bb_, kk)])
                else:
                    K.op("act", lambda e, kk=kk, bank=bank: e.activation(out=xT[bb_][:, kk, 0:n_], in_=ps[bank][:, 0:n_], func=AF.Identity), r=[("ps", bank)], w=[("xT", bb_, kk)])
            if pos + 1 < len(BORD):
                load_x(BORD[pos + 1])

        do_transposes(0)
        for pos_, bi in enumerate(BORD):
            t0, n = BLOCKS[bi]
            bb = pos_ % 2
            if pos_ == 2:
                scratch_piece(2)
            nt = 4 if n == 512 else 1
            rows = 128 if n == 512 else NS
            if bi == 2:
                scratch_piece(0)
            for mi_, m in enumerate(M_ORDER):
                for _ in range(min(3, len(act_pending))):
                    act_pending.pop(0)()
                if mi_ == 10 and pos_ + 1 < len(BORD):
                    do_transposes(pos_ + 1)
                if bi == 3 and mi_ == 12:
                    scratch_piece(1)
                if pos_ == 0 and mi_ in ST_AT:
                    kind_, c_ = ST_AT[mi_]
                    (st_dma if kind_ == "d" else st_copy)(c_)
                bank = zi % 4
                zi += 1
                cc = m // 2
                w_ensure(cc)
                for kk in range(8):
                    K.op("pe", lambda e, m=m, kk=kk, bank=bank: e.matmul(out=ps[bank][:, 0:n], lhsT=w_bf[:, kk, 128 * m:128 * (m + 1)], rhs=xT[bb][:, kk, 0:n], start=(kk == 0), stop=(kk == 7)),
                         r=[("wbf", cc, kk // 4), ("xT", bb, kk)], w=[("ps", bank)], signal=(kk == 7))
                g1_tick()
                pz = ps[bank][:, 0:n]
                bcol = vecs[:, m:m + 1]
                c = m % 4
                if m < 8:
                    dst_t = u_perm[c] if m < 4 else gs_perm[c]
                    key = ("uperm" if m < 4 else "gs", c, bi)
                    func = AF.Identity if m < 4 else AF.Silu
                    if n == 512:
                        k0 = t0 // 8
                        o_ap = dst_t[:, :, k0:k0 + 64]
                        i_ap = pz.rearrange("p (k r) -> p r k", r=8)
                    else:
                        o_ap = dst_t[:, 4:8, KP:KC]
                        i_ap = pz.rearrange("p (s t) -> p t s", t=4)
                    K.op("act", lambda e, o_ap=o_ap, i_ap=i_ap, func=func, bcol=bcol: e.activation(out=o_ap, in_=i_ap, func=func, bias=bcol),
                         r=[("ps", bank), "vecs"], w=[key])
                elif m >= 16:
                    if n == 512:
                        k0 = t0 // 8
                        o_ap = gc_perm[c][:, :, k0:k0 + 64]
                        i_ap = pz.rearrange("p (k r) -> p r k", r=8)
                    else:
                        o_ap = gc_perm[c][:, 4:8, KP:KC]
                        i_ap = pz.rearrange("p (s t) -> p t s", t=4)
                    K.op("act", lambda e, o_ap=o_ap, i_ap=i_ap, bcol=bcol: e.activation(out=o_ap, in_=i_ap, func=AF.Silu, bias=bcol),
                         r=[("ps", bank), "vecs"], w=[("gc", c, bi)])
                elif m >= 12:
                    K.op("act", lambda e, c=c, pz=pz: e.activation(out=th[c % 2][:, 0:n], in_=pz, func=AF.Tanh, bias=hvec[:, 4 + c:5 + c], scale=0.5),
                         r=[("ps", bank), "hvec"], w=[("th", c % 2)])
                else:
                    K.op("act", lambda e, c=c, pz=pz: e.activation(out=ah[c % 2][:, 0:n], in_=pz, func=AF.Identity, bias=hvec[:, c:c + 1], scale=0.5),
                         r=[("ps", bank), "hvec"], w=[("ah", c % 2)])
                    if n == 512:
                        k0 = t0 // 8
                        o_ap = v_perm[c][:, :, k0:k0 + 64]
                        i0 = th[c % 2][:, 0:n].rearrange("p (k r) -> p r k", r=8)
                        i1 = ah[c % 2][:, 0:n].rearrange("p (k r) -> p r k", r=8)
                        key = ("vperm", c, bi)
                    else:
                        o_ap = v_perm[c][:, 4:8, KP:KV].rearrange("p t (s j) -> p t s j", j=5)[:, :, :, 4]
                        i0 = th[c % 2][:, 0:n].rearrange("p (s t) -> p t s", t=4)
                        i1 = ah[c % 2][:, 0:n].rearrange("p (s t) -> p t s", t=4)
                        key = ("vperm", c, bi)
                    K.op("dve", lambda e, o_ap=o_ap, i0=i0, i1=i1: e.scalar_tensor_tensor(out=o_ap, in0=i0, scalar=1.0, in1=i1, op0=ALU.add, op1=ALU.mult),
                         r=[("th", c % 2), ("ah", c % 2)], w=[key])
                    if bi == 3:
                        K.op("dve", lambda e, c=c: e.scalar_tensor_tensor(out=v32p[:, c, :], in0=th[c % 2][:, 482:512], scalar=1.0, in1=ah[c % 2][:, 482:512], op0=ALU.add, op1=ALU.mult),
                             r=[("th", c % 2), ("ah", c % 2)], w=[("v32p", c)])
                    if bi == 4:
                        j0 = th[c % 2][:, 0:n].rearrange("p (s t) -> p s t", t=4)
                        j1 = ah[c % 2][:, 0:n].rearrange("p (s t) -> p s t", t=4)
                        K.op("dve", lambda e, c=c, i0=j0, i1=j1: e.scalar_tensor_tensor(out=ncs[:, c, :, 26:30], in0=i0, scalar=1.0, in1=i1, op0=ALU.add, op1=ALU.mult),
                             r=[("th", c % 2), ("ah", c % 2)], w=[("ncs", c, 1)])
        while act_pending:
            act_pending.pop(0)()
        for _ in g1:
            pass
        for c in range(4):
            K.dma("pool", ncvT_p[128 * c:128 * (c + 1), :], v32p[:, c, :], r=[("v32p", c)])
            K.dma("pool", ncvT_s[128 * c:128 * (c + 1), :], ncs[:, c, :, :].rearrange("p s j -> p (s j)"), r=[("ncs", c, 0), ("ncs", c, 1)])
        if dbg:
            for c in range(4):
                K.dma("pool", dbg_out["u_perm"][128 * c:128 * (c + 1), :], u_perm[c][:, :, :].rearrange("p r k -> p (r k)"), r=[("uperm", c, b) for b in range(-1, 5)])
                K.dma("pool", dbg_out["gs"][128 * c:128 * (c + 1), :], gs_perm[c][:, :, :].rearrange("p r k -> p (r k)"), r=[("gs", c, b) for b in range(-1, 5)])
                K.dma("pool", dbg_out["gc"][128 * c:128 * (c + 1), :], gc_perm[c][:, :, :].rearrange("p r k -> p (r k)"), r=[("gc", c, b) for b in range(-1, 5)])
                K.dma("pool", dbg_out["vperm"][128 * c:128 * (c + 1), :], v_perm[c][:, :, :].rearrange("p r k -> p (r k)"), r=[("vperm", c, b) for b in range(-1, 5)])
        K.barrier(dma_queues=("pool",))

    DU_KEYS = [("d_u", c, pc, gl) for c in range(4) for pc in range(3) for gl in range(8)]
    DV_KEYS = [("d_v", c, pc, gl) for c in range(4) for pc in range(3) for gl in range(8)]
    BA = fr("BA", [128, 32, 16]); BB = fr("BB", [128, 32, 16]); CA = fr("CA", [128, 32, 16]); CB = fr("CB", [128, 32, 16])
    BbA = fr("BbA", [128, 32, 16]); BbB = fr("BbB", [128, 32, 16]); BbA_bf = fr("BbA_bf", [128, 32, 16], BF16)
    G1t = fr("G1t", [128, 32, 16]); G2t = fr("G2t", [128, 32, 16])
    dskT = fr("dskT", [16, 32]); Dd = fr("Dd", [16, 32, 16])
    h0bf = fr("h0bf", [128, 32, NSEQ], BF16); H4 = fr("H4", [128, 32, NSEQ])
    Mj = [fr("Mj0", [128, 8, 8, 128], BF16), None]

    def gen_mj(gb, extra_r=(), eng="pool"):
        K.op(eng, lambda e: e.tensor_tensor(out=Mj[gb % 2][:, :, :, :].rearrange("p g j (h m) -> p (g j h) m", h=2),
                                               in0=I2[:].unsqueeze(1).to_broadcast([128, 128, 64]),
                                               in1=S_bf[:, 8 * gb:8 * gb + 8, :, :].rearrange("p g j h -> p (g j h)").unsqueeze(2).to_broadcast([128, 128, 64]),
                                               op=ALU.mult),
             r=["I2", "S_bf"] + list(extra_r), w=[("Mj", gb % 2)])

    cfin_perm = [sb(f"cfin{c}", [128, 8, KC], BF16, side="right") for c in range(4)]
    if dbg:
        dbg_out["cfin"] = dout("dbg_cfin", [512, 8 * KC], BF16)
    if dbg:
        dbg_out["c1sc"] = dout("dbg_c1sc", [128, 32 * KC], BF16)
    if upto >= 2:
      with ExitStack() as s2:
        f2 = lambda name, shape, dt=F32: sb(name, shape, dt, s2)
        Wc = f2("Wc", [128, 32, 5, 128], BF16)
        Vv = f2("Vv", [128, 32, KV], BF16)
        maskc = f2("maskc", [128, 16])
        RB = f2("RB", [128, 8])
        bones = f2("bones", [128, 128], BF16)
        wp_st = f2("wp_st", [128, 4, 512])
        wp_bf = f2("wp_bf", [128, 4, 512], BF16)
        Ybf = [f2(f"Ybf{i}", [128, KC], BF16) for i in range(2)]
        Ysq = [f2(f"Ysq{i}", [128, KC], BF16) for i in range(2)]
        mean = f2("mean", [128, KC]); var = f2("var", [128, KC]); rstd = f2("rstd", [128, KC])
        t1 = [f2(f"t1_{i}", [128, KC]) for i in range(2)]
        c1sc = f2("c1sc", [128, 32, KC], BF16)
        Vflat = Vv[:, :, :].rearrange("p g k -> p (g k)")
        c1f = [Vflat[:, 8 * KC * c:8 * KC * (c + 1)] for c in range(4)]
        K.dma("sp", Vv[:, :, :].rearrange("p g k -> p (g k)"), d_v[:, :], r=DV_KEYS + DU_KEYS, w=[("Vv", 0)] + [("Vvg", g_) for g_ in range(32)])
        VK = [("Vv", 0)]
        for ci in range(4):
            K.dma("sp", wp_st[:, ci, :], w_pw2[128 * ci:128 * (ci + 1), :], w=[("wp_st", ci)])
        K.op("dve", lambda e: e.tensor_reduce(out=maskc[:], in_=ident[:, :].rearrange("p (s c) -> p c s", s=8), axis=mybir.AxisListType.X, op=ALU.add), r=["ident"], w=["maskc"])
        K.op("dve", lambda e: e.tensor_reduce(out=RB[:], in_=ident[:, :].rearrange("p (s c) -> p s c", s=8), axis=mybir.AxisListType.X, op=ALU.add), r=["ident"], w=["RB"])
        K.op("dve", lambda e: e.tensor_copy(out=bones[:].rearrange("p (s c) -> p s c", s=8), in_=RB[:].unsqueeze(2).to_broadcast([128, 8, 16])), r=["RB"], w=["bones"])
        for gq in range(4):
            K.op("dve", lambda e, gq=gq: e.tensor_tensor(out=Wc[:, 8 * gq:8 * gq + 8, :, :].rearrange("p g d (r c) -> p (g d r) c", c=16),
                                                        in0=wdg[:, 320 * gq:320 * (gq + 1)].unsqueeze(2).to_broadcast([128, 320, 16]),
                                                        in1=maskc[:].unsqueeze(1).to_broadcast([128, 320, 16]), op=ALU.mult),
                 r=["wdg", "maskc"], w=[("Wc", gq)])
        for ci in range(4):
            K.op("act", lambda e, ci=ci: e.activation(out=wp_bf[:, ci, :], in_=wp_st[:, ci, :], func=AF.Identity), r=[("wp_st", ci)], w=[("wp_bf", ci)])
        if upto >= 3:
            X1 = G1t
            X2 = G2t
            K.dma("sp", BA[:], BA_d[:, :, :], w=["BA"]); K.dma("sp", BB[:], BB_d[:, :, :], w=["BB"])
            K.dma("sp", CA[:], CA_d[:, :, :], w=["CA"]); K.dma("sp", CB[:], CB_d[:, :, :], w=["CB"])
            K.dma("sp", dskT[:], dskT_d[:, :], w=["dskT"])
            K.dma("sp", X1[:], h0A_d[:, :, :], w=["G1t"]); K.dma("sp", X2[:], h0B_d[:, :, :], w=["G2t"])
            ts(X2[0:64, :, :], X2[0:64, :, :], -1.0, ALU.mult, ["G2t", ("Wc", 3)], ["G2t"], eng=G)
            b16s = lambda ap: ap.unsqueeze(2).to_broadcast([128, 32, NSEQ])
            V(lambda e: e.tensor_copy(out=h0bf[:], in_=X1[:]), ["G1t"], ["h0bf"], G)
            tt(H4[:], b16s(PR[:, 4, :]), X1[:], ALU.mult, PK + ["G1t"], ["H4"], G)
            tt(X1[:], b16s(PI[:, 4, :]), X2[:], ALU.mult, PK + ["G2t", "H4"], ["G1t"], G)
            tt(H4[:], H4[:], X1[:], ALU.add, ["H4", "G1t"], ["H4"], G)
            ts(BB[0:64, :, :], BB[0:64, :, :], -1.0, ALU.mult, ["BB"], ["BB"], eng=G)
            ts(CA[64:128, :, :], CA[64:128, :, :], -1.0, ALU.mult, ["CA"], ["CA"], eng=G)
            ts(CB[:], CB[:], -1.0, ALU.mult, ["CB"], ["CB"], eng=G)
            bc16 = lambda ap: ap.unsqueeze(2).to_broadcast([128, 32, 16])
            tt(G1t[:], bc16(KR[:]), BA[:], ALU.mult, ["KR", "BA", "H4"], ["G1t"], G); tt(G2t[:], bc16(KI[:]), BB[:], ALU.mult, ["KI", "BB", "H4", "G1t"], ["G2t"], G)
            tt(BbA[:], G1t[:], G2t[:], ALU.add, ["G1t", "G2t"], ["BbA"], G)
            tt(G1t[:], bc16(KR[:]), BB[:], ALU.mult, ["KR", "BB"], ["G1t"], G); tt(G2t[:], bc16(KI[:]), BA[:], ALU.mult, ["KI", "BA"], ["G2t"], G)
            tt(BbB[:], G1t[:], G2t[:], ALU.subtract, ["G1t", "G2t"], ["BbB"], G)
            V(lambda e: e.tensor_copy(out=BbA_bf[:], in_=BbA[:]), ["BbA"], ["BbA_bf"], G)
            tt(Dd[:], ident[0:16, 0:16].unsqueeze(1).to_broadcast([16, 32, 16]), dskT[:].unsqueeze(2).to_broadcast([16, 32, 16]), ALU.mult, ["ident", "dskT"], ["Dd"], G)
        bdw = lambda g: sccol[:, 0, g:g + 1]
        gln = lambda g: sccol[:, 1, g:g + 1]
        bln = lambda g: sccol[:, 2, g:g + 1]

        def conv_g(g, bank):
            for d in range(5):
                K.op("pe", lambda e, g=g, d=d, bank=bank: e.matmul(out=ps[bank][:, d:KP], lhsT=Wc[:, g, d, :], rhs=Vv[:, g, 0:KP - d], start=(d == 0), stop=(d == 4), skip_group_check=True),
                     r=[("Vvg", g), ("Wc", g // 8)], w=[("ps", bank)], signal=False)
            for d in range(5):
                K.op("pe", lambda e, g=g, d=d, bank=bank: e.matmul(out=ps[bank][:, KP:KC], lhsT=Wc[:, g, d, :], rhs=Vv[:, g, KP:KV].rearrange("p (s j) -> p s j", j=5)[:, :, 4 - d], start=False, stop=(d == 4), skip_group_check=True),
                     r=[("Vvg", g), ("Wc", g // 8)], w=[("ps", bank)], signal=(d == 4))

        def stats_g(g, bank):
            i = g % 2
            K.op("act", lambda e: e.activation(out=Ybf[i][:, :], in_=ps[bank][:, 0:KC], func=AF.Identity, bias=bdw(g)), r=[("ps", bank), "sccol"], w=[("Ybf", i)])
            K.op("dve", lambda e: e.scalar_tensor_tensor(out=Ysq[i][:, :], in0=ps[bank][:, 0:KC], scalar=bdw(g), in1=Ybf[i][:, :], op0=ALU.add, op1=ALU.mult), r=[("ps", bank), "sccol", ("Ybf", i)], w=[("Ysq", i)])
            K.op("pe", lambda e: e.matmul(out=ps[6][:, 0:KC], lhsT=bones[:], rhs=Ybf[i][:, :], start=(g == 0), stop=(g == 31), skip_group_check=True), r=["bones", ("Ybf", i)], w=[("ps", 6)], signal=(g == 31))
            K.op("pe", lambda e: e.matmul(out=ps[7][:, 0:KC], lhsT=bones[:], rhs=Ysq[i][:, :], start=(g == 0), stop=(g == 31), skip_group_check=True), r=["bones", ("Ysq", i)], w=[("ps", 7)], signal=(g == 31))

        conv_g(0, 0)
        for g in range(32):
            if g + 1 < 32:
                conv_g(g + 1, (g + 1) % 4)
            stats_g(g, g % 4)
        K.op("act", lambda e: e.activation(out=mean[:], in_=ps[6][:, 0:KC], func=AF.Identity, scale=1.0 / 512.0), r=[("ps", 6)], w=["mean"])
        K.op("dve", lambda e: e.tensor_tensor(out=var[:], in0=mean[:], in1=mean[:], op=ALU.mult), r=["mean"], w=["var"])
        K.op("dve", lambda e: e.scalar_tensor_tensor(out=var[:], in0=ps[7][:, 0:KC], scalar=1.0 / 512.0, in1=var[:], op0=ALU.mult, op1=ALU.subtract), r=[("ps", 7), "var"], w=["var"])
        K.op("dve", lambda e: e.tensor_scalar(out=var[:], in0=var[:], scalar1=EPS, scalar2=None, op0=ALU.add), r=["var"], w=["var"])
        K.op("act", lambda e: e.activation(out=var[:], in_=var[:], func=AF.Sqrt), r=["var"], w=["var"])
        K.op("dve", lambda e: e.reciprocal(out=rstd[:], in_=var[:]), r=["var"], w=["rstd"])

        def norm_g(g, bank):
            i = g % 2
            K.op("dve", lambda e: e.scalar_tensor_tensor(out=t1[i][:, :], in0=ps[bank][:, 0:KC], scalar=bdw(g), in1=mean[:], op0=ALU.add, op1=ALU.subtract),
                 r=[("ps", bank), "sccol", "mean"], w=[("t1", i)])
            K.op("dve", lambda e: e.tensor_tensor(out=t1[i][:, :], in0=t1[i][:, :], in1=rstd[:], op=ALU.mult), r=[("t1", i), "rstd"], w=[("t1", i)])
            K.op("act", lambda e: e.activation(out=c1sc[:, g, :], in_=t1[i][:, :], func=AF.Silu, bias=bln(g), scale=gln(g)), r=[("t1", i), "sccol"], w=[("c1sc", g)])

        conv_g(0, 0)
        for g in range(32):
            if g + 1 < 32:
                conv_g(g + 1, (g + 1) % 4)
            norm_g(g, g % 4)
            if g % 8 == 7:
                c = g // 8
                for rx in range(8):
                    K.dma("sp", d_c[128 * c:128 * (c + 1), KC * rx:KC * (rx + 1)].rearrange("(g q) k -> q g k", q=16), c1sc[16 * rx:16 * (rx + 1), 8 * c:8 * c + 8, :],
                          r=[("c1sc", g_) for g_ in range(8 * c, 8 * c + 8)], w=[("d_c", c, rx)])
                glo = (8 * KC * c) // KV
                ghi = min(31, (8 * KC * (c + 1) - 1) // KV)
                K.dma("sp", c1f[c], d_c[128 * c:128 * (c + 1), :], r=[("d_c", c, rx) for rx in range(8)], w=[("c1f", c)] + [("Vvg", g_) for g_ in range(glo, ghi + 1)])
        CK = [("c1sc", g) for g in range(32)]
        if upto >= 3:
            gen_mj(0, eng="dve")
        if dbg:
            K.dma("pool", dbg_out["c1sc"][:, :], c1sc[:, :, :].rearrange("p g k -> p (g k)"), r=CK)
        NPC = 8 * KC
        pcount = 0
        for col0 in range(0, NPC, 512):
            n = min(512, NPC - col0)
            for mo in range(4):
                bank = pcount % 4
                pcount += 1
                for ci in range(4):
                    K.op("pe", lambda e, mo=mo, ci=ci, bank=bank: e.matmul(out=ps[bank][:, 0:n], lhsT=wp_bf[:, ci, 128 * mo:128 * (mo + 1)], rhs=c1f[ci][:, col0:col0 + n], start=(ci == 0), stop=(ci == 3)),
                         r=[("wp_bf", ci), ("c1f", ci)], w=[("ps", bank)], signal=(ci == 3))
                K.op("dve", lambda e, mo=mo, bank=bank: e.scalar_tensor_tensor(out=cfin_perm[mo][:, :, :].rearrange("p r k -> p (r k)")[:, col0:col0 + n], in0=ps[bank][:, 0:n], scalar=vecs[:, 36 + mo:37 + mo],
                                                                             in1=gc_perm[mo][:, :, :].rearrange("p r k -> p (r k)")[:, col0:col0 + n], op0=ALU.add, op1=ALU.mult),
                     r=[("ps", bank), "vecs"] + [("gc", mo, b) for b in range(-1, 5)], w=[("cfin", mo, col0)])
        if dbg:
            for c in range(4):
                K.dma("pool", dbg_out["cfin"][128 * c:128 * (c + 1), :], cfin_perm[c][:, :, :].rearrange("p r k -> p (r k)"), r=[("cfin", c, col0) for col0 in range(0, NPC, 512)] + [("cfin", c, -1)])
        K.barrier()
    sA.close()

    if dbg:
        dbg_out["P"] = dout("dbg_P", [128, 2 * 9 * 32], F32)
        dbg_out["Tc"] = dout("dbg_Tc", [128, 32 * 128], BF16)
        dbg_out["BcT"] = dout("dbg_BcT", [128, 32 * 128], BF16)
        dbg_out["Fc"] = dout("dbg_Fc", [128, 32 * 192], BF16)
        dbg_out["s1rq"] = dout("dbg_s1rq", [128, 32 * KC], BF16)
        dbg_out["Hbf"] = dout("dbg_Hbf", [128, 32 * KP], BF16)
    if upto >= 3:
      with ExitStack() as s3:
        f3 = lambda name, shape, dt=F32: sb(name, shape, dt, s3)
        Fc = f3("Fc", [128, 32, 192], BF16)
        Tc = f3("Tc", [128, 32, 128], BF16); BcT = f3("BcT", [128, 32, 128], BF16)
        Hfin = f3("Hfin", [128, 32]); Hns = f3("Hns", [128, 32, NSEQ])
        E32 = f3("E32", [128, 8, 128]); Gb = f3("Gb", [128, 8, 128])
        FT1 = f3("FT1", [128, 8, 144]); FT2 = f3("FT2", [128, 8, 144])
        K_bf = f3("K_bf", [16, 8, 128], BF16)
        Mj[1] = f3("Mj1", [128, 8, 8, 128], BF16)
        U = f3("U", [128, 32, KC], BF16)
        Hbf2 = [f3(f"Hbf{i}", [128, 8, KP], BF16) for i in range(2)]
        s1rq2 = [f3(f"s1rq{i}", [128, 8, KC], BF16) for i in range(2)]
        s1f = [sb(f"s1f{c}", [128, 8, KC], BF16, side="right") for c in range(4)]
        K.dma("sp", U[:, :, :].rearrange("p g k -> p (g k)"), d_u[:, :], r=DU_KEYS, w=[("U", 0)])
        UK = [("U", 0)]
        K.op("act", lambda e: e.memzero(Tc[:]), w=["Tc0"])
        K.op("act", lambda e: e.memzero(Fc[:, :, 0:48]), w=[("Fc", -1)])
        if dbg:
            K.dma("pool", dbg_out["P"][:, 0:288], PR[:, :, :].rearrange("p t g -> p (t g)"), r=PK)
            K.dma("pool", dbg_out["P"][:, 288:576], PI[:, :, :].rearrange("p t g -> p (t g)"), r=PK)

        def gen_thunks(gb):
            gs_ = slice(8 * gb, 8 * gb + 8)
            pw = lambda P_, nt: P_[:, 0:nt, gs_].rearrange("p t g -> p g t").unsqueeze(3).to_broadcast([128, 8, nt, 16])
            bt = lambda X_, nt: X_[:, gs_, :].unsqueeze(2).to_broadcast([128, 8, nt, 16])
            v8 = lambda X_: X_[:].rearrange("p g (t q) -> p g t q", t=8)
            v9 = lambda X_: X_[:].rearrange("p g (t q) -> p g t q", t=9)
            PRK = [("Pr", sx_) for sx_ in range(8)]

            def t_e1():
                tt(v8(E32), pw(PRr, 8), bt(BbA, 8), ALU.mult, PRK + ["BbA"], ["E32a"])

            def t_e2():
                tt(v8(Gb), pw(PIr, 8), bt(BbB, 8), ALU.mult, PRK + ["BbB"], ["Gb"])

            def t_e3():
                tt(E32[:], E32[:], Gb[:], ALU.add, ["E32a", "Gb"], ["E32"])

            def t_f1():
                tt(v9(FT1), pw(PR, 9), bt(CA, 9), ALU.mult, PK + ["CA"], ["FT1"])

            def t_f2():
                tt(v9(FT2), pw(PI, 9), bt(CB, 9), ALU.mult, PK + ["CB"], ["FT2"])

            def t_f3():
                tt(Fc[:, gs_, 48:192], FT1[:], FT2[:], ALU.add, ["FT1", "FT2"], [("Fc", gb)])

            def t_tr():
                for h2 in range(2):
                    bank = 4 + h2
                    for gi in range(4):
                        gl = 4 * h2 + gi
                        K.op("pe", lambda e, gl=gl, gi=gi, bank=bank: e.transpose(out=ps[bank][:, 128 * gi:128 * (gi + 1)], in_=E32[:, gl, :], identity=ident[:]),
                             r=["E32", "ident"], w=[("ps", bank)], signal=(gi == 3))
                    K.op("act", lambda e, h2=h2, bank=bank: e.activation(out=BcT[:, 8 * gb + 4 * h2:8 * gb + 4 * h2 + 4, :], in_=ps[bank][:, :].rearrange("p (g m) -> p g m", g=4), func=AF.Identity),
                         r=[("ps", bank)], w=[("BcT", gb, h2)])

            def t_k():
                for h2 in range(2):
                    bank = 6 + h2
                    for gi in range(4):
                        gl = 4 * h2 + gi
                        g = 8 * gb + gl
                        K.op("pe", lambda e, g=g, gi=gi, bank=bank: e.matmul(out=ps[bank][0:16, 128 * gi:128 * (gi + 1)], lhsT=BbA_bf[:, g, :], rhs=Fc[:, g, 48:176], start=True, stop=False, skip_group_check=True),
                             r=["BbA_bf", ("Fc", gb)], w=[("ps", bank)], signal=False)
                        K.op("pe", lambda e, g=g, gi=gi, bank=bank: e.matmul(out=ps[bank][0:16, 128 * gi:128 * gi + 16], lhsT=Dd[:, g, :], rhs=ident[0:16, 0:16], start=False, stop=True, skip_group_check=True),
                             r=["Dd", "ident"], w=[("ps", bank)], signal=(gi == 3))
                    K.op("act", lambda e, h2=h2, bank=bank: e.activation(out=K_bf[:, 4 * h2:4 * h2 + 4, :], in_=ps[bank][0:16, :].rearrange("p (g m) -> p g m", g=4), func=AF.Identity),
                         r=[("ps", bank)], w=[("K_bf", h2)])

            def t_tc():
                for sx in range(8):
                    K.dma("sp", Tc[16 * sx:16 * (sx + 1), gs_, 16 * sx:128], K_bf[0:16, :, 0:128 - 16 * sx], r=[("K_bf", 0), ("K_bf", 1), "Tc0"], w=[("Tc", gb, sx)])

            return [t_e1, t_e2, t_e3, t_f1, t_f2, t_f3, t_tr, t_k, t_tc]

        def gen_batch(gb):
            for th in gen_thunks(gb):
                th()

        TKb = lambda gb: [("Tc", gb, sx) for sx in range(8)]
        BKb = lambda gb: [("BcT", gb, 0), ("BcT", gb, 1)]
        FKb = lambda gb: [("Fc", -1), ("Fc", gb)]

        def sample_batch(gb):
            for gl in range(8):
                g = 8 * gb + gl
                K.op("pe", lambda e, g=g, gl=gl: e.matmul(out=ps[0][:, NSEQ * gl:NSEQ * (gl + 1)], lhsT=BcT[:, g, :], rhs=U[:, g, KP:KC], start=(gl == 0), stop=False, skip_group_check=True),
                     r=BKb(gb) + UK, w=[("ps", 0)], signal=(gl == 7))
            for gl in range(8):
                g = 8 * gb + gl
                K.op("pe", lambda e, g=g, gl=gl: e.matmul(out=ps[1][:, NSEQ * gl:NSEQ * (gl + 1)], lhsT=Tc[:, g, :], rhs=U[:, g, KP:KC], start=True, stop=False, skip_group_check=True),
                     r=TKb(gb) + UK, w=[("ps", 1)], signal=False)
                K.op("pe", lambda e, g=g, gl=gl: e.matmul(out=ps[1][:, NSEQ * gl:NSEQ * (gl + 1)], lhsT=Fc[:, g, 0:128], rhs=h0bf[:, g, :], start=False, stop=True, skip_group_check=True),
                     r=FKb(gb) + ["h0bf"], w=[("ps", 1)], signal=(gl == 7))
            gs_ = slice(8 * gb, 8 * gb + 8)
            K.op("pe", lambda e: e.matmul(out=ps[0][:, 0:8 * NSEQ], lhsT=ident[:], rhs=H4[:, gs_, :].rearrange("p g s -> p (g s)"), start=False, stop=True, skip_group_check=True),
                 r=["ident", "H4"], w=[("ps", 0)], signal=True)
            K.op("act", lambda e: e.activation(out=Hns[:, gs_, :], in_=ps[0][:, 0:8 * NSEQ].rearrange("p (g s) -> p g s", g=8), func=AF.Identity), r=[("ps", 0)], w=[("Hns", gb)])
            K.op("act", lambda e: e.activation(out=s1rq2[gb % 2][:, :, KP:KC], in_=ps[1][:, 0:8 * NSEQ].rearrange("p (g s) -> p g s", g=8), func=AF.Gelu_apprx_tanh),
                 r=[("ps", 1)], w=[("s1rq", gb % 2, "s")])

        def shuffle_out(gb):
            gs_ = slice(8 * gb, 8 * gb + 8)
            rk = [("s1rq", gb % 2, b_) for b_ in range(4)] + [("s1rq", gb % 2, "s")]
            for rx in range(8):
                K.dma("sp", d_s[128 * gb:128 * (gb + 1), KC * rx:KC * (rx + 1)].rearrange("(g q) k -> q g k", q=16), s1rq2[gb % 2][16 * rx:16 * (rx + 1), :, :], r=rk, w=[("d_s", gb, rx)])
            K.dma("sp", s1f[gb][:, :, :].rearrange("p r k -> p (r k)"), d_s[128 * gb:128 * (gb + 1), :], r=[("d_s", gb, rx) for rx in range(8)], w=[("s1f", gb)])

        def cast_bank(gb, bank, eng, lo=0, hi=KP):
            hb = Hbf2[gb % 2]
            key = ("Hbf", gb % 2, bank)
            src = ps[bank][:, :].rearrange("p (g k) -> p g k", g=2)[:, :, lo:hi]
            dst = hb[:, 2 * bank:2 * bank + 2, lo:hi]
            if eng == "act":
                K.op("act", lambda e: e.activation(out=dst, in_=src, func=AF.Identity), r=[("ps", bank)], w=[key])
            else:
                V(lambda e: e.tensor_copy(out=dst, in_=src), [("ps", bank)], [key])

        CAST_ENG = {0: "act", 1: "dve", 2: "act", 3: "dve"}

        NWARM = 0
        CAST3 = {0: "act", 1: "act", 2: "act", 3: "dve"}

        def scan_batch(gb, extra=()):
            extra = list(extra)
            mj = Mj[gb % 2]
            for half in range(2):
                for gl in range(4 * half, 4 * half + 4):
                    g = 8 * gb + gl
                    bank = gl // 2
                    K.op("pe", lambda e, g=g, gl=gl, bank=bank: e.matmul(out=ps[bank][:, 256 * (gl % 2):256 * (gl % 2 + 1)], lhsT=BcT[:, g, :], rhs=U[:, g, 0:KP], start=(gl % 2 == 0), stop=True, skip_group_check=True),
                         r=BKb(gb) + UK, w=[("ps", bank)], signal=(gl % 2 == 1))
            for bank in range(4):
                cast_bank(gb, bank, CAST_ENG[bank])
            for j in range(8):
                d = 1 << j
                for half in range(2):
                    for gl in range(4 * half, 4 * half + 4):
                        g = 8 * gb + gl
                        bank = gl // 2
                        c0 = 256 * (gl % 2)
                        K.op("pe", lambda e, g=g, gl=gl, bank=bank, c0=c0, d=d, j=j: e.matmul(out=ps[bank][:, c0 + d:c0 + 256], lhsT=mj[:, gl, j, :], rhs=Hbf2[gb % 2][:, gl, 0:256 - d], start=False, stop=True, skip_group_check=True),
                             r=[("Mj", gb % 2), ("Hbf", gb % 2, bank)], w=[("ps", bank)], signal=(gl % 2 == 1))
                if 0 <= j <= 5:
                    for _w in range(NWARM):
                        K.op("pe", lambda e: e.matmul(out=ps[7][:, 0:KP], lhsT=BcT[:, 8 * gb, :], rhs=U[:, 8 * gb, 0:KP], start=True, stop=True, skip_group_check=True),
                             r=BKb(gb) + UK, w=[("ps", 7)], signal=(_w == NWARM - 1))
                heavy = bool(extra) and 0 <= j <= 5
                for bank in range(4):
                    if heavy:
                        eng_ = CAST3[bank] if j % 2 == 0 else CAST3[3 - bank]
                    else:
                        eng_ = CAST_ENG[bank] if j % 2 == 0 else CAST_ENG[3 - bank]
                    lo_, hi_ = (d, KP - 2 * d) if j < 7 else (0, KP)
                    cast_bank(gb, bank, eng_, lo_, hi_)
                if extra:
                    extra.pop(0)()
            for th in extra:
                th()
            for bank in range(4):
                g0 = 8 * gb + 2 * bank
                K.op("act", lambda e, g0=g0, bank=bank: e.activation(out=Hfin[:, g0:g0 + 2], in_=ps[bank][:, :].rearrange("p (g k) -> p g k", g=2)[:, :, 255], func=AF.Identity), r=[("ps", bank)], w=[("Hfin", g0)])

        def y_batch(gb):
            for gl in range(8):
                g = 8 * gb + gl
                bank = 4 + gl // 2
                c0 = 256 * (gl % 2)
                K.op("pe", lambda e, g=g, bank=bank, c0=c0: e.matmul(out=ps[bank][:, c0:c0 + 256], lhsT=Tc[:, g, :], rhs=U[:, g, 0:KP], start=True, stop=False, skip_group_check=True),
                     r=TKb(gb) + UK, w=[("ps", bank)], signal=False)
                K.op("pe", lambda e, g=g, bank=bank, c0=c0: e.matmul(out=ps[bank][:, c0 + 1:c0 + 256], lhsT=Fc[:, g, 64:192], rhs=Hbf2[gb % 2][:, gl, 0:255], start=False, stop=True, skip_group_check=True),
                     r=FKb(gb) + [("Hbf", gb % 2, gl // 2)], w=[("ps", bank)], signal=(gl % 2 == 1))
            for bank in range(4, 8):
                gl0 = 2 * (bank - 4)
                K.op("act", lambda e, gl0=gl0, bank=bank: e.activation(out=s1rq2[gb % 2][:, gl0:gl0 + 2, 0:KP], in_=ps[bank][:, :].rearrange("p (g k) -> p g k", g=2), func=AF.Gelu_apprx_tanh),
                     r=[("ps", bank)], w=[("s1rq", gb % 2, bank - 4)])

        import os
        ILV = os.environ.get("ILV", "1") == "1"
        def thunks_with_mj(gb_next):
            ths = gen_thunks(gb_next)
            nop = lambda: None
            return [nop] + ths[0:6] + [lambda: gen_mj(gb_next, extra_r=[("Fc", gb_next)])] + ths[6:]

        gen_batch(0)
        scan_batch(0, thunks_with_mj(1))
        y_batch(0)
        sample_batch(0)
        shuffle_out(0)
        scan_batch(1, thunks_with_mj(2))
        y_batch(1)
        sample_batch(1)
        shuffle_out(1)
        scan_batch(2, thunks_with_mj(3))
        y_batch(2)
        sample_batch(2)
        shuffle_out(2)
        scan_batch(3)
        y_batch(3)
        sample_batch(3)
        shuffle_out(3)
        K.dma("pool", hnew_s[:, :, :], Hns[:], r=[("Hns", gb) for gb in range(4)])
        K.dma("pool", hfin_p[:, :], Hfin[:], r=[("Hfin", g0) for g0 in range(0, 32, 2)])
        if dbg:
            K.dma("pool", dbg_out["U"][:, :], U[:, :, :].rearrange("p g k -> p (g k)"), r=UK)
            K.dma("pool", dbg_out["Tc"][:, :], Tc[:, :, :].rearrange("p g m -> p (g m)"), r=[k_ for gb in range(4) for k_ in TKb(gb)])
            K.dma("pool", dbg_out["BcT"][:, :], BcT[:, :, :].rearrange("p g m -> p (g m)"), r=[k_ for gb in range(4) for k_ in BKb(gb)])
            K.dma("pool", dbg_out["Fc"][:, :], Fc[:, :, :].rearrange("p g m -> p (g m)"), r=[("Fc", -1)] + [("Fc", gb) for gb in range(4)])
        K.barrier()

    if dbg:
        dbg_out["s2p"] = dout("dbg_s2p", [512, 8 * KC], BF16)
    if upto >= 4:
      with ExitStack() as s4:
        f4 = lambda name, shape, dt=F32: sb(name, shape, dt, s4)
        s2p = [f4(f"s2p{c}", [128, 8, KC], BF16) for c in range(4)]
        wg_st = f4("wg_st", [128, 4, 512]); wg_bf = f4("wg_bf", [128, 4, 512], BF16)
        wo_st = [f4(f"wo_st{i}", [128, DM]) for i in range(2)]; wo_bf = f4("wo_bf", [128, 8, DM], BF16)
        gpb = f4("gpb", [128, DM]); bpb = f4("bpb", [128, DM])
        thg = [f4(f"thg{i}", [128, 512]) for i in range(2)]; tg = [f4(f"tg{i}", [128, 512]) for i in range(2)]
        NXB = 3
        xt = [f4(f"xt{i}", [128, DM]) for i in range(NXB)]
        yn = [f4(f"yn{i}", [128, DM]) for i in range(2)]
        yo = [f4(f"yo{i}", [128, DM]) for i in range(2)]
        st6 = [f4(f"st6_{i}", [128, 2, 6]) for i in range(2)]
        mv = [f4(f"mv{i}", [128, 2]) for i in range(2)]
        ve = [f4(f"ve{i}", [128, 1]) for i in range(2)]
        rsd = [f4(f"rsd{i}", [128, 1]) for i in range(2)]
        nbv = [f4(f"nbv{i}", [128, 1]) for i in range(2)]
        smix = f4("smix", [128, 8, NS], BF16)
        aI = f4("aI", [128, 128])
        K.op("dve", lambda e: e.tensor_scalar(out=aI[:], in0=ident[:], scalar1=ALPHA, scalar2=None, op0=ALU.mult), r=["ident"], w=["aI"])
        epsc = f4("epsc", [128, 1])
        K.op("dve", lambda e: e.memset(epsc[:], EPS), w=["epsc"])
        for ci in range(4):
            K.dma("sp", wg_st[:, ci, :], w_glu[128 * ci:128 * (ci + 1), :], w=[("wg_st", ci)])
            K.op("act", lambda e, ci=ci: e.activation(out=wg_bf[:, ci, :], in_=wg_st[:, ci, :], func=AF.Identity), r=[("wg_st", ci)], w=[("wg_bf", ci)])
        wo_state = {"dma": 0, "cast": 0}

        def wo_dma(n_):
            while wo_state["dma"] < min(n_, 8):
                kk = wo_state["dma"]
                K.dma("sp", wo_st[kk % 2][:, :], w_out[128 * kk:128 * (kk + 1), :], w=[("wo_st", kk % 2)])
                wo_state["dma"] += 1

        def wo_cast_next():
            kk = wo_state["cast"]
            if kk >= 8:
                return
            wo_dma(kk + 1)
            if kk % 2 == 0:
                K.op("act", lambda e: e.activation(out=wo_bf[:, kk, :], in_=wo_st[kk % 2][:, :], func=AF.Identity), r=[("wo_st", kk % 2)], w=[("wo_bf", kk)])
            else:
                K.op("dve", lambda e: e.tensor_copy(out=wo_bf[:, kk, :], in_=wo_st[kk % 2][:, :]), r=[("wo_st", kk % 2)], w=[("wo_bf", kk)])
            wo_state["cast"] += 1
            wo_dma(kk + 3)

        wo_dma(2)
        K.dma("sp", gpb[:], gpost_d[:, :], w=["gpb"])
        K.dma("sp", bpb[:], bpost_d[:, :], w=["bpb"])

        xp_v = xp.rearrange("(k r) d -> r k d", r=8)
        yp_v = y_p.rearrange("(k r) d -> r k d", r=8)
        xs_v = xs.rearrange("(s t) d -> t s d", t=4)
        ys_v = y_s.rearrange("(s t) d -> t s d", t=4)
        blocks4 = [("p", r_) for r_ in range(8)] + [("s", 4)]
        gcount = 0
        units = [("p", r_) for r_ in range(0, 8, 2)] + [("s", 4)]
        for (kind, r_) in units:
            n = 512 if kind == "p" else NS
            view = (lambda t_, r_=r_: t_[:, r_:r_ + 2, 0:KP]) if kind == "p" else (lambda t_: t_[:, 4:8, KP:KC])
            pview = (lambda ap: ap.rearrange("p (a k) -> p a k", a=2)) if kind == "p" else (lambda ap: ap.rearrange("p (t s) -> p t s", t=4))
            wkeys = (lambda mo: [("s2p", mo, ("p", r_)), ("s2p", mo, ("p", r_ + 1))]) if kind == "p" else (lambda mo: [("s2p", mo, ("s", 4))])
            for mo in range(4):
                bank = gcount % 4
                ti = gcount % 2
                gcount += 1
                for ci in range(4):
                    K.op("pe", lambda e, mo=mo, ci=ci, bank=bank: e.matmul(out=pview(ps[bank][:, 0:n]), lhsT=wg_bf[:, ci, 128 * mo:128 * (mo + 1)], rhs=view(s1f[ci]), start=(ci == 0), stop=(ci == 3)),
                         r=[("wg_bf", ci), ("s1f", ci)], w=[("ps", bank)], signal=(ci == 3))
                K.op("act", lambda e, mo=mo, bank=bank, ti=ti: e.activation(out=thg[ti][:, 0:n], in_=ps[bank][:, 0:n], func=AF.Tanh, bias=hvec[:, 8 + mo:9 + mo], scale=0.5),
                     r=[("ps", bank), "hvec"], w=[("thg", ti)])
                K.op("dve", lambda e, mo=mo, ti=ti: e.scalar_tensor_tensor(out=pview(tg[ti][:, 0:n]), in0=pview(thg[ti][:, 0:n]), scalar=1.0, in1=view(s1f[mo]), op0=ALU.add, op1=ALU.mult),
                     r=[("thg", ti), ("s1f", mo)], w=[("tg", ti)])
                K.op("dve", lambda e, mo=mo, ti=ti: e.scalar_tensor_tensor(out=view(s2p[mo]), in0=pview(tg[ti][:, 0:n]), scalar=0.5, in1=view(gs_perm[mo]), op0=ALU.mult, op1=ALU.mult),
                     r=[("tg", ti)] + [("gs", mo, b_) for b_ in range(-1, 5)], w=wkeys(mo))
            wo_cast_next()
            wo_cast_next()
        while wo_state["cast"] < 8:
            wo_cast_next()
        for kk in range(8):
            src = s2p[kk] if kk < 4 else cfin_perm[kk - 4]
            rk = [("s2p", kk, ("s", 4)), ("s2p", kk, "z")] if kk < 4 else [("cfin", kk - 4, col0_) for col0_ in range(0, 8 * KC, 512)] + [("cfin", kk - 4, -1)]
            K.op("dve", lambda e, kk=kk, src=src: e.tensor_copy(out=smix[:, kk, :].rearrange("p (t s) -> p t s", t=4), in_=src[:, 4:8, KP:KC]), r=rk, w=[("smix", kk)])
        tcount = 0
        ocount = 0
        ln_back = []
        for (kind, r_) in blocks4:
            bkey = (kind, r_)
            tiles = [0, 128] if kind == "p" else [None]
            for k0 in tiles:
                rows = 128 if kind == "p" else NS
                xi = tcount % NXB
                oi = tcount % 2
                tcount += 1
                if kind == "p":
                    K.dma("sp", xt[xi][:, :], xp_v[r_, k0:k0 + 128, :], w=[("xt", xi)])
                else:
                    for t_ in range(4):
                        K.dma("sp", xt[xi][16 * t_:16 * (t_ + 1), :], xs_v[t_, :, :], w=[("xt", xi)] if t_ == 0 else [("xt", xi, t_)])
                xkeys = [("xt", xi)] + ([("xt", xi, t_) for t_ in range(1, 4)] if kind == "s" else [])
                banks = []
                for half in range(2):
                    bank = 2 + (ocount % 6)
                    ocount += 1
                    banks.append(bank)
                    for kk in range(8):
                        src = s2p[kk] if kk < 4 else cfin_perm[kk - 4]
                        if kind == "p":
                            lt = src[:, r_, k0:k0 + 128]
                            rk = [("s2p", kk, bkey)] if kk < 4 else [("cfin", kk - 4, col0_) for col0_ in range(0, 8 * KC, 512)] + [("cfin", kk - 4, -1)]
                        else:
                            lt = smix[:, kk, :]
                            rk = [("smix", kk)]
                        K.op("pe", lambda e, lt=lt, kk=kk, half=half, bank=bank: e.matmul(out=ps[bank][0:rows, 0:512], lhsT=lt, rhs=wo_bf[:, kk, 512 * half:512 * (half + 1)], start=(kk == 0), stop=False),
                             r=rk + [("wo_bf", kk)], w=[("ps", bank)], signal=False)
                    K.op("pe", lambda e, half=half, bank=bank: e.matmul(out=ps[bank][0:rows, 0:512], lhsT=aI[0:rows, 0:rows], rhs=xt[xi][0:rows, 512 * half:512 * (half + 1)], start=False, stop=True),
                         r=xkeys + ["aI"], w=[("ps", bank)], signal=True)
                for half in range(2):
                    K.op("dve", lambda e, half=half, bank=banks[half]: e.bn_stats(out=st6[oi][0:rows, half, :], in_=ps[bank][0:rows, 0:512]), r=[("ps", banks[half])], w=[("st6", oi, half)])
                K.op("dve", lambda e: e.bn_aggr(out=mv[oi][0:rows, :], in_=st6[oi][0:rows, :, :].rearrange("p a b -> p (a b)")), r=[("st6", oi, 0), ("st6", oi, 1)], w=[("mv", oi)])
                K.op("act", lambda e: e.activation(out=ve[oi][0:rows, :], in_=mv[oi][0:rows, 1:2], func=AF.Sqrt, bias=epsc[0:rows, :]), r=[("mv", oi), "epsc"], w=[("ve", oi)])
                for th in ln_back:
                    th()
                ln_back.clear()
                K.op("dve", lambda e: e.reciprocal(out=rsd[oi][0:rows, :], in_=ve[oi][0:rows, :]), r=[("ve", oi)], w=[("rsd", oi)])
                K.op("dve", lambda e: e.scalar_tensor_tensor(out=nbv[oi][0:rows, :], in0=mv[oi][0:rows, 0:1], scalar=-1.0, in1=rsd[oi][0:rows, :], op0=ALU.mult, op1=ALU.mult),
                     r=[("mv", oi), ("rsd", oi)], w=[("nbv", oi)])
                for half in range(2):
                    K.op("act", lambda e, half=half, bank=banks[half]: e.activation(out=yn[oi][0:rows, 512 * half:512 * (half + 1)], in_=ps[bank][0:rows, 0:512], func=AF.Identity, bias=nbv[oi][0:rows, :], scale=rsd[oi][0:rows, :]),
                         r=[("ps", banks[half]), ("nbv", oi), ("rsd", oi)], w=[("yn", oi, half)])
                def back(oi=oi, rows=rows, kind=kind, r_=r_, k0=k0):
                    K.op("dve", lambda e: e.tensor_tensor(out=yn[oi][0:rows, :], in0=yn[oi][0:rows, :], in1=gpb[0:rows, :], op=ALU.mult), r=[("yn", oi, 0), ("yn", oi, 1), "gpb"], w=[("yn", oi, 2)])
                    K.op("dve", lambda e: e.tensor_tensor(out=yo[oi][0:rows, :], in0=yn[oi][0:rows, :], in1=bpb[0:rows, :], op=ALU.add), r=[("yn", oi, 2), "bpb"], w=[("yo", oi)])
                    if kind == "p":
                        K.dma("pool", yp_v[r_, k0:k0 + 128, :], yo[oi][:, :], r=[("yo", oi)])
                    else:
                        for t_ in range(4):
                            K.dma("pool", ys_v[t_, :, :], yo[oi][16 * t_:16 * (t_ + 1), :], r=[("yo", oi)])
                ln_back.append(back)
        for th in ln_back:
            th()
        ln_back.clear()
        if dbg:
            for c in range(4):
                K.dma("pool", dbg_out["s2p"][128 * c:128 * (c + 1), :], s2p[c][:, :, :].rearrange("p r k -> p (r k)"), r=[("s2p", c, bk) for bk in [("p", r_) for r_ in range(8)] + [("s", 4), "z"]])
        K.barrier()

    K.barrier(["sp"])
    es.close()
    return nc


def _prep_inputs(inp):
    f = lambda k: np.ascontiguousarray(np.asarray(inp[k], np.float32))
    x_prompt = f("x_prompt")
    x_sample = f("x_sample")
    sre = f("state_ssm_re")[0]
    sim = f("state_ssm_im")[0]
    scv = f("state_conv")[0]
    b_in = f("b_in")[0]
    vecs = np.concatenate([
        b_in.reshape(20, 128), f("b_glu")[0].reshape(4, 128), f("b_dw")[0].reshape(4, 128),
        f("g_conv_ln")[0].reshape(4, 128), f("b_conv_ln")[0].reshape(4, 128), f("b_pw2")[0].reshape(4, 128)], 0).T
    wdwT = f("w_dw")[0].reshape(31, 4, 128).transpose(2, 1, 0)
    lam_re = f("lam_re")[0]
    lam_im = f("lam_im")[0]
    lamT = np.stack([lam_re.T, lam_im.T], 1)
    lamT = np.concatenate([lamT, lamT], 0)
    br = f("b_re")[0].transpose(1, 0, 2)
    bi = f("b_im")[0].transpose(1, 0, 2)
    cr = f("c_re")[0].transpose(2, 0, 1)
    ci = f("c_im")[0].transpose(2, 0, 1)
    wdw = f("w_dw")[0]
    wdg = np.zeros((8, 16, 32, 5, 8), np.float32)
    for s_ in range(8):
        for d_ in range(5):
            for r_ in range(8):
                tau = 8 * d_ + r_ - s_
                if 0 <= tau <= 30:
                    wdg[s_, :, :, d_, r_] = wdw[30 - tau].reshape(32, 16).T
    sccol = np.stack([np.tile(f(k)[0].reshape(32, 16).T, (8, 1)) for k in ("b_dw", "g_conv_ln", "b_conv_ln")], 1)
    shared = {
        "w_in": f("w_in")[0], "w_glu": f("w_glu")[0], "w_pw2": f("w_pw2")[0], "w_out": f("w_out")[0],
        "vecs": np.ascontiguousarray(vecs), "wdwT": np.ascontiguousarray(wdwT),
        "gpost": np.ascontiguousarray(np.broadcast_to(f("g_post")[0].reshape(1, DM), (128, DM))), "bpost": np.ascontiguousarray(np.broadcast_to(f("b_post")[0].reshape(1, DM), (128, DM))),
        "ident": np.eye(128, dtype=np.float32),
        "lamT": np.ascontiguousarray(lamT), "logdt": np.ascontiguousarray(np.broadcast_to(f("log_dt")[0].reshape(1, 32), (128, 32))),
        "BA": np.ascontiguousarray(np.concatenate([br, bi], 0)), "BBraw": np.ascontiguousarray(np.concatenate([bi, br], 0)),
        "CAraw": np.ascontiguousarray(np.concatenate([cr, ci], 0)), "CBraw": np.ascontiguousarray(np.concatenate([ci, cr], 0)),
        "dskT": np.ascontiguousarray(f("d_skip")[0].reshape(32, 16).T),
        "wdg": np.ascontiguousarray(wdg.reshape(128, 32 * 5 * 8)), "sccol": np.ascontiguousarray(sccol.astype(np.float32)),
    }
    in_maps = []
    for i in range(NCORES):
        sl = slice(NSEQ * i, NSEQ * (i + 1))
        hre = sre[sl].transpose(2, 1, 0)
        him = sim[sl].transpose(2, 1, 0)
        m = dict(shared)
        m["xp"] = x_prompt[i]
        m["xs"] = np.ascontiguousarray(x_sample[sl].reshape(NS, DM))
        m["h0A"] = np.ascontiguousarray(np.concatenate([hre, him], 0))
        m["h0Braw"] = np.ascontiguousarray(np.concatenate([him, hre], 0))
        m["stT"] = np.ascontiguousarray(scv[sl].transpose(2, 0, 1))
        pad = np.zeros((NSEQ, 40, 512), np.float32)
        pad[:, 6:36, :] = scv[sl]
        m["stP"] = np.ascontiguousarray(pad.reshape(NSEQ, 5, 8, 512).transpose(3, 2, 0, 1).reshape(512, 8, 5 * NSEQ))
        in_maps.append(m)
    return in_maps


def _assemble(results):
    y_p = np.stack([r["y_p"] for r in results], 0)
    y_s = np.concatenate([r["y_s"].reshape(NSEQ, 4, DM) for r in results], 0)
    hp = np.stack([r["hfin_p"] for r in results], 0)
    re_p = hp[:, 0:64, :].transpose(0, 2, 1)[None]
    im_p = hp[:, 64:128, :].transpose(0, 2, 1)[None]
    cv_p = np.stack([r["ncvT_p"].T for r in results], 0)[None]
    hs = np.concatenate([r["hnew_s"].transpose(2, 1, 0) for r in results], 0)
    re_s = hs[:, :, 0:64][None]
    im_s = hs[:, :, 64:128][None]
    cv_s = np.concatenate([r["ncvT_s"].reshape(512, NSEQ, 30).transpose(1, 2, 0) for r in results], 0)[None]
    c = lambda a: np.ascontiguousarray(a.astype(np.float32))
    return (c(y_p), c(y_s), c(re_p), c(im_p), c(cv_p), c(re_s), c(im_s), c(cv_s))


_NC_CACHE = {}


def kernel(**inputs):
    in_maps = _prep_inputs(inputs)
    if "nc" not in _NC_CACHE:
        _NC_CACHE["nc"] = build_program()
    res = run_bass_kernel_spmd(_NC_CACHE["nc"], in_maps, core_ids=list(range(NCORES)))
    return _assemble(res.results)
```

```python
import numpy as np
import ml_dtypes
from contextlib import ExitStack
import concourse.bass as bass
import concourse.mybir as mybir
from concourse.bass_utils import run_bass_kernel_spmd

F32 = mybir.dt.float32
BF16 = mybir.dt.bfloat16
I32 = mybir.dt.int32
AF = mybir.ActivationFunctionType
ALU = mybir.AluOpType

NCORES = 8
DM = 1024
NP = 2048
NSEQ = 16
NS = 64
NT = NP + NS
KP = 256
KC = KP + NSEQ
KV = KP + 5 * NSEQ
ALPHA = 2.0 ** 0.25
EPS = 1e-5
TWO_PI = 6.283185307179586
BLOCKS = [(0, 512), (512, 512), (1024, 512), (1536, 512), (2048, 64)]


class KB:
    def __init__(self, nc, es):
        self.nc = nc
        self.E = {"pe": nc.tensor, "act": nc.scalar, "dve": nc.vector, "pool": nc.gpsimd, "sp": nc.sync}
        self.sem = {}
        self.cnt = {}
        for k in self.E:
            self.sem[k] = es.enter_context(nc.semaphore("sem_" + k))
            self.cnt[k] = 0
        self.NDS = 20
        self.dq = ("sp", "pool", "act")
        self.dsem = {q: [es.enter_context(nc.semaphore(f"d_{q}{i}")) for i in range(self.NDS)] for q in self.dq}
        self.dval = {q: [0] * self.NDS for q in self.dq}
        self.drr = {q: 0 for q in self.dq}
        self.seen = {k: {} for k in self.E}
        self.lw = {}
        self.rd = {}
        self.pend = {k: [] for k in self.E}

    def _semobj(self, sk):
        return self.sem[sk] if isinstance(sk, str) else self.dsem[sk[0]][sk[1]]

    def _wait(self, eng, sk, val):
        if val <= 0:
            return
        if eng == "pe" and sk == "pe":
            return
        if self.seen[eng].get(sk, 0) >= val:
            return
        self.E[eng].wait_ge(self._semobj(sk), val)
        self.seen[eng][sk] = val

    def _deps(self, eng, r, w):
        for k in r:
            t = self.lw.get(k)
            if t:
                self._wait(eng, *t)
        for k in w:
            t = self.lw.get(k)
            if t:
                self._wait(eng, *t)
            for sk, v in self.rd.get(k, {}).items():
                self._wait(eng, sk, v)

    def _commit(self, t, r, w):
        for k in w:
            self.lw[k] = t
            self.rd[k] = {}
        for k in r:
            d = self.rd.setdefault(k, {})
            d[t[0]] = max(d.get(t[0], 0), t[1])

    def op(self, eng, fn, r=(), w=(), signal=True):
        r = list(r)
        w = list(w)
        self._deps(eng, r, w)
        ins = fn(self.E[eng])
        if not signal:
            self.pend[eng].append((r, w))
            return None
        self.cnt[eng] += 1
        ins.then_inc(self.sem[eng], 1)
        t = (eng, self.cnt[eng])
        for (pr, pw) in self.pend[eng]:
            self._commit(t, pr, pw)
        self.pend[eng] = []
        self._commit(t, r, w)
        return t

    def dma(self, q, out, in_, r=(), w=()):
        r = list(r)
        w = list(w)
        self._deps(q, r, w)
        i = self.drr[q] % self.NDS
        self.drr[q] += 1
        self._wait(q, (q, i), self.dval[q][i])
        ins = self.E[q].dma_start(out=out, in_=in_)
        self.dval[q][i] += 16
        ins.then_inc(self.dsem[q][i], 16)
        t = ((q, i), self.dval[q][i])
        self._commit(t, r, w)
        return t

    def barrier(self, engines=None, dma_queues=None):
        engines = engines or list(self.E)
        dma_queues = self.dq if dma_queues is None else dma_queues
        for e in engines:
            for k in self.E:
                if k == "sp" and "sp" not in dma_queues:
                    continue
                self._wait(e, k, self.cnt[k])
            for q in dma_queues:
                for i in range(self.NDS):
                    self._wait(e, (q, i), self.dval[q][i])


def build_program(upto=9, dbg=False):
    nc = bass.Bass("TRN2", target_bir_lowering=False)
    es = ExitStack()
    K = KB(nc, es)

    def din(name, shape, dt=F32):
        return nc.dram_tensor(name, list(shape), dt, kind="ExternalInput").ap()

    def dout(name, shape, dt=F32):
        return nc.dram_tensor(name, list(shape), dt, kind="ExternalOutput").ap()

    def dscr(name, shape, dt):
        return nc.dram_tensor(name, list(shape), dt).ap()

    xp = din("xp", [NP, DM])
    xs = din("xs", [NS, DM])
    w_in = din("w_in", [DM, 2560])
    w_glu = din("w_glu", [512, 512])
    w_pw2 = din("w_pw2", [512, 512])
    w_out = din("w_out", [DM, DM])
    vecs_d = din("vecs", [128, 40])
    wdwT_d = din("wdwT", [128, 4, 31])
    gpost_d = din("gpost", [128, DM])
    bpost_d = din("bpost", [128, DM])
    ident_d = din("ident", [128, 128])
    lamT_d = din("lamT", [128, 2, 32])
    logdt_d = din("logdt", [128, 32])
    BA_d = din("BA", [128, 32, 16])
    BB_d = din("BBraw", [128, 32, 16])
    CA_d = din("CAraw", [128, 32, 16])
    CB_d = din("CBraw", [128, 32, 16])
    dskT_d = din("dskT", [16, 32])
    h0A_d = din("h0A", [128, 32, NSEQ])
    h0B_d = din("h0Braw", [128, 32, NSEQ])
    stT_d = din("stT", [512, NSEQ, 30])
    stP_d = din("stP", [512, 8, 5 * NSEQ])
    wdg_d = din("wdg", [128, 32 * 5 * 8])
    sccol_d = din("sccol", [128, 3, 32])

    y_p = dout("y_p", [NP, DM])
    y_s = dout("y_s", [NS, DM])
    hfin_p = dout("hfin_p", [128, 32])
    hnew_s = dout("hnew_s", [128, 32, NSEQ])
    ncvT_p = dout("ncvT_p", [512, 30])
    ncvT_s = dout("ncvT_s", [512, NSEQ * 30])

    d_u = dscr("d_u", [128, 32 * KC], BF16)
    d_s = dscr("d_s", [512, 8 * KC], BF16)
    d_v = dscr("d_v", [128, 32 * KV], BF16)
    d_c = dscr("d_c", [512, 8 * KC], BF16)

    dbg_out = {}
    if dbg:
        dbg_out["u_perm"] = dout("dbg_u_perm", [512, 8 * KC], BF16)
        dbg_out["gs"] = dout("dbg_gs", [512, 8 * KC], BF16)
        dbg_out["gc"] = dout("dbg_gc", [512, 8 * KC], BF16)
        dbg_out["vperm"] = dout("dbg_vperm", [512, 8 * KV], BF16)
        dbg_out["U"] = dout("dbg_U", [128, 32 * KC], BF16)

    def sb(name, shape, dt, stack=es, side=None):
        return stack.enter_context(nc.sbuf_tensor("sb_" + name, list(shape), dt, side=side))

    ps = [es.enter_context(nc.psum_tensor(f"ps{i}", [128, 512], F32)) for i in range(8)]

    ident = sb("ident", [128, 128], F32, side="right")
    vecs = sb("vecs", [128, 40], F32, side="right")
    hvec = sb("hvec", [128, 12], F32, side="right")
    gs_perm = [sb(f"gs_perm{c}", [128, 8, KC], BF16, side="right") for c in range(4)]
    sA = ExitStack()
    gc_perm = [sb(f"gc_perm{c}", [128, 8, KC], BF16, sA) for c in range(4)]

    wdg = sb("wdg", [128, 32 * 5 * 8], F32, side="right")
    sccol = sb("sccol", [128, 3, 32], F32, side="right")
    K.dma("sp", ident[:], ident_d[:, :], w=["ident"])
    K.dma("sp", vecs[:], vecs_d[:, :], w=["vecs"])
    K.op("dve", lambda e: e.tensor_scalar(out=hvec[:, 0:8], in0=vecs[:, 8:16], scalar1=0.5, scalar2=None, op0=ALU.mult),
         r=["vecs"], w=["hvec"])
    K.op("dve", lambda e: e.tensor_scalar(out=hvec[:, 8:12], in0=vecs[:, 20:24], scalar1=0.5, scalar2=None, op0=ALU.mult),
         r=["vecs"], w=["hvec"])
    for c in range(4):
        K.op("dve", lambda e, c=c: e.memset(gc_perm[c][:, 0:4, KP:KC], 0.0), w=[("gc", c, -1)])
        K.op("dve", lambda e, c=c: e.memset(gs_perm[c][:, 0:4, KP:KC], 0.0), w=[("gs", c, -1)])

    fr = lambda name, shape, dt=F32: sb(name, shape, dt, side="right")
    lam = fr("lam", [128, 2, 32]); dtb = fr("dtb", [128, 32])
    T = [fr(f"T{i}", [128, 4, 32]) for i in range(4)]
    PR = fr("PR", [128, 9, 32]); PI = fr("PI", [128, 9, 32])
    QR = fr("QR", [128, 8, 32]); QI = fr("QI", [128, 8, 32])
    mag = fr("mag", [128, 32]); phi = fr("phi", [128, 32]); kf = fr("kf", [128, 32]); ki = fr("ki", [128, 32], I32)
    rr = fr("rr", [128, 32]); rc = fr("rc", [128, 32]); msk = fr("msk", [128, 32]); sinv = fr("sinv", [128, 32]); cosv = fr("cosv", [128, 32])
    KR = fr("KR", [128, 32]); KI = fr("KI", [128, 32]); den = fr("den", [128, 32]); am1 = fr("am1", [128, 32])
    S4 = fr("S4", [128, 32, 8, 2]); S_bf = fr("S_bf", [128, 32, 8, 2], BF16); I2 = fr("I2", [128, 64], BF16)
    PRr = fr("PRr", [128, 8, 32]); PIr = fr("PIr", [128, 8, 32])

    def V(fn, r, w, eng="dve"):
        return K.op(eng, fn, r=r, w=w)

    def tt(o, a, b, op, r, w, eng="dve"):
        return K.op(eng, lambda e: e.tensor_tensor(out=o, in0=a, in1=b, op=op), r=r, w=w)

    def ts(o, a, s1_, op0, r, w, s2_=None, op1=None, eng="dve"):
        if op1 is None:
            return K.op(eng, lambda e: e.tensor_scalar(out=o, in0=a, scalar1=s1_, scalar2=None, op0=op0), r=r, w=w)
        return K.op(eng, lambda e: e.tensor_scalar(out=o, in0=a, scalar1=s1_, scalar2=s2_, op0=op0, op1=op1), r=r, w=w)

    G = "pool"

    def cmul(oR, oI, xR, xI, yR, yI, nslots, r, w):
        a, b, c_, d_ = (T[i][:, 0:nslots, :] for i in range(4))
        tk = ["T0", "T1", "T2", "T3"]
        tt(a, xR, yR, ALU.mult, r, [tk[0]], G); tt(b, xI, yI, ALU.mult, r, [tk[1]], G)
        tt(c_, xR, yI, ALU.mult, r, [tk[2]], G); tt(d_, xI, yR, ALU.mult, r, [tk[3]], G)
        tt(oR, a, b, ALU.subtract, [tk[0], tk[1]], w, G); tt(oI, c_, d_, ALU.add, [tk[2], tk[3]], w, G)

    def g1_gen():
        K.dma("sp", lam[:], lamT_d[:, :, :], w=["lam"]); K.dma("sp", dtb[:], logdt_d[:, :], w=["dtb"])
        lr = lam[:, 0, :]; li = lam[:, 1, :]
        K.op("act", lambda e: e.activation(out=dtb[:], in_=dtb[:], func=AF.Exp), r=["dtb"], w=["dtb"])
        yield
        tt(mag[:], lr, dtb[:], ALU.mult, ["lam", "dtb"], ["mag"], G)
        yield
        K.op("act", lambda e: e.activation(out=mag[:], in_=mag[:], func=AF.Exp), r=["mag"], w=["mag"])
        yield
        tt(phi[:], li, dtb[:], ALU.mult, ["lam", "dtb"], ["phi"], G)
        ts(kf[:], phi[:], 1.0 / TWO_PI, ALU.mult, ["phi"], ["kf"], eng=G)
        yield
        V(lambda e: e.tensor_copy(out=ki[:], in_=kf[:]), ["kf"], ["ki"])
        V(lambda e: e.tensor_copy(out=kf[:], in_=ki[:]), ["ki"], ["kf"])
        yield
        ts(kf[:], kf[:], -TWO_PI, ALU.mult, ["kf"], ["kf"], eng=G)
        tt(rr[:], kf[:], phi[:], ALU.add, ["kf", "phi"], ["rr"], G)
        PIS = 3.141592
        ts(rr[:], rr[:], -PIS, ALU.max, ["rr"], ["rr"], PIS, ALU.min, eng=G)
        ts(rc[:], rr[:], TWO_PI / 4.0, ALU.add, ["rr"], ["rc"], eng=G)
        ts(msk[:], rc[:], PIS, ALU.is_gt, ["rc"], ["msk"], eng=G)
        ts(msk[:], msk[:], -TWO_PI, ALU.mult, ["msk"], ["msk"], eng=G)
        tt(rc[:], rc[:], msk[:], ALU.add, ["msk", "rc"], ["rc"], G)
        ts(rc[:], rc[:], -PIS, ALU.max, ["rc"], ["rc"], PIS, ALU.min, eng=G)
        yield
        K.op("act", lambda e: e.activation(out=sinv[:], in_=rr[:], func=AF.Sin), r=["rr"], w=["sinv"])
        K.op("act", lambda e: e.activation(out=cosv[:], in_=rc[:], func=AF.Sin), r=["rc"], w=["cosv"])
        yield
        V(lambda e: e.memset(PR[:, 0, :], 1.0), [], ["P0"], G); V(lambda e: e.memset(PI[:, 0, :], 0.0), [], ["P0"], G)
        tt(PR[:, 1, :], mag[:], cosv[:], ALU.mult, ["mag", "cosv"], ["P1"], G); tt(PI[:, 1, :], mag[:], sinv[:], ALU.mult, ["mag", "sinv"], ["P1"], G)
        cmul(PR[:, 2:3, :], PI[:, 2:3, :], PR[:, 1:2, :], PI[:, 1:2, :], PR[:, 1:2, :], PI[:, 1:2, :], 1, ["P1"], ["P2"])
        cmul(PR[:, 3:5, :], PI[:, 3:5, :], PR[:, 1:3, :], PI[:, 1:3, :], PR[:, 2:3, :].to_broadcast([128, 2, 32]), PI[:, 2:3, :].to_broadcast([128, 2, 32]), 2, ["P1", "P2"], ["P34"])
        cmul(PR[:, 5:9, :], PI[:, 5:9, :], PR[:, 1:5, :], PI[:, 1:5, :], PR[:, 4:5, :].to_broadcast([128, 4, 32]), PI[:, 4:5, :].to_broadcast([128, 4, 32]), 4, ["P1", "P2", "P34"], ["P58"])
        PK = ["P0", "P1", "P2", "P34", "P58"]
        V(lambda e: e.tensor_copy(out=QR[:, 0, :], in_=PR[:, 8, :]), PK, [("Q", 0)], G); V(lambda e: e.tensor_copy(out=QI[:, 0, :], in_=PI[:, 8, :]), PK, [("Q", 0)], G)
        for j in range(7):
            cmul(QR[:, j + 1:j + 2, :], QI[:, j + 1:j + 2, :], QR[:, j:j + 1, :], QI[:, j:j + 1, :], QR[:, j:j + 1, :], QI[:, j:j + 1, :], 1, [("Q", j)], [("Q", j + 1)])
        QK = [("Q", j) for j in range(8)]
        tt(den[:], lr, lr, ALU.mult, ["lam"], ["den"], G); tt(kf[:], li, li, ALU.mult, ["lam"], ["kf"], G)
        tt(den[:], den[:], kf[:], ALU.add, ["den", "kf"], ["den"], G)
        yield
        V(lambda e: e.reciprocal(out=den[:], in_=den[:]), ["den"], ["den"])
        yield
        ts(am1[:], PR[:, 1, :], -1.0, ALU.add, ["P1"], ["am1"], eng=G)
        tt(KR[:], am1[:], lr, ALU.mult, ["am1", "lam"], ["KR"], G); tt(kf[:], PI[:, 1, :], li, ALU.mult, ["P1", "lam"], ["kf"], G)
        tt(KR[:], KR[:], kf[:], ALU.add, ["KR", "kf"], ["KR"], G); tt(KR[:], KR[:], den[:], ALU.mult, ["KR", "den"], ["KR"], G)
        tt(KI[:], PI[:, 1, :], lr, ALU.mult, ["P1", "lam"], ["KI"], G); tt(kf[:], am1[:], li, ALU.mult, ["am1", "lam"], ["kf"], G)
        tt(KI[:], KI[:], kf[:], ALU.subtract, ["KI", "kf"], ["KI"], G); tt(KI[:], KI[:], den[:], ALU.mult, ["KI", "den"], ["KI"], G)
        V(lambda e: e.tensor_copy(out=S4[0:64, :, :, 0], in_=QR[0:64, :, :].rearrange("p j g -> p g j")), QK, ["S4a"], G)
        ts(S4[64:128, :, :, 0], QI[64:128, :, :].rearrange("p j g -> p g j"), -1.0, ALU.mult, QK, ["S4b"], eng=G)
        V(lambda e: e.tensor_copy(out=S4[0:64, :, :, 1], in_=QI[0:64, :, :].rearrange("p j g -> p g j")), QK, ["S4c"], G)
        V(lambda e: e.tensor_copy(out=S4[64:128, :, :, 1], in_=QR[64:128, :, :].rearrange("p j g -> p g j")), QK, ["S4d"], G)
        V(lambda e: e.tensor_copy(out=S_bf[:], in_=S4[:]), ["S4a", "S4b", "S4c", "S4d"], ["S_bf"], G)
        tt(I2[:], ident[:, 0:64], ident[:, 64:128], ALU.add, ["ident"], ["I2"], G)
        for sx_ in range(8):
            V(lambda e, sx_=sx_: e.tensor_copy(out=PRr[:, sx_, :], in_=PR[:, 7 - sx_, :]), PK, [("Pr", sx_)], G)
            V(lambda e, sx_=sx_: e.tensor_copy(out=PIr[:, sx_, :], in_=PI[:, 7 - sx_, :]), PK, [("Pr", sx_)], G)


        yield

    g1 = g1_gen()
    next(g1)

    PK = ["P0", "P1", "P2", "P34", "P58"]
    QK = [("Q", j) for j in range(8)]

    with ExitStack() as s1:
        w_bf = sb("w_bf", [128, 8, 2560], BF16, s1)
        wst = [sb(f"wst{i}", [128, 8, 256], F32, s1) for i in range(2)]
        x_sb = [sb(f"x_sb{i}", [128, DM], F32, s1) for i in range(4)]
        xT = [sb(f"xT{i}", [128, 8, 512], BF16, s1) for i in range(2)]
        th = [sb(f"th{i}", [128, 512], F32, s1) for i in range(2)]
        ah = [sb(f"ah{i}", [128, 512], F32, s1) for i in range(2)]
        st32 = sb("st32", [128, 4, NSEQ, 30], F32, s1)
        stp32 = sb("stp32", [128, 8, 5 * NSEQ], F32, s1)
        v32p = sb("v32p", [128, 4, 30], F32, s1)
        ncs = sb("ncs", [128, 4, NSEQ, 30], F32, s1)
        u_perm = [sb(f"u_perm{c}", [128, 8, KC], BF16, s1) for c in range(4)]
        v_perm = [sb(f"v_perm{c}", [128, 8, KV], BF16, s1) for c in range(4)]

        def load_x(bi):
            t0, n = BLOCKS[bi]
            if n == 512:
                for i in range(4):
                    K.dma("sp", x_sb[i][:, :], xp[t0 + 128 * i:t0 + 128 * (i + 1), :], w=[("x", i)])
            else:
                K.dma("sp", x_sb[0][0:NS, :], xs[:, :], w=[("x", 0)])

        for c in range(4):
            K.op("dve", lambda e, c=c: e.memset(u_perm[c][:, 0:4, KP:KC], 0.0), w=[("uperm", c, -1)])
        BORD = [0, 4, 1, 2, 3]
        load_x(BORD[0])
        M_ORDER = [12, 8, 13, 9, 14, 10, 15, 11, 0, 1, 2, 3, 4, 5, 6, 7, 16, 17, 18, 19]
        CH_ORDER = []
        for m_ in M_ORDER:
            if m_ // 2 not in CH_ORDER:
                CH_ORDER.append(m_ // 2)
        w_state = {"dma": 0, "cast": set()}

        def w_issue_dma(upto_n):
            while w_state["dma"] < min(upto_n, len(CH_ORDER)):
                i = w_state["dma"]
                cc = CH_ORDER[i]
                for kk in range(8):
                    K.dma("sp", wst[i % 2][:, kk, :], w_in[128 * kk:128 * (kk + 1), 256 * cc:256 * (cc + 1)], w=[("wst", i % 2, kk)])
                w_state["dma"] += 1

        def w_ensure(cc):
            if cc in w_state["cast"]:
                return
            i = CH_ORDER.index(cc)
            w_issue_dma(i + 1)
            K.op("dve", lambda e: e.tensor_copy(out=w_bf[:, 0:4, 256 * cc:256 * (cc + 1)], in_=wst[i % 2][:, 0:4, :]),
                 r=[("wst", i % 2, kk) for kk in range(0, 4)], w=[("wbf", cc, 0)])
            K.op("act", lambda e: e.activation(out=w_bf[:, 4:8, 256 * cc:256 * (cc + 1)], in_=wst[i % 2][:, 4:8, :], func=AF.Identity),
                 r=[("wst", i % 2, kk) for kk in range(4, 8)], w=[("wbf", cc, 1)])
            w_state["cast"].add(cc)
            w_issue_dma(i + 3)

        w_issue_dma(2)
        K.dma("sp", wdg[:], wdg_d[:, :], w=["wdg"])
        K.dma("sp", sccol[:], sccol_d[:, :, :], w=["sccol"])
        def st_dma(c):
            K.dma("sp", st32[:, c, :, :], stT_d[128 * c:128 * (c + 1), :, :], w=[("st32", c)])
            K.dma("sp", stp32[:, :, :], stP_d[128 * c:128 * (c + 1), :, :], w=["stp32"])

        def st_copy(c):
            K.op("dve", lambda e, c=c: e.tensor_copy(out=ncs[:, c, :, 0:26], in_=st32[:, c, :, 4:30]), r=[("st32", c)], w=[("ncs", c, 0)])
            K.op("dve", lambda e, c=c: e.tensor_copy(out=v_perm[c][:, :, KP:KV], in_=stp32[:, :, :]), r=["stp32"], w=[("vperm", c, -1)])
        ST_AT = {1: ("d", 0), 3: ("c", 0), 4: ("d", 1), 6: ("c", 1), 7: ("d", 2), 9: ("c", 2), 11: ("d", 3), 13: ("c", 3)}
        zi = 0
        ev = 0
        ct_count = [0]
        G1_AT = {2: 1, 3: 1, 4: 1, 10: 1, 11: 1, 22: 1, 23: 1, 60: 1, 61: 1}

        def g1_tick():
            ct_count[0] += 1
            if ct_count[0] in G1_AT:
                try:
                    next(g1)
                except StopIteration:
                    pass
        du_v = d_u.rearrange("(r c) (g k) -> c r g k", c=16, g=32)
        dv_v = d_v.rearrange("(r c) (g k) -> c r g k", c=16, g=32)

        act_pending = []

        def scratch_piece(pc):
            for c in range(4):
                for gl in range(8):
                    g = 8 * c + gl
                    ps_ = slice(16 * gl, 16 * (gl + 1))
                    if pc < 2:
                        ks = slice(128 * pc, 128 * (pc + 1))
                        rk_u = [("uperm", c, 2 * pc), ("uperm", c, 2 * pc + 1)]
                        rk_v = [("vperm", c, 2 * pc), ("vperm", c, 2 * pc + 1)]
                        K.dma("sp", du_v[:, :, g, ks], u_perm[c][ps_, :, ks], r=rk_u, w=[("d_u", c, pc, gl)])
                        if pc == 1 and gl % 2 == 0:
                            act_pending.append(lambda g=g, ks=ks, c=c, ps_=ps_, rk_v=rk_v, pc=pc, gl=gl: K.dma("act", dv_v[:, :, g, ks], v_perm[c][ps_, :, ks], r=rk_v, w=[("d_v", c, pc, gl)]))
                        else:
                            K.dma("sp", dv_v[:, :, g, ks], v_perm[c][ps_, :, ks], r=rk_v, w=[("d_v", c, pc, gl)])
                    else:
                        K.dma("sp", du_v[:, :, g, KP:KC], u_perm[c][ps_, :, KP:KC], r=[("uperm", c, -1), ("uperm", c, 4)], w=[("d_u", c, pc, gl)])
                        K.dma("sp", dv_v[:, :, g, KP:KV], v_perm[c][ps_, :, KP:KV], r=[("vperm", c, -1), ("vperm", c, 4)], w=[("d_v", c, pc, gl)])

        def do_transposes(pos):
            bj = BORD[pos]
            t0_, n_ = BLOCKS[bj]
            bb_ = pos % 2
            nt_ = 4 if n_ == 512 else 1
            rows_ = 128 if n_ == 512 else NS
            for kk in range(8):
                bank = 4 + (kk % 2)
                for i in range(nt_):
                    K.op("pe", lambda e, i=i, kk=kk, bank=bank: e.transpose(out=ps[bank][:, rows_ * i:rows_ * (i + 1)], in_=x_sb[i][0:rows_, 128 * kk:128 * (kk + 1)], identity=ident[0:rows_, 0:rows_]),
                         r=[("x", i), "ident"], w=[("ps", bank)], signal=(i == nt_ - 1))
                if kk % 2 == 0:
                    K.op("dve", lambda e, kk=kk, bank=bank: e.tensor_copy(out=xT[bb_][:, kk, 0:n_], in_=ps[bank][:, 0:n_]), r=[("ps", bank)], w=[("xT", bb_, kk)])
                else:
                    K.op("act", lambda e, kk=kk, bank=bank: e.activation(out=xT[bb_][:, kk, 0:n_], in_=ps[bank][:, 0:n_], func=AF.Identity), r=[("ps", bank)], w=[("xT", bb_, kk)])
            if pos + 1 < len(BORD):
                load_x(BORD[pos + 1])

        do_transposes(0)
        for pos_, bi in enumerate(BORD):
            t0, n = BLOCKS[bi]
            bb = pos_ % 2
            if pos_ == 2:
                scratch_piece(2)
            nt = 4 if n == 512 else 1
            rows = 128 if n == 512 else NS
            if bi == 2:
                scratch_piece(0)
            for mi_, m in enumerate(M_ORDER):
                for _ in range(min(2, len(act_pending))):
                    act_pending.pop(0)()
                if mi_ == 10 and pos_ + 1 < len(BORD):
                    do_transposes(pos_ + 1)
                if bi == 3 and mi_ == 12:
                    scratch_piece(1)
                if pos_ == 0 and mi_ in ST_AT:
                    kind_, c_ = ST_AT[mi_]
                    (st_dma if kind_ == "d" else st_copy)(c_)
                bank = zi % 4
                zi += 1
                cc = m // 2
                w_ensure(cc)
                for kk in range(8):
                    K.op("pe", lambda e, m=m, kk=kk, bank=bank: e.matmul(out=ps[bank][:, 0:n], lhsT=w_bf[:, kk, 128 * m:128 * (m + 1)], rhs=xT[bb][:, kk, 0:n], start=(kk == 0), stop=(kk == 7)),
                         r=[("wbf", cc, kk // 4), ("xT", bb, kk)], w=[("ps", bank)], signal=(kk == 7))
                g1_tick()
                pz = ps[bank][:, 0:n]
                bcol = vecs[:, m:m + 1]
                c = m % 4
                if m < 8:
                    dst_t = u_perm[c] if m < 4 else gs_perm[c]
                    key = ("uperm" if m < 4 else "gs", c, bi)
                    func = AF.Identity if m < 4 else AF.Silu
                    if n == 512:
                        k0 = t0 // 8
                        o_ap = dst_t[:, :, k0:k0 + 64]
                        i_ap = pz.rearrange("p (k r) -> p r k", r=8)
                    else:
                        o_ap = dst_t[:, 4:8, KP:KC]
                        i_ap = pz.rearrange("p (s t) -> p t s", t=4)
                    K.op("act", lambda e, o_ap=o_ap, i_ap=i_ap, func=func, bcol=bcol: e.activation(out=o_ap, in_=i_ap, func=func, bias=bcol),
                         r=[("ps", bank), "vecs"], w=[key])
                elif m >= 16:
                    if n == 512:
                        k0 = t0 // 8
                        o_ap = gc_perm[c][:, :, k0:k0 + 64]
                        i_ap = pz.rearrange("p (k r) -> p r k", r=8)
                    else:
                        o_ap = gc_perm[c][:, 4:8, KP:KC]
                        i_ap = pz.rearrange("p (s t) -> p t s", t=4)
                    K.op("act", lambda e, o_ap=o_ap, i_ap=i_ap, bcol=bcol: e.activation(out=o_ap, in_=i_ap, func=AF.Silu, bias=bcol),
                         r=[("ps", bank), "vecs"], w=[("gc", c, bi)])
                elif m >= 12:
                    K.op("act", lambda e, c=c, pz=pz: e.activation(out=th[c % 2][:, 0:n], in_=pz, func=AF.Tanh, bias=hvec[:, 4 + c:5 + c], scale=0.5),
                         r=[("ps", bank), "hvec"], w=[("th", c % 2)])
                else:
                    K.op("act", lambda e, c=c, pz=pz: e.activation(out=ah[c % 2][:, 0:n], in_=pz, func=AF.Identity, bias=hvec[:, c:c + 1], scale=0.5),
                         r=[("ps", bank), "hvec"], w=[("ah", c % 2)])
                    if n == 512:
                        k0 = t0 // 8
                        o_ap = v_perm[c][:, :, k0:k0 + 64]
                        i0 = th[c % 2][:, 0:n].rearrange("p (k r) -> p r k", r=8)
                        i1 = ah[c % 2][:, 0:n].rearrange("p (k r) -> p r k", r=8)
                        key = ("vperm", c, bi)
                    else:
                        o_ap = v_perm[c][:, 4:8, KP:KV].rearrange("p t (s j) -> p t s j", j=5)[:, :, :, 4]
                        i0 = th[c % 2][:, 0:n].rearrange("p (s t) -> p t s", t=4)
                        i1 = ah[c % 2][:, 0:n].rearrange("p (s t) -> p t s", t=4)
                        key = ("vperm", c, bi)
                    K.op("dve", lambda e, o_ap=o_ap, i0=i0, i1=i1: e.scalar_tensor_tensor(out=o_ap, in0=i0, scalar=1.0, in1=i1, op0=ALU.add, op1=ALU.mult),
                         r=[("th", c % 2), ("ah", c % 2)], w=[key])
                    if bi == 3:
                        K.op("dve", lambda e, c=c: e.scalar_tensor_tensor(out=v32p[:, c, :], in0=th[c % 2][:, 482:512], scalar=1.0, in1=ah[c % 2][:, 482:512], op0=ALU.add, op1=ALU.mult),
                             r=[("th", c % 2), ("ah", c % 2)], w=[("v32p", c)])
                    if bi == 4:
                        j0 = th[c % 2][:, 0:n].rearrange("p (s t) -> p s t", t=4)
                        j1 = ah[c % 2][:, 0:n].rearrange("p (s t) -> p s t", t=4)
                        K.op("dve", lambda e, c=c, i0=j0, i1=j1: e.scalar_tensor_tensor(out=ncs[:, c, :, 26:30], in0=i0, scalar=1.0, in1=i1, op0=ALU.add, op1=ALU.mult),
                             r=[("th", c % 2), ("ah", c % 2)], w=[("ncs", c, 1)])
        while act_pending:
            act_pending.pop(0)()
        for _ in g1:
            pass
        for c in range(4):
            K.dma("pool", ncvT_p[128 * c:128 * (c + 1), :], v32p[:, c, :], r=[("v32p", c)])
            K.dma("pool", ncvT_s[128 * c:128 * (c + 1), :], ncs[:, c, :, :].rearrange("p s j -> p (s j)"), r=[("ncs", c, 0), ("ncs", c, 1)])
        if dbg:
            for c in range(4):
                K.dma("pool", dbg_out["u_perm"][128 * c:128 * (c + 1), :], u_perm[c][:, :, :].rearrange("p r k -> p (r k)"), r=[("uperm", c, b) for b in range(-1, 5)])
                K.dma("pool", dbg_out["gs"][128 * c:128 * (c + 1), :], gs_perm[c][:, :, :].rearrange("p r k -> p (r k)"), r=[("gs", c, b) for b in range(-1, 5)])
                K.dma("pool", dbg_out["gc"][128 * c:128 * (c + 1), :], gc_perm[c][:, :, :].rearrange("p r k -> p (r k)"), r=[("gc", c, b) for b in range(-1, 5)])
                K.dma("pool", dbg_out["vperm"][128 * c:128 * (c + 1), :], v_perm[c][:, :, :].rearrange("p r k -> p (r k)"), r=[("vperm", c, b) for b in range(-1, 5)])
        K.barrier(dma_queues=("pool",))

    DU_KEYS = [("d_u", c, pc, gl) for c in range(4) for pc in range(3) for gl in range(8)]
    DV_KEYS = [("d_v", c, pc, gl) for c in range(4) for pc in range(3) for gl in range(8)]
    BA = fr("BA", [128, 32, 16]); BB = fr("BB", [128, 32, 16]); CA = fr("CA", [128, 32, 16]); CB = fr("CB", [128, 32, 16])
    BbA = fr("BbA", [128, 32, 16]); BbB = fr("BbB", [128, 32, 16]); BbA_bf = fr("BbA_bf", [128, 32, 16], BF16)
    G1t = fr("G1t", [128, 32, 16]); G2t = fr("G2t", [128, 32, 16])
    dskT = fr("dskT", [16, 32]); Dd = fr("Dd", [16, 32, 16])
    h0bf = fr("h0bf", [128, 32, NSEQ], BF16); H4 = fr("H4", [128, 32, NSEQ])
    Mj = [fr("Mj0", [128, 8, 8, 128], BF16), None]

    def gen_mj(gb, extra_r=(), eng="pool"):
        K.op(eng, lambda e: e.tensor_tensor(out=Mj[gb % 2][:, :, :, :].rearrange("p g j (h m) -> p (g j h) m", h=2),
                                               in0=I2[:].unsqueeze(1).to_broadcast([128, 128, 64]),
                                               in1=S_bf[:, 8 * gb:8 * gb + 8, :, :].rearrange("p g j h -> p (g j h)").unsqueeze(2).to_broadcast([128, 128, 64]),
                                               op=ALU.mult),
             r=["I2", "S_bf"] + list(extra_r), w=[("Mj", gb % 2)])

    cfin_perm = [sb(f"cfin{c}", [128, 8, KC], BF16, side="right") for c in range(4)]
    if dbg:
        dbg_out["cfin"] = dout("dbg_cfin", [512, 8 * KC], BF16)
    if dbg:
        dbg_out["c1sc"] = dout("dbg_c1sc", [128, 32 * KC], BF16)
    if upto >= 2:
      with ExitStack() as s2:
        f2 = lambda name, shape, dt=F32: sb(name, shape, dt, s2)
        Wc = f2("Wc", [128, 32, 5, 128], BF16)
        Vv = f2("Vv", [128, 32, KV], BF16)
        maskc = f2("maskc", [128, 16])
        RB = f2("RB", [128, 8])
        bones = f2("bones", [128, 128], BF16)
        wp_st = f2("wp_st", [128, 4, 512])
        wp_bf = f2("wp_bf", [128, 4, 512], BF16)
        Ybf = [f2(f"Ybf{i}", [128, KC], BF16) for i in range(2)]
        Ysq = [f2(f"Ysq{i}", [128, KC], BF16) for i in range(2)]
        mean = f2("mean", [128, KC]); var = f2("var", [128, KC]); rstd = f2("rstd", [128, KC])
        t1 = [f2(f"t1_{i}", [128, KC]) for i in range(2)]
        c1sc = f2("c1sc", [128, 32, KC], BF16)
        Vflat = Vv[:, :, :].rearrange("p g k -> p (g k)")
        c1f = [Vflat[:, 8 * KC * c:8 * KC * (c + 1)] for c in range(4)]
        K.dma("sp", Vv[:, :, :].rearrange("p g k -> p (g k)"), d_v[:, :], r=DV_KEYS + DU_KEYS, w=[("Vv", 0)] + [("Vvg", g_) for g_ in range(32)])
        VK = [("Vv", 0)]
        for ci in range(4):
            K.dma("sp", wp_st[:, ci, :], w_pw2[128 * ci:128 * (ci + 1), :], w=[("wp_st", ci)])
        K.op("dve", lambda e: e.tensor_reduce(out=maskc[:], in_=ident[:, :].rearrange("p (s c) -> p c s", s=8), axis=mybir.AxisListType.X, op=ALU.add), r=["ident"], w=["maskc"])
        K.op("dve", lambda e: e.tensor_reduce(out=RB[:], in_=ident[:, :].rearrange("p (s c) -> p s c", s=8), axis=mybir.AxisListType.X, op=ALU.add), r=["ident"], w=["RB"])
        K.op("dve", lambda e: e.tensor_copy(out=bones[:].rearrange("p (s c) -> p s c", s=8), in_=RB[:].unsqueeze(2).to_broadcast([128, 8, 16])), r=["RB"], w=["bones"])
        for gq in range(4):
            K.op("dve", lambda e, gq=gq: e.tensor_tensor(out=Wc[:, 8 * gq:8 * gq + 8, :, :].rearrange("p g d (r c) -> p (g d r) c", c=16),
                                                        in0=wdg[:, 320 * gq:320 * (gq + 1)].unsqueeze(2).to_broadcast([128, 320, 16]),
                                                        in1=maskc[:].unsqueeze(1).to_broadcast([128, 320, 16]), op=ALU.mult),
                 r=["wdg", "maskc"], w=[("Wc", gq)])
        for ci in range(4):
            K.op("act", lambda e, ci=ci: e.activation(out=wp_bf[:, ci, :], in_=wp_st[:, ci, :], func=AF.Identity), r=[("wp_st", ci)], w=[("wp_bf", ci)])
        if upto >= 3:
            X1 = G1t
            X2 = G2t
            K.dma("sp", BA[:], BA_d[:, :, :], w=["BA"]); K.dma("sp", BB[:], BB_d[:, :, :], w=["BB"])
            K.dma("sp", CA[:], CA_d[:, :, :], w=["CA"]); K.dma("sp", CB[:], CB_d[:, :, :], w=["CB"])
            K.dma("sp", dskT[:], dskT_d[:, :], w=["dskT"])
            K.dma("sp", X1[:], h0A_d[:, :, :], w=["G1t"]); K.dma("sp", X2[:], h0B_d[:, :, :], w=["G2t"])
            ts(X2[0:64, :, :], X2[0:64, :, :], -1.0, ALU.mult, ["G2t", ("Wc", 3)], ["G2t"], eng=G)
            b16s = lambda ap: ap.unsqueeze(2).to_broadcast([128, 32, NSEQ])
            V(lambda e: e.tensor_copy(out=h0bf[:], in_=X1[:]), ["G1t"], ["h0bf"], G)
            tt(H4[:], b16s(PR[:, 4, :]), X1[:], ALU.mult, PK + ["G1t"], ["H4"], G)
            tt(X1[:], b16s(PI[:, 4, :]), X2[:], ALU.mult, PK + ["G2t", "H4"], ["G1t"], G)
            tt(H4[:], H4[:], X1[:], ALU.add, ["H4", "G1t"], ["H4"], G)
            ts(BB[0:64, :, :], BB[0:64, :, :], -1.0, ALU.mult, ["BB"], ["BB"], eng=G)
            ts(CA[64:128, :, :], CA[64:128, :, :], -1.0, ALU.mult, ["CA"], ["CA"], eng=G)
            ts(CB[:], CB[:], -1.0, ALU.mult, ["CB"], ["CB"], eng=G)
            bc16 = lambda ap: ap.unsqueeze(2).to_broadcast([128, 32, 16])
            tt(G1t[:], bc16(KR[:]), BA[:], ALU.mult, ["KR", "BA", "H4"], ["G1t"], G); tt(G2t[:], bc16(KI[:]), BB[:], ALU.mult, ["KI", "BB", "H4", "G1t"], ["G2t"], G)
            tt(BbA[:], G1t[:], G2t[:], ALU.add, ["G1t", "G2t"], ["BbA"], G)
            tt(G1t[:], bc16(KR[:]), BB[:], ALU.mult, ["KR", "BB"], ["G1t"], G); tt(G2t[:], bc16(KI[:]), BA[:], ALU.mult, ["KI", "BA"], ["G2t"], G)
            tt(BbB[:], G1t[:], G2t[:], ALU.subtract, ["G1t", "G2t"], ["BbB"], G)
            V(lambda e: e.tensor_copy(out=BbA_bf[:], in_=BbA[:]), ["BbA"], ["BbA_bf"], G)
            tt(Dd[:], ident[0:16, 0:16].unsqueeze(1).to_broadcast([16, 32, 16]), dskT[:].unsqueeze(2).to_broadcast([16, 32, 16]), ALU.mult, ["ident", "dskT"], ["Dd"], G)
        bdw = lambda g: sccol[:, 0, g:g + 1]
        gln = lambda g: sccol[:, 1, g:g + 1]
        bln = lambda g: sccol[:, 2, g:g + 1]

        def conv_g(g, bank):
            for d in range(5):
                K.op("pe", lambda e, g=g, d=d, bank=bank: e.matmul(out=ps[bank][:, d:KP], lhsT=Wc[:, g, d, :], rhs=Vv[:, g, 0:KP - d], start=(d == 0), stop=(d == 4), skip_group_check=True),
                     r=[("Vvg", g), ("Wc", g // 8)], w=[("ps", bank)], signal=False)
            for d in range(5):
                K.op("pe", lambda e, g=g, d=d, bank=bank: e.matmul(out=ps[bank][:, KP:KC], lhsT=Wc[:, g, d, :], rhs=Vv[:, g, KP:KV].rearrange("p (s j) -> p s j", j=5)[:, :, 4 - d], start=False, stop=(d == 4), skip_group_check=True),
                     r=[("Vvg", g), ("Wc", g // 8)], w=[("ps", bank)], signal=(d == 4))

        def stats_g(g, bank):
            i = g % 2
            K.op("act", lambda e: e.activation(out=Ybf[i][:, :], in_=ps[bank][:, 0:KC], func=AF.Identity, bias=bdw(g)), r=[("ps", bank), "sccol"], w=[("Ybf", i)])
            K.op("dve", lambda e: e.scalar_tensor_tensor(out=Ysq[i][:, :], in0=ps[bank][:, 0:KC], scalar=bdw(g), in1=Ybf[i][:, :], op0=ALU.add, op1=ALU.mult), r=[("ps", bank), "sccol", ("Ybf", i)], w=[("Ysq", i)])
            K.op("pe", lambda e: e.matmul(out=ps[6][:, 0:KC], lhsT=bones[:], rhs=Ybf[i][:, :], start=(g == 0), stop=(g == 31), skip_group_check=True), r=["bones", ("Ybf", i)], w=[("ps", 6)], signal=(g == 31))
            K.op("pe", lambda e: e.matmul(out=ps[7][:, 0:KC], lhsT=bones[:], rhs=Ysq[i][:, :], start=(g == 0), stop=(g == 31), skip_group_check=True), r=["bones", ("Ysq", i)], w=[("ps", 7)], signal=(g == 31))

        conv_g(0, 0)
        for g in range(32):
            if g + 1 < 32:
                conv_g(g + 1, (g + 1) % 4)
            stats_g(g, g % 4)
        K.op("act", lambda e: e.activation(out=mean[:], in_=ps[6][:, 0:KC], func=AF.Identity, scale=1.0 / 512.0), r=[("ps", 6)], w=["mean"])
        K.op("dve", lambda e: e.tensor_tensor(out=var[:], in0=mean[:], in1=mean[:], op=ALU.mult), r=["mean"], w=["var"])
        K.op("dve", lambda e: e.scalar_tensor_tensor(out=var[:], in0=ps[7][:, 0:KC], scalar=1.0 / 512.0, in1=var[:], op0=ALU.mult, op1=ALU.subtract), r=[("ps", 7), "var"], w=["var"])
        K.op("dve", lambda e: e.tensor_scalar(out=var[:], in0=var[:], scalar1=EPS, scalar2=None, op0=ALU.add), r=["var"], w=["var"])
        K.op("act", lambda e: e.activation(out=var[:], in_=var[:], func=AF.Sqrt), r=["var"], w=["var"])
        K.op("dve", lambda e: e.reciprocal(out=rstd[:], in_=var[:]), r=["var"], w=["rstd"])

        def norm_g(g, bank):
            i = g % 2
            K.op("dve", lambda e: e.scalar_tensor_tensor(out=t1[i][:, :], in0=ps[bank][:, 0:KC], scalar=bdw(g), in1=mean[:], op0=ALU.add, op1=ALU.subtract),
                 r=[("ps", bank), "sccol", "mean"], w=[("t1", i)])
            K.op("dve", lambda e: e.tensor_tensor(out=t1[i][:, :], in0=t1[i][:, :], in1=rstd[:], op=ALU.mult), r=[("t1", i), "rstd"], w=[("t1", i)])
            K.op("act", lambda e: e.activation(out=c1sc[:, g, :], in_=t1[i][:, :], func=AF.Silu, bias=bln(g), scale=gln(g)), r=[("t1", i), "sccol"], w=[("c1sc", g)])

        conv_g(0, 0)
        for g in range(32):
            if g + 1 < 32:
                conv_g(g + 1, (g + 1) % 4)
            norm_g(g, g % 4)
            if g % 8 == 7:
                c = g // 8
                for rx in range(8):
                    K.dma("sp", d_c[128 * c:128 * (c + 1), KC * rx:KC * (rx + 1)].rearrange("(g q) k -> q g k", q=16), c1sc[16 * rx:16 * (rx + 1), 8 * c:8 * c + 8, :],
                          r=[("c1sc", g_) for g_ in range(8 * c, 8 * c + 8)], w=[("d_c", c, rx)])
                glo = (8 * KC * c) // KV
                ghi = min(31, (8 * KC * (c + 1) - 1) // KV)
                K.dma("sp", c1f[c], d_c[128 * c:128 * (c + 1), :], r=[("d_c", c, rx) for rx in range(8)], w=[("c1f", c)] + [("Vvg", g_) for g_ in range(glo, ghi + 1)])
        CK = [("c1sc", g) for g in range(32)]
        if upto >= 3:
            gen_mj(0, eng="dve")
        if dbg:
            K.dma("pool", dbg_out["c1sc"][:, :], c1sc[:, :, :].rearrange("p g k -> p (g k)"), r=CK)
        NPC = 8 * KC
        pcount = 0
        for col0 in range(0, NPC, 512):
            n = min(512, NPC - col0)
            for mo in range(4):
                bank = pcount % 4
                pcount += 1
                for ci in range(4):
                    K.op("pe", lambda e, mo=mo, ci=ci, bank=bank: e.matmul(out=ps[bank][:, 0:n], lhsT=wp_bf[:, ci, 128 * mo:128 * (mo + 1)], rhs=c1f[ci][:, col0:col0 + n], start=(ci == 0), stop=(ci == 3)),
                         r=[("wp_bf", ci), ("c1f", ci)], w=[("ps", bank)], signal=(ci == 3))
                K.op("dve", lambda e, mo=mo, bank=bank: e.scalar_tensor_tensor(out=cfin_perm[mo][:, :, :].rearrange("p r k -> p (r k)")[:, col0:col0 + n], in0=ps[bank][:, 0:n], scalar=vecs[:, 36 + mo:37 + mo],
                                                                             in1=gc_perm[mo][:, :, :].rearrange("p r k -> p (r k)")[:, col0:col0 + n], op0=ALU.add, op1=ALU.mult),
                     r=[("ps", bank), "vecs"] + [("gc", mo, b) for b in range(-1, 5)], w=[("cfin", mo, col0)])
        if dbg:
            for c in range(4):
                K.dma("pool", dbg_out["cfin"][128 * c:128 * (c + 1), :], cfin_perm[c][:, :, :].rearrange("p r k -> p (r k)"), r=[("cfin", c, col0) for col0 in range(0, NPC, 512)] + [("cfin", c, -1)])
        K.barrier()
    sA.close()

    if dbg:
        dbg_out["P"] = dout("dbg_P", [128, 2 * 9 * 32], F32)
        dbg_out["Tc"] = dout("dbg_Tc", [128, 32 * 128], BF16)
        dbg_out["BcT"] = dout("dbg_BcT", [128, 32 * 128], BF16)
        dbg_out["Fc"] = dout("dbg_Fc", [128, 32 * 192], BF16)
        dbg_out["s1rq"] = dout("dbg_s1rq", [128, 32 * KC], BF16)
        dbg_out["Hbf"] = dout("dbg_Hbf", [128, 32 * KP], BF16)
    if upto >= 3:
      with ExitStack() as s3:
        f3 = lambda name, shape, dt=F32: sb(name, shape, dt, s3)
        Fc = f3("Fc", [128, 32, 192], BF16)
        Tc = f3("Tc", [128, 32, 128], BF16); BcT = f3("BcT", [128, 32, 128], BF16)
        Hfin = f3("Hfin", [128, 32]); Hns = f3("Hns", [128, 32, NSEQ])
        E32 = f3("E32", [128, 8, 128]); Gb = f3("Gb", [128, 8, 128])
        FT1 = f3("FT1", [128, 8, 144]); FT2 = f3("FT2", [128, 8, 144])
        K_bf = f3("K_bf", [16, 8, 128], BF16)
        Mj[1] = f3("Mj1", [128, 8, 8, 128], BF16)
        U = f3("U", [128, 32, KC], BF16)
        Hbf2 = [f3(f"Hbf{i}", [128, 8, KP], BF16) for i in range(2)]
        s1rq2 = [f3(f"s1rq{i}", [128, 8, KC], BF16) for i in range(2)]
        s1f = [sb(f"s1f{c}", [128, 8, KC], BF16, side="right") for c in range(4)]
        K.dma("sp", U[:, :, :].rearrange("p g k -> p (g k)"), d_u[:, :], r=DU_KEYS, w=[("U", 0)])
        UK = [("U", 0)]
        K.op("act", lambda e: e.memzero(Tc[:]), w=["Tc0"])
        K.op("act", lambda e: e.memzero(Fc[:, :, 0:48]), w=[("Fc", -1)])
        if dbg:
            K.dma("pool", dbg_out["P"][:, 0:288], PR[:, :, :].rearrange("p t g -> p (t g)"), r=PK)
            K.dma("pool", dbg_out["P"][:, 288:576], PI[:, :, :].rearrange("p t g -> p (t g)"), r=PK)

        def gen_thunks(gb):
            gs_ = slice(8 * gb, 8 * gb + 8)
            pw = lambda P_, nt: P_[:, 0:nt, gs_].rearrange("p t g -> p g t").unsqueeze(3).to_broadcast([128, 8, nt, 16])
            bt = lambda X_, nt: X_[:, gs_, :].unsqueeze(2).to_broadcast([128, 8, nt, 16])
            v8 = lambda X_: X_[:].rearrange("p g (t q) -> p g t q", t=8)
            v9 = lambda X_: X_[:].rearrange("p g (t q) -> p g t q", t=9)
            PRK = [("Pr", sx_) for sx_ in range(8)]

            def t_e1():
                tt(v8(E32), pw(PRr, 8), bt(BbA, 8), ALU.mult, PRK + ["BbA"], ["E32a"])

            def t_e2():
                tt(v8(Gb), pw(PIr, 8), bt(BbB, 8), ALU.mult, PRK + ["BbB"], ["Gb"])

            def t_e3():
                tt(E32[:], E32[:], Gb[:], ALU.add, ["E32a", "Gb"], ["E32"])

            def t_f1():
                tt(v9(FT1), pw(PR, 9), bt(CA, 9), ALU.mult, PK + ["CA"], ["FT1"])

            def t_f2():
                tt(v9(FT2), pw(PI, 9), bt(CB, 9), ALU.mult, PK + ["CB"], ["FT2"])

            def t_f3():
                tt(Fc[:, gs_, 48:192], FT1[:], FT2[:], ALU.add, ["FT1", "FT2"], [("Fc", gb)])

            def t_tr():
                for h2 in range(2):
                    bank = 4 + h2
                    for gi in range(4):
                        gl = 4 * h2 + gi
                        K.op("pe", lambda e, gl=gl, gi=gi, bank=bank: e.transpose(out=ps[bank][:, 128 * gi:128 * (gi + 1)], in_=E32[:, gl, :], identity=ident[:]),
                             r=["E32", "ident"], w=[("ps", bank)], signal=(gi == 3))
                    K.op("act", lambda e, h2=h2, bank=bank: e.activation(out=BcT[:, 8 * gb + 4 * h2:8 * gb + 4 * h2 + 4, :], in_=ps[bank][:, :].rearrange("p (g m) -> p g m", g=4), func=AF.Identity),
                         r=[("ps", bank)], w=[("BcT", gb, h2)])

            def t_k():
                for h2 in range(2):
                    bank = 6 + h2
                    for gi in range(4):
                        gl = 4 * h2 + gi
                        g = 8 * gb + gl
                        K.op("pe", lambda e, g=g, gi=gi, bank=bank: e.matmul(out=ps[bank][0:16, 128 * gi:128 * (gi + 1)], lhsT=BbA_bf[:, g, :], rhs=Fc[:, g, 48:176], start=True, stop=False, skip_group_check=True),
                             r=["BbA_bf", ("Fc", gb)], w=[("ps", bank)], signal=False)
                        K.op("pe", lambda e, g=g, gi=gi, bank=bank: e.matmul(out=ps[bank][0:16, 128 * gi:128 * gi + 16], lhsT=Dd[:, g, :], rhs=ident[0:16, 0:16], start=False, stop=True, skip_group_check=True),
                             r=["Dd", "ident"], w=[("ps", bank)], signal=(gi == 3))
                    K.op("act", lambda e, h2=h2, bank=bank: e.activation(out=K_bf[:, 4 * h2:4 * h2 + 4, :], in_=ps[bank][0:16, :].rearrange("p (g m) -> p g m", g=4), func=AF.Identity),
                         r=[("ps", bank)], w=[("K_bf", h2)])

            def t_tc():
                for sx in range(8):
                    K.dma("sp", Tc[16 * sx:16 * (sx + 1), gs_, 16 * sx:128], K_bf[0:16, :, 0:128 - 16 * sx], r=[("K_bf", 0), ("K_bf", 1), "Tc0"], w=[("Tc", gb, sx)])

            return [t_e1, t_e2, t_e3, t_f1, t_f2, t_f3, t_tr, t_k, t_tc]

        def gen_batch(gb):
            for th in gen_thunks(gb):
                th()

        TKb = lambda gb: [("Tc", gb, sx) for sx in range(8)]
        BKb = lambda gb: [("BcT", gb, 0), ("BcT", gb, 1)]
        FKb = lambda gb: [("Fc", -1), ("Fc", gb)]

        def sample_batch(gb):
            for gl in range(8):
                g = 8 * gb + gl
                K.op("pe", lambda e, g=g, gl=gl: e.matmul(out=ps[0][:, NSEQ * gl:NSEQ * (gl + 1)], lhsT=BcT[:, g, :], rhs=U[:, g, KP:KC], start=(gl == 0), stop=False, skip_group_check=True),
                     r=BKb(gb) + UK, w=[("ps", 0)], signal=(gl == 7))
            for gl in range(8):
                g = 8 * gb + gl
                K.op("pe", lambda e, g=g, gl=gl: e.matmul(out=ps[1][:, NSEQ * gl:NSEQ * (gl + 1)], lhsT=Tc[:, g, :], rhs=U[:, g, KP:KC], start=True, stop=False, skip_group_check=True),
                     r=TKb(gb) + UK, w=[("ps", 1)], signal=False)
                K.op("pe", lambda e, g=g, gl=gl: e.matmul(out=ps[1][:, NSEQ * gl:NSEQ * (gl + 1)], lhsT=Fc[:, g, 0:128], rhs=h0bf[:, g, :], start=False, stop=True, skip_group_check=True),
                     r=FKb(gb) + ["h0bf"], w=[("ps", 1)], signal=(gl == 7))
            gs_ = slice(8 * gb, 8 * gb + 8)
            K.op("pe", lambda e: e.matmul(out=ps[0][:, 0:8 * NSEQ], lhsT=ident[:], rhs=H4[:, gs_, :].rearrange("p g s -> p (g s)"), start=False, stop=True, skip_group_check=True),
                 r=["ident", "H4"], w=[("ps", 0)], signal=True)
            K.op("act", lambda e: e.activation(out=Hns[:, gs_, :], in_=ps[0][:, 0:8 * NSEQ].rearrange("p (g s) -> p g s", g=8), func=AF.Identity), r=[("ps", 0)], w=[("Hns", gb)])
            K.op("act", lambda e: e.activation(out=s1rq2[gb % 2][:, :, KP:KC], in_=ps[1][:, 0:8 * NSEQ].rearrange("p (g s) -> p g s", g=8), func=AF.Gelu_apprx_tanh),
                 r=[("ps", 1)], w=[("s1rq", gb % 2, "s")])

        def shuffle_out(gb):
            gs_ = slice(8 * gb, 8 * gb + 8)
            rk = [("s1rq", gb % 2, b_) for b_ in range(4)] + [("s1rq", gb % 2, "s")]
            for rx in range(8):
                K.dma("sp", d_s[128 * gb:128 * (gb + 1), KC * rx:KC * (rx + 1)].rearrange("(g q) k -> q g k", q=16), s1rq2[gb % 2][16 * rx:16 * (rx + 1), :, :], r=rk, w=[("d_s", gb, rx)])
            K.dma("sp", s1f[gb][:, :, :].rearrange("p r k -> p (r k)"), d_s[128 * gb:128 * (gb + 1), :], r=[("d_s", gb, rx) for rx in range(8)], w=[("s1f", gb)])

        def cast_bank(gb, bank, eng, lo=0, hi=KP):
            hb = Hbf2[gb % 2]
            key = ("Hbf", gb % 2, bank)
            src = ps[bank][:, :].rearrange("p (g k) -> p g k", g=2)[:, :, lo:hi]
            dst = hb[:, 2 * bank:2 * bank + 2, lo:hi]
            if eng == "act":
                K.op("act", lambda e: e.activation(out=dst, in_=src, func=AF.Identity), r=[("ps", bank)], w=[key])
            else:
                V(lambda e: e.tensor_copy(out=dst, in_=src), [("ps", bank)], [key])

        CAST_ENG = {0: "act", 1: "dve", 2: "act", 3: "dve"}

        NWARM = 0
        CAST3 = {0: "act", 1: "act", 2: "act", 3: "dve"}

        def scan_batch(gb, extra=()):
            extra = list(extra)
            mj = Mj[gb % 2]
            for half in range(2):
                for gl in range(4 * half, 4 * half + 4):
                    g = 8 * gb + gl
                    bank = gl // 2
                    K.op("pe", lambda e, g=g, gl=gl, bank=bank: e.matmul(out=ps[bank][:, 256 * (gl % 2):256 * (gl % 2 + 1)], lhsT=BcT[:, g, :], rhs=U[:, g, 0:KP], start=(gl % 2 == 0), stop=True, skip_group_check=True),
                         r=BKb(gb) + UK, w=[("ps", bank)], signal=(gl % 2 == 1))
            for bank in range(4):
                cast_bank(gb, bank, CAST_ENG[bank])
            for j in range(8):
                d = 1 << j
                for half in range(2):
                    for gl in range(4 * half, 4 * half + 4):
                        g = 8 * gb + gl
                        bank = gl // 2
                        c0 = 256 * (gl % 2)
                        K.op("pe", lambda e, g=g, gl=gl, bank=bank, c0=c0, d=d, j=j: e.matmul(out=ps[bank][:, c0 + d:c0 + 256], lhsT=mj[:, gl, j, :], rhs=Hbf2[gb % 2][:, gl, 0:256 - d], start=False, stop=True, skip_group_check=True),
                             r=[("Mj", gb % 2), ("Hbf", gb % 2, bank)], w=[("ps", bank)], signal=(gl % 2 == 1))
                if 0 <= j <= 5:
                    for _w in range(NWARM):
                        K.op("pe", lambda e: e.matmul(out=ps[7][:, 0:KP], lhsT=BcT[:, 8 * gb, :], rhs=U[:, 8 * gb, 0:KP], start=True, stop=True, skip_group_check=True),
                             r=BKb(gb) + UK, w=[("ps", 7)], signal=(_w == NWARM - 1))
                heavy = bool(extra) and 0 <= j <= 5
                for bank in range(4):
                    if heavy:
                        eng_ = CAST3[bank] if j % 2 == 0 else CAST3[3 - bank]
                    else:
                        eng_ = CAST_ENG[bank] if j % 2 == 0 else CAST_ENG[3 - bank]
                    lo_, hi_ = (d, KP - 2 * d) if j < 7 else (0, KP)
                    cast_bank(gb, bank, eng_, lo_, hi_)
                if extra:
                    extra.pop(0)()
            for th in extra:
                th()
            for bank in range(4):
                g0 = 8 * gb + 2 * bank
                K.op("act", lambda e, g0=g0, bank=bank: e.activation(out=Hfin[:, g0:g0 + 2], in_=ps[bank][:, :].rearrange("p (g k) -> p g k", g=2)[:, :, 255], func=AF.Identity), r=[("ps", bank)], w=[("Hfin", g0)])

        def y_batch(gb):
            for gl in range(8):
                g = 8 * gb + gl
                bank = 4 + gl // 2
                c0 = 256 * (gl % 2)
                K.op("pe", lambda e, g=g, bank=bank, c0=c0: e.matmul(out=ps[bank][:, c0:c0 + 256], lhsT=Tc[:, g, :], rhs=U[:, g, 0:KP], start=True, stop=False, skip_group_check=True),
                     r=TKb(gb) + UK, w=[("ps", bank)], signal=False)
                K.op("pe", lambda e, g=g, bank=bank, c0=c0: e.matmul(out=ps[bank][:, c0 + 1:c0 + 256], lhsT=Fc[:, g, 64:192], rhs=Hbf2[gb % 2][:, gl, 0:255], start=False, stop=True, skip_group_check=True),
                     r=FKb(gb) + [("Hbf", gb % 2, gl // 2)], w=[("ps", bank)], signal=(gl % 2 == 1))
            for bank in range(4, 8):
                gl0 = 2 * (bank - 4)
                K.op("act", lambda e, gl0=gl0, bank=bank: e.activation(out=s1rq2[gb % 2][:, gl0:gl0 + 2, 0:KP], in_=ps[bank][:, :].rearrange("p (g k) -> p g k", g=2), func=AF.Gelu_apprx_tanh),
                     r=[("ps", bank)], w=[("s1rq", gb % 2, bank - 4)])

        import os
        ILV = os.environ.get("ILV", "1") == "1"
        def thunks_with_mj(gb_next):
            ths = gen_thunks(gb_next)
            nop = lambda: None
            return [nop] + ths[0:6] + [lambda: gen_mj(gb_next, extra_r=[("Fc", gb_next)])] + ths[6:]

        gen_batch(0)
        scan_batch(0, thunks_with_mj(1))
        y_batch(0)
        sample_batch(0)
        shuffle_out(0)
        scan_batch(1, thunks_with_mj(2))
        y_batch(1)
        sample_batch(1)
        shuffle_out(1)
        scan_batch(2, thunks_with_mj(3))
        y_batch(2)
        sample_batch(2)
        shuffle_out(2)
        scan_batch(3)
        y_batch(3)
        sample_batch(3)
        shuffle_out(3)
        K.dma("pool", hnew_s[:, :, :], Hns[:], r=[("Hns", gb) for gb in range(4)])
        K.dma("pool", hfin_p[:, :], Hfin[:], r=[("Hfin", g0) for g0 in range(0, 32, 2)])
        if dbg:
            K.dma("pool", dbg_out["U"][:, :], U[:, :, :].rearrange("p g k -> p (g k)"), r=UK)
            K.dma("pool", dbg_out["Tc"][:, :], Tc[:, :, :].rearrange("p g m -> p (g m)"), r=[k_ for gb in range(4) for k_ in TKb(gb)])
            K.dma("pool", dbg_out["BcT"][:, :], BcT[:, :, :].rearrange("p g m -> p (g m)"), r=[k_ for gb in range(4) for k_ in BKb(gb)])
            K.dma("pool", dbg_out["Fc"][:, :], Fc[:, :, :].rearrange("p g m -> p (g m)"), r=[("Fc", -1)] + [("Fc", gb) for gb in range(4)])
        K.barrier()

    if dbg:
        dbg_out["s2p"] = dout("dbg_s2p", [512, 8 * KC], BF16)
    if upto >= 4:
      with ExitStack() as s4:
        f4 = lambda name, shape, dt=F32: sb(name, shape, dt, s4)
        s2p = [f4(f"s2p{c}", [128, 8, KC], BF16) for c in range(4)]
        wg_st = f4("wg_st", [128, 4, 512]); wg_bf = f4("wg_bf", [128, 4, 512], BF16)
        wo_st = [f4(f"wo_st{i}", [128, DM]) for i in range(2)]; wo_bf = f4("wo_bf", [128, 8, DM], BF16)
        gpb = f4("gpb", [128, DM]); bpb = f4("bpb", [128, DM])
        thg = [f4(f"thg{i}", [128, 512]) for i in range(2)]; tg = [f4(f"tg{i}", [128, 512]) for i in range(2)]
        NXB = 3
        xt = [f4(f"xt{i}", [128, DM]) for i in range(NXB)]
        yn = [f4(f"yn{i}", [128, DM]) for i in range(2)]
        yo = [f4(f"yo{i}", [128, DM]) for i in range(2)]
        st6 = [f4(f"st6_{i}", [128, 2, 6]) for i in range(2)]
        mv = [f4(f"mv{i}", [128, 2]) for i in range(2)]
        ve = [f4(f"ve{i}", [128, 1]) for i in range(2)]
        rsd = [f4(f"rsd{i}", [128, 1]) for i in range(2)]
        nbv = [f4(f"nbv{i}", [128, 1]) for i in range(2)]
        smix = f4("smix", [128, 8, NS], BF16)
        aI = f4("aI", [128, 128])
        K.op("dve", lambda e: e.tensor_scalar(out=aI[:], in0=ident[:], scalar1=ALPHA, scalar2=None, op0=ALU.mult), r=["ident"], w=["aI"])
        epsc = f4("epsc", [128, 1])
        K.op("dve", lambda e: e.memset(epsc[:], EPS), w=["epsc"])
        for ci in range(4):
            K.dma("sp", wg_st[:, ci, :], w_glu[128 * ci:128 * (ci + 1), :], w=[("wg_st", ci)])
            K.op("act", lambda e, ci=ci: e.activation(out=wg_bf[:, ci, :], in_=wg_st[:, ci, :], func=AF.Identity), r=[("wg_st", ci)], w=[("wg_bf", ci)])
        wo_state = {"dma": 0, "cast": 0}

        def wo_dma(n_):
            while wo_state["dma"] < min(n_, 8):
                kk = wo_state["dma"]
                K.dma("sp", wo_st[kk % 2][:, :], w_out[128 * kk:128 * (kk + 1), :], w=[("wo_st", kk % 2)])
                wo_state["dma"] += 1

        def wo_cast_next():
            kk = wo_state["cast"]
            if kk >= 8:
                return
            wo_dma(kk + 1)
            if kk % 2 == 0:
                K.op("act", lambda e: e.activation(out=wo_bf[:, kk, :], in_=wo_st[kk % 2][:, :], func=AF.Identity), r=[("wo_st", kk % 2)], w=[("wo_bf", kk)])
            else:
                K.op("dve", lambda e: e.tensor_copy(out=wo_bf[:, kk, :], in_=wo_st[kk % 2][:, :]), r=[("wo_st", kk % 2)], w=[("wo_bf", kk)])
            wo_state["cast"] += 1
            wo_dma(kk + 3)

        wo_dma(2)
        K.dma("sp", gpb[:], gpost_d[:, :], w=["gpb"])
        K.dma("sp", bpb[:], bpost_d[:, :], w=["bpb"])

        xp_v = xp.rearrange("(k r) d -> r k d", r=8)
        yp_v = y_p.rearrange("(k r) d -> r k d", r=8)
        xs_v = xs.rearrange("(s t) d -> t s d", t=4)
        ys_v = y_s.rearrange("(s t) d -> t s d", t=4)
        blocks4 = [("p", r_) for r_ in range(8)] + [("s", 4)]
        gcount = 0
        units = [("p", r_) for r_ in range(0, 8, 2)] + [("s", 4)]
        for (kind, r_) in units:
            n = 512 if kind == "p" else NS
            view = (lambda t_, r_=r_: t_[:, r_:r_ + 2, 0:KP]) if kind == "p" else (lambda t_: t_[:, 4:8, KP:KC])
            pview = (lambda ap: ap.rearrange("p (a k) -> p a k", a=2)) if kind == "p" else (lambda ap: ap.rearrange("p (t s) -> p t s", t=4))
            wkeys = (lambda mo: [("s2p", mo, ("p", r_)), ("s2p", mo, ("p", r_ + 1))]) if kind == "p" else (lambda mo: [("s2p", mo, ("s", 4))])
            for mo in range(4):
                bank = gcount % 4
                ti = gcount % 2
                gcount += 1
                for ci in range(4):
                    K.op("pe", lambda e, mo=mo, ci=ci, bank=bank: e.matmul(out=pview(ps[bank][:, 0:n]), lhsT=wg_bf[:, ci, 128 * mo:128 * (mo + 1)], rhs=view(s1f[ci]), start=(ci == 0), stop=(ci == 3)),
                         r=[("wg_bf", ci), ("s1f", ci)], w=[("ps", bank)], signal=(ci == 3))
                K.op("act", lambda e, mo=mo, bank=bank, ti=ti: e.activation(out=thg[ti][:, 0:n], in_=ps[bank][:, 0:n], func=AF.Tanh, bias=hvec[:, 8 + mo:9 + mo], scale=0.5),
                     r=[("ps", bank), "hvec"], w=[("thg", ti)])
                K.op("dve", lambda e, mo=mo, ti=ti: e.scalar_tensor_tensor(out=pview(tg[ti][:, 0:n]), in0=pview(thg[ti][:, 0:n]), scalar=1.0, in1=view(s1f[mo]), op0=ALU.add, op1=ALU.mult),
                     r=[("thg", ti), ("s1f", mo)], w=[("tg", ti)])
                K.op("dve", lambda e, mo=mo, ti=ti: e.scalar_tensor_tensor(out=view(s2p[mo]), in0=pview(tg[ti][:, 0:n]), scalar=0.5, in1=view(gs_perm[mo]), op0=ALU.mult, op1=ALU.mult),
                     r=[("tg", ti)] + [("gs", mo, b_) for b_ in range(-1, 5)], w=wkeys(mo))
            wo_cast_next()
            wo_cast_next()
        while wo_state["cast"] < 8:
            wo_cast_next()
        for kk in range(8):
            src = s2p[kk] if kk < 4 else cfin_perm[kk - 4]
            rk = [("s2p", kk, ("s", 4)), ("s2p", kk, "z")] if kk < 4 else [("cfin", kk - 4, col0_) for col0_ in range(0, 8 * KC, 512)] + [("cfin", kk - 4, -1)]
            K.op("dve", lambda e, kk=kk, src=src: e.tensor_copy(out=smix[:, kk, :].rearrange("p (t s) -> p t s", t=4), in_=src[:, 4:8, KP:KC]), r=rk, w=[("smix", kk)])
        tcount = 0
        ocount = 0
        ln_back = []
        for (kind, r_) in blocks4:
            bkey = (kind, r_)
            tiles = [0, 128] if kind == "p" else [None]
            for k0 in tiles:
                rows = 128 if kind == "p" else NS
                xi = tcount % NXB
                oi = tcount % 2
                tcount += 1
                if kind == "p":
                    K.dma("sp", xt[xi][:, :], xp_v[r_, k0:k0 + 128, :], w=[("xt", xi)])
                else:
                    for t_ in range(4):
                        K.dma("sp", xt[xi][16 * t_:16 * (t_ + 1), :], xs_v[t_, :, :], w=[("xt", xi)] if t_ == 0 else [("xt", xi, t_)])
                xkeys = [("xt", xi)] + ([("xt", xi, t_) for t_ in range(1, 4)] if kind == "s" else [])
                banks = []
                for half in range(2):
                    bank = 2 + (ocount % 6)
                    ocount += 1
                    banks.append(bank)
                    for kk in range(8):
                        src = s2p[kk] if kk < 4 else cfin_perm[kk - 4]
                        if kind == "p":
                            lt = src[:, r_, k0:k0 + 128]
                            rk = [("s2p", kk, bkey)] if kk < 4 else [("cfin", kk - 4, col0_) for col0_ in range(0, 8 * KC, 512)] + [("cfin", kk - 4, -1)]
                        else:
                            lt = smix[:, kk, :]
                            rk = [("smix", kk)]
                        K.op("pe", lambda e, lt=lt, kk=kk, half=half, bank=bank: e.matmul(out=ps[bank][0:rows, 0:512], lhsT=lt, rhs=wo_bf[:, kk, 512 * half:512 * (half + 1)], start=(kk == 0), stop=False),
                             r=rk + [("wo_bf", kk)], w=[("ps", bank)], signal=False)
                    K.op("pe", lambda e, half=half, bank=bank: e.matmul(out=ps[bank][0:rows, 0:512], lhsT=aI[0:rows, 0:rows], rhs=xt[xi][0:rows, 512 * half:512 * (half + 1)], start=False, stop=True),
                         r=xkeys + ["aI"], w=[("ps", bank)], signal=True)
                for half in range(2):
                    K.op("dve", lambda e, half=half, bank=banks[half]: e.bn_stats(out=st6[oi][0:rows, half, :], in_=ps[bank][0:rows, 0:512]), r=[("ps", banks[half])], w=[("st6", oi, half)])
                K.op("dve", lambda e: e.bn_aggr(out=mv[oi][0:rows, :], in_=st6[oi][0:rows, :, :].rearrange("p a b -> p (a b)")), r=[("st6", oi, 0), ("st6", oi, 1)], w=[("mv", oi)])
                K.op("act", lambda e: e.activation(out=ve[oi][0:rows, :], in_=mv[oi][0:rows, 1:2], func=AF.Sqrt, bias=epsc[0:rows, :]), r=[("mv", oi), "epsc"], w=[("ve", oi)])
                for th in ln_back:
                    th()
                ln_back.clear()
                K.op("dve", lambda e: e.reciprocal(out=rsd[oi][0:rows, :], in_=ve[oi][0:rows, :]), r=[("ve", oi)], w=[("rsd", oi)])
                K.op("dve", lambda e: e.scalar_tensor_tensor(out=nbv[oi][0:rows, :], in0=mv[oi][0:rows, 0:1], scalar=-1.0, in1=rsd[oi][0:rows, :], op0=ALU.mult, op1=ALU.mult),
                     r=[("mv", oi), ("rsd", oi)], w=[("nbv", oi)])
                for half in range(2):
                    K.op("act", lambda e, half=half, bank=banks[half]: e.activation(out=yn[oi][0:rows, 512 * half:512 * (half + 1)], in_=ps[bank][0:rows, 0:512], func=AF.Identity, bias=nbv[oi][0:rows, :], scale=rsd[oi][0:rows, :]),
                         r=[("ps", banks[half]), ("nbv", oi), ("rsd", oi)], w=[("yn", oi, half)])
                def back(oi=oi, rows=rows, kind=kind, r_=r_, k0=k0):
                    K.op("dve", lambda e: e.tensor_tensor(out=yn[oi][0:rows, :], in0=yn[oi][0:rows, :], in1=gpb[0:rows, :], op=ALU.mult), r=[("yn", oi, 0), ("yn", oi, 1), "gpb"], w=[("yn", oi, 2)])
                    K.op("dve", lambda e: e.tensor_tensor(out=yo[oi][0:rows, :], in0=yn[oi][0:rows, :], in1=bpb[0:rows, :], op=ALU.add), r=[("yn", oi, 2), "bpb"], w=[("yo", oi)])
                    if kind == "p":
                        K.dma("pool", yp_v[r_, k0:k0 + 128, :], yo[oi][:, :], r=[("yo", oi)])
                    else:
                        for t_ in range(4):
                            K.dma("pool", ys_v[t_, :, :], yo[oi][16 * t_:16 * (t_ + 1), :], r=[("yo", oi)])
                ln_back.append(back)
        for th in ln_back:
            th()
        ln_back.clear()
        if dbg:
            for c in range(4):
                K.dma("pool", dbg_out["s2p"][128 * c:128 * (c + 1), :], s2p[c][:, :, :].rearrange("p r k -> p (r k)"), r=[("s2p", c, bk) for bk in [("p", r_) for r_ in range(8)] + [("s", 4), "z"]])
        K.barrier()

    K.barrier(["sp"])
    es.close()
    return nc


def _prep_inputs(inp):
    f = lambda k: np.ascontiguousarray(np.asarray(inp[k], np.float32))
    x_prompt = f("x_prompt")
    x_sample = f("x_sample")
    sre = f("state_ssm_re")[0]
    sim = f("state_ssm_im")[0]
    scv = f("state_conv")[0]
    b_in = f("b_in")[0]
    vecs = np.concatenate([
        b_in.reshape(20, 128), f("b_glu")[0].reshape(4, 128), f("b_dw")[0].reshape(4, 128),
        f("g_conv_ln")[0].reshape(4, 128), f("b_conv_ln")[0].reshape(4, 128), f("b_pw2")[0].reshape(4, 128)], 0).T
    wdwT = f("w_dw")[0].reshape(31, 4, 128).transpose(2, 1, 0)
    lam_re = f("lam_re")[0]
    lam_im = f("lam_im")[0]
    lamT = np.stack([lam_re.T, lam_im.T], 1)
    lamT = np.concatenate([lamT, lamT], 0)
    br = f("b_re")[0].transpose(1, 0, 2)
    bi = f("b_im")[0].transpose(1, 0, 2)
    cr = f("c_re")[0].transpose(2, 0, 1)
    ci = f("c_im")[0].transpose(2, 0, 1)
    wdw = f("w_dw")[0]
    wdg = np.zeros((8, 16, 32, 5, 8), np.float32)
    for s_ in range(8):
        for d_ in range(5):
            for r_ in range(8):
                tau = 8 * d_ + r_ - s_
                if 0 <= tau <= 30:
                    wdg[s_, :, :, d_, r_] = wdw[30 - tau].reshape(32, 16).T
    sccol = np.stack([np.tile(f(k)[0].reshape(32, 16).T, (8, 1)) for k in ("b_dw", "g_conv_ln", "b_conv_ln")], 1)
    shared = {
        "w_in": f("w_in")[0], "w_glu": f("w_glu")[0], "w_pw2": f("w_pw2")[0], "w_out": f("w_out")[0],
        "vecs": np.ascontiguousarray(vecs), "wdwT": np.ascontiguousarray(wdwT),
        "gpost": np.ascontiguousarray(np.broadcast_to(f("g_post")[0].reshape(1, DM), (128, DM))), "bpost": np.ascontiguousarray(np.broadcast_to(f("b_post")[0].reshape(1, DM), (128, DM))),
        "ident": np.eye(128, dtype=np.float32),
        "lamT": np.ascontiguousarray(lamT), "logdt": np.ascontiguousarray(np.broadcast_to(f("log_dt")[0].reshape(1, 32), (128, 32))),
        "BA": np.ascontiguousarray(np.concatenate([br, bi], 0)), "BBraw": np.ascontiguousarray(np.concatenate([bi, br], 0)),
        "CAraw": np.ascontiguousarray(np.concatenate([cr, ci], 0)), "CBraw": np.ascontiguousarray(np.concatenate([ci, cr], 0)),
        "dskT": np.ascontiguousarray(f("d_skip")[0].reshape(32, 16).T),
        "wdg": np.ascontiguousarray(wdg.reshape(128, 32 * 5 * 8)), "sccol": np.ascontiguousarray(sccol.astype(np.float32)),
    }
    in_maps = []
    for i in range(NCORES):
        sl = slice(NSEQ * i, NSEQ * (i + 1))
        hre = sre[sl].transpose(2, 1, 0)
        him = sim[sl].transpose(2, 1, 0)
        m = dict(shared)
        m["xp"] = x_prompt[i]
        m["xs"] = np.ascontiguousarray(x_sample[sl].reshape(NS, DM))
        m["h0A"] = np.ascontiguousarray(np.concatenate([hre, him], 0))
        m["h0Braw"] = np.ascontiguousarray(np.concatenate([him, hre], 0))
        m["stT"] = np.ascontiguousarray(scv[sl].transpose(2, 0, 1))
        pad = np.zeros((NSEQ, 40, 512), np.float32)
        pad[:, 6:36, :] = scv[sl]
        m["stP"] = np.ascontiguousarray(pad.reshape(NSEQ, 5, 8, 512).transpose(3, 2, 0, 1).reshape(512, 8, 5 * NSEQ))
        in_maps.append(m)
    return in_maps


def _assemble(results):
    y_p = np.stack([r["y_p"] for r in results], 0)
    y_s = np.concatenate([r["y_s"].reshape(NSEQ, 4, DM) for r in results], 0)
    hp = np.stack([r["hfin_p"] for r in results], 0)
    re_p = hp[:, 0:64, :].transpose(0, 2, 1)[None]
    im_p = hp[:, 64:128, :].transpose(0, 2, 1)[None]
    cv_p = np.stack([r["ncvT_p"].T for r in results], 0)[None]
    hs = np.concatenate([r["hnew_s"].transpose(2, 1, 0) for r in results], 0)
    re_s = hs[:, :, 0:64][None]
    im_s = hs[:, :, 64:128][None]
    cv_s = np.concatenate([r["ncvT_s"].reshape(512, NSEQ, 30).transpose(1, 2, 0) for r in results], 0)[None]
    c = lambda a: np.ascontiguousarray(a.astype(np.float32))
    return (c(y_p), c(y_s), c(re_p), c(im_p), c(cv_p), c(re_s), c(im_s), c(cv_s))


_NC_CACHE = {}


def kernel(**inputs):
    in_maps = _prep_inputs(inputs)
    if "nc" not in _NC_CACHE:
        _NC_CACHE["nc"] = build_program()
    res = run_bass_kernel_spmd(_NC_CACHE["nc"], in_maps, core_ids=list(range(NCORES)))
    return _assemble(res.results)
```

```python
import numpy as np
import ml_dtypes
from contextlib import ExitStack
import concourse.bass as bass
import concourse.mybir as mybir
from concourse.bass_utils import run_bass_kernel_spmd

F32 = mybir.dt.float32
BF16 = mybir.dt.bfloat16
I32 = mybir.dt.int32
AF = mybir.ActivationFunctionType
ALU = mybir.AluOpType

NCORES = 8
DM = 1024
NP = 2048
NSEQ = 16
NS = 64
NT = NP + NS
KP = 256
KC = KP + NSEQ
KV = KP + 5 * NSEQ
ALPHA = 2.0 ** 0.25
EPS = 1e-5
TWO_PI = 6.283185307179586
BLOCKS = [(0, 512), (512, 512), (1024, 512), (1536, 512), (2048, 64)]


class KB:
    def __init__(self, nc, es):
        self.nc = nc
        self.E = {"pe": nc.tensor, "act": nc.scalar, "dve": nc.vector, "pool": nc.gpsimd, "sp": nc.sync}
        self.sem = {}
        self.cnt = {}
        for k in self.E:
            self.sem[k] = es.enter_context(nc.semaphore("sem_" + k))
            self.cnt[k] = 0
        self.NDS = 20
        self.dq = ("sp", "pool", "act")
        self.dsem = {q: [es.enter_context(nc.semaphore(f"d_{q}{i}")) for i in range(self.NDS)] for q in self.dq}
        self.dval = {q: [0] * self.NDS for q in self.dq}
        self.drr = {q: 0 for q in self.dq}
        self.seen = {k: {} for k in self.E}
        self.lw = {}
        self.rd = {}
        self.pend = {k: [] for k in self.E}

    def _semobj(self, sk):
        return self.sem[sk] if isinstance(sk, str) else self.dsem[sk[0]][sk[1]]

    def _wait(self, eng, sk, val):
        if val <= 0:
            return
        if eng == "pe" and sk == "pe":
            return
        if self.seen[eng].get(sk, 0) >= val:
            return
        self.E[eng].wait_ge(self._semobj(sk), val)
        self.seen[eng][sk] = val

    def _deps(self, eng, r, w):
        for k in r:
            t = self.lw.get(k)
            if t:
                self._wait(eng, *t)
        for k in w:
            t = self.lw.get(k)
            if t:
                self._wait(eng, *t)
            for sk, v in self.rd.get(k, {}).items():
                self._wait(eng, sk, v)

    def _commit(self, t, r, w):
        for k in w:
            self.lw[k] = t
            self.rd[k] = {}
        for k in r:
            d = self.rd.setdefault(k, {})
            d[t[0]] = max(d.get(t[0], 0), t[1])

    def op(self, eng, fn, r=(), w=(), signal=True):
        r = list(r)
        w = list(w)
        self._deps(eng, r, w)
        ins = fn(self.E[eng])
        if not signal:
            self.pend[eng].append((r, w))
            return None
        self.cnt[eng] += 1
        ins.then_inc(self.sem[eng], 1)
        t = (eng, self.cnt[eng])
        for (pr, pw) in self.pend[eng]:
            self._commit(t, pr, pw)
        self.pend[eng] = []
        self._commit(t, r, w)
        return t

    def dma(self, q, out, in_, r=(), w=()):
        r = list(r)
        w = list(w)
        self._deps(q, r, w)
        i = self.drr[q] % self.NDS
        self.drr[q] += 1
        self._wait(q, (q, i), self.dval[q][i])
        ins = self.E[q].dma_start(out=out, in_=in_)
        self.dval[q][i] += 16
        ins.then_inc(self.dsem[q][i], 16)
        t = ((q, i), self.dval[q][i])
        self._commit(t, r, w)
        return t

    def barrier(self, engines=None, dma_queues=None):
        engines = engines or list(self.E)
        dma_queues = self.dq if dma_queues is None else dma_queues
        for e in engines:
            for k in self.E:
                if k == "sp" and "sp" not in dma_queues:
                    continue
                self._wait(e, k, self.cnt[k])
            for q in dma_queues:
                for i in range(self.NDS):
                    self._wait(e, (q, i), self.dval[q][i])


def build_program(upto=9, dbg=False):
    nc = bass.Bass("TRN2", target_bir_lowering=False)
    es = ExitStack()
    K = KB(nc, es)

    def din(name, shape, dt=F32):
        return nc.dram_tensor(name, list(shape), dt, kind="ExternalInput").ap()

    def dout(name, shape, dt=F32):
        return nc.dram_tensor(name, list(shape), dt, kind="ExternalOutput").ap()

    def dscr(name, shape, dt):
        return nc.dram_tensor(name, list(shape), dt).ap()

    xp = din("xp", [NP, DM])
    xs = din("xs", [NS, DM])
    w_in = din("w_in", [DM, 2560])
    w_glu = din("w_glu", [512, 512])
    w_pw2 = din("w_pw2", [512, 512])
    w_out = din("w_out", [DM, DM])
    vecs_d = din("vecs", [128, 40])
    wdwT_d = din("wdwT", [128, 4, 31])
    gpost_d = din("gpost", [128, DM])
    bpost_d = din("bpost", [128, DM])
    ident_d = din("ident", [128, 128])
    lamT_d = din("lamT", [128, 2, 32])
    logdt_d = din("logdt", [128, 32])
    BA_d = din("BA", [128, 32, 16])
    BB_d = din("BBraw", [128, 32, 16])
    CA_d = din("CAraw", [128, 32, 16])
    CB_d = din("CBraw", [128, 32, 16])
    dskT_d = din("dskT", [16, 32])
    h0A_d = din("h0A", [128, 32, NSEQ])
    h0B_d = din("h0Braw", [128, 32, NSEQ])
    stT_d = din("stT", [512, NSEQ, 30])
    stP_d = din("stP", [512, 8, 5 * NSEQ])
    wdg_d = din("wdg", [128, 32 * 5 * 8])
    sccol_d = din("sccol", [128, 3, 32])

    y_p = dout("y_p", [NP, DM])
    y_s = dout("y_s", [NS, DM])
    hfin_p = dout("hfin_p", [128, 32])
    hnew_s = dout("hnew_s", [128, 32, NSEQ])
    ncvT_p = dout("ncvT_p", [512, 30])
    ncvT_s = dout("ncvT_s", [512, NSEQ * 30])

    d_u = dscr("d_u", [128, 32 * KC], BF16)
    d_s = dscr("d_s", [512, 8 * KC], BF16)
    d_v = dscr("d_v", [128, 32 * KV], BF16)
    d_c = dscr("d_c", [512, 8 * KC], BF16)

    dbg_out = {}
    if dbg:
        dbg_out["u_perm"] = dout("dbg_u_perm", [512, 8 * KC], BF16)
        dbg_out["gs"] = dout("dbg_gs", [512, 8 * KC], BF16)
        dbg_out["gc"] = dout("dbg_gc", [512, 8 * KC], BF16)
        dbg_out["vperm"] = dout("dbg_vperm", [512, 8 * KV], BF16)
        dbg_out["U"] = dout("dbg_U", [128, 32 * KC], BF16)

    def sb(name, shape, dt, stack=es, side=None):
        return stack.enter_context(nc.sbuf_tensor("sb_" + name, list(shape), dt, side=side))

    ps = [es.enter_context(nc.psum_tensor(f"ps{i}", [128, 512], F32)) for i in range(8)]

    ident = sb("ident", [128, 128], F32, side="right")
    vecs = sb("vecs", [128, 40], F32, side="right")
    hvec = sb("hvec", [128, 12], F32, side="right")
    gs_perm = [sb(f"gs_perm{c}", [128, 8, KC], BF16, side="right") for c in range(4)]
    sA = ExitStack()
    gc_perm = [sb(f"gc_perm{c}", [128, 8, KC], BF16, sA) for c in range(4)]

    wdg = sb("wdg", [128, 32 * 5 * 8], F32, side="right")
    sccol = sb("sccol", [128, 3, 32], F32, side="right")
    K.dma("sp", ident[:], ident_d[:, :], w=["ident"])
    K.dma("sp", vecs[:], vecs_d[:, :], w=["vecs"])
    K.op("dve", lambda e: e.tensor_scalar(out=hvec[:, 0:8], in0=vecs[:, 8:16], scalar1=0.5, scalar2=None, op0=ALU.mult),
         r=["vecs"], w=["hvec"])
    K.op("dve", lambda e: e.tensor_scalar(out=hvec[:, 8:12], in0=vecs[:, 20:24], scalar1=0.5, scalar2=None, op0=ALU.mult),
         r=["vecs"], w=["hvec"])
    for c in range(4):
        K.op("dve", lambda e, c=c: e.memset(gc_perm[c][:, 0:4, KP:KC], 0.0), w=[("gc", c, -1)])
        K.op("dve", lambda e, c=c: e.memset(gs_perm[c][:, 0:4, KP:KC], 0.0), w=[("gs", c, -1)])

    fr = lambda name, shape, dt=F32: sb(name, shape, dt, side="right")
    lam = fr("lam", [128, 2, 32]); dtb = fr("dtb", [128, 32])
    T = [fr(f"T{i}", [128, 4, 32]) for i in range(4)]
    PR = fr("PR", [128, 9, 32]); PI = fr("PI", [128, 9, 32])
    QR = fr("QR", [128, 8, 32]); QI = fr("QI", [128, 8, 32])
    mag = fr("mag", [128, 32]); phi = fr("phi", [128, 32]); kf = fr("kf", [128, 32]); ki = fr("ki", [128, 32], I32)
    rr = fr("rr", [128, 32]); rc = fr("rc", [128, 32]); msk = fr("msk", [128, 32]); sinv = fr("sinv", [128, 32]); cosv = fr("cosv", [128, 32])
    KR = fr("KR", [128, 32]); KI = fr("KI", [128, 32]); den = fr("den", [128, 32]); am1 = fr("am1", [128, 32])
    S4 = fr("S4", [128, 32, 8, 2]); S_bf = fr("S_bf", [128, 32, 8, 2], BF16); I2 = fr("I2", [128, 64], BF16)
    PRr = fr("PRr", [128, 8, 32]); PIr = fr("PIr", [128, 8, 32])

    def V(fn, r, w, eng="dve"):
        return K.op(eng, fn, r=r, w=w)

    def tt(o, a, b, op, r, w, eng="dve"):
        return K.op(eng, lambda e: e.tensor_tensor(out=o, in0=a, in1=b, op=op), r=r, w=w)

    def ts(o, a, s1_, op0, r, w, s2_=None, op1=None, eng="dve"):
        if op1 is None:
            return K.op(eng, lambda e: e.tensor_scalar(out=o, in0=a, scalar1=s1_, scalar2=None, op0=op0), r=r, w=w)
        return K.op(eng, lambda e: e.tensor_scalar(out=o, in0=a, scalar1=s1_, scalar2=s2_, op0=op0, op1=op1), r=r, w=w)

    G = "pool"

    def cmul(oR, oI, xR, xI, yR, yI, nslots, r, w):
        a, b, c_, d_ = (T[i][:, 0:nslots, :] for i in range(4))
        tk = ["T0", "T1", "T2", "T3"]
        tt(a, xR, yR, ALU.mult, r, [tk[0]], G); tt(b, xI, yI, ALU.mult, r, [tk[1]], G)
        tt(c_, xR, yI, ALU.mult, r, [tk[2]], G); tt(d_, xI, yR, ALU.mult, r, [tk[3]], G)
        tt(oR, a, b, ALU.subtract, [tk[0], tk[1]], w, G); tt(oI, c_, d_, ALU.add, [tk[2], tk[3]], w, G)

    def g1_gen():
        K.dma("sp", lam[:], lamT_d[:, :, :], w=["lam"]); K.dma("sp", dtb[:], logdt_d[:, :], w=["dtb"])
        lr = lam[:, 0, :]; li = lam[:, 1, :]
        K.op("act", lambda e: e.activation(out=dtb[:], in_=dtb[:], func=AF.Exp), r=["dtb"], w=["dtb"])
        yield
        tt(mag[:], lr, dtb[:], ALU.mult, ["lam", "dtb"], ["mag"], G)
        yield
        K.op("act", lambda e: e.activation(out=mag[:], in_=mag[:], func=AF.Exp), r=["mag"], w=["mag"])
        yield
        tt(phi[:], li, dtb[:], ALU.mult, ["lam", "dtb"], ["phi"], G)
        ts(kf[:], phi[:], 1.0 / TWO_PI, ALU.mult, ["phi"], ["kf"], eng=G)
        yield
        V(lambda e: e.tensor_copy(out=ki[:], in_=kf[:]), ["kf"], ["ki"])
        V(lambda e: e.tensor_copy(out=kf[:], in_=ki[:]), ["ki"], ["kf"])
        yield
        ts(kf[:], kf[:], -TWO_PI, ALU.mult, ["kf"], ["kf"], eng=G)
        tt(rr[:], kf[:], phi[:], ALU.add, ["kf", "phi"], ["rr"], G)
        PIS = 3.141592
        ts(rr[:], rr[:], -PIS, ALU.max, ["rr"], ["rr"], PIS, ALU.min, eng=G)
        ts(rc[:], rr[:], TWO_PI / 4.0, ALU.add, ["rr"], ["rc"], eng=G)
        ts(msk[:], rc[:], PIS, ALU.is_gt, ["rc"], ["msk"], eng=G)
        ts(msk[:], msk[:], -TWO_PI, ALU.mult, ["msk"], ["msk"], eng=G)
        tt(rc[:], rc[:], msk[:], ALU.add, ["msk", "rc"], ["rc"], G)
        ts(rc[:], rc[:], -PIS, ALU.max, ["rc"], ["rc"], PIS, ALU.min, eng=G)
        yield
        K.op("act", lambda e: e.activation(out=sinv[:], in_=rr[:], func=AF.Sin), r=["rr"], w=["sinv"])
        K.op("act", lambda e: e.activation(out=cosv[:], in_=rc[:], func=AF.Sin), r=["rc"], w=["cosv"])
        yield
        V(lambda e: e.memset(PR[:, 0, :], 1.0), [], ["P0"], G); V(lambda e: e.memset(PI[:, 0, :], 0.0), [], ["P0"], G)
        tt(PR[:, 1, :], mag[:], cosv[:], ALU.mult, ["mag", "cosv"], ["P1"], G); tt(PI[:, 1, :], mag[:], sinv[:], ALU.mult, ["mag", "sinv"], ["P1"], G)
        cmul(PR[:, 2:3, :], PI[:, 2:3, :], PR[:, 1:2, :], PI[:, 1:2, :], PR[:, 1:2, :], PI[:, 1:2, :], 1, ["P1"], ["P2"])
        cmul(PR[:, 3:5, :], PI[:, 3:5, :], PR[:, 1:3, :], PI[:, 1:3, :], PR[:, 2:3, :].to_broadcast([128, 2, 32]), PI[:, 2:3, :].to_broadcast([128, 2, 32]), 2, ["P1", "P2"], ["P34"])
        cmul(PR[:, 5:9, :], PI[:, 5:9, :], PR[:, 1:5, :], PI[:, 1:5, :], PR[:, 4:5, :].to_broadcast([128, 4, 32]), PI[:, 4:5, :].to_broadcast([128, 4, 32]), 4, ["P1", "P2", "P34"], ["P58"])
        PK = ["P0", "P1", "P2", "P34", "P58"]
        V(lambda e: e.tensor_copy(out=QR[:, 0, :], in_=PR[:, 8, :]), PK, [("Q", 0)], G); V(lambda e: e.tensor_copy(out=QI[:, 0, :], in_=PI[:, 8, :]), PK, [("Q", 0)], G)
        for j in range(7):
            cmul(QR[:, j + 1:j + 2, :], QI[:, j + 1:j + 2, :], QR[:, j:j + 1, :], QI[:, j:j + 1, :], QR[:, j:j + 1, :], QI[:, j:j + 1, :], 1, [("Q", j)], [("Q", j + 1)])
        QK = [("Q", j) for j in range(8)]
        tt(den[:], lr, lr, ALU.mult, ["lam"], ["den"], G); tt(kf[:], li, li, ALU.mult, ["lam"], ["kf"], G)
        tt(den[:], den[:], kf[:], ALU.add, ["den", "kf"], ["den"], G)
        yield
        V(lambda e: e.reciprocal(out=den[:], in_=den[:]), ["den"], ["den"])
        yield
        ts(am1[:], PR[:, 1, :], -1.0, ALU.add, ["P1"], ["am1"], eng=G)
        tt(KR[:], am1[:], lr, ALU.mult, ["am1", "lam"], ["KR"], G); tt(kf[:], PI[:, 1, :], li, ALU.mult, ["P1", "lam"], ["kf"], G)
        tt(KR[:], KR[:], kf[:], ALU.add, ["KR", "kf"], ["KR"], G); tt(KR[:], KR[:], den[:], ALU.mult, ["KR", "den"], ["KR"], G)
        tt(KI[:], PI[:, 1, :], lr, ALU.mult, ["P1", "lam"], ["KI"], G); tt(kf[:], am1[:], li, ALU.mult, ["am1", "lam"], ["kf"], G)
        tt(KI[:], KI[:], kf[:], ALU.subtract, ["KI", "kf"], ["KI"], G); tt(KI[:], KI[:], den[:], ALU.mult, ["KI", "den"], ["KI"], G)
        V(lambda e: e.tensor_copy(out=S4[0:64, :, :, 0], in_=QR[0:64, :, :].rearrange("p j g -> p g j")), QK, ["S4a"], G)
        ts(S4[64:128, :, :, 0], QI[64:128, :, :].rearrange("p j g -> p g j"), -1.0, ALU.mult, QK, ["S4b"], eng=G)
        V(lambda e: e.tensor_copy(out=S4[0:64, :, :, 1], in_=QI[0:64, :, :].rearrange("p j g -> p g j")), QK, ["S4c"], G)
        V(lambda e: e.tensor_copy(out=S4[64:128, :, :, 1], in_=QR[64:128, :, :].rearrange("p j g -> p g j")), QK, ["S4d"], G)
        V(lambda e: e.tensor_copy(out=S_bf[:], in_=S4[:]), ["S4a", "S4b", "S4c", "S4d"], ["S_bf"], G)
        tt(I2[:], ident[:, 0:64], ident[:, 64:128], ALU.add, ["ident"], ["I2"], G)
        for sx_ in range(8):
            V(lambda e, sx_=sx_: e.tensor_copy(out=PRr[:, sx_, :], in_=PR[:, 7 - sx_, :]), PK, [("Pr", sx_)], G)
            V(lambda e, sx_=sx_: e.tensor_copy(out=PIr[:, sx_, :], in_=PI[:, 7 - sx_, :]), PK, [("Pr", sx_)], G)


        yield

    g1 = g1_gen()
    next(g1)

    PK = ["P0", "P1", "P2", "P34", "P58"]
    QK = [("Q", j) for j in range(8)]

    with ExitStack() as s1:
        w_bf = sb("w_bf", [128, 8, 2560], BF16, s1)
        wst = [sb(f"wst{i}", [128, 8, 256], F32, s1) for i in range(2)]
        x_sb = [sb(f"x_sb{i}", [128, DM], F32, s1) for i in range(4)]
        xT = [sb(f"xT{i}", [128, 8, 512], BF16, s1) for i in range(2)]
        th = [sb(f"th{i}", [128, 512], F32, s1) for i in range(2)]
        ah = [sb(f"ah{i}", [128, 512], F32, s1) for i in range(2)]
        st32 = sb("st32", [128, 4, NSEQ, 30], F32, s1)
        stp32 = sb("stp32", [128, 8, 5 * NSEQ], F32, s1)
        v32p = sb("v32p", [128, 4, 30], F32, s1)
        ncs = sb("ncs", [128, 4, NSEQ, 30], F32, s1)
        u_perm = [sb(f"u_perm{c}", [128, 8, KC], BF16, s1) for c in range(4)]
        v_perm = [sb(f"v_perm{c}", [128, 8, KV], BF16, s1) for c in range(4)]

        def load_x(bi):
            t0, n = BLOCKS[bi]
            if n == 512:
                for i in range(4):
                    K.dma("sp", x_sb[i][:, :], xp[t0 + 128 * i:t0 + 128 * (i + 1), :], w=[("x", i)])
            else:
                K.dma("sp", x_sb[0][0:NS, :], xs[:, :], w=[("x", 0)])

        for c in range(4):
            K.op("dve", lambda e, c=c: e.memset(u_perm[c][:, 0:4, KP:KC], 0.0), w=[("uperm", c, -1)])
        BORD = [0, 4, 1, 2, 3]
        load_x(BORD[0])
        M_ORDER = [12, 8, 13, 9, 14, 10, 15, 11, 0, 1, 2, 3, 4, 5, 6, 7, 16, 17, 18, 19]
        CH_ORDER = []
        for m_ in M_ORDER:
            if m_ // 2 not in CH_ORDER:
                CH_ORDER.append(m_ // 2)
        w_state = {"dma": 0, "cast": set()}

        def w_issue_dma(upto_n):
            while w_state["dma"] < min(upto_n, len(CH_ORDER)):
                i = w_state["dma"]
                cc = CH_ORDER[i]
                for kk in range(8):
                    K.dma("sp", wst[i % 2][:, kk, :], w_in[128 * kk:128 * (kk + 1), 256 * cc:256 * (cc + 1)], w=[("wst", i % 2, kk)])
                w_state["dma"] += 1

        def w_ensure(cc):
            if cc in w_state["cast"]:
                return
            i = CH_ORDER.index(cc)
            w_issue_dma(i + 1)
            K.op("dve", lambda e: e.tensor_copy(out=w_bf[:, 0:4, 256 * cc:256 * (cc + 1)], in_=wst[i % 2][:, 0:4, :]),
                 r=[("wst", i % 2, kk) for kk in range(0, 4)], w=[("wbf", cc, 0)])
            K.op("act", lambda e: e.activation(out=w_bf[:, 4:8, 256 * cc:256 * (cc + 1)], in_=wst[i % 2][:, 4:8, :], func=AF.Identity),
                 r=[("wst", i % 2, kk) for kk in range(4, 8)], w=[("wbf", cc, 1)])
            w_state["cast"].add(cc)
            w_issue_dma(i + 3)

        w_issue_dma(2)
        K.dma("sp", wdg[:], wdg_d[:, :], w=["wdg"])
        K.dma("sp", sccol[:], sccol_d[:, :, :], w=["sccol"])
        def st_dma(c):
            K.dma("sp", st32[:, c, :, :], stT_d[128 * c:128 * (c + 1), :, :], w=[("st32", c)])
            K.dma("sp", stp32[:, :, :], stP_d[128 * c:128 * (c + 1), :, :], w=["stp32"])

        def st_copy(c):
            K.op("dve", lambda e, c=c: e.tensor_copy(out=ncs[:, c, :, 0:26], in_=st32[:, c, :, 4:30]), r=[("st32", c)], w=[("ncs", c, 0)])
            K.op("dve", lambda e, c=c: e.tensor_copy(out=v_perm[c][:, :, KP:KV], in_=stp32[:, :, :]), r=["stp32"], w=[("vperm", c, -1)])
        ST_AT = {1: ("d", 0), 3: ("c", 0), 4: ("d", 1), 6: ("c", 1), 7: ("d", 2), 9: ("c", 2), 11: ("d", 3), 13: ("c", 3)}
        zi = 0
        ev = 0
        ct_count = [0]
        G1_AT = {2: 1, 3: 1, 4: 1, 10: 1, 11: 1, 22: 1, 23: 1, 60: 1, 61: 1}

        def g1_tick():
            ct_count[0] += 1
            if ct_count[0] in G1_AT:
                try:
                    next(g1)
                except StopIteration:
                    pass
        du_v = d_u.rearrange("(r c) (g k) -> c r g k", c=16, g=32)
        dv_v = d_v.rearrange("(r c) (g k) -> c r g k", c=16, g=32)

        act_pending = []

        def scratch_piece(pc):
            for c in range(4):
                for gl in range(8):
                    g = 8 * c + gl
                    ps_ = slice(16 * gl, 16 * (gl + 1))
                    if pc < 2:
                        ks = slice(128 * pc, 128 * (pc + 1))
                        rk_u = [("uperm", c, 2 * pc), ("uperm", c, 2 * pc + 1)]
                        rk_v = [("vperm", c, 2 * pc), ("vperm", c, 2 * pc + 1)]
                        K.dma("sp", du_v[:, :, g, ks], u_perm[c][ps_, :, ks], r=rk_u, w=[("d_u", c, pc, gl)])
                        if pc == 1 and gl % 2 == 0:
                            act_pending.append(lambda g=g, ks=ks, c=c, ps_=ps_, rk_v=rk_v, pc=pc, gl=gl: K.dma("act", dv_v[:, :, g, ks], v_perm[c][ps_, :, ks], r=rk_v, w=[("d_v", c, pc, gl)]))
                        else:
                            K.dma("sp", dv_v[:, :, g, ks], v_perm[c][ps_, :, ks], r=rk_v, w=[("d_v", c, pc, gl)])
                    else:
                        K.dma("sp", du_v[:, :, g, KP:KC], u_perm[c][ps_, :, KP:KC], r=[("uperm", c, -1), ("uperm", c, 4)], w=[("d_u", c, pc, gl)])
                        K.dma("sp", dv_v[:, :, g, KP:KV], v_perm[c][ps_, :, KP:KV], r=[("vperm", c, -1), ("vperm", c, 4)], w=[("d_v", c, pc, gl)])

        def do_transposes(pos):
            bj = BORD[pos]
            t0_, n_ = BLOCKS[bj]
            bb_ = pos % 2
            nt_ = 4 if n_ == 512 else 1
            rows_ = 128 if n_ == 512 else NS
            for kk in range(8):
                bank = 4 + (kk % 2)
                for i in range(nt_):
                    K.op("pe", lambda e, i=i, kk=kk, bank=bank: e.transpose(out=ps[bank][:, rows_ * i:rows_ * (i + 1)], in_=x_sb[i][0:rows_, 128 * kk:128 * (kk + 1)], identity=ident[0:rows_, 0:rows_]),
                         r=[("x", i), "ident"], w=[("ps", bank)], signal=(i == nt_ - 1))
                if kk % 2 == 0:
                    K.op("dve", lambda e, kk=kk, bank=bank: e.tensor_copy(out=xT[bb_][:, kk, 0:n_], in_=ps[bank][:, 0:n_]), r=[("ps", bank)], w=[("xT", bb_, kk)])
                else:
                    K.op("act", lambda e, kk=kk, bank=bank: e.activation(out=xT[bb_][:, kk, 0:n_], in_=ps[bank][:, 0:n_], func=AF.Identity), r=[("ps", bank)], w=[("xT", bb_, kk)])
            if pos + 1 < len(BORD):
                load_x(BORD[pos + 1])

        do_transposes(0)
        for pos_, bi in enumerate(BORD):
            t0, n = BLOCKS[bi]
            bb = pos_ % 2
            if pos_ == 2:
                scratch_piece(2)
            nt = 4 if n == 512 else 1
            rows = 128 if n == 512 else NS
            if bi == 2:
                scratch_piece(0)
            for mi_, m in enumerate(M_ORDER):
                for _ in range(min(2, len(act_pending))):
                    act_pending.pop(0)()
                if mi_ == 10 and pos_ + 1 < len(BORD):
                    do_transposes(pos_ + 1)
                if bi == 3 and mi_ == 12:
                    scratch_piece(1)
                if pos_ == 0 and mi_ in ST_AT:
                    kind_, c_ = ST_AT[mi_]
                    (st_dma if kind_ == "d" else st_copy)(c_)
                bank = zi % 4
                zi += 1
                cc = m // 2
                w_ensure(cc)
                for kk in range(8):
                    K.op("pe", lambda e, m=m, kk=kk, bank=bank: e.matmul(out=ps[bank][:, 0:n], lhsT=w_bf[:, kk, 128 * m:128 * (m + 1)], rhs=xT[bb][:, kk, 0:n], start=(kk == 0), stop=(kk == 7)),
                         r=[("wbf", cc, kk // 4), ("xT", bb, kk)], w=[("ps", bank)], signal=(kk == 7))
                g1_tick()
                pz = ps[bank][:, 0:n]
                bcol = vecs[:, m:m + 1]
                c = m % 4
                if m < 8:
                    dst_t = u_perm[c] if m < 4 else gs_perm[c]
                    key = ("uperm" if m < 4 else "gs", c, bi)
                    func = AF.Identity if m < 4 else AF.Silu
                    if n == 512:
                        k0 = t0 // 8
                        o_ap = dst_t[:, :, k0:k0 + 64]
                        i_ap = pz.rearrange("p (k r) -> p r k", r=8)
                    else:
                        o_ap = dst_t[:, 4:8, KP:KC]
                        i_ap = pz.rearrange("p (s t) -> p t s", t=4)
                    K.op("act", lambda e, o_ap=o_ap, i_ap=i_ap, func=func, bcol=bcol: e.activation(out=o_ap, in_=i_ap, func=func, bias=bcol),
                         r=[("ps", bank), "vecs"], w=[key])
                elif m >= 16:
                    if n == 512:
                        k0 = t0 // 8
                        o_ap = gc_perm[c][:, :, k0:k0 + 64]
                        i_ap = pz.rearrange("p (k r) -> p r k", r=8)
                    else:
                        o_ap = gc_perm[c][:, 4:8, KP:KC]
                        i_ap = pz.rearrange("p (s t) -> p t s", t=4)
                    K.op("act", lambda e, o_ap=o_ap, i_ap=i_ap, bcol=bcol: e.activation(out=o_ap, in_=i_ap, func=AF.Silu, bias=bcol),
                         r=[("ps", bank), "vecs"], w=[("gc", c, bi)])
                elif m >= 12:
                    K.op("act", lambda e, c=c, pz=pz: e.activation(out=th[c % 2][:, 0:n], in_=pz, func=AF.Tanh, bias=hvec[:, 4 + c:5 + c], scale=0.5),
                         r=[("ps", bank), "hvec"], w=[("th", c % 2)])
                else:
                    K.op("act", lambda e, c=c, pz=pz: e.activation(out=ah[c % 2][:, 0:n], in_=pz, func=AF.Identity, bias=hvec[:, c:c + 1], scale=0.5),
                         r=[("ps", bank), "hvec"], w=[("ah", c % 2)])
                    if n == 512:
                        k0 = t0 // 8
                        o_ap = v_perm[c][:, :, k0:k0 + 64]
                        i0 = th[c % 2][:, 0:n].rearrange("p (k r) -> p r k", r=8)
                        i1 = ah[c % 2][:, 0:n].rearrange("p (k r) -> p r k", r=8)
                        key = ("vperm", c, bi)
                    else:
                        o_ap = v_perm[c][:, 4:8, KP:KV].rearrange("p t (s j) -> p t s j", j=5)[:, :, :, 4]
                        i0 = th[c % 2][:, 0:n].rearrange("p (s t) -> p t s", t=4)
                        i1 = ah[c % 2][:, 0:n].rearrange("p (s t) -> p t s", t=4)
                        key = ("vperm", c, bi)
                    K.op("dve", lambda e, o_ap=o_ap, i0=i0, i1=i1: e.scalar_tensor_tensor(out=o_ap, in0=i0, scalar=1.0, in1=i1, op0=ALU.add, op1=ALU.mult),
                         r=[("th", c % 2), ("ah", c % 2)], w=[key])
                    if bi == 3:
                        K.op("dve", lambda e, c=c: e.scalar_tensor_tensor(out=v32p[:, c, :], in0=th[c % 2][:, 482:512], scalar=1.0, in1=ah[c % 2][:, 482:512], op0=ALU.add, op1=ALU.mult),
                             r=[("th", c % 2), ("ah", c % 2)], w=[("v32p", c)])
                    if bi == 4:
                        j0 = th[c % 2][:, 0:n].rearrange("p (s t) -> p s t", t=4)
                        j1 = ah[c % 2][:, 0:n].rearrange("p (s t) -> p s t", t=4)
                        K.op("dve", lambda e, c=c, i0=j0, i1=j1: e.scalar_tensor_tensor(out=ncs[:, c, :, 26:30], in0=i0, scalar=1.0, in1=i1, op0=ALU.add, op1=ALU.mult),
                             r=[("th", c % 2), ("ah", c % 2)], w=[("ncs", c, 1)])
        while act_pending:
            act_pending.pop(0)()
        for _ in g1:
            pass
        for c in range(4):
            K.dma("pool", ncvT_p[128 * c:128 * (c + 1), :], v32p[:, c, :], r=[("v32p", c)])
            K.dma("pool", ncvT_s[128 * c:128 * (c + 1), :], ncs[:, c, :, :].rearrange("p s j -> p (s j)"), r=[("ncs", c, 0), ("ncs", c, 1)])
        if dbg:
            for c in range(4):
                K.dma("pool", dbg_out["u_perm"][128 * c:128 * (c + 1), :], u_perm[c][:, :, :].rearrange("p r k -> p (r k)"), r=[("uperm", c, b) for b in range(-1, 5)])
                K.dma("pool", dbg_out["gs"][128 * c:128 * (c + 1), :], gs_perm[c][:, :, :].rearrange("p r k -> p (r k)"), r=[("gs", c, b) for b in range(-1, 5)])
                K.dma("pool", dbg_out["gc"][128 * c:128 * (c + 1), :], gc_perm[c][:, :, :].rearrange("p r k -> p (r k)"), r=[("gc", c, b) for b in range(-1, 5)])
                K.dma("pool", dbg_out["vperm"][128 * c:128 * (c + 1), :], v_perm[c][:, :, :].rearrange("p r k -> p (r k)"), r=[("vperm", c, b) for b in range(-1, 5)])
        K.barrier(dma_queues=("pool",))

    DU_KEYS = [("d_u", c, pc, gl) for c in range(4) for pc in range(3) for gl in range(8)]
    DV_KEYS = [("d_v", c, pc, gl) for c in range(4) for pc in range(3) for gl in range(8)]
    BA = fr("BA", [128, 32, 16]); BB = fr("BB", [128, 32, 16]); CA = fr("CA", [128, 32, 16]); CB = fr("CB", [128, 32, 16])
    BbA = fr("BbA", [128, 32, 16]); BbB = fr("BbB", [128, 32, 16]); BbA_bf = fr("BbA_bf", [128, 32, 16], BF16)
    G1t = fr("G1t", [128, 32, 16]); G2t = fr("G2t", [128, 32, 16])
    dskT = fr("dskT", [16, 32]); Dd = fr("Dd", [16, 32, 16])
    h0bf = fr("h0bf", [128, 32, NSEQ], BF16); H4 = fr("H4", [128, 32, NSEQ])
    Mj = [fr("Mj0", [128, 8, 8, 128], BF16), None]

    def gen_mj(gb, extra_r=(), eng="pool"):
        K.op(eng, lambda e: e.tensor_tensor(out=Mj[gb % 2][:, :, :, :].rearrange("p g j (h m) -> p (g j h) m", h=2),
                                               in0=I2[:].unsqueeze(1).to_broadcast([128, 128, 64]),
                                               in1=S_bf[:, 8 * gb:8 * gb + 8, :, :].rearrange("p g j h -> p (g j h)").unsqueeze(2).to_broadcast([128, 128, 64]),
                                               op=ALU.mult),
             r=["I2", "S_bf"] + list(extra_r), w=[("Mj", gb % 2)])

    cfin_perm = [sb(f"cfin{c}", [128, 8, KC], BF16, side="right") for c in range(4)]
    if dbg:
        dbg_out["cfin"] = dout("dbg_cfin", [512, 8 * KC], BF16)
    if dbg:
        dbg_out["c1sc"] = dout("dbg_c1sc", [128, 32 * KC], BF16)
    if upto >= 2:
      with ExitStack() as s2:
        f2 = lambda name, shape, dt=F32: sb(name, shape, dt, s2)
        Wc = f2("Wc", [128, 32, 5, 128], BF16)
        Vv = f2("Vv", [128, 32, KV], BF16)
        maskc = f2("maskc", [128, 16])
        RB = f2("RB", [128, 8])
        bones = f2("bones", [128, 128], BF16)
        wp_st = f2("wp_st", [128, 4, 512])
        wp_bf = f2("wp_bf", [128, 4, 512], BF16)
        Ybf = [f2(f"Ybf{i}", [128, KC], BF16) for i in range(2)]
        Ysq = [f2(f"Ysq{i}", [128, KC], BF16) for i in range(2)]
        mean = f2("mean", [128, KC]); var = f2("var", [128, KC]); rstd = f2("rstd", [128, KC])
        t1 = [f2(f"t1_{i}", [128, KC]) for i in range(2)]
        c1sc = f2("c1sc", [128, 32, KC], BF16)
        Vflat = Vv[:, :, :].rearrange("p g k -> p (g k)")
        c1f = [Vflat[:, 8 * KC * c:8 * KC * (c + 1)] for c in range(4)]
        K.dma("sp", Vv[:, :, :].rearrange("p g k -> p (g k)"), d_v[:, :], r=DV_KEYS + DU_KEYS, w=[("Vv", 0)] + [("Vvg", g_) for g_ in range(32)])
        VK = [("Vv", 0)]
        for ci in range(4):
            K.dma("sp", wp_st[:, ci, :], w_pw2[128 * ci:128 * (ci + 1), :], w=[("wp_st", ci)])
        K.op("dve", lambda e: e.tensor_reduce(out=maskc[:], in_=ident[:, :].rearrange("p (s c) -> p c s", s=8), axis=mybir.AxisListType.X, op=ALU.add), r=["ident"], w=["maskc"])
        K.op("dve", lambda e: e.tensor_reduce(out=RB[:], in_=ident[:, :].rearrange("p (s c) -> p s c", s=8), axis=mybir.AxisListType.X, op=ALU.add), r=["ident"], w=["RB"])
        K.op("dve", lambda e: e.tensor_copy(out=bones[:].rearrange("p (s c) -> p s c", s=8), in_=RB[:].unsqueeze(2).to_broadcast([128, 8, 16])), r=["RB"], w=["bones"])
        for gq in range(4):
            K.op("dve", lambda e, gq=gq: e.tensor_tensor(out=Wc[:, 8 * gq:8 * gq + 8, :, :].rearrange("p g d (r c) -> p (g d r) c", c=16),
                                                        in0=wdg[:, 320 * gq:320 * (gq + 1)].unsqueeze(2).to_broadcast([128, 320, 16]),
                                                        in1=maskc[:].unsqueeze(1).to_broadcast([128, 320, 16]), op=ALU.mult),
                 r=["wdg", "maskc"], w=[("Wc", gq)])
        for ci in range(4):
            K.op("act", lambda e, ci=ci: e.activation(out=wp_bf[:, ci, :], in_=wp_st[:, ci, :], func=AF.Identity), r=[("wp_st", ci)], w=[("wp_bf", ci)])
        if upto >= 3:
            X1 = G1t
            X2 = G2t
            K.dma("sp", BA[:], BA_d[:, :, :], w=["BA"]); K.dma("sp", BB[:], BB_d[:, :, :], w=["BB"])
            K.dma("sp", CA[:], CA_d[:, :, :], w=["CA"]); K.dma("sp", CB[:], CB_d[:, :, :], w=["CB"])
            K.dma("sp", dskT[:], dskT_d[:, :], w=["dskT"])
            K.dma("sp", X1[:], h0A_d[:, :, :], w=["G1t"]); K.dma("sp", X2[:], h0B_d[:, :, :], w=["G2t"])
            ts(X2[0:64, :, :], X2[0:64, :, :], -1.0, ALU.mult, ["G2t", ("Wc", 3)], ["G2t"], eng=G)
            b16s = lambda ap: ap.unsqueeze(2).to_broadcast([128, 32, NSEQ])
            V(lambda e: e.tensor_copy(out=h0bf[:], in_=X1[:]), ["G1t"], ["h0bf"], G)
            tt(H4[:], b16s(PR[:, 4, :]), X1[:], ALU.mult, PK + ["G1t"], ["H4"], G)
            tt(X1[:], b16s(PI[:, 4, :]), X2[:], ALU.mult, PK + ["G2t", "H4"], ["G1t"], G)
            tt(H4[:], H4[:], X1[:], ALU.add, ["H4", "G1t"], ["H4"], G)
            ts(BB[0:64, :, :], BB[0:64, :, :], -1.0, ALU.mult, ["BB"], ["BB"], eng=G)
            ts(CA[64:128, :, :], CA[64:128, :, :], -1.0, ALU.mult, ["CA"], ["CA"], eng=G)
            ts(CB[:], CB[:], -1.0, ALU.mult, ["CB"], ["CB"], eng=G)
            bc16 = lambda ap: ap.unsqueeze(2).to_broadcast([128, 32, 16])
            tt(G1t[:], bc16(KR[:]), BA[:], ALU.mult, ["KR", "BA", "H4"], ["G1t"], G); tt(G2t[:], bc16(KI[:]), BB[:], ALU.mult, ["KI", "BB", "H4", "G1t"], ["G2t"], G)
            tt(BbA[:], G1t[:], G2t[:], ALU.add, ["G1t", "G2t"], ["BbA"], G)
            tt(G1t[:], bc16(KR[:]), BB[:], ALU.mult, ["KR", "BB"], ["G1t"], G); tt(G2t[:], bc16(KI[:]), BA[:], ALU.mult, ["KI", "BA"], ["G2t"], G)
            tt(BbB[:], G1t[:], G2t[:], ALU.subtract, ["G1t", "G2t"], ["BbB"], G)
            V(lambda e: e.tensor_copy(out=BbA_bf[:], in_=BbA[:]), ["BbA"], ["BbA_bf"], G)
            tt(Dd[:], ident[0:16, 0:16].unsqueeze(1).to_broadcast([16, 32, 16]), dskT[:].unsqueeze(2).to_broadcast([16, 32, 16]), ALU.mult, ["ident", "dskT"], ["Dd"], G)
        bdw = lambda g: sccol[:, 0, g:g + 1]
        gln = lambda g: sccol[:, 1, g:g + 1]
        bln = lambda g: sccol[:, 2, g:g + 1]

        def conv_g(g, bank):
            for d in range(5):
                K.op("pe", lambda e, g=g, d=d, bank=bank: e.matmul(out=ps[bank][:, d:KP], lhsT=Wc[:, g, d, :], rhs=Vv[:, g, 0:KP - d], start=(d == 0), stop=(d == 4), skip_group_check=True),
                     r=[("Vvg", g), ("Wc", g // 8)], w=[("ps", bank)], signal=False)
            for d in range(5):
                K.op("pe", lambda e, g=g, d=d, bank=bank: e.matmul(out=ps[bank][:, KP:KC], lhsT=Wc[:, g, d, :], rhs=Vv[:, g, KP:KV].rearrange("p (s j) -> p s j", j=5)[:, :, 4 - d], start=False, stop=(d == 4), skip_group_check=True),
                     r=[("Vvg", g), ("Wc", g // 8)], w=[("ps", bank)], signal=(d == 4))

        def stats_g(g, bank):
            i = g % 2
            K.op("act", lambda e: e.activation(out=Ybf[i][:, :], in_=ps[bank][:, 0:KC], func=AF.Identity, bias=bdw(g)), r=[("ps", bank), "sccol"], w=[("Ybf", i)])
            K.op("dve", lambda e: e.scalar_tensor_tensor(out=Ysq[i][:, :], in0=ps[bank][:, 0:KC], scalar=bdw(g), in1=Ybf[i][:, :], op0=ALU.add, op1=ALU.mult), r=[("ps", bank), "sccol", ("Ybf", i)], w=[("Ysq", i)])
            K.op("pe", lambda e: e.matmul(out=ps[6][:, 0:KC], lhsT=bones[:], rhs=Ybf[i][:, :], start=(g == 0), stop=(g == 31), skip_group_check=True), r=["bones", ("Ybf", i)], w=[("ps", 6)], signal=(g == 31))
            K.op("pe", lambda e: e.matmul(out=ps[7][:, 0:KC], lhsT=bones[:], rhs=Ysq[i][:, :], start=(g == 0), stop=(g == 31), skip_group_check=True), r=["bones", ("Ysq", i)], w=[("ps", 7)], signal=(g == 31))

        conv_g(0, 0)
        for g in range(32):
            if g + 1 < 32:
                conv_g(g + 1, (g + 1) % 4)
            stats_g(g, g % 4)
        K.op("act", lambda e: e.activation(out=mean[:], in_=ps[6][:, 0:KC], func=AF.Identity, scale=1.0 / 512.0), r=[("ps", 6)], w=["mean"])
        K.op("dve", lambda e: e.tensor_tensor(out=var[:], in0=mean[:], in1=mean[:], op=ALU.mult), r=["mean"], w=["var"])
        K.op("dve", lambda e: e.scalar_tensor_tensor(out=var[:], in0=ps[7][:, 0:KC], scalar=1.0 / 512.0, in1=var[:], op0=ALU.mult, op1=ALU.subtract), r=[("ps", 7), "var"], w=["var"])
        K.op("dve", lambda e: e.tensor_scalar(out=var[:], in0=var[:], scalar1=EPS, scalar2=None, op0=ALU.add), r=["var"], w=["var"])
        K.op("act", lambda e: e.activation(out=var[:], in_=var[:], func=AF.Sqrt), r=["var"], w=["var"])
        K.op("dve", lambda e: e.reciprocal(out=rstd[:], in_=var[:]), r=["var"], w=["rstd"])

        def norm_g(g, bank):
            i = g % 2
            K.op("dve", lambda e: e.scalar_tensor_tensor(out=t1[i][:, :], in0=ps[bank][:, 0:KC], scalar=bdw(g), in1=mean[:], op0=ALU.add, op1=ALU.subtract),
                 r=[("ps", bank), "sccol", "mean"], w=[("t1", i)])
            K.op("dve", lambda e: e.tensor_tensor(out=t1[i][:, :], in0=t1[i][:, :], in1=rstd[:], op=ALU.mult), r=[("t1", i), "rstd"], w=[("t1", i)])
            K.op("act", lambda e: e.activation(out=c1sc[:, g, :], in_=t1[i][:, :], func=AF.Silu, bias=bln(g), scale=gln(g)), r=[("t1", i), "sccol"], w=[("c1sc", g)])

        conv_g(0, 0)
        for g in range(32):
            if g + 1 < 32:
                conv_g(g + 1, (g + 1) % 4)
            norm_g(g, g % 4)
            if g % 8 == 7:
                c = g // 8
                for rx in range(8):
                    K.dma("sp", d_c[128 * c:128 * (c + 1), KC * rx:KC * (rx + 1)].rearrange("(g q) k -> q g k", q=16), c1sc[16 * rx:16 * (rx + 1), 8 * c:8 * c + 8, :],
                          r=[("c1sc", g_) for g_ in range(8 * c, 8 * c + 8)], w=[("d_c", c, rx)])
                glo = (8 * KC * c) // KV
                ghi = min(31, (8 * KC * (c + 1) - 1) // KV)
                K.dma("sp", c1f[c], d_c[128 * c:128 * (c + 1), :], r=[("d_c", c, rx) for rx in range(8)], w=[("c1f", c)] + [("Vvg", g_) for g_ in range(glo, ghi + 1)])
        CK = [("c1sc", g) for g in range(32)]
        if upto >= 3:
            gen_mj(0, eng="dve")
        if dbg:
            K.dma("pool", dbg_out["c1sc"][:, :], c1sc[:, :, :].rearrange("p g k -> p (g k)"), r=CK)
        NPC = 8 * KC
        pcount = 0
        for col0 in range(0, NPC, 512):
            n = min(512, NPC - col0)
            for mo in range(4):
                bank = pcount % 4
                pcount += 1
                for ci in range(4):
                    K.op("pe", lambda e, mo=mo, ci=ci, bank=bank: e.matmul(out=ps[bank][:, 0:n], lhsT=wp_bf[:, ci, 128 * mo:128 * (mo + 1)], rhs=c1f[ci][:, col0:col0 + n], start=(ci == 0), stop=(ci == 3)),
                         r=[("wp_bf", ci), ("c1f", ci)], w=[("ps", bank)], signal=(ci == 3))
                K.op("dve", lambda e, mo=mo, bank=bank: e.scalar_tensor_tensor(out=cfin_perm[mo][:, :, :].rearrange("p r k -> p (r k)")[:, col0:col0 + n], in0=ps[bank][:, 0:n], scalar=vecs[:, 36 + mo:37 + mo],
                                                                             in1=gc_perm[mo][:, :, :].rearrange("p r k -> p (r k)")[:, col0:col0 + n], op0=ALU.add, op1=ALU.mult),
                     r=[("ps", bank), "vecs"] + [("gc", mo, b) for b in range(-1, 5)], w=[("cfin", mo, col0)])
        if dbg:
            for c in range(4):
                K.dma("pool", dbg_out["cfin"][128 * c:128 * (c + 1), :], cfin_perm[c][:, :, :].rearrange("p r k -> p (r k)"), r=[("cfin", c, col0) for col0 in range(0, NPC, 512)] + [("cfin", c, -1)])
        K.barrier()
    sA.close()

    if dbg:
        dbg_out["P"] = dout("dbg_P", [128, 2 * 9 * 32], F32)
        dbg_out["Tc"] = dout("dbg_Tc", [128, 32 * 128], BF16)
        dbg_out["BcT"] = dout("dbg_BcT", [128, 32 * 128], BF16)
        dbg_out["Fc"] = dout("dbg_Fc", [128, 32 * 192], BF16)
        dbg_out["s1rq"] = dout("dbg_s1rq", [128, 32 * KC], BF16)
        dbg_out["Hbf"] = dout("dbg_Hbf", [128, 32 * KP], BF16)
    if upto >= 3:
      with ExitStack() as s3:
        f3 = lambda name, shape, dt=F32: sb(name, shape, dt, s3)
        Fc = f3("Fc", [128, 32, 192], BF16)
        Tc = f3("Tc", [128, 32, 128], BF16); BcT = f3("BcT", [128, 32, 128], BF16)
        Hfin = f3("Hfin", [128, 32]); Hns = f3("Hns", [128, 32, NSEQ])
        E32 = f3("E32", [128, 8, 128]); Gb = f3("Gb", [128, 8, 128])
        FT1 = f3("FT1", [128, 8, 144]); FT2 = f3("FT2", [128, 8, 144])
        K_bf = f3("K_bf", [16, 8, 128], BF16)
        Mj[1] = f3("Mj1", [128, 8, 8, 128], BF16)
        U = f3("U", [128, 32, KC], BF16)
        Hbf2 = [f3(f"Hbf{i}", [128, 8, KP], BF16) for i in range(2)]
        s1rq2 = [f3(f"s1rq{i}", [128, 8, KC], BF16) for i in range(2)]
        s1f = [sb(f"s1f{c}", [128, 8, KC], BF16, side="right") for c in range(4)]
        K.dma("sp", U[:, :, :].rearrange("p g k -> p (g k)"), d_u[:, :], r=DU_KEYS, w=[("U", 0)])
        UK = [("U", 0)]
        K.op("act", lambda e: e.memzero(Tc[:]), w=["Tc0"])
        K.op("act", lambda e: e.memzero(Fc[:, :, 0:48]), w=[("Fc", -1)])
        if dbg:
            K.dma("pool", dbg_out["P"][:, 0:288], PR[:, :, :].rearrange("p t g -> p (t g)"), r=PK)
            K.dma("pool", dbg_out["P"][:, 288:576], PI[:, :, :].rearrange("p t g -> p (t g)"), r=PK)

        def gen_thunks(gb):
            gs_ = slice(8 * gb, 8 * gb + 8)
            pw = lambda P_, nt: P_[:, 0:nt, gs_].rearrange("p t g -> p g t").unsqueeze(3).to_broadcast([128, 8, nt, 16])
            bt = lambda X_, nt: X_[:, gs_, :].unsqueeze(2).to_broadcast([128, 8, nt, 16])
            v8 = lambda X_: X_[:].rearrange("p g (t q) -> p g t q", t=8)
            v9 = lambda X_: X_[:].rearrange("p g (t q) -> p g t q", t=9)
            PRK = [("Pr", sx_) for sx_ in range(8)]

            def t_e1():
                tt(v8(E32), pw(PRr, 8), bt(BbA, 8), ALU.mult, PRK + ["BbA"], ["E32a"])

            def t_e2():
                tt(v8(Gb), pw(PIr, 8), bt(BbB, 8), ALU.mult, PRK + ["BbB"], ["Gb"])

            def t_e3():
                tt(E32[:], E32[:], Gb[:], ALU.add, ["E32a", "Gb"], ["E32"])

            def t_f1():
                tt(v9(FT1), pw(PR, 9), bt(CA, 9), ALU.mult, PK + ["CA"], ["FT1"])

            def t_f2():
                tt(v9(FT2), pw(PI, 9), bt(CB, 9), ALU.mult, PK + ["CB"], ["FT2"])

            def t_f3():
                tt(Fc[:, gs_, 48:192], FT1[:], FT2[:], ALU.add, ["FT1", "FT2"], [("Fc", gb)])

            def t_tr():
                for h2 in range(2):
                    bank = 4 + h2
                    for gi in range(4):
                        gl = 4 * h2 + gi
                        K.op("pe", lambda e, gl=gl, gi=gi, bank=bank: e.transpose(out=ps[bank][:, 128 * gi:128 * (gi + 1)], in_=E32[:, gl, :], identity=ident[:]),
                             r=["E32", "ident"], w=[("ps", bank)], signal=(gi == 3))
                    K.op("act", lambda e, h2=h2, bank=bank: e.activation(out=BcT[:, 8 * gb + 4 * h2:8 * gb + 4 * h2 + 4, :], in_=ps[bank][:, :].rearrange("p (g m) -> p g m", g=4), func=AF.Identity),
                         r=[("ps", bank)], w=[("BcT", gb, h2)])

            def t_k():
                for h2 in range(2):
                    bank = 6 + h2
                    for gi in range(4):
                        gl = 4 * h2 + gi
                        g = 8 * gb + gl
                        K.op("pe", lambda e, g=g, gi=gi, bank=bank: e.matmul(out=ps[bank][0:16, 128 * gi:128 * (gi + 1)], lhsT=BbA_bf[:, g, :], rhs=Fc[:, g, 48:176], start=True, stop=False, skip_group_check=True),
                             r=["BbA_bf", ("Fc", gb)], w=[("ps", bank)], signal=False)
                        K.op("pe", lambda e, g=g, gi=gi, bank=bank: e.matmul(out=ps[bank][0:16, 128 * gi:128 * gi + 16], lhsT=Dd[:, g, :], rhs=ident[0:16, 0:16], start=False, stop=True, skip_group_check=True),
                             r=["Dd", "ident"], w=[("ps", bank)], signal=(gi == 3))
                    K.op("act", lambda e, h2=h2, bank=bank: e.activation(out=K_bf[:, 4 * h2:4 * h2 + 4, :], in_=ps[bank][0:16, :].rearrange("p (g m) -> p g m", g=4), func=AF.Identity),
                         r=[("ps", bank)], w=[("K_bf", h2)])

            def t_tc():
                for sx in range(8):
                    K.dma("sp", Tc[16 * sx:16 * (sx + 1), gs_, 16 * sx:128], K_bf[0:16, :, 0:128 - 16 * sx], r=[("K_bf", 0), ("K_bf", 1), "Tc0"], w=[("Tc", gb, sx)])

            return [t_e1, t_e2, t_e3, t_f1, t_f2, t_f3, t_tr, t_k, t_tc]

        def gen_batch(gb):
            for th in gen_thunks(gb):
                th()

        TKb = lambda gb: [("Tc", gb, sx) for sx in range(8)]
        BKb = lambda gb: [("BcT", gb, 0), ("BcT", gb, 1)]
        FKb = lambda gb: [("Fc", -1), ("Fc", gb)]

        def sample_batch(gb):
            for gl in range(8):
                g = 8 * gb + gl
                K.op("pe", lambda e, g=g, gl=gl: e.matmul(out=ps[0][:, NSEQ * gl:NSEQ * (gl + 1)], lhsT=BcT[:, g, :], rhs=U[:, g, KP:KC], start=(gl == 0), stop=False, skip_group_check=True),
                     r=BKb(gb) + UK, w=[("ps", 0)], signal=(gl == 7))
            for gl in range(8):
                g = 8 * gb + gl
                K.op("pe", lambda e, g=g, gl=gl: e.matmul(out=ps[1][:, NSEQ * gl:NSEQ * (gl + 1)], lhsT=Tc[:, g, :], rhs=U[:, g, KP:KC], start=True, stop=False, skip_group_check=True),
                     r=TKb(gb) + UK, w=[("ps", 1)], signal=False)
                K.op("pe", lambda e, g=g, gl=gl: e.matmul(out=ps[1][:, NSEQ * gl:NSEQ * (gl + 1)], lhsT=Fc[:, g, 0:128], rhs=h0bf[:, g, :], start=False, stop=True, skip_group_check=True),
                     r=FKb(gb) + ["h0bf"], w=[("ps", 1)], signal=(gl == 7))
            gs_ = slice(8 * gb, 8 * gb + 8)
            K.op("pe", lambda e: e.matmul(out=ps[0][:, 0:8 * NSEQ], lhsT=ident[:], rhs=H4[:, gs_, :].rearrange("p g s -> p (g s)"), start=False, stop=True, skip_group_check=True),
                 r=["ident", "H4"], w=[("ps", 0)], signal=True)
            K.op("act", lambda e: e.activation(out=Hns[:, gs_, :], in_=ps[0][:, 0:8 * NSEQ].rearrange("p (g s) -> p g s", g=8), func=AF.Identity), r=[("ps", 0)], w=[("Hns", gb)])
            K.op("act", lambda e: e.activation(out=s1rq2[gb % 2][:, :, KP:KC], in_=ps[1][:, 0:8 * NSEQ].rearrange("p (g s) -> p g s", g=8), func=AF.Gelu_apprx_tanh),
                 r=[("ps", 1)], w=[("s1rq", gb % 2, "s")])

        def shuffle_out(gb):
            gs_ = slice(8 * gb, 8 * gb + 8)
            rk = [("s1rq", gb % 2, b_) for b_ in range(4)] + [("s1rq", gb % 2, "s")]
            for rx in range(8):
                K.dma("act" if (gb == 3 and rx % 2 == 1) else "sp", d_s[128 * gb:128 * (gb + 1), KC * rx:KC * (rx + 1)].rearrange("(g q) k -> q g k", q=16), s1rq2[gb % 2][16 * rx:16 * (rx + 1), :, :], r=rk, w=[("d_s", gb, rx)])
            K.dma("sp", s1f[gb][:, :, :].rearrange("p r k -> p (r k)"), d_s[128 * gb:128 * (gb + 1), :], r=[("d_s", gb, rx) for rx in range(8)], w=[("s1f", gb)])

        def cast_bank(gb, bank, eng, lo=0, hi=KP):
            hb = Hbf2[gb % 2]
            key = ("Hbf", gb % 2, bank)
            src = ps[bank][:, :].rearrange("p (g k) -> p g k", g=2)[:, :, lo:hi]
            dst = hb[:, 2 * bank:2 * bank + 2, lo:hi]
            if eng == "act":
                K.op("act", lambda e: e.activation(out=dst, in_=src, func=AF.Identity), r=[("ps", bank)], w=[key])
            else:
                V(lambda e: e.tensor_copy(out=dst, in_=src), [("ps", bank)], [key])

        CAST_ENG = {0: "act", 1: "dve", 2: "act", 3: "dve"}

        NWARM = 0
        CAST3 = {0: "act", 1: "act", 2: "act", 3: "dve"}

        def scan_batch(gb, extra=()):
            extra = list(extra)
            mj = Mj[gb % 2]
            for half in range(2):
                for gl in range(4 * half, 4 * half + 4):
                    g = 8 * gb + gl
                    bank = gl // 2
                    K.op("pe", lambda e, g=g, gl=gl, bank=bank: e.matmul(out=ps[bank][:, 256 * (gl % 2):256 * (gl % 2 + 1)], lhsT=BcT[:, g, :], rhs=U[:, g, 0:KP], start=(gl % 2 == 0), stop=True, skip_group_check=True),
                         r=BKb(gb) + UK, w=[("ps", bank)], signal=(gl % 2 == 1))
            for bank in range(4):
                cast_bank(gb, bank, CAST_ENG[bank])
            for j in range(8):
                d = 1 << j
                for half in range(2):
                    for gl in range(4 * half, 4 * half + 4):
                        g = 8 * gb + gl
                        bank = gl // 2
                        c0 = 256 * (gl % 2)
                        K.op("pe", lambda e, g=g, gl=gl, bank=bank, c0=c0, d=d, j=j: e.matmul(out=ps[bank][:, c0 + d:c0 + 256], lhsT=mj[:, gl, j, :], rhs=Hbf2[gb % 2][:, gl, 0:256 - d], start=False, stop=True, skip_group_check=True),
                             r=[("Mj", gb % 2), ("Hbf", gb % 2, bank)], w=[("ps", bank)], signal=(gl % 2 == 1))
                if 0 <= j <= 5:
                    for _w in range(NWARM):
                        K.op("pe", lambda e: e.matmul(out=ps[7][:, 0:KP], lhsT=BcT[:, 8 * gb, :], rhs=U[:, 8 * gb, 0:KP], start=True, stop=True, skip_group_check=True),
                             r=BKb(gb) + UK, w=[("ps", 7)], signal=(_w == NWARM - 1))
                heavy = bool(extra) and 0 <= j <= 5
                for bank in range(4):
                    if heavy:
                        eng_ = CAST3[bank] if j % 2 == 0 else CAST3[3 - bank]
                    else:
                        eng_ = CAST_ENG[bank] if j % 2 == 0 else CAST_ENG[3 - bank]
                    lo_, hi_ = (d, KP - 2 * d) if j < 7 else (0, KP)
                    cast_bank(gb, bank, eng_, lo_, hi_)
                if extra:
                    extra.pop(0)()
            for th in extra:
                th()
            for bank in range(4):
                g0 = 8 * gb + 2 * bank
                K.op("act", lambda e, g0=g0, bank=bank: e.activation(out=Hfin[:, g0:g0 + 2], in_=ps[bank][:, :].rearrange("p (g k) -> p g k", g=2)[:, :, 255], func=AF.Identity), r=[("ps", bank)], w=[("Hfin", g0)])

        def y_batch(gb):
            for gl in range(8):
                g = 8 * gb + gl
                bank = 4 + gl // 2
                c0 = 256 * (gl % 2)
                K.op("pe", lambda e, g=g, bank=bank, c0=c0: e.matmul(out=ps[bank][:, c0:c0 + 256], lhsT=Tc[:, g, :], rhs=U[:, g, 0:KP], start=True, stop=False, skip_group_check=True),
                     r=TKb(gb) + UK, w=[("ps", bank)], signal=False)
                K.op("pe", lambda e, g=g, bank=bank, c0=c0: e.matmul(out=ps[bank][:, c0 + 1:c0 + 256], lhsT=Fc[:, g, 64:192], rhs=Hbf2[gb % 2][:, gl, 0:255], start=False, stop=True, skip_group_check=True),
                     r=FKb(gb) + [("Hbf", gb % 2, gl // 2)], w=[("ps", bank)], signal=(gl % 2 == 1))
            for bank in range(4, 8):
                gl0 = 2 * (bank - 4)
                K.op("act", lambda e, gl0=gl0, bank=bank: e.activation(out=s1rq2[gb % 2][:, gl0:gl0 + 2, 0:KP], in_=ps[bank][:, :].rearrange("p (g k) -> p g k", g=2), func=AF.Gelu_apprx_tanh),
                     r=[("ps", bank)], w=[("s1rq", gb % 2, bank - 4)])

        import os
        ILV = os.environ.get("ILV", "1") == "1"
        def thunks_with_mj(gb_next):
            ths = gen_thunks(gb_next)
            nop = lambda: None
            return [nop] + ths[0:6] + [lambda: gen_mj(gb_next, extra_r=[("Fc", gb_next)])] + ths[6:]

        gen_batch(0)
        scan_batch(0, thunks_with_mj(1))
        y_batch(0)
        sample_batch(0)
        shuffle_out(0)
        scan_batch(1, thunks_with_mj(2))
        y_batch(1)
        sample_batch(1)
        shuffle_out(1)
        scan_batch(2, thunks_with_mj(3))
        y_batch(2)
        sample_batch(2)
        shuffle_out(2)
        scan_batch(3)
        y_batch(3)
        sample_batch(3)
        shuffle_out(3)
        K.dma("pool", hnew_s[:, :, :], Hns[:], r=[("Hns", gb) for gb in range(4)])
        K.dma("pool", hfin_p[:, :], Hfin[:], r=[("Hfin", g0) for g0 in range(0, 32, 2)])
        if dbg:
            K.dma("pool", dbg_out["U"][:, :], U[:, :, :].rearrange("p g k -> p (g k)"), r=UK)
            K.dma("pool", dbg_out["Tc"][:, :], Tc[:, :, :].rearrange("p g m -> p (g m)"), r=[k_ for gb in range(4) for k_ in TKb(gb)])
            K.dma("pool", dbg_out["BcT"][:, :], BcT[:, :, :].rearrange("p g m -> p (g m)"), r=[k_ for gb in range(4) for k_ in BKb(gb)])
            K.dma("pool", dbg_out["Fc"][:, :], Fc[:, :, :].rearrange("p g m -> p (g m)"), r=[("Fc", -1)] + [("Fc", gb) for gb in range(4)])
        K.barrier()

    if dbg:
        dbg_out["s2p"] = dout("dbg_s2p", [512, 8 * KC], BF16)
    if upto >= 4:
      with ExitStack() as s4:
        f4 = lambda name, shape, dt=F32: sb(name, shape, dt, s4)
        s2p = [f4(f"s2p{c}", [128, 8, KC], BF16) for c in range(4)]
        wg_st = f4("wg_st", [128, 4, 512]); wg_bf = f4("wg_bf", [128, 4, 512], BF16)
        wo_st = [f4(f"wo_st{i}", [128, DM]) for i in range(2)]; wo_bf = f4("wo_bf", [128, 8, DM], BF16)
        gpb = f4("gpb", [128, DM]); bpb = f4("bpb", [128, DM])
        thg = [f4(f"thg{i}", [128, 512]) for i in range(2)]; tg = [f4(f"tg{i}", [128, 512]) for i in range(2)]
        NXB = 3
        xt = [f4(f"xt{i}", [128, DM]) for i in range(NXB)]
        yn = [f4(f"yn{i}", [128, DM]) for i in range(2)]
        yo = [f4(f"yo{i}", [128, DM]) for i in range(2)]
        st6 = [f4(f"st6_{i}", [128, 2, 6]) for i in range(2)]
        mv = [f4(f"mv{i}", [128, 2]) for i in range(2)]
        ve = [f4(f"ve{i}", [128, 1]) for i in range(2)]
        rsd = [f4(f"rsd{i}", [128, 1]) for i in range(2)]
        nbv = [f4(f"nbv{i}", [128, 1]) for i in range(2)]
        smix = f4("smix", [128, 8, NS], BF16)
        aI = f4("aI", [128, 128])
        K.op("dve", lambda e: e.tensor_scalar(out=aI[:], in0=ident[:], scalar1=ALPHA, scalar2=None, op0=ALU.mult), r=["ident"], w=["aI"])
        epsc = f4("epsc", [128, 1])
        K.op("dve", lambda e: e.memset(epsc[:], EPS), w=["epsc"])
        for ci in range(4):
            K.dma("sp", wg_st[:, ci, :], w_glu[128 * ci:128 * (ci + 1), :], w=[("wg_st", ci)])
            K.op("act", lambda e, ci=ci: e.activation(out=wg_bf[:, ci, :], in_=wg_st[:, ci, :], func=AF.Identity), r=[("wg_st", ci)], w=[("wg_bf", ci)])
        wo_state = {"dma": 0, "cast": 0}

        def wo_dma(n_):
            while wo_state["dma"] < min(n_, 8):
                kk = wo_state["dma"]
                K.dma("sp", wo_st[kk % 2][:, :], w_out[128 * kk:128 * (kk + 1), :], w=[("wo_st", kk % 2)])
                wo_state["dma"] += 1

        def wo_cast_next():
            kk = wo_state["cast"]
            if kk >= 8:
                return
            wo_dma(kk + 1)
            if kk % 2 == 0:
                K.op("act", lambda e: e.activation(out=wo_bf[:, kk, :], in_=wo_st[kk % 2][:, :], func=AF.Identity), r=[("wo_st", kk % 2)], w=[("wo_bf", kk)])
            else:
                K.op("dve", lambda e: e.tensor_copy(out=wo_bf[:, kk, :], in_=wo_st[kk % 2][:, :]), r=[("wo_st", kk % 2)], w=[("wo_bf", kk)])
            wo_state["cast"] += 1
            wo_dma(kk + 3)

        wo_dma(2)
        K.dma("sp", gpb[:], gpost_d[:, :], w=["gpb"])
        K.dma("sp", bpb[:], bpost_d[:, :], w=["bpb"])

        xp_v = xp.rearrange("(k r) d -> r k d", r=8)
        yp_v = y_p.rearrange("(k r) d -> r k d", r=8)
        xs_v = xs.rearrange("(s t) d -> t s d", t=4)
        ys_v = y_s.rearrange("(s t) d -> t s d", t=4)
        blocks4 = [("p", r_) for r_ in range(8)] + [("s", 4)]
        gcount = 0
        units = [("p", r_) for r_ in range(0, 8, 2)] + [("s", 4)]
        for (kind, r_) in units:
            n = 512 if kind == "p" else NS
            view = (lambda t_, r_=r_: t_[:, r_:r_ + 2, 0:KP]) if kind == "p" else (lambda t_: t_[:, 4:8, KP:KC])
            pview = (lambda ap: ap.rearrange("p (a k) -> p a k", a=2)) if kind == "p" else (lambda ap: ap.rearrange("p (t s) -> p t s", t=4))
            wkeys = (lambda mo: [("s2p", mo, ("p", r_)), ("s2p", mo, ("p", r_ + 1))]) if kind == "p" else (lambda mo: [("s2p", mo, ("s", 4))])
            for mo in range(4):
                bank = gcount % 4
                ti = gcount % 2
                gcount += 1
                for ci in range(4):
                    K.op("pe", lambda e, mo=mo, ci=ci, bank=bank: e.matmul(out=pview(ps[bank][:, 0:n]), lhsT=wg_bf[:, ci, 128 * mo:128 * (mo + 1)], rhs=view(s1f[ci]), start=(ci == 0), stop=(ci == 3)),
                         r=[("wg_bf", ci), ("s1f", ci)], w=[("ps", bank)], signal=(ci == 3))
                K.op("act", lambda e, mo=mo, bank=bank, ti=ti: e.activation(out=thg[ti][:, 0:n], in_=ps[bank][:, 0:n], func=AF.Tanh, bias=hvec[:, 8 + mo:9 + mo], scale=0.5),
                     r=[("ps", bank), "hvec"], w=[("thg", ti)])
                K.op("dve", lambda e, mo=mo, ti=ti: e.scalar_tensor_tensor(out=pview(tg[ti][:, 0:n]), in0=pview(thg[ti][:, 0:n]), scalar=1.0, in1=view(s1f[mo]), op0=ALU.add, op1=ALU.mult),
                     r=[("thg", ti), ("s1f", mo)], w=[("tg", ti)])
                K.op("dve", lambda e, mo=mo, ti=ti: e.scalar_tensor_tensor(out=view(s2p[mo]), in0=pview(tg[ti][:, 0:n]), scalar=0.5, in1=view(gs_perm[mo]), op0=ALU.mult, op1=ALU.mult),
                     r=[("tg", ti)] + [("gs", mo, b_) for b_ in range(-1, 5)], w=wkeys(mo))
            wo_cast_next()
            wo_cast_next()
        while wo_state["cast"] < 8:
            wo_cast_next()
        for kk in range(8):
            src = s2p[kk] if kk < 4 else cfin_perm[kk - 4]
            rk = [("s2p", kk, ("s", 4)), ("s2p", kk, "z")] if kk < 4 else [("cfin", kk - 4, col0_) for col0_ in range(0, 8 * KC, 512)] + [("cfin", kk - 4, -1)]
            K.op("dve", lambda e, kk=kk, src=src: e.tensor_copy(out=smix[:, kk, :].rearrange("p (t s) -> p t s", t=4), in_=src[:, 4:8, KP:KC]), r=rk, w=[("smix", kk)])
        tcount = 0
        ocount = 0
        ln_back = []
        for (kind, r_) in blocks4:
            bkey = (kind, r_)
            tiles = [0, 128] if kind == "p" else [None]
            for k0 in tiles:
                rows = 128 if kind == "p" else NS
                xi = tcount % NXB
                oi = tcount % 2
                tcount += 1
                if kind == "p":
                    K.dma("sp", xt[xi][:, :], xp_v[r_, k0:k0 + 128, :], w=[("xt", xi)])
                else:
                    for t_ in range(4):
                        K.dma("sp", xt[xi][16 * t_:16 * (t_ + 1), :], xs_v[t_, :, :], w=[("xt", xi)] if t_ == 0 else [("xt", xi, t_)])
                xkeys = [("xt", xi)] + ([("xt", xi, t_) for t_ in range(1, 4)] if kind == "s" else [])
                banks = []
                for half in range(2):
                    bank = 2 + (ocount % 6)
                    ocount += 1
                    banks.append(bank)
                    for kk in range(8):
                        src = s2p[kk] if kk < 4 else cfin_perm[kk - 4]
                        if kind == "p":
                            lt = src[:, r_, k0:k0 + 128]
                            rk = [("s2p", kk, bkey)] if kk < 4 else [("cfin", kk - 4, col0_) for col0_ in range(0, 8 * KC, 512)] + [("cfin", kk - 4, -1)]
                        else:
                            lt = smix[:, kk, :]
                            rk = [("smix", kk)]
                        K.op("pe", lambda e, lt=lt, kk=kk, half=half, bank=bank: e.matmul(out=ps[bank][0:rows, 0:512], lhsT=lt, rhs=wo_bf[:, kk, 512 * half:512 * (half + 1)], start=(kk == 0), stop=False),
                             r=rk + [("wo_bf", kk)], w=[("ps", bank)], signal=False)
                    K.op("pe", lambda e, half=half, bank=bank: e.matmul(out=ps[bank][0:rows, 0:512], lhsT=aI[0:rows, 0:rows], rhs=xt[xi][0:rows, 512 * half:512 * (half + 1)], start=False, stop=True),
                         r=xkeys + ["aI"], w=[("ps", bank)], signal=True)
                for half in range(2):
                    K.op("dve", lambda e, half=half, bank=banks[half]: e.bn_stats(out=st6[oi][0:rows, half, :], in_=ps[bank][0:rows, 0:512]), r=[("ps", banks[half])], w=[("st6", oi, half)])
                K.op("dve", lambda e: e.bn_aggr(out=mv[oi][0:rows, :], in_=st6[oi][0:rows, :, :].rearrange("p a b -> p (a b)")), r=[("st6", oi, 0), ("st6", oi, 1)], w=[("mv", oi)])
                K.op("act", lambda e: e.activation(out=ve[oi][0:rows, :], in_=mv[oi][0:rows, 1:2], func=AF.Sqrt, bias=epsc[0:rows, :]), r=[("mv", oi), "epsc"], w=[("ve", oi)])
                for th in ln_back:
                    th()
                ln_back.clear()
                K.op("dve", lambda e: e.reciprocal(out=rsd[oi][0:rows, :], in_=ve[oi][0:rows, :]), r=[("ve", oi)], w=[("rsd", oi)])
                K.op("dve", lambda e: e.scalar_tensor_tensor(out=nbv[oi][0:rows, :], in0=mv[oi][0:rows, 0:1], scalar=-1.0, in1=rsd[oi][0:rows, :], op0=ALU.mult, op1=ALU.mult),
                     r=[("mv", oi), ("rsd", oi)], w=[("nbv", oi)])
                for half in range(2):
                    K.op("act", lambda e, half=half, bank=banks[half]: e.activation(out=yn[oi][0:rows, 512 * half:512 * (half + 1)], in_=ps[bank][0:rows, 0:512], func=AF.Identity, bias=nbv[oi][0:rows, :], scale=rsd[oi][0:rows, :]),
                         r=[("ps", banks[half]), ("nbv", oi), ("rsd", oi)], w=[("yn", oi, half)])
                def back(oi=oi, rows=rows, kind=kind, r_=r_, k0=k0):
                    K.op("dve", lambda e: e.tensor_tensor(out=yn[oi][0:rows, :], in0=yn[oi][0:rows, :], in1=gpb[0:rows, :], op=ALU.mult), r=[("yn", oi, 0), ("yn", oi, 1), "gpb"], w=[("yn", oi, 2)])
                    K.op("dve", lambda e: e.tensor_tensor(out=yo[oi][0:rows, :], in0=yn[oi][0:rows, :], in1=bpb[0:rows, :], op=ALU.add), r=[("yn", oi, 2), "bpb"], w=[("yo", oi)])
                    if kind == "p":
                        K.dma("pool", yp_v[r_, k0:k0 + 128, :], yo[oi][:, :], r=[("yo", oi)])
                    else:
                        for t_ in range(4):
                            K.dma("pool", ys_v[t_, :, :], yo[oi][16 * t_:16 * (t_ + 1), :], r=[("yo", oi)])
                ln_back.append(back)
        for th in ln_back:
            th()
        ln_back.clear()
        if dbg:
            for c in range(4):
                K.dma("pool", dbg_out["s2p"][128 * c:128 * (c + 1), :], s2p[c][:, :, :].rearrange("p r k -> p (r k)"), r=[("s2p", c, bk) for bk in [("p", r_) for r_ in range(8)] + [("s", 4), "z"]])
        K.barrier()

    K.barrier(["sp"])
    es.close()
    return nc


def _prep_inputs(inp):
    f = lambda k: np.ascontiguousarray(np.asarray(inp[k], np.float32))
    x_prompt = f("x_prompt")
    x_sample = f("x_sample")
    sre = f("state_ssm_re")[0]
    sim = f("state_ssm_im")[0]
    scv = f("state_conv")[0]
    b_in = f("b_in")[0]
    vecs = np.concatenate([
        b_in.reshape(20, 128), f("b_glu")[0].reshape(4, 128), f("b_dw")[0].reshape(4, 128),
        f("g_conv_ln")[0].reshape(4, 128), f("b_conv_ln")[0].reshape(4, 128), f("b_pw2")[0].reshape(4, 128)], 0).T
    wdwT = f("w_dw")[0].reshape(31, 4, 128).transpose(2, 1, 0)
    lam_re = f("lam_re")[0]
    lam_im = f("lam_im")[0]
    lamT = np.stack([lam_re.T, lam_im.T], 1)
    lamT = np.concatenate([lamT, lamT], 0)
    br = f("b_re")[0].transpose(1, 0, 2)
    bi = f("b_im")[0].transpose(1, 0, 2)
    cr = f("c_re")[0].transpose(2, 0, 1)
    ci = f("c_im")[0].transpose(2, 0, 1)
    wdw = f("w_dw")[0]
    wdg = np.zeros((8, 16, 32, 5, 8), np.float32)
    for s_ in range(8):
        for d_ in range(5):
            for r_ in range(8):
                tau = 8 * d_ + r_ - s_
                if 0 <= tau <= 30:
                    wdg[s_, :, :, d_, r_] = wdw[30 - tau].reshape(32, 16).T
    sccol = np.stack([np.tile(f(k)[0].reshape(32, 16).T, (8, 1)) for k in ("b_dw", "g_conv_ln", "b_conv_ln")], 1)
    shared = {
        "w_in": f("w_in")[0], "w_glu": f("w_glu")[0], "w_pw2": f("w_pw2")[0], "w_out": f("w_out")[0],
        "vecs": np.ascontiguousarray(vecs), "wdwT": np.ascontiguousarray(wdwT),
        "gpost": np.ascontiguousarray(np.broadcast_to(f("g_post")[0].reshape(1, DM), (128, DM))), "bpost": np.ascontiguousarray(np.broadcast_to(f("b_post")[0].reshape(1, DM), (128, DM))),
        "ident": np.eye(128, dtype=np.float32),
        "lamT": np.ascontiguousarray(lamT), "logdt": np.ascontiguousarray(np.broadcast_to(f("log_dt")[0].reshape(1, 32), (128, 32))),
        "BA": np.ascontiguousarray(np.concatenate([br, bi], 0)), "BBraw": np.ascontiguousarray(np.concatenate([bi, br], 0)),
        "CAraw": np.ascontiguousarray(np.concatenate([cr, ci], 0)), "CBraw": np.ascontiguousarray(np.concatenate([ci, cr], 0)),
        "dskT": np.ascontiguousarray(f("d_skip")[0].reshape(32, 16).T),
        "wdg": np.ascontiguousarray(wdg.reshape(128, 32 * 5 * 8)), "sccol": np.ascontiguousarray(sccol.astype(np.float32)),
    }
    in_maps = []
    for i in range(NCORES):
        sl = slice(NSEQ * i, NSEQ * (i + 1))
        hre = sre[sl].transpose(2, 1, 0)
        him = sim[sl].transpose(2, 1, 0)
        m = dict(shared)
        m["xp"] = x_prompt[i]
        m["xs"] = np.ascontiguousarray(x_sample[sl].reshape(NS, DM))
        m["h0A"] = np.ascontiguousarray(np.concatenate([hre, him], 0))
        m["h0Braw"] = np.ascontiguousarray(np.concatenate([him, hre], 0))
        m["stT"] = np.ascontiguousarray(scv[sl].transpose(2, 0, 1))
        pad = np.zeros((NSEQ, 40, 512), np.float32)
        pad[:, 6:36, :] = scv[sl]
        m["stP"] = np.ascontiguousarray(pad.reshape(NSEQ, 5, 8, 512).transpose(3, 2, 0, 1).reshape(512, 8, 5 * NSEQ))
        in_maps.append(m)
    return in_maps


def _assemble(results):
    y_p = np.stack([r["y_p"] for r in results], 0)
    y_s = np.concatenate([r["y_s"].reshape(NSEQ, 4, DM) for r in results], 0)
    hp = np.stack([r["hfin_p"] for r in results], 0)
    re_p = hp[:, 0:64, :].transpose(0, 2, 1)[None]
    im_p = hp[:, 64:128, :].transpose(0, 2, 1)[None]
    cv_p = np.stack([r["ncvT_p"].T for r in results], 0)[None]
    hs = np.concatenate([r["hnew_s"].transpose(2, 1, 0) for r in results], 0)
    re_s = hs[:, :, 0:64][None]
    im_s = hs[:, :, 64:128][None]
    cv_s = np.concatenate([r["ncvT_s"].reshape(512, NSEQ, 30).transpose(1, 2, 0) for r in results], 0)[None]
    c = lambda a: np.ascontiguousarray(a.astype(np.float32))
    return (c(y_p), c(y_s), c(re_p), c(im_p), c(cv_p), c(re_s), c(im_s), c(cv_s))


_NC_CACHE = {}


def kernel(**inputs):
    in_maps = _prep_inputs(inputs)
    if "nc" not in _NC_CACHE:
        _NC_CACHE["nc"] = build_program()
    res = run_bass_kernel_spmd(_NC_CACHE["nc"], in_maps, core_ids=list(range(NCORES)))
    return _assemble(res.results)
```
